# Optimizing a Trainium2 kernel written in Bass

```python
import math
import jax, jax.numpy as jnp
from jax import lax

D_MODEL = 1024
BATCH = 16
SEQ = 256
DEPTH = 4
DEC_BATCH = 2
DEC_SEQ = 4096
PAST_LEN = 512

GRID_W = 64
HEAD_DIM = 64
A_HEADS = 8
A_DK = HEAD_DIM
A_DV = HEAD_DIM
A_W = A_HEADS * A_DV
CONV_K = 5
CHUNK = 64
B_HEADS = 8
B_KV = 2
C_HEADS = 8
C_KV = 2
WINDOW = 128
QBLOCK = 128
D_FF = 4 * D_MODEL
ROPE_THETA = 10000.0
EPS = 1e-6
IN_SIZES = (3 * A_W, A_W, 2 * A_HEADS, 2 * A_HEADS,
            B_HEADS * HEAD_DIM, B_KV * HEAD_DIM, B_KV * HEAD_DIM,
            C_HEADS * HEAD_DIM, C_KV * HEAD_DIM, C_KV * HEAD_DIM, 3 * D_MODEL)
IN_COLS = sum(IN_SIZES)

kernel_name = "hybrid_flow_prefix_gdn_gqa_swa"


def _rmsnorm(x, w):
    xf = x.astype(jnp.float32)
    y = xf * lax.rsqrt(jnp.mean(xf * xf, axis=-1, keepdims=True) + EPS)
    return (y * w.astype(jnp.float32)).astype(x.dtype)


def _l2norm(x):
    xf = x.astype(jnp.float32)
    return xf * lax.rsqrt(jnp.sum(xf * xf, axis=-1, keepdims=True) + EPS)


def _mod_norm(x, w, shift, scale):
    return _rmsnorm(x, w) * (1 + scale) + shift


def _split_cols(proj):
    parts, off = [], 0
    for size in IN_SIZES:
        parts.append(proj[..., off:off + size])
        off += size
    return parts


def _heads(x, n_heads):
    return x.reshape(x.shape[0], x.shape[1], n_heads, HEAD_DIM)


def _axial_rope_tables(n_tokens):
    rows = n_tokens // GRID_W
    row_id = jnp.repeat(jnp.arange(rows, dtype=jnp.float32), GRID_W)
    col_id = jnp.tile(jnp.arange(GRID_W, dtype=jnp.float32), rows)
    n_freq = HEAD_DIM // 4
    inv_freq = ROPE_THETA ** (-jnp.arange(n_freq, dtype=jnp.float32) / n_freq)
    ang = jnp.concatenate([row_id[:, None] * inv_freq, col_id[:, None] * inv_freq], axis=-1)
    return jnp.cos(ang), jnp.sin(ang)


def _rope(x, cos, sin):
    half = HEAD_DIM // 2
    xf = x.astype(jnp.float32)
    x1, x2 = xf[..., :half], xf[..., half:]
    cs, sn = cos[None, :, None, :], sin[None, :, None, :]
    return jnp.concatenate([x1 * cs - x2 * sn, x2 * cs + x1 * sn], axis=-1).astype(x.dtype)


def _short_conv(x, w):
    ch = x.shape[-1]
    return lax.conv_general_dilated(x, w[:, None, :].astype(x.dtype), (1,),
                                    [(CONV_K // 2, CONV_K // 2)],
                                    dimension_numbers=('NWC', 'WIO', 'NWC'),
                                    feature_group_count=ch)


def _gdn_features(qkv_raw, beta_raw, alpha_raw, conv_w, a_log, dt_bias):
    qkv = jax.nn.silu(_short_conv(qkv_raw, conv_w))
    bn, t = qkv.shape[0], qkv.shape[1]
    q, k, v = jnp.split(qkv, 3, axis=-1)
    q = _l2norm(q.reshape(bn, t, A_HEADS, A_DK)) * (A_DK ** -0.5)
    k = _l2norm(k.reshape(bn, t, A_HEADS, A_DK))
    v = v.reshape(bn, t, A_HEADS, A_DV)
    beta = jax.nn.sigmoid(beta_raw.astype(jnp.float32)).reshape(bn, t, 2, A_HEADS)
    g = -jnp.exp(a_log.astype(jnp.float32)) * jax.nn.softplus(
        alpha_raw.astype(jnp.float32).reshape(bn, t, 2, A_HEADS) + dt_bias.astype(jnp.float32))
    return q, k, v, beta, g


def _chunked_delta(q, k, v, beta, g, s0):
    bn, t, h, dk = q.shape
    dv = v.shape[-1]
    n = t // CHUNK

    def chunks4(x):
        return x.astype(jnp.float32).reshape(bn, n, CHUNK, h, x.shape[-1]).transpose(1, 0, 3, 2, 4)

    def chunks3(x):
        return x.astype(jnp.float32).reshape(bn, n, CHUNK, h).transpose(1, 0, 3, 2)

    q_, k_, v_ = chunks4(q), chunks4(k), chunks4(v)
    b_, g_ = chunks3(beta), chunks3(g)
    gc = jnp.cumsum(g_, axis=-1)
    idx = jnp.arange(CHUNK)
    incl = idx[:, None] >= idx[None, :]
    strict = idx[:, None] > idx[None, :]
    diff = gc[..., :, None] - gc[..., None, :]
    decay = jnp.where(incl, jnp.exp(jnp.where(incl, diff, 0.0)), 0.0)
    kk = jnp.einsum('nbhid,nbhjd->nbhij', k_, k_)
    a_mat = jnp.where(strict, b_[..., :, None] * kk * decay, 0.0) + jnp.eye(CHUNK, dtype=jnp.float32)
    eg = jnp.exp(gc)
    rhs = jnp.concatenate([b_[..., None] * v_, (b_ * eg)[..., None] * k_], axis=-1)
    sol = lax.linalg.triangular_solve(a_mat, rhs, left_side=True, lower=True, unit_diagonal=True)
    u_p, w = sol[..., :dv], sol[..., dv:]
    attn = jnp.einsum('nbhid,nbhjd->nbhij', q_, k_) * decay
    q_g = q_ * eg[..., None]
    k_g = k_ * jnp.exp(gc[..., -1:] - gc)[..., None]
    g_last = jnp.exp(gc[..., -1])

    def step(s, xs):
        up_c, w_c, q_c, a_c, k_c, gl = xs
        u = up_c - jnp.einsum('bhcd,bhde->bhce', w_c, s)
        o = jnp.einsum('bhcd,bhde->bhce', q_c, s) + jnp.einsum('bhij,bhje->bhie', a_c, u)
        s_new = gl[..., None, None] * s + jnp.einsum('bhcd,bhce->bhde', k_c, u)
        return s_new, o

    s_fin, o = lax.scan(step, s0.astype(jnp.float32), (u_p, w, q_g, attn, k_g, g_last))
    o = o.transpose(1, 0, 3, 2, 4).reshape(bn, t, h, dv)
    return o, s_fin


def _gdn_bidir(q, k, v, beta, g, s_fwd, s_bwd):
    o_f, sf = _chunked_delta(q, k, v, beta[:, :, 0], g[:, :, 0], s_fwd)
    flip = lambda a: jnp.flip(a, axis=1)
    o_b, sb = _chunked_delta(flip(q), flip(k), flip(v), flip(beta[:, :, 1]), flip(g[:, :, 1]), s_bwd)
    return o_f + flip(o_b), sf, sb


def _gated_out(o, z, w):
    bn, t = o.shape[0], o.shape[1]
    on = _rmsnorm(o, w)
    zf = jax.nn.silu(z.astype(jnp.float32)).reshape(bn, t, A_HEADS, A_DV)
    return (on * zf).reshape(bn, t, A_W).astype(z.dtype)


def _attend_dense(q, k, v, sink):
    bn, t, hq, hd = q.shape
    kv = k.shape[2]
    grp = hq // kv
    nb = t // QBLOCK
    scale = hd ** -0.5
    qb = q.reshape(bn, nb, QBLOCK, kv, grp, hd).transpose(1, 0, 2, 3, 4, 5)

    def blk(qi):
        s = jnp.einsum('bqkgd,bskd->bkgqs', qi, k).astype(jnp.float32) * scale
        if sink is not None:
            sk = jnp.broadcast_to(sink.astype(jnp.float32).reshape(kv, grp)[None, :, :, None, None],
                                  s.shape[:-1] + (1,))
            s = jnp.concatenate([s, sk], axis=-1)
        p = jax.nn.softmax(s, axis=-1)
        if sink is not None:
            p = p[..., :-1]
        o = jnp.einsum('bkgqs,bskd->bqkgd', p.astype(v.dtype), v)
        return o.reshape(bn, QBLOCK, hq * hd)

    out = lax.map(blk, qb)
    return out.transpose(1, 0, 2, 3).reshape(bn, t, hq * hd)


def _attend_window(q, k, v, k_ctx, v_ctx, sink):
    bn, t, hq, hd = q.shape
    kv = k.shape[2]
    grp = hq // kv
    nb = t // QBLOCK
    n_ctx = k_ctx.shape[1]
    span = QBLOCK + 2 * WINDOW
    scale = hd ** -0.5
    pad = ((0, 0), (WINDOW, WINDOW), (0, 0), (0, 0))
    kp, vp = jnp.pad(k, pad), jnp.pad(v, pad)
    qb = q.reshape(bn, nb, QBLOCK, kv, grp, hd).transpose(1, 0, 2, 3, 4, 5)
    rel = (jnp.arange(span) - WINDOW)[None, :] - jnp.arange(QBLOCK)[:, None]
    band = jnp.abs(rel) <= WINDOW
    sink_l = sink.astype(jnp.float32).reshape(kv, grp)[None, :, :, None, None]

    def blk(args):
        qi, bi = args
        start = bi * QBLOCK
        kb = lax.dynamic_slice_in_dim(kp, start, span, axis=1)
        vb = lax.dynamic_slice_in_dim(vp, start, span, axis=1)
        kpos = start - WINDOW + jnp.arange(span)
        mask = band & ((kpos >= 0) & (kpos < t))[None, :]
        s_ctx = jnp.einsum('bqkgd,bskd->bkgqs', qi, k_ctx).astype(jnp.float32) * scale
        s_loc = jnp.einsum('bqkgd,bskd->bkgqs', qi, kb).astype(jnp.float32) * scale
        s_loc = jnp.where(mask, s_loc, -1e30)
        s_snk = jnp.broadcast_to(sink_l, s_ctx.shape[:-1] + (1,))
        p = jax.nn.softmax(jnp.concatenate([s_ctx, s_loc, s_snk], axis=-1), axis=-1)
        o = (jnp.einsum('bkgqs,bskd->bqkgd', p[..., :n_ctx].astype(v.dtype), v_ctx)
             + jnp.einsum('bkgqs,bskd->bqkgd', p[..., n_ctx:n_ctx + span].astype(v.dtype), vb))
        return o.reshape(bn, QBLOCK, hq * hd)

    out = lax.map(blk, (qb, jnp.arange(nb)))
    return out.transpose(1, 0, 2, 3).reshape(bn, t, hq * hd)


def _mixers_in(h, p):
    (a_qkv, a_z, a_beta, a_alpha, b_q, b_k, b_v, c_q, c_k, c_v, gates) = _split_cols(h @ p['w_in'])
    gdn = _gdn_features(a_qkv, a_beta, a_alpha, p['conv_qkv'], p['a_log'], p['dt_bias'])
    qn = p['qk_norm']
    att_b = (_rmsnorm(_heads(b_q, B_HEADS), qn[0]), _rmsnorm(_heads(b_k, B_KV), qn[1]), _heads(b_v, B_KV))
    att_c = (_rmsnorm(_heads(c_q, C_HEADS), qn[2]), _rmsnorm(_heads(c_k, C_KV), qn[3]), _heads(c_v, C_KV))
    return gdn, a_z, att_b, att_c, gates


def _mixers_out(x, ya, yb, yc, gates, gate1, shift2, scale2, gate2, p):
    ga, gb, gc = jnp.split(jax.nn.sigmoid(gates), 3, axis=-1)
    merged = ga * (ya @ p['w_br_a']) + gb * (yb @ p['w_br_b']) + gc * (yc @ p['w_br_c'])
    x = x + gate1 * (merged @ p['w_o'])
    h2 = _mod_norm(x, p['ln2'], shift2, scale2)
    return x + gate2 * (jnp.square(jax.nn.relu(h2 @ p['w_ff1'])) @ p['w_ff2'])


def _context_layer(x, mod, p):
    sh1, sc1, g1, sh2, sc2, g2 = jnp.split(mod, 6, axis=-1)
    h = _mod_norm(x, p['ln1'], sh1, sc1)
    (q, k, v, beta, g), a_z, (qb, kb, vb), (qc, kc, vc), gates = _mixers_in(h, p)
    zero = jnp.zeros((x.shape[0], A_HEADS, A_DK, A_DV), jnp.float32)
    oa, s_f, s_b = _gdn_bidir(q, k, v, beta, g, zero, zero)
    ya = _gated_out(oa, a_z, p['a_norm'])
    yb = _attend_dense(qb, kb, vb, None)
    yc = _attend_dense(qc, kc, vc, p['sink'])
    x = _mixers_out(x, ya, yb, yc, gates, g1, sh2, sc2, g2, p)
    return x, kb, vb, kc, vc, jnp.stack([s_f, s_b], axis=1).astype(x.dtype)


def _latent_layer(x, mod, p, cos, sin, k_glob, v_glob, k_win, v_win, s_ctx):
    sh1, sc1, g1, sh2, sc2, g2 = jnp.split(mod, 6, axis=-1)
    h = _mod_norm(x, p['ln1'], sh1, sc1)
    (q, k, v, beta, g), a_z, (qb, kb, vb), (qc, kc, vc), gates = _mixers_in(h, p)
    oa, _, _ = _gdn_bidir(q, k, v, beta, g, s_ctx[:, 0], s_ctx[:, 1])
    ya = _gated_out(oa, a_z, p['a_norm'])
    qb, kb = _rope(qb, cos, sin), _rope(kb, cos, sin)
    yb = _attend_dense(qb, jnp.concatenate([k_glob, kb], axis=1), jnp.concatenate([v_glob, vb], axis=1), None)
    qc, kc = _rope(qc, cos, sin), _rope(kc, cos, sin)
    yc = _attend_window(qc, kc, vc, k_win, v_win, p['sink'])
    return _mixers_out(x, ya, yb, yc, gates, g1, sh2, sc2, g2, p)


def setup_inputs(seed: int = 0) -> dict:
    key = jax.random.key(seed)
    ks = jax.random.split(key, 26)
    f32 = jnp.float32

    def nrm(k, shape, scale):
        return scale * jax.random.normal(k, shape, f32)

    dt = jnp.exp(jax.random.uniform(ks[13], (DEPTH, 2, A_HEADS), f32, math.log(1e-3), math.log(1e-1)))
    dt_bias = dt + jnp.log(-jnp.expm1(-dt))
    a_log = jnp.log(jax.random.uniform(ks[14], (DEPTH, 2, A_HEADS), f32, 1.0, 16.0))
    return {
        'x_prompt': nrm(ks[0], (BATCH, SEQ, D_MODEL), 1.0),
        'x_sample': nrm(ks[1], (DEC_BATCH, DEC_SEQ, D_MODEL), 1.0),
        'cache_k_glob': nrm(ks[2], (DEC_BATCH, DEPTH, PAST_LEN, B_KV, HEAD_DIM), 1.0),
        'cache_v_glob': nrm(ks[3], (DEC_BATCH, DEPTH, PAST_LEN, B_KV, HEAD_DIM), 1.0),
        'cache_k_win': nrm(ks[4], (DEC_BATCH, DEPTH, PAST_LEN, C_KV, HEAD_DIM), 1.0),
        'cache_v_win': nrm(ks[5], (DEC_BATCH, DEPTH, PAST_LEN, C_KV, HEAD_DIM), 1.0),
        'state_delta': nrm(ks[6], (DEC_BATCH, DEPTH, 2, A_HEADS, A_DK, A_DV), A_DK ** -0.5),
        'c': nrm(ks[7], (DEC_BATCH, D_MODEL), 1.0),
        'c_ctx': nrm(ks[8], (D_MODEL,), 1.0),
        'w_mod': nrm(ks[9], (DEPTH, D_MODEL, 6 * D_MODEL), 0.5 * D_MODEL ** -0.5),
        'b_mod': nrm(ks[10], (DEPTH, 6 * D_MODEL), 0.02),
        'ln1': 1.0 + nrm(ks[11], (DEPTH, D_MODEL), 0.02),
        'ln2': 1.0 + nrm(ks[12], (DEPTH, D_MODEL), 0.02),
        'w_in': nrm(ks[15], (DEPTH, D_MODEL, IN_COLS), D_MODEL ** -0.5),
        'conv_qkv': nrm(ks[16], (DEPTH, CONV_K, 3 * A_W), CONV_K ** -0.5),
        'a_log': a_log,
        'dt_bias': dt_bias,
        'a_norm': 1.0 + nrm(ks[17], (DEPTH, A_DV), 0.02),
        'qk_norm': 1.0 + nrm(ks[18], (DEPTH, 4, HEAD_DIM), 0.02),
        'sink': nrm(ks[19], (DEPTH, C_HEADS), 0.5),
        'w_br_a': nrm(ks[20], (DEPTH, A_W, D_MODEL), A_W ** -0.5),
        'w_br_b': nrm(ks[21], (DEPTH, B_HEADS * HEAD_DIM, D_MODEL), (B_HEADS * HEAD_DIM) ** -0.5),
        'w_br_c': nrm(ks[22], (DEPTH, C_HEADS * HEAD_DIM, D_MODEL), (C_HEADS * HEAD_DIM) ** -0.5),
        'w_o': nrm(ks[23], (DEPTH, D_MODEL, D_MODEL), D_MODEL ** -0.5),
        'w_ff1': nrm(ks[24], (DEPTH, D_MODEL, D_FF), D_MODEL ** -0.5),
        'w_ff2': nrm(ks[25], (DEPTH, D_FF, D_MODEL), D_FF ** -0.5),
    }


def reference(x_prompt, x_sample, cache_k_glob, cache_v_glob, cache_k_win, cache_v_win, state_delta, c,
              c_ctx, w_mod, b_mod, ln1, ln2, w_in, conv_qkv, a_log, dt_bias, a_norm, qk_norm, sink,
              w_br_a, w_br_b, w_br_c, w_o, w_ff1, w_ff2):
    cos, sin = _axial_rope_tables(x_sample.shape[1])
    xp, xs = x_prompt, x_sample
    new_kg, new_vg, new_kw, new_vw, new_st = [], [], [], [], []
    for l in range(DEPTH):
        p = dict(ln1=ln1[l], ln2=ln2[l], w_in=w_in[l], conv_qkv=conv_qkv[l], a_log=a_log[l],
                 dt_bias=dt_bias[l], a_norm=a_norm[l], qk_norm=qk_norm[l], sink=sink[l],
                 w_br_a=w_br_a[l], w_br_b=w_br_b[l], w_br_c=w_br_c[l], w_o=w_o[l],
                 w_ff1=w_ff1[l], w_ff2=w_ff2[l])
        mod_ctx = jax.nn.silu(c_ctx) @ w_mod[l] + b_mod[l]
        mod_lat = (jax.nn.silu(c) @ w_mod[l] + b_mod[l])[:, None, :]
        xp, kb, vb, kc, vc, st = _context_layer(xp, mod_ctx, p)
        new_kg.append(kb)
        new_vg.append(vb)
        new_kw.append(kc)
        new_vw.append(vc)
        new_st.append(st)
        xs = _latent_layer(xs, mod_lat, p, cos, sin, cache_k_glob[:, l], cache_v_glob[:, l],
                           cache_k_win[:, l], cache_v_win[:, l], state_delta[:, l])
    return (xp, xs, jnp.stack(new_kg, axis=1), jnp.stack(new_vg, axis=1), jnp.stack(new_kw, axis=1),
            jnp.stack(new_vw, axis=1), jnp.stack(new_st, axis=1))
```

```python
import os
import numpy as np
import concourse.bass as bass
import concourse.mybir as mybir
from concourse.bass_utils import run_bass_kernel_spmd

F32 = mybir.dt.float32
BF16 = mybir.dt.bfloat16
F32R = mybir.dt.float32r
ALU = mybir.AluOpType
AF = mybir.ActivationFunctionType

ENGS = ("pe", "act", "dve", "pool", "sp")
DMA_RING = 12


class Buf:
    __slots__ = ("lw", "rd", "excl")

    def __init__(self, excl=False):
        self.lw = None
        self.rd = []
        self.excl = excl


class Op:
    __slots__ = ("eng", "fn", "dma", "deps", "signal", "semval", "waits", "dsem", "dval")

    def __init__(self, eng, fn, dma):
        self.eng = eng
        self.fn = fn
        self.dma = dma
        self.deps = []
        self.signal = False
        self.semval = None
        self.waits = []
        self.dsem = None
        self.dval = None


class Sched:
    def __init__(self, nc):
        self.nc = nc
        self.ops = []
        self.last = {}
        self.pend_dma = []

    def op(self, eng, fn, reads=(), writes=(), dma=False):
        o = Op(eng, fn, dma)
        deps = o.deps
        if any(b.excl for b in reads):
            writes = list(writes) + [b for b in reads if b.excl]
            reads = [b for b in reads if not b.excl]
        for b in reads:
            if b.lw is not None:
                deps.append(b.lw)
        for b in writes:
            if b.lw is not None:
                deps.append(b.lw)
            deps.extend(b.rd)
        for b in reads:
            if not dma:
                b.rd = [x for x in b.rd if x.dma or x.eng != eng]
            b.rd.append(o)
        for b in writes:
            b.lw = o
            b.rd = []
        self.ops.append(o)
        if dma:
            self.pend_dma.append(o)
        else:
            self.last[eng] = o
        return o

    def barrier(self):
        deps = list(self.last.values()) + self.pend_dma
        self.pend_dma = []
        for e in ENGS:
            o = Op(e, lambda en: en.nop(), False)
            o.deps = list(deps)
            self.ops.append(o)
            self.last[e] = o

    def pe(self, fn, reads=(), writes=()):
        return self.op("pe", fn, reads, writes)

    def act(self, fn, reads=(), writes=()):
        return self.op("act", fn, reads, writes)

    def dve(self, fn, reads=(), writes=()):
        return self.op("dve", fn, reads, writes)

    def pool(self, fn, reads=(), writes=()):
        return self.op("pool", fn, reads, writes)

    def dma(self, q, out, in_, reads=(), writes=()):
        return self.op(q, lambda e: e.dma_start(out=out, in_=in_), reads, writes, dma=True)

    def emit(self):
        nc = self.nc
        ops = self.ops
        cnt = {e: 0 for e in ENGS}
        for o in ops:
            for d in o.deps:
                if d.dma:
                    continue
                if d.eng == "pe" and o.eng == "pe" and not o.dma:
                    continue
                d.signal = True
        last = {}
        for o in ops:
            if not o.dma:
                last[o.eng] = o
        for o in last.values():
            o.signal = True
        dq = {e: 0 for e in ENGS}
        dma_hist = {e: [] for e in ENGS}
        for o in ops:
            if o.dma:
                j = dq[o.eng]
                dq[o.eng] += 1
                o.dsem = (o.eng, j % DMA_RING)
                o.dval = 16 * (j // DMA_RING + 1)
                dma_hist[o.eng].append(o)
            elif o.signal:
                cnt[o.eng] += 1
                o.semval = cnt[o.eng]
        known = {e: {} for e in ENGS}
        dcount = {e: 0 for e in ENGS}
        for o in ops:
            kn = known[o.eng]
            need = {}
            if o.dma:
                j = dcount[o.eng]
                dcount[o.eng] += 1
                if j >= DMA_RING:
                    prev = dma_hist[o.eng][j - DMA_RING]
                    need[("d",) + prev.dsem] = prev.dval
            for d in o.deps:
                if d.dma:
                    k = ("d",) + d.dsem
                    v = d.dval
                else:
                    if d.eng == "pe" and o.eng == "pe" and not o.dma:
                        continue
                    k = ("c", d.eng)
                    v = d.semval
                if need.get(k, 0) < v:
                    need[k] = v
            for k, v in need.items():
                if kn.get(k, 0) < v:
                    kn[k] = v
                    o.waits.append((k, v))
        final_waits = []
        for e, o in last.items():
            final_waits.append((("c", e), o.semval))
        for e in ENGS:
            for o in dma_hist[e][-DMA_RING:]:
                final_waits.append((("d",) + o.dsem, o.dval))
        from contextlib import ExitStack
        with ExitStack() as st:
            sems = {}
            for e in ENGS:
                sems[("c", e)] = st.enter_context(nc.semaphore(f"c_{e}"))
            for e in ENGS:
                for r in range(min(DMA_RING, dq[e])):
                    sems[("d", e, r)] = st.enter_context(nc.semaphore(f"d_{e}_{r}"))
            block = st.enter_context(nc.Block())
            per = {e: [o for o in ops if o.eng == e] for e in ENGS}

            def run(engobj, lst, final=None):
                for o in lst:
                    for k, v in o.waits:
                        engobj.wait_ge(sems[k], v)
                    ins = o.fn(engobj)
                    if o.dma:
                        ins.then_inc(sems[("d",) + o.dsem], 16)
                    elif o.signal:
                        ins.then_inc(sems[("c", o.eng)], 1)
                if final:
                    for k, v in final:
                        engobj.wait_ge(sems[k], v)

            @block.tensor
            def _(e):
                run(e, per["pe"])

            @block.scalar
            def _(e):
                run(e, per["act"])

            @block.vector
            def _(e):
                run(e, per["dve"])

            @block.gpsimd
            def _(e):
                run(e, per["pool"])

            @block.sync
            def _(e):
                run(e, per["sp"], final_waits)
        self.ops = []


D = 1024
NCH = 8
HD = 64
TP = 256
PAST = 512
DFF = 4096
IN_COLS = 6688
C_QKV, C_Z, C_BETA, C_ALPHA, C_BQ, C_BK, C_BV, C_CQ, C_CK, C_CV, C_GATE = (
    0, 1536, 2048, 2064, 2080, 2592, 2720, 2848, 3360, 3488, 3616)
EPS = 1e-6
NEG = -30000.0
NCST = 2176
SB_BASE = 16512
SB_TOP = 229344


def make_consts(T_s):
    cst = np.zeros((128, NCST), np.float32)
    cst[:, 0:128] = np.eye(128)
    cst[0:64, 128:192] = 1.0
    cst[64:128, 192:256] = 1.0
    for m in range(128):
        if m % 64 < 32:
            cst[m + 32, 256 + m] = -1.0
        else:
            cst[m - 32, 256 + m] = 1.0
    p = np.arange(64)[:, None]
    f = np.arange(64)[None, :]

    def ms(m):
        return ((p // (2 * m) == f // (2 * m)) & (p % (2 * m) >= m) & (f % (2 * m) < m)).astype(np.float32)

    bdm = (p // 16 == f // 16).astype(np.float32)
    for d in range(2):
        R = (p >= f) if d == 0 else (p <= f)
        RT = R.T
        base = 384 + d * 640
        ms1 = ms(16) if d == 0 else ms(16).T
        ms2 = ms(32) if d == 0 else ms(32).T
        tabs = [RT.astype(np.float32), np.where(R, 0.0, NEG), np.where(RT, 0.0, NEG),
                -(R & (p != f)).astype(np.float32), -(RT & (p != f)).astype(np.float32),
                bdm, ms1, ms1.T, ms2, ms2.T]
        for k, tb in enumerate(tabs):
            cst[0:64, base + k * 64:base + (k + 1) * 64] = tb
    cst[:, 1664:1792] = 1.0
    pk = np.arange(128)[:, None]
    fq = np.arange(128)[None, :]
    cst[:, 1792:1920] = (pk <= fq)
    cst[:, 1920:2048] = 1.0
    cst[:, 2048:2176] = (fq <= pk)
    t = np.arange(T_s)
    row_id = (t // 64).astype(np.float32)
    col_id = (t % 64).astype(np.float32)
    inv_freq = (10000.0 ** (-np.arange(16, dtype=np.float32) / 16)).astype(np.float32)
    ang = np.concatenate([row_id[:, None] * inv_freq, col_id[:, None] * inv_freq], axis=-1)
    fidx = np.arange(128) % 32
    rope = np.stack([np.cos(ang)[:, fidx].T, np.sin(ang)[:, fidx].T]).astype(np.float32)
    return cst, np.ascontiguousarray(rope)


def build(depth, T_s, debug=False, stop=None):
    nc = bass.Bass("TRN2", target_bir_lowering=False)
    Ttot = 2 * TP + T_s
    NB = Ttot // 512
    Tk = Ttot + PAST
    seqs = [(0, TP, 0), (TP, TP, 0), (2 * TP, T_s, 1)]
    NR5 = depth * 5

    def din(name, shape, dt=F32):
        return nc.dram_tensor(name, list(shape), dt, kind="ExternalInput").ap()

    def dout(name, shape, dt=F32):
        return nc.dram_tensor(name, list(shape), dt, kind="ExternalOutput").ap()

    def dscr(name, shape, dt=F32):
        if debug:
            return nc.dram_tensor(name, list(shape), dt, kind="ExternalOutput").ap()
        return nc.dram_tensor(name, list(shape), dt).ap()

    xp_d = din("xp", [2 * TP, D])
    xs_d = din("xs", [T_s, D])
    ck_d = [din("ckg", [depth, PAST, 128]), din("ckw", [depth, PAST, 128])]
    cv_d = [din("cvg", [depth, PAST, 128]), din("cvw", [depth, PAST, 128])]
    sd_d = din("sd", [depth, 2, 8, 64, 64])
    vecs_d = din("vecs", [16, D])
    wmod_d = din("w_mod", [depth, D, 6 * D])
    bmod_d = din("b_mod", [depth, 6 * D])
    win_d = din("w_in", [depth, D, IN_COLS])
    conv_d = din("conv", [NR5, 1536])
    alog_d = din("a_log", [depth, 16])
    dtb_d = din("dt_bias", [depth, 16])
    anorm_d = din("a_norm", [depth, 64])
    qkn_d = din("qk_norm", [depth * 4, 64])
    sink_d = din("sink", [depth, 8])
    wbr_d = [din("w_br_a", [depth, 512, D]), din("w_br_b", [depth, 512, D]), din("w_br_c", [depth, 512, D])]
    wo_d = din("w_o", [depth, D, D])
    wf1_d = din("w_ff1", [depth, D, DFF])
    wf2_d = din("w_ff2", [depth, DFF, D])
    cst_d = din("cst", [128, NCST])
    rope_d = din("rope", [2, 128, T_s])
    yp_d = dout("yp", [2 * TP, D])
    ys_d = dout("ys", [T_s, D])
    nk_d = [dout("nkg", [2, depth, TP, 128]), dout("nkw", [2, depth, TP, 128])]
    nv_d = [dout("nvg", [2, depth, TP, 128]), dout("nvw", [2, depth, TP, 128])]
    nst_d = dout("nst", [2, depth, 2, 8, 64, 64])
    xT_d = dscr("xT", [D, Ttot])
    raw_d = dscr("raw", [1536, Ttot])
    zs_d = dscr("zs", [512, Ttot])
    bg_d = dscr("bg", [Ttot, 32])
    qT_d = [dscr("qTB", [512, Ttot], BF16), dscr("qTC", [512, Ttot], BF16)]
    kT_d = dscr("kT", [2, 128, Tk], BF16)
    vt_d = dscr("vt", [2, Tk, 128], BF16)
    sig_d = dscr("sig", [3 * D, Ttot], BF16)
    gq_d = dscr("gq", [16, 64, Ttot])
    ktok_d = dscr("ktok", [Ttot, 512])
    vtok_d = dscr("vtok", [Ttot, 512])
    oT_d = dscr("oT", [2, 512, Ttot])
    yT_d = dscr("yT", [3, 512, Ttot], BF16)
    h2T_d = dscr("h2T", [D, Ttot], BF16)

    S = Sched(nc)
    uid = [0]

    class Arena:
        def __init__(self, base, top):
            self.base = base
            self.top = top
            self.off = base

        def reset(self):
            self.off = self.base

        def alloc(self, shape, dt):
            n = 1
            for s_ in shape[1:]:
                n *= s_
            nbytes = n * (4 if dt in (F32, F32R) else 2)
            nbytes = (nbytes + 63) // 64 * 64
            assert self.off + nbytes <= self.top, (self.off, nbytes, self.top)
            uid[0] += 1
            t = nc.alloc_sbuf_tensor_at(f"t{uid[0]}", list(shape), dt, offset=self.off)
            self.off += nbytes
            return t

    pers = Arena(SB_BASE, SB_BASE + 16384)
    ar = Arena(SB_BASE + 16384, SB_TOP)

    class TL:
        def __init__(self, shape, dt, nb=1, arena=None):
            self.t = (arena or ar).alloc(shape, dt)
            self.b = [Buf() for _ in range(nb)]

    pst = [nc.alloc_psum_tensor(f"ps{i}", [128, 1024], F32) for i in range(4)]
    pbuf = [Buf(excl=True) for _ in range(8)]

    def bank(i):
        return pst[i // 2][:, (i % 2) * 512:(i % 2) * 512 + 512]

    dbufs = {}

    def db(name, i=0):
        k = (name, i)
        if k not in dbufs:
            dbufs[k] = Buf()
        return dbufs[k]

    def mm(out, lhsT, rhs, start, stop, reads, writes, **kw):
        S.pe(lambda e: e.matmul(out, lhsT=lhsT, rhs=rhs, start=start, stop=stop, **kw), reads, writes)

    def tr(out, in_, ident, reads, writes):
        S.pe(lambda e: e.transpose(out=out, in_=in_, identity=ident), reads, writes)

    cst = TL([128, NCST], F32, arena=pers)
    ones_bf = TL([128, 128], BF16, arena=pers)
    bd_bf = TL([128, 128], BF16, arena=pers)
    mw_bf = TL([128, 384], BF16, arena=pers)
    vecsT = TL([128, 8, 16], F32, arena=pers)
    silT = TL([128, 8, 2], BF16, arena=pers)
    convq = TL([128, 16, NR5], F32, arena=pers)
    convv = TL([128, 4, NR5], F32, arena=pers)
    qkw = TL([128, depth * 4], F32, arena=pers)
    anw = TL([128, depth], F32, arena=pers)
    dtb = TL([128, depth * 16], F32, arena=pers)
    nexpA = TL([128, depth * 16], F32, arena=pers)
    esink = TL([128, depth * 8], F32, arena=pers)
    modT = TL([128, 48, 2], F32, arena=pers)
    A1 = TL([128, 8, 2], F32, arena=pers)
    A2 = TL([128, 8, 2], F32, arena=pers)
    ones2 = TL([128, 2], BF16, arena=pers)
    C = cst.t
    ident = C[:, 0:128]
    rotT = C[:, 256:384]
    ones_f = C[:, 1664:1792]

    def gm(d, k):
        b0 = 384 + d * 640 + k * 64
        return C[0:64, b0:b0 + 64]

    S.dma("sp", cst.t[:], cst_d, writes=cst.b)
    S.act(lambda e: e.copy(out=ones_bf.t[:], in_=C[:, 1664:1792]), cst.b, ones_bf.b)
    S.act(lambda e: e.copy(out=bd_bf.t[:], in_=C[:, 128:256]), cst.b, bd_bf.b)
    S.act(lambda e: e.copy(out=mw_bf.t[:], in_=C[:, 1792:2176]), cst.b, mw_bf.b)
    S.pool(lambda e: e.memset(ones2.t[:], 1.0), (), ones2.b)
    ar.reset()
    vr = TL([16, D], F32)
    cr = TL([NR5, 1536], F32)
    qr = TL([depth * 4, 128], F32)
    anr = TL([depth, 64], F32)
    S.dma("sp", vr.t[:], vecs_d, writes=vr.b)
    S.dma("sp", cr.t[:], conv_d, writes=cr.b)
    S.dma("sp", qr.t[:, 0:64], qkn_d, writes=qr.b)
    S.dma("sp", qr.t[:, 64:128], qkn_d, writes=qr.b)
    S.dma("sp", anr.t[:], anorm_d, writes=anr.b)
    S.dma("sp", dtb.t[:], dtb_d.rearrange("l x -> (l x)").partition_broadcast(128), writes=dtb.b)
    S.dma("sp", nexpA.t[:], alog_d.rearrange("l x -> (l x)").partition_broadcast(128), writes=nexpA.b)
    S.dma("sp", esink.t[:], sink_d.rearrange("l x -> (l x)").partition_broadcast(128), writes=esink.b)
    for c in range(8):
        tr(bank(0)[:, c * 16:(c + 1) * 16], vr.t[0:16, c * 128:(c + 1) * 128], ident[0:16, 0:16], vr.b + cst.b, [pbuf[0]])
    S.dve(lambda e: e.tensor_copy(out=vecsT.t[:], in_=bank(0)[:, 0:128].rearrange("p (c r) -> p c r", r=16)), [pbuf[0]], vecsT.b)
    for j in range(16):
        tr(bank(1)[0:64, j * NR5:(j + 1) * NR5], cr.t[0:NR5, j * 64:(j + 1) * 64], ident[0:NR5, 0:NR5], cr.b + cst.b, [pbuf[1]])
    S.dve(lambda e: e.tensor_copy(out=convq.t[0:64], in_=bank(1)[0:64, 0:16 * NR5].rearrange("p (c r) -> p c r", r=NR5)), [pbuf[1]], convq.b)
    for j in range(4):
        tr(bank(2)[:, j * NR5:(j + 1) * NR5], cr.t[0:NR5, 1024 + j * 128:1024 + (j + 1) * 128], ident[0:NR5, 0:NR5], cr.b + cst.b, [pbuf[2]])
    S.dve(lambda e: e.tensor_copy(out=convv.t[:], in_=bank(2)[:, 0:4 * NR5].rearrange("p (c r) -> p c r", r=NR5)), [pbuf[2]], convv.b)
    tr(bank(3)[:, 0:depth * 4], qr.t[0:depth * 4, :], ident[0:depth * 4, 0:depth * 4], qr.b + cst.b, [pbuf[3]])
    S.dve(lambda e: e.tensor_copy(out=qkw.t[:], in_=bank(3)[:, 0:depth * 4]), [pbuf[3]], qkw.b)
    tr(bank(3)[0:64, 64:64 + depth], anr.t[0:depth, :], ident[0:depth, 0:depth], anr.b + cst.b, [pbuf[3]])
    S.dve(lambda e: e.tensor_copy(out=anw.t[0:64], in_=bank(3)[0:64, 64:64 + depth]), [pbuf[3]], anw.b)
    sg = TL([128, 8, 2], F32)
    S.act(lambda e: e.activation(out=sg.t[:], in_=vecsT.t[:, :, 0:2], func=AF.Exp, scale=-1.0), vecsT.b, sg.b)
    S.act(lambda e: e.activation(out=sg.t[:], in_=sg.t[:], func=AF.Ln, bias=1.0), sg.b, sg.b)
    S.act(lambda e: e.activation(out=sg.t[:], in_=sg.t[:], func=AF.Exp, scale=-1.0), sg.b, sg.b)
    S.dve(lambda e: e.tensor_tensor(out=silT.t[:], in0=sg.t[:], in1=vecsT.t[:, :, 0:2], op=ALU.mult), sg.b + vecsT.b, silT.b)
    S.act(lambda e: e.activation(out=nexpA.t[:], in_=nexpA.t[:], func=AF.Exp), nexpA.b, nexpA.b)
    S.dve(lambda e: e.tensor_scalar(out=nexpA.t[:], in0=nexpA.t[:], scalar1=-1.0, scalar2=None, op0=ALU.mult), nexpA.b, nexpA.b)
    S.act(lambda e: e.activation(out=esink.t[:], in_=esink.t[:], func=AF.Exp), esink.b, esink.b)
    S.barrier()

    ar.reset()
    xin = [TL([128, D], F32) for _ in range(2)]
    xTb = [TL([128, 8, 512], F32) for _ in range(2)]
    for bi in range(NB):
        xo = xTb[bi % 2]
        for tt in range(4):
            t0 = bi * 512 + tt * 128
            xi = xin[tt % 2]
            src = xp_d[t0:t0 + 128, :] if bi == 0 else xs_d[t0 - 512:t0 - 512 + 128, :]
            S.dma("sp", xi.t[:], src, writes=xi.b)
            for c in range(8):
                bk = 2 * (c // 4) + (tt % 2) * 4
                tr(bank(bk)[:, (c % 4) * 128:(c % 4 + 1) * 128], xi.t[:, c * 128:(c + 1) * 128], ident, xi.b + cst.b, [pbuf[bk]])
            for half in range(2):
                bk = 2 * half + (tt % 2) * 4
                eng = S.act if half == 0 else S.dve
                src_ps = bank(bk).rearrange("p (c t) -> p c t", t=128)
                dst = xo.t[:, half * 4:half * 4 + 4, tt * 128:(tt + 1) * 128]
                if half == 0:
                    S.act(lambda e, dst=dst, src_ps=src_ps: e.copy(out=dst, in_=src_ps), [pbuf[bk]], xo.b)
                else:
                    S.dve(lambda e, dst=dst, src_ps=src_ps: e.tensor_copy(out=dst, in_=src_ps), [pbuf[bk]], xo.b)
        S.dma("sp", xT_d.rearrange("(c p) t -> p c t", p=128)[:, :, bi * 512:(bi + 1) * 512], xo.t[:], reads=xo.b, writes=[db("xT", bi)])
    S.barrier()

    def load_w_bf16(dst_tl, src_ap_fn, nchunk):
        for c in range(nchunk):
            S.dma("pool", dst_tl.t[:, c], src_ap_fn(c), writes=dst_tl.b)

    def phase_M(l):
        ar.reset()
        wm = [TL([128, 8, 1536], BF16) for _ in range(2)]
        bm = TL([1, 6 * D], BF16)
        S.dma("pool", bm.t[:], bmod_d[l:l + 1, :], writes=bm.b)
        for q in range(4):
            w = wm[q % 2]
            load_w_bf16(w, lambda c: wmod_d[l, c * 128:(c + 1) * 128, q * 1536:(q + 1) * 1536], 8)
            for gg in range(12):
                g = q * 12 + gg
                o_ = bank(0)[:, 2 * g:2 * g + 2]
                for c in range(8):
                    mm(o_, w.t[:, c, gg * 128:(gg + 1) * 128], silT.t[:, c, :], c == 0, False, w.b + silT.b, [pbuf[0]])
                mm(o_, bm.t[0:1, g * 128:(g + 1) * 128], ones2.t[0:1, :], False, True, bm.b + ones2.b, [pbuf[0]])
        S.dve(lambda e: e.tensor_copy(out=modT.t[:], in_=bank(0)[:, 0:96].rearrange("p (g v) -> p g v", v=2)), [pbuf[0]], modT.b)
        ln1 = vecsT.t[:, :, 2 + l:3 + l].to_broadcast([128, 8, 2])
        ln2 = vecsT.t[:, :, 2 + depth + l:3 + depth + l].to_broadcast([128, 8, 2])
        S.dve(lambda e: e.scalar_tensor_tensor(out=A1.t[:], in0=modT.t[:, 8:16, :], scalar=1.0, in1=ln1, op0=ALU.add, op1=ALU.mult), modT.b + vecsT.b, A1.b)
        S.dve(lambda e: e.scalar_tensor_tensor(out=A2.t[:], in0=modT.t[:, 32:40, :], scalar=1.0, in1=ln2, op0=ALU.add, op1=ALU.mult), modT.b + vecsT.b, A2.b)
        S.barrier()

    def norm_block(xT, hT, mv, Aap, shg, sq, tmp, statb, lnv, rstd):
        for c in range(8):
            s_ = sq[c % 2]
            S.act(lambda e, s_=s_, c=c: e.activation(out=s_.t[:], in_=xT.t[:, c, :], func=AF.Square), xT.b, s_.b)
            mm(bank(statb), ones_bf.t[:], s_.t[:], c == 0, c == 7, ones_bf.b + s_.b, [pbuf[statb]])
        S.act(lambda e: e.activation(out=lnv.t[:], in_=bank(statb), func=AF.Ln, bias=EPS, scale=1.0 / D), [pbuf[statb]], lnv.b)
        S.act(lambda e: e.activation(out=rstd.t[:], in_=lnv.t[:], func=AF.Exp, scale=-0.5), lnv.b, rstd.b)
        for c in range(8):
            t_ = tmp[c % 2]
            S.dve(lambda e, t_=t_, c=c: e.tensor_tensor(out=t_.t[:], in0=xT.t[:, c, :], in1=rstd.t[:], op=ALU.mult), xT.b + rstd.b, t_.b)
            S.act(lambda e, t_=t_, c=c: e.activation(out=hT.t[:, c, :], in_=t_.t[:], func=AF.Identity,
                                                    bias=modT.t[:, shg + c, mv:mv + 1], scale=Aap.t[:, c, mv:mv + 1]),
                  t_.b + modT.b + Aap.b, hT.b)

    def phase_A(l):
        ar.reset()
        win = TL([128, 8, IN_COLS], BF16)
        load_w_bf16(win, lambda c: win_d[l, c * 128:(c + 1) * 128, :], 8)
        xT = [TL([128, 8, 512], F32) for _ in range(2)]
        hT = TL([128, 8, 512], BF16)
        sq = [TL([128, 512], BF16) for _ in range(2)]
        tmp = [TL([128, 512], F32) for _ in range(2)]
        lnv = TL([128, 512], F32)
        rstd = TL([128, 512], F32)
        stf = [TL([128, 512], F32) for _ in range(4)]
        stb = [TL([128, 512], BF16) for _ in range(4)]
        qn = [TL([128, 512], F32) for _ in range(2)]
        t1 = [TL([128, 512], F32) for _ in range(2)]
        t2 = [TL([128, 512], F32) for _ in range(2)]
        cs = [TL([128, 2, 512], F32) for _ in range(2)]
        bgst = [TL([128, 32], F32) for _ in range(2)]
        e1 = [TL([128, 32], F32) for _ in range(2)]
        vst = [TL([128, 256], F32) for _ in range(2)]
        kst = t1
        cnt = {"f": 0, "b": 0, "q": 0, "g": 0, "t": 0}

        def nxt(k, n):
            cnt[k] += 1
            return cnt[k] % n

        def load_x(bi):
            S.dma("sp", xT[bi % 2].t[:], xT_d.rearrange("(c p) t -> p c t", p=128)[:, :, bi * 512:(bi + 1) * 512],
                  reads=[db("xT", bi)], writes=xT[bi % 2].b)

        load_x(0)
        for bi in range(NB):
            mv = 0 if bi == 0 else 1
            x_ = xT[bi % 2]
            if bi + 1 < NB:
                load_x(bi + 1)
            if mv:
                c_ = cs[bi % 2]
                S.dma("sp", c_.t[:], rope_d.rearrange("a p t -> p a t")[:, :, (bi - 1) * 512:bi * 512], writes=c_.b)
            norm_block(x_, hT, mv, A1, 0, sq, tmp, 7, lnv, rstd)
            t0 = bi * 512
            groups = []
            for j in range(16):
                groups.append(("raw", C_QKV + j * 64, 64, j * 64))
            for j in range(4):
                groups.append(("raw", C_QKV + 1024 + j * 128, 128, 1024 + j * 128))
            for j in range(8):
                groups.append(("z", C_Z + j * 64, 64, j * 64))
            for a, (cq, ckk) in enumerate(((C_BQ, C_BK), (C_CQ, C_CK))):
                for m in range(4):
                    groups.append(("q", (cq + m * 64, cq + (m + 4) * 64), 128, (a, m)))
                groups.append(("k", ckk, 128, (a, 0)))
            for j in range(24):
                groups.append(("gate", C_GATE + j * 128, 128, j * 128))
            for gi, (kind, col, M, info) in enumerate(groups):
                bk = gi % 4
                for c in range(8):
                    if kind == "q":
                        mm(bank(bk)[0:64, :], win.t[:, c, col[0]:col[0] + 64], hT.t[:, c, :], c == 0, c == 7, win.b + hT.b, [pbuf[bk]])
                        mm(bank(bk)[64:128, :], win.t[:, c, col[1]:col[1] + 64], hT.t[:, c, :], c == 0, c == 7, win.b + hT.b, [pbuf[bk]],
                           tile_position=(0, 64))
                    else:
                        mm(bank(bk)[0:M, :], win.t[:, c, col:col + M], hT.t[:, c, :], c == 0, c == 7, win.b + hT.b, [pbuf[bk]])
                P = bank(bk)
                if kind == "raw":
                    s_ = stf[nxt("f", 4)]
                    S.act(lambda e, s_=s_, P=P, M=M: e.copy(out=s_.t[0:M, :], in_=P[0:M, :]), [pbuf[bk]], s_.b)
                    S.dma("sp", raw_d[info:info + M, t0:t0 + 512], s_.t[0:M, :], reads=s_.b, writes=[db("raw", bi)])
                elif kind == "z":
                    s_ = stf[nxt("f", 4)]
                    u_ = stf[nxt("f", 4)]
                    S.act(lambda e, s_=s_, P=P: e.activation(out=s_.t[0:64, :], in_=P[0:64, :], func=AF.Exp, scale=-1.0), [pbuf[bk]], s_.b)
                    S.act(lambda e, s_=s_: e.activation(out=s_.t[0:64, :], in_=s_.t[0:64, :], func=AF.Ln, bias=1.0), s_.b, s_.b)
                    S.act(lambda e, s_=s_: e.activation(out=s_.t[0:64, :], in_=s_.t[0:64, :], func=AF.Exp, scale=-1.0), s_.b, s_.b)
                    S.dve(lambda e, s_=s_, u_=u_, P=P: e.tensor_tensor(out=u_.t[0:64, :], in0=P[0:64, :], in1=s_.t[0:64, :], op=ALU.mult),
                          [pbuf[bk]] + s_.b, u_.b)
                    S.dma("sp", zs_d[info:info + 64, t0:t0 + 512], u_.t[0:64, :], reads=u_.b, writes=[db("zs", bi)])
                elif kind == "gate":
                    s_ = stf[nxt("f", 4)]
                    o_ = stb[nxt("b", 4)]
                    S.act(lambda e, s_=s_, P=P: e.activation(out=s_.t[:], in_=P, func=AF.Exp, scale=-1.0), [pbuf[bk]], s_.b)
                    S.act(lambda e, s_=s_: e.activation(out=s_.t[:], in_=s_.t[:], func=AF.Ln, bias=1.0), s_.b, s_.b)
                    S.act(lambda e, s_=s_, o_=o_: e.activation(out=o_.t[:], in_=s_.t[:], func=AF.Exp, scale=-1.0), s_.b, o_.b)
                    S.dma("sp", sig_d[info:info + 128, t0:t0 + 512], o_.t[:], reads=o_.b, writes=[db("sig", bi)])
                else:
                    a, m = info
                    wi = a * 2 + (0 if kind == "q" else 1)
                    s_ = sq[nxt("q", 2)]
                    S.act(lambda e, s_=s_, P=P: e.activation(out=s_.t[:], in_=P, func=AF.Square), [pbuf[bk]], s_.b)
                    sb_ = 4 + nxt("g", 2)
                    mm(bank(sb_), bd_bf.t[:], s_.t[:], True, True, bd_bf.b + s_.b, [pbuf[sb_]])
                    r_ = stf[nxt("f", 4)]
                    S.act(lambda e, r_=r_, sb_=sb_: e.activation(out=r_.t[:], in_=bank(sb_), func=AF.Ln, bias=EPS, scale=1.0 / HD), [pbuf[sb_]], r_.b)
                    S.act(lambda e, r_=r_: e.activation(out=r_.t[:], in_=r_.t[:], func=AF.Exp, scale=-0.5), r_.b, r_.b)
                    q_ = qn[nxt("q", 2)] if False else qn[cnt["q"] % 2]
                    S.dve(lambda e, q_=q_, P=P, r_=r_, wi=wi: e.scalar_tensor_tensor(out=q_.t[:], in0=P, scalar=qkw.t[:, l * 4 + wi:l * 4 + wi + 1],
                                                                              in1=r_.t[:], op0=ALU.mult, op1=ALU.mult),
                          [pbuf[bk]] + r_.b + qkw.b, q_.b)
                    o_ = stb[nxt("b", 4)]
                    if mv:
                        c_ = cs[bi % 2]
                        rb = 6
                        mm(bank(rb), rotT, q_.t[:], True, True, cst.b + q_.b, [pbuf[rb]])
                        a_ = t1[cnt["q"] % 2]
                        b_ = t2[cnt["q"] % 2]
                        S.dve(lambda e, a_=a_, q_=q_, c_=c_: e.tensor_tensor(out=a_.t[:], in0=q_.t[:], in1=c_.t[:, 0, :], op=ALU.mult), q_.b + c_.b, a_.b)
                        S.dve(lambda e, b_=b_, c_=c_: e.tensor_tensor(out=b_.t[:], in0=bank(6), in1=c_.t[:, 1, :], op=ALU.mult), [pbuf[rb]] + c_.b, b_.b)
                        S.pool(lambda e, a_=a_, b_=b_, o_=o_: e.tensor_tensor(out=o_.t[:], in0=a_.t[:], in1=b_.t[:], op=ALU.add), a_.b + b_.b, o_.b)
                    else:
                        S.act(lambda e, q_=q_, o_=o_: e.copy(out=o_.t[:], in_=q_.t[:]), q_.b, o_.b)
                    if kind == "q":
                        S.dma("sp", qT_d[a][m * 128:(m + 1) * 128, t0:t0 + 512], o_.t[:], reads=o_.b, writes=[db("qT%d" % a, bi)])
                    else:
                        kc0 = t0 if bi == 0 else t0 + PAST
                        S.dma("sp", kT_d[a, :, kc0:kc0 + 512], o_.t[:], reads=o_.b, writes=[db("kT", bi)])
                        if bi == 0:
                            for tt in range(4):
                                tr(bank(6)[:, (tt % 4) * 128:(tt % 4 + 1) * 128], q_.t[:, tt * 128:(tt + 1) * 128], ident, q_.b + cst.b, [pbuf[6]])
                            k_ = kst[a]
                            S.dve(lambda e, k_=k_: e.tensor_copy(out=k_.t[:], in_=bank(6)), [pbuf[6]], k_.b)
                            for tt in range(4):
                                S.dma("sp", nk_d[a][tt // 2, l, (tt % 2) * 128:(tt % 2 + 1) * 128, :], k_.t[:, tt * 128:(tt + 1) * 128], reads=k_.b)
            for tt in range(4):
                tb = 4 + (tt % 2)
                P = bank(tb)
                tok = slice(tt * 128, (tt + 1) * 128)
                for (c0, n_, o0) in ((C_BETA, 32, 0), (C_BV, 128, 32), (C_CV, 128, 160)):
                    for c in range(8):
                        mm(P[:, o0:o0 + n_], hT.t[:, c, tok], win.t[:, c, c0:c0 + n_], c == 0, c == 7, win.b + hT.b, [pbuf[tb]])
                g_ = bgst[tt % 2]
                e_ = e1[tt % 2]
                S.dve(lambda e, e_=e_, P=P: e.tensor_tensor(out=e_.t[:, 16:32], in0=P[:, 16:32], in1=dtb.t[:, l * 16:(l + 1) * 16], op=ALU.add), [pbuf[tb]] + dtb.b, e_.b)
                S.act(lambda e, e_=e_: e.activation(out=e_.t[:, 16:32], in_=e_.t[:, 16:32], func=AF.Exp), e_.b, e_.b)
                S.act(lambda e, e_=e_: e.activation(out=e_.t[:, 16:32], in_=e_.t[:, 16:32], func=AF.Ln, bias=1.0), e_.b, e_.b)
                S.dve(lambda e, e_=e_, g_=g_: e.tensor_tensor(out=g_.t[:, 16:32], in0=e_.t[:, 16:32], in1=nexpA.t[:, l * 16:(l + 1) * 16], op=ALU.mult), e_.b + nexpA.b, g_.b)
                S.act(lambda e, e_=e_, P=P: e.activation(out=e_.t[:, 0:16], in_=P[:, 0:16], func=AF.Exp, scale=-1.0), [pbuf[tb]], e_.b)
                S.act(lambda e, e_=e_: e.activation(out=e_.t[:, 0:16], in_=e_.t[:, 0:16], func=AF.Ln, bias=1.0), e_.b, e_.b)
                S.act(lambda e, e_=e_, g_=g_: e.activation(out=g_.t[:, 0:16], in_=e_.t[:, 0:16], func=AF.Exp, scale=-1.0), e_.b, g_.b)
                S.dma("sp", bg_d[t0 + tt * 128:t0 + (tt + 1) * 128, :], g_.t[:], reads=g_.b, writes=[db("bg", bi)])
                v_ = vst[tt % 2]
                S.act(lambda e, v_=v_, P=P: e.copy(out=v_.t[:], in_=P[:, 32:288]), [pbuf[tb]], v_.b)
                kc0 = (t0 if bi == 0 else t0 + PAST) + tt * 128
                for a in range(2):
                    S.dma("pool", vt_d[a, kc0:kc0 + 128, :], v_.t[:, a * 128:(a + 1) * 128], reads=v_.b, writes=[db("vt", bi)])
                    if bi == 0:
                        S.dma("sp", nv_d[a][tt // 2, l, (tt % 2) * 128:(tt % 2 + 1) * 128, :], v_.t[:, a * 128:(a + 1) * 128], reads=v_.b)
        S.barrier()

    def phase_B1(l):
        ar.reset()
        rawt = [TL([128, 520], F32) for _ in range(3)]
        acc = [TL([128, 512], F32) for _ in range(2)]
        fq = [TL([128, 512], F32) for _ in range(2)]
        sqb = [TL([128, 512], BF16) for _ in range(2)]
        rs = [TL([128, 512], F32) for _ in range(2)]
        kn = TL([64, 8, 512], F32)
        qst = [TL([128, 512], F32) for _ in range(2)]
        fv = TL([128, 4, 512], F32)
        tst = [TL([128, 512], F32) for _ in range(2)]
        n_ = [0]
        for bi in range(NB):
            t0 = bi * 512
            segs = [(0, 256, 0, 256), (256, 512, 256, 512)] if bi == 0 else [(0, 512, seqs[2][0] - t0, seqs[2][0] + seqs[2][1] - t0)]
            for ti in range(20):
                M = 64 if ti < 16 else 128
                ch0 = ti * 64 if ti < 16 else 1024 + (ti - 16) * 128
                wt = convq.t[0:64, ti, :] if ti < 16 else convv.t[:, ti - 16, :]
                wtb = convq.b if ti < 16 else convv.b
                n_[0] += 1
                r_ = rawt[n_[0] % 3]
                a_ = acc[n_[0] % 2]
                ceng = S.dve
                for (a0, a1, s0, s1) in segs:
                    lo = max(a0 - 2, s0)
                    hi = min(a1 + 2, s1)
                    off = a0 if len(segs) == 1 else a0 + (4 if a0 else 0)
                    if lo > a0 - 2:
                        S.pool(lambda e, r_=r_, off=off, M=M: e.memset(r_.t[0:M, off:off + 2], 0.0), (), r_.b)
                    if hi < a1 + 2:
                        S.pool(lambda e, r_=r_, off=off, M=M, a0=a0, a1=a1: e.memset(r_.t[0:M, off + (a1 - a0) + 2:off + (a1 - a0) + 4], 0.0), (), r_.b)
                    S.dma("sp", r_.t[0:M, off + (lo - (a0 - 2)):off + (hi - (a0 - 2))], raw_d[ch0:ch0 + M, t0 + lo:t0 + hi],
                          reads=[db("raw", bi), db("raw", max(bi - 1, 0)), db("raw", min(bi + 1, NB - 1))], writes=r_.b)
                    n = a1 - a0
                    for k in range(5):
                        src = r_.t[0:M, off + k:off + k + n]
                        dst = a_.t[0:M, a0:a1]
                        sc = wt[0:M, l * 5 + k:l * 5 + k + 1] if ti >= 16 else convq.t[0:64, ti, l * 5 + k:l * 5 + k + 1]
                        if k == 0:
                            ceng(lambda e, dst=dst, src=src, sc=sc: e.tensor_scalar(out=dst, in0=src, scalar1=sc, scalar2=None, op0=ALU.mult), r_.b + wtb, a_.b)
                        else:
                            ceng(lambda e, dst=dst, src=src, sc=sc: e.scalar_tensor_tensor(out=dst, in0=src, scalar=sc, in1=dst, op0=ALU.mult, op1=ALU.add),
                                 r_.b + wtb + a_.b, a_.b)
                f_ = fq[n_[0] % 2]
                S.act(lambda e, f_=f_, a_=a_, M=M: e.activation(out=f_.t[0:M, :], in_=a_.t[0:M, :], func=AF.Exp, scale=-1.0), a_.b, f_.b)
                S.act(lambda e, f_=f_, M=M: e.activation(out=f_.t[0:M, :], in_=f_.t[0:M, :], func=AF.Ln, bias=1.0), f_.b, f_.b)
                S.act(lambda e, f_=f_, M=M: e.activation(out=f_.t[0:M, :], in_=f_.t[0:M, :], func=AF.Exp, scale=-1.0), f_.b, f_.b)
                if ti >= 16:
                    S.dve(lambda e, f_=f_, a_=a_, ti=ti: e.tensor_tensor(out=fv.t[:, ti - 16, :], in0=f_.t[:], in1=a_.t[:], op=ALU.mult), f_.b + a_.b, fv.b)
                    continue
                S.dve(lambda e, f_=f_, a_=a_: e.tensor_tensor(out=f_.t[0:64, :], in0=f_.t[0:64, :], in1=a_.t[0:64, :], op=ALU.mult), f_.b + a_.b, f_.b)
                s_ = sqb[n_[0] % 2]
                S.act(lambda e, s_=s_, f_=f_: e.activation(out=s_.t[0:64, :], in_=f_.t[0:64, :], func=AF.Square), f_.b, s_.b)
                sb_ = n_[0] % 2
                mm(bank(sb_)[0:64, :], ones_bf.t[0:64, 0:64], s_.t[0:64, :], True, True, ones_bf.b + s_.b, [pbuf[sb_]])
                r2 = rs[n_[0] % 2]
                S.act(lambda e, r2=r2, sb_=sb_: e.activation(out=r2.t[0:64, :], in_=bank(sb_)[0:64, :], func=AF.Ln, bias=EPS), [pbuf[sb_]], r2.b)
                S.act(lambda e, r2=r2: e.activation(out=r2.t[0:64, :], in_=r2.t[0:64, :], func=AF.Exp, scale=-0.5), r2.b, r2.b)
                if ti < 8:
                    o_ = qst[n_[0] % 2]
                    S.dve(lambda e, o_=o_, f_=f_, r2=r2: e.scalar_tensor_tensor(out=o_.t[0:64, :], in0=f_.t[0:64, :], scalar=HD ** -0.5, in1=r2.t[0:64, :],
                                                                              op0=ALU.mult, op1=ALU.mult), f_.b + r2.b, o_.b)
                    S.dma("sp", gq_d[ti, :, t0:t0 + 512], o_.t[0:64, :], reads=o_.b, writes=[db("gq", bi)])
                else:
                    S.dve(lambda e, f_=f_, r2=r2, ti=ti: e.tensor_tensor(out=kn.t[:, ti - 8, :], in0=f_.t[0:64, :], in1=r2.t[0:64, :], op=ALU.mult), f_.b + r2.b, kn.b)
                    S.dma("sp", gq_d[ti, :, t0:t0 + 512], kn.t[:, ti - 8, :], reads=kn.b, writes=[db("gq", bi)])
            for tt in range(4):
                kb, vb = 2 + (tt % 2) * 2, 3 + (tt % 2) * 2
                for h in range(8):
                    tr(bank(kb)[:, h * 64:(h + 1) * 64], kn.t[:, h, tt * 128:(tt + 1) * 128], ident[0:64, 0:64], kn.b + cst.b, [pbuf[kb]])
                for j in range(4):
                    tr(bank(vb)[:, j * 128:(j + 1) * 128], fv.t[:, j, tt * 128:(tt + 1) * 128], ident, fv.b + cst.b, [pbuf[vb]])
                k_ = tst[0]
                v_ = tst[1]
                S.act(lambda e, k_=k_, kb=kb: e.copy(out=k_.t[:], in_=bank(kb)), [pbuf[kb]], k_.b)
                S.dve(lambda e, v_=v_, vb=vb: e.tensor_copy(out=v_.t[:], in_=bank(vb)), [pbuf[vb]], v_.b)
                S.dma("sp", ktok_d[t0 + tt * 128:t0 + (tt + 1) * 128, :], k_.t[:], reads=k_.b, writes=[db("ktok", bi)])
                S.dma("sp", vtok_d[t0 + tt * 128:t0 + (tt + 1) * 128, :], v_.t[:], reads=v_.b, writes=[db("vtok", bi)])
        S.barrier()

    def phase_B23(l):
        ar.reset()
        f3 = [64, 8, 64]

        def T3(dt=F32):
            return TL(f3, dt)

        D_ = []
        for d in range(2):
            o = {}
            o["qk"] = [TL([64, 16, 256], BF16) for _ in range(1)]
            o["ktok"] = [TL([64, 8, 64], F32) for _ in range(2)]
            o["vtok"] = [TL([64, 8, 64], F32) for _ in range(2)]
            o["bg"] = [TL([64, 32], F32) for _ in range(2)]
            tmp_ = [T3() for _ in range(7)]
            o["Gexp"], o["Bexp"], o["XL"], o["XU"], o["dl"], o["du"], o["M1T"] = tmp_
            o["decay"], o["decayT"], o["egB"], o["nbU"], o["nbL"], o["M1"] = tmp_[2], tmp_[3], tmp_[0], tmp_[1], tmp_[4], tmp_[5]
            o["gc"] = TL([64, 8], F32)
            o["eg"] = TL([64, 8], F32)
            o["beg"] = TL([64, 8], F32)
            o["PA"] = [TL([64, 8, 128], F32R) for _ in range(2)]
            o["PW"] = [TL([64, 8, 128], F32R) for _ in range(2)]
            o["Zm1"], o["ZmT1"], o["ZmT2"], o["X"], o["Xp"] = T3(F32R), T3(F32R), T3(F32R), T3(F32R), T3(F32R)
            o["bv"] = T3(F32R)
            o["bek"] = T3(F32R)
            o["up"] = [T3() for _ in range(2)]
            o["wT"] = [T3(BF16) for _ in range(2)]
            o["qgT"] = [T3(BF16) for _ in range(2)]
            o["attnT"] = [T3(BF16) for _ in range(2)]
            o["kg"] = [T3(BF16) for _ in range(2)]
            o["glB"] = [TL([64, 8], F32) for _ in range(2)]
            o["u"] = T3(BF16)
            o["s"] = T3()
            o["s1"] = T3()
            o["sbf"] = T3(BF16)
            o["ost"] = [TL([64, 8, 256], F32) for _ in range(1)]
            D_.append(o)
        ones64 = ones_f[0:64, 0:64]
        id64 = ident[0:64, 0:64]

        def bc_h(ap2):
            return ap2.unsqueeze(2).to_broadcast(f3)

        def bc_m(ap2):
            return ap2.unsqueeze(1).to_broadcast(f3)

        def v3(ap):
            return ap.rearrange("p (h f) -> p h f", f=64)

        for (s0, T, is_s) in seqs:
            si = seqs.index((s0, T, is_s))
            N = T // 64
            for d in range(2):
                o = D_[d]
                if is_s:
                    S.dma("sp", o["s"].t[:], sd_d[l, d].rearrange("h k v -> k h v"), writes=o["s"].b)
                else:
                    S.pool(lambda e, o=o: e.memset(o["s"].t[:], 0.0), (), o["s"].b)
                S.act(lambda e, o=o: e.copy(out=o["sbf"].t[:], in_=o["s"].t[:]), o["s"].b, o["sbf"].b)

            CUT = int(os.environ.get('B23CUT', '99'))

            def b2(d, c, n):
                o = D_[d]
                yield
                PA, PB, PC, PD = bank(4 * d), bank(4 * d + 1), bank(4 * d + 2), bank(4 * d + 3)
                bA, bB, bC, bD = pbuf[4 * d], pbuf[4 * d + 1], pbuf[4 * d + 2], pbuf[4 * d + 3]
                PCD = pst[2 * d + 1][0:64, :].rearrange("p (h f) -> p h f", f=128)
                tok0 = s0 + c * 64
                bi, cb = tok0 // 512, (tok0 % 512) // 64
                r = n % 2
                c4 = (tok0 % 256) // 64
                qk = o["qk"][0]
                if c4 == (0 if d == 0 else 3):
                    hb0 = (tok0 // 256) * 256
                    S.dma("pool", qk.t[:], gq_d.rearrange("j p t -> p j t")[:, :, hb0:hb0 + 256], reads=[db("gq", bi)], writes=qk.b)
                kt, vt, bgt = o["ktok"][r], o["vtok"][r], o["bg"][r]
                S.dma("sp", kt.t[:], ktok_d[tok0:tok0 + 64, :].rearrange("p (h f) -> p h f", f=64), reads=[db("ktok", bi)], writes=kt.b)
                S.dma("sp", vt.t[:], vtok_d[tok0:tok0 + 64, :].rearrange("p (h f) -> p h f", f=64), reads=[db("vtok", bi)], writes=vt.b)
                S.dma("sp", bgt.t[:], bg_d[tok0:tok0 + 64, :], reads=[db("bg", bi)], writes=bgt.b)
                yield
                g = bgt.t[:, 16 + d * 8:24 + d * 8]
                b = bgt.t[:, d * 8:d * 8 + 8]
                Ud, nm, nmT, st, stT = (gm(d, k) for k in range(5))
                last = 63 if d == 0 else 0
                qc = qk.t[:, 0:8, c4 * 64:(c4 + 1) * 64]
                kc = qk.t[:, 8:16, c4 * 64:(c4 + 1) * 64]
                if CUT < 2:
                    return
                mm(PC[0:64, 0:8], Ud, g, True, True, cst.b + bgt.b, [bC])
                S.dve(lambda e: e.tensor_tensor(out=o["Gexp"].t[:], in0=bc_h(g), in1=bc_m(Ud), op=ALU.mult), bgt.b + cst.b, o["Gexp"].b)
                S.pool(lambda e: e.tensor_tensor(out=o["Bexp"].t[:], in0=bc_h(b), in1=bc_m(id64), op=ALU.mult), bgt.b + cst.b, o["Bexp"].b)
                yield
                mm(PA[0:64, :], ones64, o["Gexp"].t[:].rearrange("p h f -> p (h f)"), True, True, cst.b + o["Gexp"].b, [bA])
                mm(PB[0:64, :], ones64, o["Bexp"].t[:].rearrange("p h f -> p (h f)"), True, True, cst.b + o["Bexp"].b, [bB])
                S.act(lambda e: e.copy(out=o["gc"].t[:], in_=PC[0:64, 0:8]), [bC], o["gc"].b)
                yield
                if CUT < 3:
                    return
                gcb = bc_h(o["gc"].t[:])
                S.dve(lambda e: e.tensor_tensor(out=o["XL"].t[:], in0=gcb, in1=bc_m(nm), op=ALU.add), o["gc"].b + cst.b, o["XL"].b)
                S.pool(lambda e: e.tensor_tensor(out=o["XU"].t[:], in0=bc_m(nmT), in1=gcb, op=ALU.subtract), o["gc"].b + cst.b, o["XU"].b)
                yield
                S.dve(lambda e: e.scalar_tensor_tensor(out=o["dl"].t[:], in0=v3(PA[0:64, :]), scalar=-1.0, in1=o["XL"].t[:], op0=ALU.mult, op1=ALU.add),
                      [bA] + o["XL"].b, o["dl"].b)
                S.dve(lambda e: e.tensor_tensor(out=o["du"].t[:], in0=v3(PA[0:64, :]), in1=o["XU"].t[:], op=ALU.add), [bA] + o["XU"].b, o["du"].b)
                S.act(lambda e: e.activation(out=o["egB"].t[:], in_=v3(PA[0:64, :]), func=AF.Exp), [bA], o["egB"].b)
                S.act(lambda e: e.activation(out=o["decay"].t[:], in_=o["dl"].t[:], func=AF.Exp), o["dl"].b, o["decay"].b)
                S.act(lambda e: e.activation(out=o["decayT"].t[:], in_=o["du"].t[:], func=AF.Exp), o["du"].b, o["decayT"].b)
                S.act(lambda e: e.activation(out=o["eg"].t[:], in_=o["gc"].t[:], func=AF.Exp), o["gc"].b, o["eg"].b)
                yield
                S.dve(lambda e: e.tensor_tensor(out=o["nbU"].t[:], in0=v3(PB[0:64, :]), in1=bc_m(stT), op=ALU.mult), [bB] + cst.b, o["nbU"].b)
                S.pool(lambda e: e.tensor_tensor(out=o["nbL"].t[:], in0=bc_h(b), in1=bc_m(st), op=ALU.mult), bgt.b + cst.b, o["nbL"].b)
                yield
                if CUT < 4:
                    return
                for h in range(8):
                    mm(PA[0:64, h * 64:(h + 1) * 64], kc[:, h, :], kc[:, h, :], True, True, qk.b, [bA])
                for h in range(8):
                    mm(PB[0:64, h * 64:(h + 1) * 64], kc[:, h, :], qc[:, h, :], True, True, qk.b, [bB])
                S.dve(lambda e: e.tensor_tensor(out=o["M1"].t[:], in0=v3(PA[0:64, :]), in1=o["decay"].t[:], op=ALU.mult), [bA] + o["decay"].b, o["M1"].b)
                S.dve(lambda e: e.tensor_tensor(out=o["M1T"].t[:], in0=v3(PA[0:64, :]), in1=o["decayT"].t[:], op=ALU.mult), [bA] + o["decayT"].b, o["M1T"].b)
                at = o["attnT"][r]
                S.dve(lambda e: e.tensor_tensor(out=at.t[:], in0=v3(PB[0:64, :]), in1=o["decayT"].t[:], op=ALU.mult), [bB] + o["decayT"].b, at.b)
                yield
                S.pool(lambda e: e.tensor_tensor(out=o["M1"].t[:], in0=o["M1"].t[:], in1=o["nbL"].t[:], op=ALU.mult), o["M1"].b + o["nbL"].b, o["M1"].b)
                S.pool(lambda e: e.tensor_tensor(out=o["M1T"].t[:], in0=o["M1T"].t[:], in1=o["nbU"].t[:], op=ALU.mult), o["M1T"].b + o["nbU"].b, o["M1T"].b)
                yield
                if CUT < 5:
                    return
                P0f, P0Tf = o["M1"], o["M1T"]
                BD, MA1, MA1T, MA2, MA2T = (gm(d, k) for k in range(5, 10))
                PAc, PWc = o["PA"][0], o["PW"][0]
                S.dve(lambda e, PAc=PAc: e.tensor_tensor(out=PAc.t[:, :, 0:64], in0=P0f.t[:], in1=bc_m(BD), op=ALU.mult), P0f.b + cst.b, PAc.b)
                S.pool(lambda e, PWc=PWc: e.tensor_tensor(out=PWc.t[:, :, 0:64], in0=P0Tf.t[:], in1=bc_m(BD), op=ALU.mult), P0Tf.b + cst.b, PWc.b)
                S.dve(lambda e, PAc=PAc: e.tensor_tensor(out=PAc.t[:, :, 64:128], in0=PAc.t[:, :, 0:64], in1=bc_m(id64), op=ALU.add), PAc.b + cst.b, PAc.b)
                S.pool(lambda e, PWc=PWc: e.tensor_tensor(out=PWc.t[:, :, 64:128], in0=PWc.t[:, :, 0:64], in1=bc_m(id64), op=ALU.add), PWc.b + cst.b, PWc.b)
                S.pool(lambda e: e.tensor_tensor(out=o["Zm1"].t[:], in0=P0Tf.t[:], in1=bc_m(MA1T), op=ALU.mult), P0Tf.b + cst.b, o["Zm1"].b)
                S.pool(lambda e: e.tensor_tensor(out=o["ZmT1"].t[:], in0=P0f.t[:], in1=bc_m(MA1), op=ALU.mult), P0f.b + cst.b, o["ZmT1"].b)
                S.pool(lambda e: e.tensor_tensor(out=o["ZmT2"].t[:], in0=P0f.t[:], in1=bc_m(MA2), op=ALU.mult), P0f.b + cst.b, o["ZmT2"].b)
                yield
                PAB = pst[2 * d][0:64, :].rearrange("p (h f) -> p h f", f=128)
                for k in range(4):
                    PAc, PWc = o["PA"][k % 2], o["PW"][k % 2]
                    PAn, PWn = o["PA"][(k + 1) % 2], o["PW"][(k + 1) % 2]
                    if k == 0:
                        for h in range(8):
                            mm(PA[0:64, h * 64:(h + 1) * 64], PWc.t[:, h, 0:64], PAc.t[:, h, 0:64], True, True, PAc.b + PWc.b, [bA])
                        for h in range(8):
                            mm(PC[0:64, h * 64:(h + 1) * 64], PAc.t[:, h, 0:64], PWc.t[:, h, 0:64], True, True, PAc.b + PWc.b, [bC])
                        S.act(lambda e, PAn=PAn: e.copy(out=PAn.t[:, :, 0:64], in_=v3(PA[0:64, :])), [bA], PAn.b)
                        S.dve(lambda e, PWn=PWn: e.tensor_copy(out=PWn.t[:, :, 0:64], in_=v3(PC[0:64, :])), [bC], PWn.b)
                        S.pool(lambda e, PAn=PAn, PAc=PAc: e.tensor_copy(out=PAn.t[:, :, 64:128], in_=PAc.t[:, :, 64:128]), PAc.b, PAn.b)
                        S.pool(lambda e, PWn=PWn, PWc=PWc: e.tensor_copy(out=PWn.t[:, :, 64:128], in_=PWc.t[:, :, 64:128]), PWc.b, PWn.b)
                    elif k < 3:
                        for h in range(8):
                            mm(PAB[:, h, :], PWc.t[:, h, 0:64], PAc.t[:, h, :], True, True, PAc.b + PWc.b, [bA, bB])
                        for h in range(8):
                            mm(PCD[:, h, :], PAc.t[:, h, 0:64], PWc.t[:, h, :], True, True, PAc.b + PWc.b, [bC, bD])
                        for hh in (0, 4):
                            bk1 = [bA] if hh == 0 else [bB]
                            bk2 = [bC] if hh == 0 else [bD]
                            S.act(lambda e, PAn=PAn, hh=hh: e.copy(out=PAn.t[:, hh:hh + 4, 0:64], in_=PAB[:, hh:hh + 4, 0:64]), bk1, PAn.b)
                            S.dve(lambda e, PAn=PAn, PAc=PAc, hh=hh: e.tensor_tensor(out=PAn.t[:, hh:hh + 4, 64:128], in0=PAB[:, hh:hh + 4, 64:128],
                                                                               in1=PAc.t[:, hh:hh + 4, 64:128], op=ALU.add), bk1 + PAc.b, PAn.b)
                            S.act(lambda e, PWn=PWn, hh=hh: e.copy(out=PWn.t[:, hh:hh + 4, 0:64], in_=PCD[:, hh:hh + 4, 0:64]), bk2, PWn.b)
                            S.dve(lambda e, PWn=PWn, PWc=PWc, hh=hh: e.tensor_tensor(out=PWn.t[:, hh:hh + 4, 64:128], in0=PCD[:, hh:hh + 4, 64:128],
                                                                               in1=PWc.t[:, hh:hh + 4, 64:128], op=ALU.add), bk2 + PWc.b, PWn.b)
                    else:
                        for h in range(8):
                            mm(PA[0:64, h * 64:(h + 1) * 64], PWc.t[:, h, 0:64], PAc.t[:, h, 64:128], True, True, PAc.b + PWc.b, [bA])
                        for h in range(8):
                            mm(PC[0:64, h * 64:(h + 1) * 64], PAc.t[:, h, 0:64], PWc.t[:, h, 64:128], True, True, PAc.b + PWc.b, [bC])
                        S.dve(lambda e, PAn=PAn, PAc=PAc: e.tensor_tensor(out=PAn.t[:, :, 64:128], in0=v3(PA[0:64, :]), in1=PAc.t[:, :, 64:128], op=ALU.add),
                              [bA] + PAc.b, PAn.b)
                        S.dve(lambda e, PWn=PWn, PWc=PWc: e.tensor_tensor(out=PWn.t[:, :, 64:128], in0=v3(PC[0:64, :]), in1=PWc.t[:, :, 64:128], op=ALU.add),
                              [bC] + PWc.b, PWn.b)
                    yield
                Tt, Wt = o["PA"][0], o["PW"][0]
                Tv, Wv = Tt.t[:, :, 64:128], Wt.t[:, :, 64:128]
                for h in range(8):
                    mm(PA[0:64, h * 64:(h + 1) * 64], o["ZmT1"].t[:, h, :], Wt.t[:, h, 64:128], True, True, o["ZmT1"].b + Wt.b, [bA])
                for h in range(8):
                    mm(PB[0:64, h * 64:(h + 1) * 64], o["Zm1"].t[:, h, :], Tt.t[:, h, 64:128], True, True, o["Zm1"].b + Tt.b, [bB])
                S.act(lambda e: e.copy(out=o["X"].t[:], in_=v3(PA[0:64, :])), [bA], o["X"].b)
                S.dve(lambda e: e.tensor_copy(out=o["Xp"].t[:], in_=v3(PB[0:64, :])), [bB], o["Xp"].b)
                yield
                for h in range(8):
                    mm(PC[0:64, h * 64:(h + 1) * 64], Tt.t[:, h, 64:128], o["X"].t[:, h, :], True, True, Tt.b + o["X"].b, [bC])
                for h in range(8):
                    mm(PD[0:64, h * 64:(h + 1) * 64], Wt.t[:, h, 64:128], o["Xp"].t[:, h, :], True, True, Wt.b + o["Xp"].b, [bD])
                S.dve(lambda e: e.tensor_tensor(out=Wv, in0=v3(PC[0:64, :]), in1=Wv, op=ALU.add), [bC] + Wt.b, Wt.b)
                S.dve(lambda e: e.tensor_tensor(out=Tv, in0=v3(PD[0:64, :]), in1=Tv, op=ALU.add), [bD] + Tt.b, Tt.b)
                yield
                for h in range(8):
                    mm(PA[0:64, h * 64:(h + 1) * 64], o["ZmT2"].t[:, h, :], Wt.t[:, h, 64:128], True, True, o["ZmT2"].b + Wt.b, [bA])
                S.act(lambda e: e.copy(out=o["X"].t[:], in_=v3(PA[0:64, :])), [bA], o["X"].b)
                for h in range(8):
                    mm(PC[0:64, h * 64:(h + 1) * 64], Tt.t[:, h, 64:128], o["X"].t[:, h, :], True, True, Tt.b + o["X"].b, [bC])
                S.dve(lambda e: e.tensor_tensor(out=Wv, in0=v3(PC[0:64, :]), in1=Wv, op=ALU.add), [bC] + Wt.b, Wt.b)
                if CUT < 6:
                    return
                TT = Wt
                S.dve(lambda e: e.tensor_tensor(out=o["beg"].t[:], in0=b, in1=o["eg"].t[:], op=ALU.mult), bgt.b + o["eg"].b, o["beg"].b)
                S.pool(lambda e: e.tensor_tensor(out=o["bv"].t[:], in0=vt.t[:], in1=bc_h(b), op=ALU.mult), vt.b + bgt.b, o["bv"].b)
                S.pool(lambda e: e.tensor_tensor(out=o["bek"].t[:], in0=kt.t[:], in1=bc_h(o["beg"].t[:]), op=ALU.mult), kt.b + o["beg"].b, o["bek"].b)
                kg, qg, gl = o["kg"][r], o["qgT"][r], o["glB"][r]
                S.pool(lambda e: e.tensor_tensor(out=kg.t[:], in0=kt.t[:], in1=bc_h(o["decayT"].t[:, :, last]), op=ALU.mult), kt.b + o["decayT"].b, kg.b)
                S.dve(lambda e: e.tensor_tensor(out=qg.t[:], in0=qc, in1=o["egB"].t[:], op=ALU.mult), qk.b + o["egB"].b, qg.b)
                S.act(lambda e: e.copy(out=gl.t[:], in_=o["egB"].t[:, :, last]), o["egB"].b, gl.b)
                yield
                for h in range(8):
                    mm(PA[0:64, h * 64:(h + 1) * 64], TT.t[:, h, 64:128], o["bv"].t[:, h, :], True, True, TT.b + o["bv"].b, [bA])
                for h in range(8):
                    mm(PB[0:64, h * 64:(h + 1) * 64], o["bek"].t[:, h, :], TT.t[:, h, 64:128], True, True, TT.b + o["bek"].b, [bB])
                up, wT = o["up"][r], o["wT"][r]
                S.act(lambda e: e.copy(out=up.t[:], in_=v3(PA[0:64, :])), [bA], up.b)
                S.dve(lambda e: e.tensor_copy(out=wT.t[:], in_=v3(PB[0:64, :])), [bB], wT.b)

            def b3(d, c, n):
                o = D_[d]
                yield
                PA, PB, PC = bank(4 * d), bank(4 * d + 1), bank(4 * d + 2)
                bA, bB, bC = pbuf[4 * d], pbuf[4 * d + 1], pbuf[4 * d + 2]
                tok0 = s0 + c * 64
                bi, cb = tok0 // 512, (tok0 % 512) // 64
                r = n % 2
                up, wT, kg, qg, gl, at = o["up"][r], o["wT"][r], o["kg"][r], o["qgT"][r], o["glB"][r], o["attnT"][r]
                for h in range(8):
                    mm(PA[0:64, h * 64:(h + 1) * 64], wT.t[:, h, :], o["sbf"].t[:, h, :], True, True, wT.b + o["sbf"].b, [bA])
                if CUT < 8:
                    return
                S.dve(lambda e: e.tensor_tensor(out=o["u"].t[:], in0=up.t[:], in1=v3(PA[0:64, :]), op=ALU.subtract), up.b + [bA], o["u"].b)
                yield
                if CUT < 9:
                    return
                for h in range(8):
                    mm(PC[0:64, h * 64:(h + 1) * 64], o["sbf"].t[:, h, :], qg.t[:, h, :], True, False, o["sbf"].b + qg.b, [bC])
                    mm(PC[0:64, h * 64:(h + 1) * 64], o["u"].t[:, h, :], at.t[:, h, :], False, True, o["u"].b + at.b, [bC])
                if CUT < 10:
                    return
                for h in range(8):
                    mm(PB[0:64, h * 64:(h + 1) * 64], kg.t[:, h, :], o["u"].t[:, h, :], True, True, kg.b + o["u"].b, [bB])
                if CUT < 11:
                    return
                ost = o["ost"][0]
                c4 = (tok0 % 256) // 64
                S.act(lambda e: e.copy(out=ost.t[:, :, c4 * 64:(c4 + 1) * 64], in_=v3(PC[0:64, :])), [bC], ost.b)
                S.dve(lambda e: e.tensor_tensor(out=o["s1"].t[:], in0=o["s"].t[:], in1=bc_h(gl.t[:]), op=ALU.mult), o["s"].b + gl.b, o["s1"].b)
                S.dve(lambda e: e.tensor_tensor(out=o["s"].t[:], in0=o["s1"].t[:], in1=v3(PB[0:64, :]), op=ALU.add), o["s1"].b + [bB], o["s"].b)
                S.act(lambda e: e.copy(out=o["sbf"].t[:], in_=o["s"].t[:]), o["s"].b, o["sbf"].b)
                if CUT < 12:
                    return
                if c4 == (3 if d == 0 else 0):
                    lo = (tok0 // 256) * 256
                    for h in range(8):
                        S.dma("sp", oT_d[d, h * 64:(h + 1) * 64, lo:lo + 256], ost.t[:, h, :], reads=ost.b, writes=[db("oT", bi)])

            def run_gens(gs):
                while gs:
                    for g_ in list(gs):
                        try:
                            next(g_)
                        except StopIteration:
                            gs.remove(g_)

            for n in range(N):
                run_gens([b2(0, n, n), b2(1, N - 1 - n, n)])
                run_gens([b3(0, n, n), b3(1, N - 1 - n, n)])
            if not is_s:
                for d in range(2):
                    S.dma("sp", nst_d[si, l, d].rearrange("h k v -> k h v"), D_[d]["s"].t[:], reads=D_[d]["s"].b)
        S.barrier()

    def phase_B4(l):
        ar.reset()
        of = [TL([64, 8, 512], F32) for _ in range(2)]
        ob = [TL([64, 8, 512], F32) for _ in range(2)]
        zt = [TL([64, 8, 512], F32) for _ in range(2)]
        sqb = [TL([64, 512], BF16) for _ in range(2)]
        rs = [TL([64, 512], F32) for _ in range(2)]
        yo = [TL([64, 8, 512], BF16) for _ in range(2)]
        for bi in range(NB):
            r = bi % 2
            sl = slice(bi * 512, (bi + 1) * 512)
            for h in range(8):
                S.dma("sp", of[r].t[:, h, :], oT_d[0, h * 64:(h + 1) * 64, sl], reads=[db("oT", bi)], writes=of[r].b)
                S.dma("sp", ob[r].t[:, h, :], oT_d[1, h * 64:(h + 1) * 64, sl], reads=[db("oT", bi)], writes=ob[r].b)
                S.dma("sp", zt[r].t[:, h, :], zs_d[h * 64:(h + 1) * 64, sl], reads=[db("zs", bi)], writes=zt[r].b)
            S.dve(lambda e, r=r: e.tensor_tensor(out=of[r].t[:], in0=of[r].t[:], in1=ob[r].t[:], op=ALU.add), of[r].b + ob[r].b, of[r].b)
            for h in range(8):
                s_ = sqb[h % 2]
                r2 = rs[h % 2]
                S.act(lambda e, s_=s_, h=h, r=r: e.activation(out=s_.t[:], in_=of[r].t[:, h, :], func=AF.Square), of[r].b, s_.b)
                sb_ = h % 2
                mm(bank(sb_)[0:64, :], ones_bf.t[0:64, 0:64], s_.t[:], True, True, ones_bf.b + s_.b, [pbuf[sb_]])
                S.act(lambda e, r2=r2, sb_=sb_: e.activation(out=r2.t[:], in_=bank(sb_)[0:64, :], func=AF.Ln, bias=EPS, scale=1.0 / 64), [pbuf[sb_]], r2.b)
                S.act(lambda e, r2=r2: e.activation(out=r2.t[:], in_=r2.t[:], func=AF.Exp, scale=-0.5), r2.b, r2.b)
                S.dve(lambda e, r2=r2, h=h, r=r: e.scalar_tensor_tensor(out=r2.t[:], in0=of[r].t[:, h, :], scalar=anw.t[0:64, l:l + 1], in1=r2.t[:],
                                                                      op0=ALU.mult, op1=ALU.mult), of[r].b + r2.b + anw.b, r2.b)
                S.pool(lambda e, r2=r2, h=h, r=r: e.tensor_tensor(out=yo[r].t[:, h, :], in0=r2.t[:], in1=zt[r].t[:, h, :], op=ALU.mult), r2.b + zt[r].b, yo[r].b)
            for h in range(8):
                S.dma("sp", yT_d[0, h * 64:(h + 1) * 64, sl], yo[r].t[:, h, :], reads=yo[r].b, writes=[db("yT", bi)])
        S.barrier()

    def phase_C(l):
        ar.reset()
        NKT = Tk // 128
        KT = [TL([128, Tk], BF16) for _ in range(2)]
        V1 = [TL([128, NKT, 2, 65], BF16) for _ in range(2)]
        QT = [TL([128, 4, 512], BF16) for _ in range(2)]
        PTt = [TL([128, 512], BF16) for _ in range(4)]
        ckt = [TL([128, 128], F32) for _ in range(2)]
        kcs = TL([128, 512], BF16)
        rr = [TL([128, 512], F32) for _ in range(2)]
        bcs = [TL([64, 512], F32) for _ in range(2)]
        yst = [TL([64, 512], BF16) for _ in range(2)]
        for a in range(2):
            S.pool(lambda e, a=a: e.memset(V1[a].t[:, :, :, 64:65], 1.0), (), V1[a].b)
            S.dma("sp", KT[a].t[:, 0:512], kT_d[a, :, 0:512], reads=[db("kT", i) for i in range(NB)], writes=KT[a].b)
            S.dma("sp", KT[a].t[:, 1024:Tk], kT_d[a, :, 1024:Tk], reads=[db("kT", i) for i in range(NB)], writes=KT[a].b)
            for kv in range(2):
                S.dma("sp", V1[a].t[:, 0:4, kv, 0:64], vt_d[a, 0:512, kv * 64:(kv + 1) * 64].rearrange("(n p) d -> p n d", p=128),
                      reads=[db("vt", i) for i in range(NB)], writes=V1[a].b)
                S.dma("sp", V1[a].t[:, 8:NKT, kv, 0:64], vt_d[a, 1024:Tk, kv * 64:(kv + 1) * 64].rearrange("(n p) d -> p n d", p=128),
                      reads=[db("vt", i) for i in range(NB)], writes=V1[a].b)
                S.dma("pool", V1[a].t[:, 4:8, kv, 0:64], cv_d[a][l, :, kv * 64:(kv + 1) * 64].rearrange("(n p) d -> p n d", p=128), writes=V1[a].b)
            for kt_ in range(4):
                c_ = ckt[kt_ % 2]
                S.dma("sp", c_.t[:], ck_d[a][l, kt_ * 128:(kt_ + 1) * 128, :], writes=c_.b)
                tr(bank(7)[:, kt_ * 128:(kt_ + 1) * 128], c_.t[:], ident, c_.b + cst.b, [pbuf[7]])
            S.dve(lambda e, a=a: e.tensor_copy(out=KT[a].t[:, 512:1024], in_=bank(7)), [pbuf[7]], KT[a].b)
        cnt = {"s": 0, "p": 0, "acc": 0, "q": 0}
        for (s0, T, is_s) in seqs:
            QB = min(T, 512)
            for qb in range(T // QB):
                q0 = s0 + qb * QB
                bi = q0 // 512
                for a in range(2):
                    cnt["q"] += 1
                    Q = QT[cnt["q"] % 2]
                    S.dma("sp", Q.t[:, :, 0:QB], qT_d[a].rearrange("(m p) t -> p m t", p=128)[:, :, q0:q0 + QB], reads=[db("qT%d" % a, bi)], writes=Q.b)
                    chunks = []
                    if not is_s:
                        chunks = [(s0 // 128 + j, 0, QB, None) for j in range(T // 128)]
                    elif a == 0:
                        chunks = [(4 + j, 0, QB, None) for j in range(4)] + [(8 + j, 0, QB, None) for j in range(T // 128)]
                    else:
                        ctx = [(4 + j, 0, QB, None) for j in range(4)]
                        loc = []
                        for kc in range(T // 128):
                            lo = max((kc - 1) * 128, qb * QB)
                            hi = min((kc + 2) * 128, (qb + 1) * QB)
                            if lo >= hi:
                                continue
                            loc.append((8 + kc, lo - qb * QB, hi - qb * QB, (lo - (kc - 1) * 128)))
                        chunks = ctx[:1] + loc + ctx[1:]
                    nck = len(chunks)
                    items = [(h, ci) for h in range(8) for ci in range(nck)]
                    st_ = {}
                    accb = {}

                    def stage1(h, ci, Q=Q, a=a, chunks=chunks):
                        m, half = h % 4, h // 4
                        pb0 = 64 * half
                        if ci == 0:
                            cnt["acc"] += 1
                            accb[h] = 4 + cnt["acc"] % 2
                        kti, qlo, qhi, mcol = chunks[ci]
                        cnt["s"] += 1
                        sbk = cnt["s"] % 4
                        nq = qhi - qlo
                        mm(bank(sbk)[:, 0:nq], KT[a].t[pb0:pb0 + 64, kti * 128:(kti + 1) * 128], Q.t[pb0:pb0 + 64, m, qlo:qhi], True, True,
                           KT[a].b + Q.b, [pbuf[sbk]])
                        cnt["p"] += 1
                        Pt = PTt[cnt["p"] % 4]
                        S.act(lambda e, Pt=Pt, sbk=sbk, nq=nq: e.activation(out=Pt.t[:, 0:nq], in_=bank(sbk)[:, 0:nq], func=AF.Exp, scale=HD ** -0.5),
                              [pbuf[sbk]], Pt.b)
                        if mcol is not None and not (mcol == 128 and nq == 128):
                            S.dve(lambda e, Pt=Pt, nq=nq, mcol=mcol: e.tensor_tensor(out=Pt.t[:, 0:nq], in0=Pt.t[:, 0:nq], in1=mw_bf.t[:, mcol:mcol + nq], op=ALU.mult),
                                  Pt.b + mw_bf.b, Pt.b)
                        st_[(h, ci)] = Pt

                    def stage2(h, ci, a=a, chunks=chunks, nck=nck, QB=QB, q0=q0, bi=bi):
                        half = h // 4
                        kti, qlo, qhi, mcol = chunks[ci]
                        nq = qhi - qlo
                        Pt = st_.pop((h, ci))
                        ab = accb[h]
                        ACC = bank(ab)
                        mm(ACC[0:65, qlo:qhi], V1[a].t[:, kti, half, :], Pt.t[:, 0:nq], ci == 0, ci == nck - 1, V1[a].b + Pt.b, [pbuf[ab]])
                        if ci != nck - 1:
                            return
                        r_ = rr[h % 2]
                        if a == 1:
                            S.act(lambda e, r_=r_, ACC=ACC, h=h: e.activation(out=r_.t[64:65, 0:QB], in_=ACC[64:65, 0:QB], func=AF.Ln,
                                                                          bias=esink.t[64:65, l * 8 + h:l * 8 + h + 1]), [pbuf[ab]] + esink.b, r_.b)
                        else:
                            S.act(lambda e, r_=r_, ACC=ACC: e.activation(out=r_.t[64:65, 0:QB], in_=ACC[64:65, 0:QB], func=AF.Ln), [pbuf[ab]], r_.b)
                        S.act(lambda e, r_=r_: e.activation(out=r_.t[64:65, 0:QB], in_=r_.t[64:65, 0:QB], func=AF.Exp, scale=-1.0), r_.b, r_.b)
                        bb = 6 + h % 2
                        mm(bank(bb)[0:64, 0:QB], ones_f[64:65, 0:64], r_.t[64:65, 0:QB], True, True, cst.b + r_.b, [pbuf[bb]])
                        bc_ = bcs[h % 2]
                        S.act(lambda e, bc_=bc_, bb=bb: e.copy(out=bc_.t[:, 0:QB], in_=bank(bb)[0:64, 0:QB]), [pbuf[bb]], bc_.b)
                        y_ = yst[h % 2]
                        S.dve(lambda e, y_=y_, ACC=ACC, bc_=bc_: e.tensor_tensor(out=y_.t[:, 0:QB], in0=ACC[0:64, 0:QB], in1=bc_.t[:, 0:QB], op=ALU.mult),
                              [pbuf[ab]] + bc_.b, y_.b)
                        S.dma("sp", yT_d[1 + a, h * 64:(h + 1) * 64, q0:q0 + QB], y_.t[:, 0:QB], reads=y_.b, writes=[db("yT", bi)])

                    LA = 2
                    for i in range(len(items) + LA):
                        if i < len(items):
                            stage1(*items[i])
                        if i >= LA:
                            stage2(*items[i - LA])
        S.barrier()

    def phase_D1(l):
        ar.reset()
        wbr = [TL([128, 4, D], BF16) for _ in range(3)]
        wo = TL([128, 8, D], BF16)
        for j in range(3):
            load_w_bf16(wbr[j], lambda c, j=j: wbr_d[j][l, c * 128:(c + 1) * 128, :], 4)
        load_w_bf16(wo, lambda c: wo_d[l, c * 128:(c + 1) * 128, :], 8)
        yt = [TL([128, 12, 512], BF16) for _ in range(2)]
        sg = [TL([128, 24, 512], BF16) for _ in range(2)]
        xT = [TL([128, 8, 512], F32) for _ in range(2)]
        mg = TL([128, 8, 512], BF16)
        hT = TL([128, 8, 512], BF16)
        ta = [TL([128, 512], F32) for _ in range(2)]
        tb_ = [TL([128, 512], F32) for _ in range(2)]
        tc = [TL([128, 512], F32) for _ in range(2)]
        sq = [TL([128, 512], BF16) for _ in range(2)]
        tmp = [TL([128, 512], F32) for _ in range(2)]
        lnv = TL([128, 512], F32)
        rstd = TL([128, 512], F32)

        def loads(bi):
            r = bi % 2
            sl = slice(bi * 512, (bi + 1) * 512)
            S.dma("sp", yt[r].t[:], yT_d.rearrange("j (c p) t -> p (j c) t", p=128)[:, :, sl], reads=[db("yT", bi)], writes=yt[r].b)
            S.dma("sp", sg[r].t[:], sig_d.rearrange("(c p) t -> p c t", p=128)[:, :, sl], reads=[db("sig", bi)], writes=sg[r].b)
            S.dma("sp", xT[r].t[:], xT_d.rearrange("(c p) t -> p c t", p=128)[:, :, sl], reads=[db("xT", bi)], writes=xT[r].b)

        loads(0)
        for bi in range(NB):
            r = bi % 2
            mv = 0 if bi == 0 else 1
            if bi + 1 < NB:
                loads(bi + 1)
            x_ = xT[r]
            for oc in range(8):
                for j in range(3):
                    for c in range(4):
                        mm(bank(j), wbr[j].t[:, c, oc * 128:(oc + 1) * 128], yt[r].t[:, j * 4 + c, :], c == 0, c == 3, wbr[j].b + yt[r].b, [pbuf[j]])
                a_, b_, c_ = ta[oc % 2], tb_[oc % 2], tc[oc % 2]
                S.dve(lambda e, a_=a_, oc=oc, r=r: e.tensor_tensor(out=a_.t[:], in0=bank(0), in1=sg[r].t[:, oc, :], op=ALU.mult), [pbuf[0]] + sg[r].b, a_.b)
                S.dve(lambda e, b_=b_, oc=oc, r=r: e.tensor_tensor(out=b_.t[:], in0=bank(1), in1=sg[r].t[:, 8 + oc, :], op=ALU.mult), [pbuf[1]] + sg[r].b, b_.b)
                S.dve(lambda e, c_=c_, oc=oc, r=r: e.tensor_tensor(out=c_.t[:], in0=bank(2), in1=sg[r].t[:, 16 + oc, :], op=ALU.mult), [pbuf[2]] + sg[r].b, c_.b)
                S.pool(lambda e, a_=a_, b_=b_: e.tensor_tensor(out=a_.t[:], in0=a_.t[:], in1=b_.t[:], op=ALU.add), a_.b + b_.b, a_.b)
                S.pool(lambda e, a_=a_, c_=c_, oc=oc: e.tensor_tensor(out=mg.t[:, oc, :], in0=a_.t[:], in1=c_.t[:], op=ALU.add), a_.b + c_.b, mg.b)
            for oc in range(8):
                bk = 3 + oc % 2
                for c in range(8):
                    mm(bank(bk), wo.t[:, c, oc * 128:(oc + 1) * 128], mg.t[:, c, :], c == 0, c == 7, wo.b + mg.b, [pbuf[bk]])
                S.dve(lambda e, oc=oc, bk=bk, x_=x_, mv=mv: e.scalar_tensor_tensor(out=x_.t[:, oc, :], in0=bank(bk), scalar=modT.t[:, 16 + oc, mv:mv + 1],
                                                                                 in1=x_.t[:, oc, :], op0=ALU.mult, op1=ALU.add),
                      [pbuf[bk]] + x_.b + modT.b, x_.b)
            norm_block(x_, hT, mv, A2, 24, sq, tmp, 7, lnv, rstd)
            sl = slice(bi * 512, (bi + 1) * 512)
            S.dma("sp", xT_d.rearrange("(c p) t -> p c t", p=128)[:, :, sl], x_.t[:], reads=x_.b, writes=[db("xT", bi)])
            S.dma("sp", h2T_d.rearrange("(c p) t -> p c t", p=128)[:, :, sl], hT.t[:], reads=hT.b, writes=[db("h2T", bi)])
        S.barrier()

    def phase_D2(l, hf):
        ar.reset()
        HH = DFF // 2
        w1 = TL([128, 8, HH], BF16)
        w2 = TL([128, 16, D], BF16)
        load_w_bf16(w1, lambda c: wf1_d[l, c * 128:(c + 1) * 128, hf * HH:(hf + 1) * HH], 8)
        load_w_bf16(w2, lambda c: wf2_d[l, hf * HH + c * 128:hf * HH + (c + 1) * 128, :], 16)
        hT = [TL([128, 8, 512], BF16) for _ in range(2)]
        xT = [TL([128, 8, 512], F32) for _ in range(2)]
        rl = [TL([128, 512], BF16) for _ in range(3)]
        aT = TL([128, 16, 512], BF16)
        final = (l == depth - 1 and hf == 1)
        xo = [TL([128, D], F32) for _ in range(2)] if final else None

        def loads(bi):
            r = bi % 2
            sl = slice(bi * 512, (bi + 1) * 512)
            S.dma("sp", hT[r].t[:], h2T_d.rearrange("(c p) t -> p c t", p=128)[:, :, sl], reads=[db("h2T", bi)], writes=hT[r].b)
            S.dma("sp", xT[r].t[:], xT_d.rearrange("(c p) t -> p c t", p=128)[:, :, sl], reads=[db("xT", bi)], writes=xT[r].b)

        loads(0)
        for bi in range(NB):
            r = bi % 2
            mv = 0 if bi == 0 else 1
            if bi + 1 < NB:
                loads(bi + 1)
            x_ = xT[r]
            for oc in range(16):
                bk = oc % 3
                for c in range(8):
                    mm(bank(bk), w1.t[:, c, oc * 128:(oc + 1) * 128], hT[r].t[:, c, :], c == 0, c == 7, w1.b + hT[r].b, [pbuf[bk]])
                r_ = rl[oc % 3]
                S.act(lambda e, r_=r_, bk=bk: e.activation(out=r_.t[:], in_=bank(bk), func=AF.Relu), [pbuf[bk]], r_.b)
                S.pool(lambda e, r_=r_, oc=oc: e.tensor_tensor(out=aT.t[:, oc, :], in0=r_.t[:], in1=r_.t[:], op=ALU.mult), r_.b, aT.b)
            for oc in range(8):
                bk = 3 + oc % 2
                for c in range(16):
                    mm(bank(bk), w2.t[:, c, oc * 128:(oc + 1) * 128], aT.t[:, c, :], c == 0, c == 15, w2.b + aT.b, [pbuf[bk]])
                S.dve(lambda e, oc=oc, bk=bk, x_=x_, mv=mv: e.scalar_tensor_tensor(out=x_.t[:, oc, :], in0=bank(bk), scalar=modT.t[:, 40 + oc, mv:mv + 1],
                                                                                 in1=x_.t[:, oc, :], op0=ALU.mult, op1=ALU.add),
                      [pbuf[bk]] + x_.b + modT.b, x_.b)
            sl = slice(bi * 512, (bi + 1) * 512)
            if not final:
                S.dma("sp", xT_d.rearrange("(c p) t -> p c t", p=128)[:, :, sl], x_.t[:], reads=x_.b, writes=[db("xT", bi)])
            else:
                for tt in range(4):
                    xo_ = xo[tt % 2]
                    for c in range(8):
                        bk = 5 + (c // 4)
                        tr(bank(bk)[:, (c % 4) * 128:(c % 4 + 1) * 128], x_.t[:, c, tt * 128:(tt + 1) * 128], ident, x_.b + cst.b, [pbuf[bk]])
                    S.act(lambda e, xo_=xo_: e.copy(out=xo_.t[:, 0:512], in_=bank(5)), [pbuf[5]], xo_.b)
                    S.dve(lambda e, xo_=xo_: e.tensor_copy(out=xo_.t[:, 512:1024], in_=bank(6)), [pbuf[6]], xo_.b)
                    t0 = bi * 512 + tt * 128
                    dst = yp_d[t0:t0 + 128, :] if bi == 0 else ys_d[t0 - 512:t0 - 512 + 128, :]
                    S.dma("sp", dst, xo_.t[:], reads=xo_.b)
        S.barrier()

    plist = [("M", phase_M), ("A", phase_A), ("B1", phase_B1), ("B23", phase_B23), ("B4", phase_B4), ("C", phase_C), ("D1", phase_D1),
             ("D2a", lambda l: phase_D2(l, 0)), ("D2b", lambda l: phase_D2(l, 1))]
    done = False
    for l in range(depth):
        for nm_, fn_ in plist:
            if stop is not None and nm_ == stop:
                done = True
                break
            fn_(l)
        if done:
            break
    n_ops = len(S.ops)
    S.emit()
    return nc, n_ops


DEPTH = 4
DEC_SEQ = 4096
_cache = {}


def make_in_maps(inputs, depth, T_s, n_cores=8):
    f = lambda a: np.ascontiguousarray(np.asarray(a, dtype=np.float32))
    cst, rope = make_consts(T_s)
    maps = []
    for core in range(n_cores):
        b = core % 2
        vecs = np.zeros((16, D), np.float32)
        vecs[0] = inputs["c_ctx"]
        vecs[1] = inputs["c"][b]
        vecs[2:2 + depth] = inputs["ln1"]
        vecs[2 + depth:2 + 2 * depth] = inputs["ln2"]
        m = {
            "xp": f(inputs["x_prompt"][2 * core:2 * core + 2]).reshape(2 * TP, D),
            "xs": f(inputs["x_sample"][b]),
            "ckg": f(inputs["cache_k_glob"][b]).reshape(depth, PAST, 128),
            "ckw": f(inputs["cache_k_win"][b]).reshape(depth, PAST, 128),
            "cvg": f(inputs["cache_v_glob"][b]).reshape(depth, PAST, 128),
            "cvw": f(inputs["cache_v_win"][b]).reshape(depth, PAST, 128),
            "sd": f(inputs["state_delta"][b]),
            "vecs": vecs,
            "w_mod": f(inputs["w_mod"]), "b_mod": f(inputs["b_mod"]), "w_in": f(inputs["w_in"]),
            "conv": f(inputs["conv_qkv"]).reshape(depth * 5, 1536),
            "a_log": f(inputs["a_log"]).reshape(depth, 16), "dt_bias": f(inputs["dt_bias"]).reshape(depth, 16),
            "a_norm": f(inputs["a_norm"]), "qk_norm": f(inputs["qk_norm"]).reshape(depth * 4, 64), "sink": f(inputs["sink"]),
            "w_br_a": f(inputs["w_br_a"]), "w_br_b": f(inputs["w_br_b"]), "w_br_c": f(inputs["w_br_c"]),
            "w_o": f(inputs["w_o"]), "w_ff1": f(inputs["w_ff1"]), "w_ff2": f(inputs["w_ff2"]),
            "cst": cst, "rope": rope,
        }
        maps.append(m)
    return maps


def assemble(results, depth, T_s):
    yp = np.concatenate([r["yp"].reshape(2, TP, D) for r in results], axis=0)
    ys = np.stack([results[0]["ys"], results[1]["ys"]], axis=0)
    outs = [yp.astype(np.float32), ys.astype(np.float32)]
    for nm_ in ("nkg", "nvg", "nkw", "nvw"):
        outs.append(np.concatenate([r[nm_].reshape(2, depth, TP, 2, HD) for r in results], axis=0).astype(np.float32))
    outs.append(np.concatenate([r["nst"] for r in results], axis=0).astype(np.float32))
    return tuple(outs)


def kernel(**inputs):
    depth, T_s = DEPTH, DEC_SEQ
    key = (depth, T_s)
    if key not in _cache:
        _cache[key] = build(depth, T_s)[0]
    nc = _cache[key]
    maps = make_in_maps(inputs, depth, T_s)
    res = run_bass_kernel_spmd(nc, maps, core_ids=list(range(8)))
    return assemble(res.results, depth, T_s)
```

```python
import os
import numpy as np
import concourse.bass as bass
import concourse.mybir as mybir
from concourse.bass_utils import run_bass_kernel_spmd

F32 = mybir.dt.float32
BF16 = mybir.dt.bfloat16
F32R = mybir.dt.float32r
ALU = mybir.AluOpType
AF = mybir.ActivationFunctionType

ENGS = ("pe", "act", "dve", "pool", "sp")
DMA_RING = 12


class Buf:
    __slots__ = ("lw", "rd", "excl")

    def __init__(self, excl=False):
        self.lw = None
        self.rd = []
        self.excl = excl


class Op:
    __slots__ = ("eng", "fn", "dma", "deps", "signal", "semval", "waits", "dsem", "dval")

    def __init__(self, eng, fn, dma):
        self.eng = eng
        self.fn = fn
        self.dma = dma
        self.deps = []
        self.signal = False
        self.semval = None
        self.waits = []
        self.dsem = None
        self.dval = None


class Sched:
    def __init__(self, nc):
        self.nc = nc
        self.ops = []
        self.last = {}
        self.pend_dma = []

    def op(self, eng, fn, reads=(), writes=(), dma=False):
        o = Op(eng, fn, dma)
        deps = o.deps
        if any(b.excl for b in reads):
            writes = list(writes) + [b for b in reads if b.excl]
            reads = [b for b in reads if not b.excl]
        for b in reads:
            if b.lw is not None:
                deps.append(b.lw)
        for b in writes:
            if b.lw is not None:
                deps.append(b.lw)
            deps.extend(b.rd)
        for b in reads:
            if not dma:
                b.rd = [x for x in b.rd if x.dma or x.eng != eng]
            b.rd.append(o)
        for b in writes:
            b.lw = o
            b.rd = []
        self.ops.append(o)
        if dma:
            self.pend_dma.append(o)
        else:
            self.last[eng] = o
        return o

    def barrier(self):
        deps = list(self.last.values()) + self.pend_dma
        self.pend_dma = []
        for e in ENGS:
            o = Op(e, lambda en: en.nop(), False)
            o.deps = list(deps)
            self.ops.append(o)
            self.last[e] = o

    def pe(self, fn, reads=(), writes=()):
        return self.op("pe", fn, reads, writes)

    def act(self, fn, reads=(), writes=()):
        return self.op("act", fn, reads, writes)

    def dve(self, fn, reads=(), writes=()):
        return self.op("dve", fn, reads, writes)

    def pool(self, fn, reads=(), writes=()):
        return self.op("pool", fn, reads, writes)

    def dma(self, q, out, in_, reads=(), writes=()):
        return self.op(q, lambda e: e.dma_start(out=out, in_=in_), reads, writes, dma=True)

    def emit(self):
        nc = self.nc
        ops = self.ops
        cnt = {e: 0 for e in ENGS}
        for o in ops:
            for d in o.deps:
                if d.dma:
                    continue
                if d.eng == "pe" and o.eng == "pe" and not o.dma:
                    continue
                d.signal = True
        last = {}
        for o in ops:
            if not o.dma:
                last[o.eng] = o
        for o in last.values():
            o.signal = True
        dq = {e: 0 for e in ENGS}
        dma_hist = {e: [] for e in ENGS}
        for o in ops:
            if o.dma:
                j = dq[o.eng]
                dq[o.eng] += 1
                o.dsem = (o.eng, j % DMA_RING)
                o.dval = 16 * (j // DMA_RING + 1)
                dma_hist[o.eng].append(o)
            elif o.signal:
                cnt[o.eng] += 1
                o.semval = cnt[o.eng]
        known = {e: {} for e in ENGS}
        dcount = {e: 0 for e in ENGS}
        for o in ops:
            kn = known[o.eng]
            need = {}
            if o.dma:
                j = dcount[o.eng]
                dcount[o.eng] += 1
                if j >= DMA_RING:
                    prev = dma_hist[o.eng][j - DMA_RING]
                    need[("d",) + prev.dsem] = prev.dval
            for d in o.deps:
                if d.dma:
                    k = ("d",) + d.dsem
                    v = d.dval
                else:
                    if d.eng == "pe" and o.eng == "pe" and not o.dma:
                        continue
                    k = ("c", d.eng)
                    v = d.semval
                if need.get(k, 0) < v:
                    need[k] = v
            for k, v in need.items():
                if kn.get(k, 0) < v:
                    kn[k] = v
                    o.waits.append((k, v))
        final_waits = []
        for e, o in last.items():
            final_waits.append((("c", e), o.semval))
        for e in ENGS:
            for o in dma_hist[e][-DMA_RING:]:
                final_waits.append((("d",) + o.dsem, o.dval))
        from contextlib import ExitStack
        with ExitStack() as st:
            sems = {}
            for e in ENGS:
                sems[("c", e)] = st.enter_context(nc.semaphore(f"c_{e}"))
            for e in ENGS:
                for r in range(min(DMA_RING, dq[e])):
                    sems[("d", e, r)] = st.enter_context(nc.semaphore(f"d_{e}_{r}"))
            block = st.enter_context(nc.Block())
            per = {e: [o for o in ops if o.eng == e] for e in ENGS}

            def run(engobj, lst, final=None):
                for o in lst:
                    for k, v in o.waits:
                        engobj.wait_ge(sems[k], v)
                    ins = o.fn(engobj)
                    if o.dma:
                        ins.then_inc(sems[("d",) + o.dsem], 16)
                    elif o.signal:
                        ins.then_inc(sems[("c", o.eng)], 1)
                if final:
                    for k, v in final:
                        engobj.wait_ge(sems[k], v)

            @block.tensor
            def _(e):
                run(e, per["pe"])

            @block.scalar
            def _(e):
                run(e, per["act"])

            @block.vector
            def _(e):
                run(e, per["dve"])

            @block.gpsimd
            def _(e):
                run(e, per["pool"])

            @block.sync
            def _(e):
                run(e, per["sp"], final_waits)
        self.ops = []


D = 1024
NCH = 8
HD = 64
TP = 256
PAST = 512
DFF = 4096
IN_COLS = 6688
C_QKV, C_Z, C_BETA, C_ALPHA, C_BQ, C_BK, C_BV, C_CQ, C_CK, C_CV, C_GATE = (
    0, 1536, 2048, 2064, 2080, 2592, 2720, 2848, 3360, 3488, 3616)
EPS = 1e-6
NEG = -30000.0
NCST = 2176
SB_BASE = 16512
SB_TOP = 229344


def make_consts(T_s):
    cst = np.zeros((128, NCST), np.float32)
    cst[:, 0:128] = np.eye(128)
    cst[0:64, 128:192] = 1.0
    cst[64:128, 192:256] = 1.0
    for m in range(128):
        if m % 64 < 32:
            cst[m + 32, 256 + m] = -1.0
        else:
            cst[m - 32, 256 + m] = 1.0
    p = np.arange(64)[:, None]
    f = np.arange(64)[None, :]

    def ms(m):
        return ((p // (2 * m) == f // (2 * m)) & (p % (2 * m) >= m) & (f % (2 * m) < m)).astype(np.float32)

    bdm = (p // 16 == f // 16).astype(np.float32)
    for d in range(2):
        R = (p >= f) if d == 0 else (p <= f)
        RT = R.T
        base = 384 + d * 640
        ms1 = ms(16) if d == 0 else ms(16).T
        ms2 = ms(32) if d == 0 else ms(32).T
        tabs = [RT.astype(np.float32), np.where(R, 0.0, NEG), np.where(RT, 0.0, NEG),
                -(R & (p != f)).astype(np.float32), -(RT & (p != f)).astype(np.float32),
                bdm, ms1, ms1.T, ms2, ms2.T]
        for k, tb in enumerate(tabs):
            cst[0:64, base + k * 64:base + (k + 1) * 64] = tb
    cst[:, 1664:1792] = 1.0
    pk = np.arange(128)[:, None]
    fq = np.arange(128)[None, :]
    cst[:, 1792:1920] = (pk <= fq)
    cst[:, 1920:2048] = 1.0
    cst[:, 2048:2176] = (fq <= pk)
    t = np.arange(T_s)
    row_id = (t // 64).astype(np.float32)
    col_id = (t % 64).astype(np.float32)
    inv_freq = (10000.0 ** (-np.arange(16, dtype=np.float32) / 16)).astype(np.float32)
    ang = np.concatenate([row_id[:, None] * inv_freq, col_id[:, None] * inv_freq], axis=-1)
    fidx = np.arange(128) % 32
    rope = np.stack([np.cos(ang)[:, fidx].T, np.sin(ang)[:, fidx].T]).astype(np.float32)
    return cst, np.ascontiguousarray(rope)


def build(depth, T_s, debug=False, stop=None):
    nc = bass.Bass("TRN2", target_bir_lowering=False)
    Ttot = 2 * TP + T_s
    NB = Ttot // 512
    Tk = Ttot + PAST
    seqs = [(0, TP, 0), (TP, TP, 0), (2 * TP, T_s, 1)]
    NR5 = depth * 5

    def din(name, shape, dt=F32):
        return nc.dram_tensor(name, list(shape), dt, kind="ExternalInput").ap()

    def dout(name, shape, dt=F32):
        return nc.dram_tensor(name, list(shape), dt, kind="ExternalOutput").ap()

    def dscr(name, shape, dt=F32):
        if debug:
            return nc.dram_tensor(name, list(shape), dt, kind="ExternalOutput").ap()
        return nc.dram_tensor(name, list(shape), dt).ap()

    xp_d = din("xp", [2 * TP, D])
    xs_d = din("xs", [T_s, D])
    ck_d = [din("ckg", [depth, PAST, 128]), din("ckw", [depth, PAST, 128])]
    cv_d = [din("cvg", [depth, PAST, 128]), din("cvw", [depth, PAST, 128])]
    sd_d = din("sd", [depth, 2, 8, 64, 64])
    vecs_d = din("vecs", [16, D])
    wmod_d = din("w_mod", [depth, D, 6 * D])
    bmod_d = din("b_mod", [depth, 6 * D])
    win_d = din("w_in", [depth, D, IN_COLS])
    conv_d = din("conv", [NR5, 1536])
    alog_d = din("a_log", [depth, 16])
    dtb_d = din("dt_bias", [depth, 16])
    anorm_d = din("a_norm", [depth, 64])
    qkn_d = din("qk_norm", [depth * 4, 64])
    sink_d = din("sink", [depth, 8])
    wbr_d = [din("w_br_a", [depth, 512, D]), din("w_br_b", [depth, 512, D]), din("w_br_c", [depth, 512, D])]
    wo_d = din("w_o", [depth, D, D])
    wf1_d = din("w_ff1", [depth, D, DFF])
    wf2_d = din("w_ff2", [depth, DFF, D])
    cst_d = din("cst", [128, NCST])
    rope_d = din("rope", [2, 128, T_s])
    yp_d = dout("yp", [2 * TP, D])
    ys_d = dout("ys", [T_s, D])
    nk_d = [dout("nkg", [2, depth, TP, 128]), dout("nkw", [2, depth, TP, 128])]
    nv_d = [dout("nvg", [2, depth, TP, 128]), dout("nvw", [2, depth, TP, 128])]
    nst_d = dout("nst", [2, depth, 2, 8, 64, 64])
    xT_d = dscr("xT", [D, Ttot])
    raw_d = dscr("raw", [1536, Ttot])
    zs_d = dscr("zs", [512, Ttot])
    bg_d = dscr("bg", [Ttot, 32])
    qT_d = [dscr("qTB", [512, Ttot], BF16), dscr("qTC", [512, Ttot], BF16)]
    kT_d = dscr("kT", [2, 128, Tk], BF16)
    vt_d = dscr("vt", [2, Tk, 128], BF16)
    sig_d = dscr("sig", [3 * D, Ttot], BF16)
    gq_d = dscr("gq", [16, 64, Ttot])
    ktok_d = dscr("ktok", [Ttot, 512])
    vtok_d = dscr("vtok", [Ttot, 512])
    oT_d = dscr("oT", [2, 512, Ttot])
    yT_d = dscr("yT", [3, 512, Ttot], BF16)
    h2T_d = dscr("h2T", [D, Ttot], BF16)

    S = Sched(nc)
    uid = [0]

    class Arena:
        def __init__(self, base, top):
            self.base = base
            self.top = top
            self.off = base

        def reset(self):
            self.off = self.base

        def alloc(self, shape, dt):
            n = 1
            for s_ in shape[1:]:
                n *= s_
            nbytes = n * (4 if dt in (F32, F32R) else 2)
            nbytes = (nbytes + 63) // 64 * 64
            assert self.off + nbytes <= self.top, (self.off, nbytes, self.top)
            uid[0] += 1
            t = nc.alloc_sbuf_tensor_at(f"t{uid[0]}", list(shape), dt, offset=self.off)
            self.off += nbytes
            return t

    pers = Arena(SB_BASE, SB_BASE + 16384)
    ar = Arena(SB_BASE + 16384, SB_TOP)

    class TL:
        def __init__(self, shape, dt, nb=1, arena=None):
            self.t = (arena or ar).alloc(shape, dt)
            self.b = [Buf() for _ in range(nb)]

    pst = [nc.alloc_psum_tensor(f"ps{i}", [128, 1024], F32) for i in range(4)]
    pbuf = [Buf(excl=True) for _ in range(8)]

    def bank(i):
        return pst[i // 2][:, (i % 2) * 512:(i % 2) * 512 + 512]

    dbufs = {}

    def db(name, i=0):
        k = (name, i)
        if k not in dbufs:
            dbufs[k] = Buf()
        return dbufs[k]

    def mm(out, lhsT, rhs, start, stop, reads, writes, **kw):
        S.pe(lambda e: e.matmul(out, lhsT=lhsT, rhs=rhs, start=start, stop=stop, **kw), reads, writes)

    def tr(out, in_, ident, reads, writes):
        S.pe(lambda e: e.transpose(out=out, in_=in_, identity=ident), reads, writes)

    cst = TL([128, NCST], F32, arena=pers)
    ones_bf = TL([128, 128], BF16, arena=pers)
    bd_bf = TL([128, 128], BF16, arena=pers)
    mw_bf = TL([128, 384], BF16, arena=pers)
    vecsT = TL([128, 8, 16], F32, arena=pers)
    silT = TL([128, 8, 2], BF16, arena=pers)
    convq = TL([128, 16, NR5], F32, arena=pers)
    convv = TL([128, 4, NR5], F32, arena=pers)
    qkw = TL([128, depth * 4], F32, arena=pers)
    anw = TL([128, depth], F32, arena=pers)
    dtb = TL([128, depth * 16], F32, arena=pers)
    nexpA = TL([128, depth * 16], F32, arena=pers)
    esink = TL([128, depth * 8], F32, arena=pers)
    modT = TL([128, 48, 2], F32, arena=pers)
    A1 = TL([128, 8, 2], F32, arena=pers)
    A2 = TL([128, 8, 2], F32, arena=pers)
    ones2 = TL([128, 2], BF16, arena=pers)
    C = cst.t
    ident = C[:, 0:128]
    rotT = C[:, 256:384]
    ones_f = C[:, 1664:1792]

    def gm(d, k):
        b0 = 384 + d * 640 + k * 64
        return C[0:64, b0:b0 + 64]

    S.dma("sp", cst.t[:], cst_d, writes=cst.b)
    S.act(lambda e: e.copy(out=ones_bf.t[:], in_=C[:, 1664:1792]), cst.b, ones_bf.b)
    S.act(lambda e: e.copy(out=bd_bf.t[:], in_=C[:, 128:256]), cst.b, bd_bf.b)
    S.act(lambda e: e.copy(out=mw_bf.t[:], in_=C[:, 1792:2176]), cst.b, mw_bf.b)
    S.pool(lambda e: e.memset(ones2.t[:], 1.0), (), ones2.b)
    ar.reset()
    vr = TL([16, D], F32)
    cr = TL([NR5, 1536], F32)
    qr = TL([depth * 4, 128], F32)
    anr = TL([depth, 64], F32)
    S.dma("sp", vr.t[:], vecs_d, writes=vr.b)
    S.dma("sp", cr.t[:], conv_d, writes=cr.b)
    S.dma("sp", qr.t[:, 0:64], qkn_d, writes=qr.b)
    S.dma("sp", qr.t[:, 64:128], qkn_d, writes=qr.b)
    S.dma("sp", anr.t[:], anorm_d, writes=anr.b)
    S.dma("sp", dtb.t[:], dtb_d.rearrange("l x -> (l x)").partition_broadcast(128), writes=dtb.b)
    S.dma("sp", nexpA.t[:], alog_d.rearrange("l x -> (l x)").partition_broadcast(128), writes=nexpA.b)
    S.dma("sp", esink.t[:], sink_d.rearrange("l x -> (l x)").partition_broadcast(128), writes=esink.b)
    for c in range(8):
        tr(bank(0)[:, c * 16:(c + 1) * 16], vr.t[0:16, c * 128:(c + 1) * 128], ident[0:16, 0:16], vr.b + cst.b, [pbuf[0]])
    S.dve(lambda e: e.tensor_copy(out=vecsT.t[:], in_=bank(0)[:, 0:128].rearrange("p (c r) -> p c r", r=16)), [pbuf[0]], vecsT.b)
    for j in range(16):
        tr(bank(1)[0:64, j * NR5:(j + 1) * NR5], cr.t[0:NR5, j * 64:(j + 1) * 64], ident[0:NR5, 0:NR5], cr.b + cst.b, [pbuf[1]])
    S.dve(lambda e: e.tensor_copy(out=convq.t[0:64], in_=bank(1)[0:64, 0:16 * NR5].rearrange("p (c r) -> p c r", r=NR5)), [pbuf[1]], convq.b)
    for j in range(4):
        tr(bank(2)[:, j * NR5:(j + 1) * NR5], cr.t[0:NR5, 1024 + j * 128:1024 + (j + 1) * 128], ident[0:NR5, 0:NR5], cr.b + cst.b, [pbuf[2]])
    S.dve(lambda e: e.tensor_copy(out=convv.t[:], in_=bank(2)[:, 0:4 * NR5].rearrange("p (c r) -> p c r", r=NR5)), [pbuf[2]], convv.b)
    tr(bank(3)[:, 0:depth * 4], qr.t[0:depth * 4, :], ident[0:depth * 4, 0:depth * 4], qr.b + cst.b, [pbuf[3]])
    S.dve(lambda e: e.tensor_copy(out=qkw.t[:], in_=bank(3)[:, 0:depth * 4]), [pbuf[3]], qkw.b)
    tr(bank(3)[0:64, 64:64 + depth], anr.t[0:depth, :], ident[0:depth, 0:depth], anr.b + cst.b, [pbuf[3]])
    S.dve(lambda e: e.tensor_copy(out=anw.t[0:64], in_=bank(3)[0:64, 64:64 + depth]), [pbuf[3]], anw.b)
    sg = TL([128, 8, 2], F32)
    S.act(lambda e: e.activation(out=sg.t[:], in_=vecsT.t[:, :, 0:2], func=AF.Exp, scale=-1.0), vecsT.b, sg.b)
    S.act(lambda e: e.activation(out=sg.t[:], in_=sg.t[:], func=AF.Ln, bias=1.0), sg.b, sg.b)
    S.act(lambda e: e.activation(out=sg.t[:], in_=sg.t[:], func=AF.Exp, scale=-1.0), sg.b, sg.b)
    S.dve(lambda e: e.tensor_tensor(out=silT.t[:], in0=sg.t[:], in1=vecsT.t[:, :, 0:2], op=ALU.mult), sg.b + vecsT.b, silT.b)
    S.act(lambda e: e.activation(out=nexpA.t[:], in_=nexpA.t[:], func=AF.Exp), nexpA.b, nexpA.b)
    S.dve(lambda e: e.tensor_scalar(out=nexpA.t[:], in0=nexpA.t[:], scalar1=-1.0, scalar2=None, op0=ALU.mult), nexpA.b, nexpA.b)
    S.act(lambda e: e.activation(out=esink.t[:], in_=esink.t[:], func=AF.Exp), esink.b, esink.b)
    S.barrier()

    ar.reset()
    xin = [TL([128, D], F32) for _ in range(2)]
    xTb = [TL([128, 8, 512], F32) for _ in range(2)]
    for bi in range(NB):
        xo = xTb[bi % 2]
        for tt in range(4):
            t0 = bi * 512 + tt * 128
            xi = xin[tt % 2]
            src = xp_d[t0:t0 + 128, :] if bi == 0 else xs_d[t0 - 512:t0 - 512 + 128, :]
            S.dma("sp", xi.t[:], src, writes=xi.b)
            for c in range(8):
                bk = 2 * (c // 4) + (tt % 2) * 4
                tr(bank(bk)[:, (c % 4) * 128:(c % 4 + 1) * 128], xi.t[:, c * 128:(c + 1) * 128], ident, xi.b + cst.b, [pbuf[bk]])
            for half in range(2):
                bk = 2 * half + (tt % 2) * 4
                eng = S.act if half == 0 else S.dve
                src_ps = bank(bk).rearrange("p (c t) -> p c t", t=128)
                dst = xo.t[:, half * 4:half * 4 + 4, tt * 128:(tt + 1) * 128]
                if half == 0:
                    S.act(lambda e, dst=dst, src_ps=src_ps: e.copy(out=dst, in_=src_ps), [pbuf[bk]], xo.b)
                else:
                    S.dve(lambda e, dst=dst, src_ps=src_ps: e.tensor_copy(out=dst, in_=src_ps), [pbuf[bk]], xo.b)
        S.dma("sp", xT_d.rearrange("(c p) t -> p c t", p=128)[:, :, bi * 512:(bi + 1) * 512], xo.t[:], reads=xo.b, writes=[db("xT", bi)])
    S.barrier()

    def load_w_bf16(dst_tl, src_ap_fn, nchunk):
        for c in range(nchunk):
            S.dma("pool", dst_tl.t[:, c], src_ap_fn(c), writes=dst_tl.b)

    def phase_M(l):
        ar.reset()
        wm = [TL([128, 8, 1536], BF16) for _ in range(2)]
        bm = TL([1, 6 * D], BF16)
        S.dma("pool", bm.t[:], bmod_d[l:l + 1, :], writes=bm.b)
        for q in range(4):
            w = wm[q % 2]
            load_w_bf16(w, lambda c: wmod_d[l, c * 128:(c + 1) * 128, q * 1536:(q + 1) * 1536], 8)
            for gg in range(12):
                g = q * 12 + gg
                o_ = bank(0)[:, 2 * g:2 * g + 2]
                for c in range(8):
                    mm(o_, w.t[:, c, gg * 128:(gg + 1) * 128], silT.t[:, c, :], c == 0, False, w.b + silT.b, [pbuf[0]])
                mm(o_, bm.t[0:1, g * 128:(g + 1) * 128], ones2.t[0:1, :], False, True, bm.b + ones2.b, [pbuf[0]])
        S.dve(lambda e: e.tensor_copy(out=modT.t[:], in_=bank(0)[:, 0:96].rearrange("p (g v) -> p g v", v=2)), [pbuf[0]], modT.b)
        ln1 = vecsT.t[:, :, 2 + l:3 + l].to_broadcast([128, 8, 2])
        ln2 = vecsT.t[:, :, 2 + depth + l:3 + depth + l].to_broadcast([128, 8, 2])
        S.dve(lambda e: e.scalar_tensor_tensor(out=A1.t[:], in0=modT.t[:, 8:16, :], scalar=1.0, in1=ln1, op0=ALU.add, op1=ALU.mult), modT.b + vecsT.b, A1.b)
        S.dve(lambda e: e.scalar_tensor_tensor(out=A2.t[:], in0=modT.t[:, 32:40, :], scalar=1.0, in1=ln2, op0=ALU.add, op1=ALU.mult), modT.b + vecsT.b, A2.b)
        S.barrier()

    def norm_block(xT, hT, mv, Aap, shg, sq, tmp, statb, lnv, rstd):
        for c in range(8):
            s_ = sq[c % 2]
            S.act(lambda e, s_=s_, c=c: e.activation(out=s_.t[:], in_=xT.t[:, c, :], func=AF.Square), xT.b, s_.b)
            mm(bank(statb), ones_bf.t[:], s_.t[:], c == 0, c == 7, ones_bf.b + s_.b, [pbuf[statb]])
        S.act(lambda e: e.activation(out=lnv.t[:], in_=bank(statb), func=AF.Ln, bias=EPS, scale=1.0 / D), [pbuf[statb]], lnv.b)
        S.act(lambda e: e.activation(out=rstd.t[:], in_=lnv.t[:], func=AF.Exp, scale=-0.5), lnv.b, rstd.b)
        for c in range(8):
            t_ = tmp[c % 2]
            S.dve(lambda e, t_=t_, c=c: e.tensor_tensor(out=t_.t[:], in0=xT.t[:, c, :], in1=rstd.t[:], op=ALU.mult), xT.b + rstd.b, t_.b)
            S.act(lambda e, t_=t_, c=c: e.activation(out=hT.t[:, c, :], in_=t_.t[:], func=AF.Identity,
                                                    bias=modT.t[:, shg + c, mv:mv + 1], scale=Aap.t[:, c, mv:mv + 1]),
                  t_.b + modT.b + Aap.b, hT.b)

    def phase_A(l):
        ar.reset()
        win = TL([128, 8, IN_COLS], BF16)
        load_w_bf16(win, lambda c: win_d[l, c * 128:(c + 1) * 128, :], 8)
        xT = [TL([128, 8, 512], F32) for _ in range(2)]
        hT = TL([128, 8, 512], BF16)
        sq = [TL([128, 512], BF16) for _ in range(2)]
        tmp = [TL([128, 512], F32) for _ in range(2)]
        lnv = TL([128, 512], F32)
        rstd = TL([128, 512], F32)
        stf = [TL([128, 512], F32) for _ in range(4)]
        stb = [TL([128, 512], BF16) for _ in range(4)]
        qn = [TL([128, 512], F32) for _ in range(2)]
        t1 = [TL([128, 512], F32) for _ in range(2)]
        t2 = [TL([128, 512], F32) for _ in range(2)]
        cs = [TL([128, 2, 512], F32) for _ in range(2)]
        bgst = [TL([128, 32], F32) for _ in range(2)]
        e1 = [TL([128, 32], F32) for _ in range(2)]
        vst = [TL([128, 256], F32) for _ in range(2)]
        kst = t1
        cnt = {"f": 0, "b": 0, "q": 0, "g": 0, "t": 0}

        def nxt(k, n):
            cnt[k] += 1
            return cnt[k] % n

        def load_x(bi):
            S.dma("sp", xT[bi % 2].t[:], xT_d.rearrange("(c p) t -> p c t", p=128)[:, :, bi * 512:(bi + 1) * 512],
                  reads=[db("xT", bi)], writes=xT[bi % 2].b)

        load_x(0)
        for bi in range(NB):
            mv = 0 if bi == 0 else 1
            x_ = xT[bi % 2]
            if bi + 1 < NB:
                load_x(bi + 1)
            if mv:
                c_ = cs[bi % 2]
                S.dma("sp", c_.t[:], rope_d.rearrange("a p t -> p a t")[:, :, (bi - 1) * 512:bi * 512], writes=c_.b)
            norm_block(x_, hT, mv, A1, 0, sq, tmp, 7, lnv, rstd)
            t0 = bi * 512
            groups = []
            for j in range(16):
                groups.append(("raw", C_QKV + j * 64, 64, j * 64))
            for j in range(4):
                groups.append(("raw", C_QKV + 1024 + j * 128, 128, 1024 + j * 128))
            for j in range(8):
                groups.append(("z", C_Z + j * 64, 64, j * 64))
            for a, (cq, ckk) in enumerate(((C_BQ, C_BK), (C_CQ, C_CK))):
                for m in range(4):
                    groups.append(("q", (cq + m * 64, cq + (m + 4) * 64), 128, (a, m)))
                groups.append(("k", ckk, 128, (a, 0)))
            for j in range(24):
                groups.append(("gate", C_GATE + j * 128, 128, j * 128))
            pending = []
            for gi, (kind, col, M, info) in enumerate(groups):
                bk = gi % 4
                for c in range(8):
                    if kind == "q":
                        mm(bank(bk)[0:64, :], win.t[:, c, col[0]:col[0] + 64], hT.t[:, c, :], c == 0, c == 7, win.b + hT.b, [pbuf[bk]])
                        mm(bank(bk)[64:128, :], win.t[:, c, col[1]:col[1] + 64], hT.t[:, c, :], c == 0, c == 7, win.b + hT.b, [pbuf[bk]],
                           tile_position=(0, 64))
                    else:
                        mm(bank(bk)[0:M, :], win.t[:, c, col:col + M], hT.t[:, c, :], c == 0, c == 7, win.b + hT.b, [pbuf[bk]])
                P = bank(bk)
                while pending and pending[0][0] <= gi:
                    pending.pop(0)[1]()
                if kind == "raw":
                    s_ = stf[nxt("f", 4)]
                    S.act(lambda e, s_=s_, P=P, M=M: e.copy(out=s_.t[0:M, :], in_=P[0:M, :]), [pbuf[bk]], s_.b)
                    S.dma("sp", raw_d[info:info + M, t0:t0 + 512], s_.t[0:M, :], reads=s_.b, writes=[db("raw", bi)])
                elif kind == "z":
                    u_ = stf[nxt("f", 4)]
                    S.act(lambda e, u_=u_, P=P: e.activation(out=u_.t[0:64, :], in_=P[0:64, :], func=AF.Silu), [pbuf[bk]], u_.b)
                    S.dma("sp", zs_d[info:info + 64, t0:t0 + 512], u_.t[0:64, :], reads=u_.b, writes=[db("zs", bi)])
                elif kind == "gate":
                    o_ = stb[nxt("b", 4)]
                    S.act(lambda e, o_=o_, P=P: e.activation(out=o_.t[:], in_=P, func=AF.Sigmoid), [pbuf[bk]], o_.b)
                    S.dma("sp", sig_d[info:info + 128, t0:t0 + 512], o_.t[:], reads=o_.b, writes=[db("sig", bi)])
                else:
                    a, m = info
                    wi = a * 2 + (0 if kind == "q" else 1)
                    s_ = sq[nxt("q", 2)]
                    S.act(lambda e, s_=s_, P=P: e.activation(out=s_.t[:], in_=P, func=AF.Square), [pbuf[bk]], s_.b)
                    q_ = qn[cnt["q"] % 2]
                    a_ = t1[cnt["q"] % 2]
                    b_ = t2[cnt["q"] % 2]

                    def step1(s_=s_, P=P, bk=bk, wi=wi, q_=q_):
                        sb_ = 4 + nxt("g", 2)
                        mm(bank(sb_), bd_bf.t[:], s_.t[:], True, True, bd_bf.b + s_.b, [pbuf[sb_]])
                        r_ = stf[nxt("f", 4)]
                        S.act(lambda e: e.activation(out=r_.t[:], in_=bank(sb_), func=AF.Ln, bias=EPS, scale=1.0 / HD), [pbuf[sb_]], r_.b)
                        S.act(lambda e: e.activation(out=r_.t[:], in_=r_.t[:], func=AF.Exp, scale=-0.5), r_.b, r_.b)
                        S.dve(lambda e: e.scalar_tensor_tensor(out=q_.t[:], in0=P, scalar=qkw.t[:, l * 4 + wi:l * 4 + wi + 1],
                                                               in1=r_.t[:], op0=ALU.mult, op1=ALU.mult),
                              [pbuf[bk]] + r_.b + qkw.b, q_.b)

                    def step2(kind=kind, a=a, m=m, q_=q_, a_=a_, b_=b_, mv=mv, bi=bi, t0=t0):
                        o_ = stb[nxt("b", 4)]
                        if mv:
                            c_ = cs[bi % 2]
                            rb = 6
                            mm(bank(rb), rotT, q_.t[:], True, True, cst.b + q_.b, [pbuf[rb]])
                            S.dve(lambda e: e.tensor_tensor(out=a_.t[:], in0=q_.t[:], in1=c_.t[:, 0, :], op=ALU.mult), q_.b + c_.b, a_.b)
                            S.dve(lambda e: e.tensor_tensor(out=b_.t[:], in0=bank(6), in1=c_.t[:, 1, :], op=ALU.mult), [pbuf[rb]] + c_.b, b_.b)
                            S.pool(lambda e: e.tensor_tensor(out=o_.t[:], in0=a_.t[:], in1=b_.t[:], op=ALU.add), a_.b + b_.b, o_.b)
                        else:
                            S.act(lambda e: e.copy(out=o_.t[:], in_=q_.t[:]), q_.b, o_.b)
                        if kind == "q":
                            S.dma("sp", qT_d[a][m * 128:(m + 1) * 128, t0:t0 + 512], o_.t[:], reads=o_.b, writes=[db("qT%d" % a, bi)])
                        else:
                            kc0 = t0 if bi == 0 else t0 + PAST
                            S.dma("sp", kT_d[a, :, kc0:kc0 + 512], o_.t[:], reads=o_.b, writes=[db("kT", bi)])
                            if bi == 0:
                                for tt in range(4):
                                    tr(bank(6)[:, (tt % 4) * 128:(tt % 4 + 1) * 128], q_.t[:, tt * 128:(tt + 1) * 128], ident, q_.b + cst.b, [pbuf[6]])
                                k_ = kst[a]
                                S.dve(lambda e: e.tensor_copy(out=k_.t[:], in_=bank(6)), [pbuf[6]], k_.b)
                                for tt in range(4):
                                    S.dma("sp", nk_d[a][tt // 2, l, (tt % 2) * 128:(tt % 2 + 1) * 128, :], k_.t[:, tt * 128:(tt + 1) * 128], reads=k_.b)

                    pending.append((gi + 1, step1))
                    pending.append((gi + 2, step2))
            while pending:
                pending.pop(0)[1]()
            for tt in range(4):
                tb = 4 + (tt % 2)
                P = bank(tb)
                tok = slice(tt * 128, (tt + 1) * 128)
                for (c0, n_, o0) in ((C_BETA, 32, 0), (C_BV, 128, 32), (C_CV, 128, 160)):
                    for c in range(8):
                        mm(P[:, o0:o0 + n_], hT.t[:, c, tok], win.t[:, c, c0:c0 + n_], c == 0, c == 7, win.b + hT.b, [pbuf[tb]])
                g_ = bgst[tt % 2]
                e_ = e1[tt % 2]
                S.dve(lambda e, e_=e_, P=P: e.tensor_tensor(out=e_.t[:, 16:32], in0=P[:, 16:32], in1=dtb.t[:, l * 16:(l + 1) * 16], op=ALU.add), [pbuf[tb]] + dtb.b, e_.b)
                S.act(lambda e, e_=e_: e.activation(out=e_.t[:, 16:32], in_=e_.t[:, 16:32], func=AF.Exp), e_.b, e_.b)
                S.act(lambda e, e_=e_: e.activation(out=e_.t[:, 16:32], in_=e_.t[:, 16:32], func=AF.Ln, bias=1.0), e_.b, e_.b)
                S.dve(lambda e, e_=e_, g_=g_: e.tensor_tensor(out=g_.t[:, 16:32], in0=e_.t[:, 16:32], in1=nexpA.t[:, l * 16:(l + 1) * 16], op=ALU.mult), e_.b + nexpA.b, g_.b)
                S.act(lambda e, e_=e_, P=P: e.activation(out=e_.t[:, 0:16], in_=P[:, 0:16], func=AF.Exp, scale=-1.0), [pbuf[tb]], e_.b)
                S.act(lambda e, e_=e_: e.activation(out=e_.t[:, 0:16], in_=e_.t[:, 0:16], func=AF.Ln, bias=1.0), e_.b, e_.b)
                S.act(lambda e, e_=e_, g_=g_: e.activation(out=g_.t[:, 0:16], in_=e_.t[:, 0:16], func=AF.Exp, scale=-1.0), e_.b, g_.b)
                S.dma("sp", bg_d[t0 + tt * 128:t0 + (tt + 1) * 128, :], g_.t[:], reads=g_.b, writes=[db("bg", bi)])
                v_ = vst[tt % 2]
                S.act(lambda e, v_=v_, P=P: e.copy(out=v_.t[:], in_=P[:, 32:288]), [pbuf[tb]], v_.b)
                kc0 = (t0 if bi == 0 else t0 + PAST) + tt * 128
                for a in range(2):
                    S.dma("pool", vt_d[a, kc0:kc0 + 128, :], v_.t[:, a * 128:(a + 1) * 128], reads=v_.b, writes=[db("vt", bi)])
                    if bi == 0:
                        S.dma("sp", nv_d[a][tt // 2, l, (tt % 2) * 128:(tt % 2 + 1) * 128, :], v_.t[:, a * 128:(a + 1) * 128], reads=v_.b)
        S.barrier()

    def phase_B1(l):
        ar.reset()
        rawt = [TL([128, 520], F32) for _ in range(3)]
        acc = [TL([128, 512], F32) for _ in range(3)]
        fqk = TL([64, 16, 512], F32, nb=16)
        sqb = [TL([64, 512], BF16) for _ in range(4)]
        rs = [TL([64, 512], F32) for _ in range(4)]
        kn = TL([64, 8, 512], F32)
        qst = [TL([64, 512], F32) for _ in range(2)]
        fv = TL([128, 4, 512], F32)
        tst = [TL([128, 512], F32) for _ in range(2)]
        stat_banks = (0, 1, 6, 7)
        n_ = [0]
        for bi in range(NB):
            t0 = bi * 512
            segs = [(0, 256, 0, 256), (256, 512, 256, 512)] if bi == 0 else [(0, 512, seqs[2][0] - t0, seqs[2][0] + seqs[2][1] - t0)]
            for ti in range(20):
                M = 64 if ti < 16 else 128
                ch0 = ti * 64 if ti < 16 else 1024 + (ti - 16) * 128
                wtb = convq.b if ti < 16 else convv.b
                n_[0] += 1
                r_ = rawt[n_[0] % 3]
                a_ = acc[n_[0] % 3]
                for (a0, a1, s0, s1) in segs:
                    lo = max(a0 - 2, s0)
                    hi = min(a1 + 2, s1)
                    off = a0 if len(segs) == 1 else a0 + (4 if a0 else 0)
                    if lo > a0 - 2:
                        S.pool(lambda e, r_=r_, off=off, M=M: e.memset(r_.t[0:M, off:off + 2], 0.0), (), r_.b)
                    if hi < a1 + 2:
                        S.pool(lambda e, r_=r_, off=off, M=M, a0=a0, a1=a1: e.memset(r_.t[0:M, off + (a1 - a0) + 2:off + (a1 - a0) + 4], 0.0), (), r_.b)
                    S.dma("sp", r_.t[0:M, off + (lo - (a0 - 2)):off + (hi - (a0 - 2))], raw_d[ch0:ch0 + M, t0 + lo:t0 + hi],
                          reads=[db("raw", bi), db("raw", max(bi - 1, 0)), db("raw", min(bi + 1, NB - 1))], writes=r_.b)
                    n = a1 - a0
                    for k in range(5):
                        src = r_.t[0:M, off + k:off + k + n]
                        dst = a_.t[0:M, a0:a1]
                        sc = convv.t[:, ti - 16, l * 5 + k:l * 5 + k + 1] if ti >= 16 else convq.t[0:64, ti, l * 5 + k:l * 5 + k + 1]
                        if k == 0:
                            S.dve(lambda e, dst=dst, src=src, sc=sc: e.tensor_scalar(out=dst, in0=src, scalar1=sc, scalar2=None, op0=ALU.mult), r_.b + wtb, a_.b)
                        else:
                            S.dve(lambda e, dst=dst, src=src, sc=sc: e.scalar_tensor_tensor(out=dst, in0=src, scalar=sc, in1=dst, op0=ALU.mult, op1=ALU.add),
                                  r_.b + wtb + a_.b, a_.b)
                if ti >= 16:
                    S.act(lambda e, a_=a_, ti=ti: e.activation(out=fv.t[:, ti - 16, :], in_=a_.t[:], func=AF.Silu), a_.b, fv.b)
                else:
                    S.act(lambda e, a_=a_, ti=ti: e.activation(out=fqk.t[:, ti, :], in_=a_.t[0:64, :], func=AF.Silu), a_.b, [fqk.b[ti]])
            LA = 2
            inflight = {}

            def p2a(ti):
                n_[0] += 1
                s_ = sqb[ti % 4]
                sb_ = stat_banks[ti % 4]
                S.act(lambda e, s_=s_, ti=ti: e.activation(out=s_.t[:], in_=fqk.t[:, ti, :], func=AF.Square), [fqk.b[ti]], s_.b)
                mm(bank(sb_)[0:64, :], ones_bf.t[0:64, 0:64], s_.t[:], True, True, ones_bf.b + s_.b, [pbuf[sb_]])

            def p2b(ti):
                sb_ = stat_banks[ti % 4]
                r2 = rs[ti % 4]
                S.act(lambda e, r2=r2, sb_=sb_: e.activation(out=r2.t[:], in_=bank(sb_)[0:64, :], func=AF.Ln, bias=EPS), [pbuf[sb_]], r2.b)
                S.act(lambda e, r2=r2: e.activation(out=r2.t[:], in_=r2.t[:], func=AF.Exp, scale=-0.5), r2.b, r2.b)
                if ti < 8:
                    o_ = qst[ti % 2]
                    S.dve(lambda e, o_=o_, r2=r2, ti=ti: e.scalar_tensor_tensor(out=o_.t[:], in0=fqk.t[:, ti, :], scalar=HD ** -0.5, in1=r2.t[:],
                                                                            op0=ALU.mult, op1=ALU.mult), [fqk.b[ti]] + r2.b, o_.b)
                    S.dma("sp", gq_d[ti, :, t0:t0 + 512], o_.t[:], reads=o_.b, writes=[db("gq", bi)])
                else:
                    S.dve(lambda e, r2=r2, ti=ti: e.tensor_tensor(out=kn.t[:, ti - 8, :], in0=fqk.t[:, ti, :], in1=r2.t[:], op=ALU.mult), [fqk.b[ti]] + r2.b, kn.b)
                    S.dma("sp", gq_d[ti, :, t0:t0 + 512], kn.t[:, ti - 8, :], reads=kn.b, writes=[db("gq", bi)])

            for i in range(16 + LA):
                if i < 16:
                    p2a(i)
                if i >= LA:
                    p2b(i - LA)
            for tt in range(4):
                kb, vb = 2 + (tt % 2) * 2, 3 + (tt % 2) * 2
                for h in range(8):
                    tr(bank(kb)[:, h * 64:(h + 1) * 64], kn.t[:, h, tt * 128:(tt + 1) * 128], ident[0:64, 0:64], kn.b + cst.b, [pbuf[kb]])
                for j in range(4):
                    tr(bank(vb)[:, j * 128:(j + 1) * 128], fv.t[:, j, tt * 128:(tt + 1) * 128], ident, fv.b + cst.b, [pbuf[vb]])
                k_ = tst[0]
                v_ = tst[1]
                S.act(lambda e, k_=k_, kb=kb: e.copy(out=k_.t[:], in_=bank(kb)), [pbuf[kb]], k_.b)
                S.dve(lambda e, v_=v_, vb=vb: e.tensor_copy(out=v_.t[:], in_=bank(vb)), [pbuf[vb]], v_.b)
                S.dma("sp", ktok_d[t0 + tt * 128:t0 + (tt + 1) * 128, :], k_.t[:], reads=k_.b, writes=[db("ktok", bi)])
                S.dma("sp", vtok_d[t0 + tt * 128:t0 + (tt + 1) * 128, :], v_.t[:], reads=v_.b, writes=[db("vtok", bi)])
        S.barrier()

    def phase_B23(l):
        ar.reset()
        f3 = [64, 8, 64]

        def T3(dt=F32):
            return TL(f3, dt)

        D_ = []
        for d in range(2):
            o = {}
            o["qk"] = [TL([64, 16, 256], BF16) for _ in range(1)]
            o["ktok"] = [TL([64, 8, 64], F32) for _ in range(2)]
            o["vtok"] = [TL([64, 8, 64], F32) for _ in range(2)]
            o["bg"] = [TL([64, 32], F32) for _ in range(2)]
            tmp_ = [T3() for _ in range(7)]
            o["Gexp"], o["Bexp"], o["XL"], o["XU"], o["dl"], o["du"], o["M1T"] = tmp_
            o["decay"], o["decayT"], o["egB"], o["nbU"], o["nbL"], o["M1"] = tmp_[2], tmp_[3], tmp_[0], tmp_[1], tmp_[4], tmp_[5]
            o["gc"] = TL([64, 8], F32)
            o["eg"] = TL([64, 8], F32)
            o["beg"] = TL([64, 8], F32)
            o["PA"] = [TL([64, 8, 128], F32R) for _ in range(2)]
            o["PW"] = [TL([64, 8, 128], F32R) for _ in range(2)]
            o["Zm1"], o["ZmT1"], o["ZmT2"], o["X"], o["Xp"] = T3(F32R), T3(F32R), T3(F32R), T3(F32R), T3(F32R)
            o["bv"] = T3(F32R)
            o["bek"] = T3(F32R)
            o["up"] = [T3() for _ in range(2)]
            o["wT"] = [T3(BF16) for _ in range(2)]
            o["qgT"] = [T3(BF16) for _ in range(2)]
            o["attnT"] = [T3(BF16) for _ in range(2)]
            o["kg"] = [T3(BF16) for _ in range(2)]
            o["glB"] = [TL([64, 8], F32) for _ in range(2)]
            o["u"] = T3(BF16)
            o["s"] = T3()
            o["s1"] = T3()
            o["sbf"] = T3(BF16)
            o["ost"] = [TL([64, 8, 256], F32) for _ in range(1)]
            D_.append(o)
        ones64 = ones_f[0:64, 0:64]
        id64 = ident[0:64, 0:64]

        def bc_h(ap2):
            return ap2.unsqueeze(2).to_broadcast(f3)

        def bc_m(ap2):
            return ap2.unsqueeze(1).to_broadcast(f3)

        def v3(ap):
            return ap.rearrange("p (h f) -> p h f", f=64)

        for (s0, T, is_s) in seqs:
            si = seqs.index((s0, T, is_s))
            N = T // 64
            for d in range(2):
                o = D_[d]
                if is_s:
                    S.dma("sp", o["s"].t[:], sd_d[l, d].rearrange("h k v -> k h v"), writes=o["s"].b)
                else:
                    S.pool(lambda e, o=o: e.memset(o["s"].t[:], 0.0), (), o["s"].b)
                S.act(lambda e, o=o: e.copy(out=o["sbf"].t[:], in_=o["s"].t[:]), o["s"].b, o["sbf"].b)

            CUT = int(os.environ.get('B23CUT', '99'))

            def b2(d, c, n):
                o = D_[d]
                yield
                PA, PB, PC, PD = bank(4 * d), bank(4 * d + 1), bank(4 * d + 2), bank(4 * d + 3)
                bA, bB, bC, bD = pbuf[4 * d], pbuf[4 * d + 1], pbuf[4 * d + 2], pbuf[4 * d + 3]
                PCD = pst[2 * d + 1][0:64, :].rearrange("p (h f) -> p h f", f=128)
                tok0 = s0 + c * 64
                bi, cb = tok0 // 512, (tok0 % 512) // 64
                r = n % 2
                c4 = (tok0 % 256) // 64
                qk = o["qk"][0]
                if c4 == (0 if d == 0 else 3):
                    hb0 = (tok0 // 256) * 256
                    S.dma("pool", qk.t[:], gq_d.rearrange("j p t -> p j t")[:, :, hb0:hb0 + 256], reads=[db("gq", bi)], writes=qk.b)
                kt, vt, bgt = o["ktok"][r], o["vtok"][r], o["bg"][r]
                S.dma("sp", kt.t[:], ktok_d[tok0:tok0 + 64, :].rearrange("p (h f) -> p h f", f=64), reads=[db("ktok", bi)], writes=kt.b)
                S.dma("sp", vt.t[:], vtok_d[tok0:tok0 + 64, :].rearrange("p (h f) -> p h f", f=64), reads=[db("vtok", bi)], writes=vt.b)
                S.dma("sp", bgt.t[:], bg_d[tok0:tok0 + 64, :], reads=[db("bg", bi)], writes=bgt.b)
                yield
                g = bgt.t[:, 16 + d * 8:24 + d * 8]
                b = bgt.t[:, d * 8:d * 8 + 8]
                Ud, nm, nmT, st, stT = (gm(d, k) for k in range(5))
                last = 63 if d == 0 else 0
                qc = qk.t[:, 0:8, c4 * 64:(c4 + 1) * 64]
                kc = qk.t[:, 8:16, c4 * 64:(c4 + 1) * 64]
                if CUT < 2:
                    return
                mm(PC[0:64, 0:8], Ud, g, True, True, cst.b + bgt.b, [bC])
                S.dve(lambda e: e.tensor_tensor(out=o["Gexp"].t[:], in0=bc_h(g), in1=bc_m(Ud), op=ALU.mult), bgt.b + cst.b, o["Gexp"].b)
                S.pool(lambda e: e.tensor_tensor(out=o["Bexp"].t[:], in0=bc_h(b), in1=bc_m(id64), op=ALU.mult), bgt.b + cst.b, o["Bexp"].b)
                yield
                mm(PA[0:64, :], ones64, o["Gexp"].t[:].rearrange("p h f -> p (h f)"), True, True, cst.b + o["Gexp"].b, [bA])
                mm(PB[0:64, :], ones64, o["Bexp"].t[:].rearrange("p h f -> p (h f)"), True, True, cst.b + o["Bexp"].b, [bB])
                S.act(lambda e: e.copy(out=o["gc"].t[:], in_=PC[0:64, 0:8]), [bC], o["gc"].b)
                yield
                if CUT < 3:
                    return
                gcb = bc_h(o["gc"].t[:])
                S.dve(lambda e: e.tensor_tensor(out=o["XL"].t[:], in0=gcb, in1=bc_m(nm), op=ALU.add), o["gc"].b + cst.b, o["XL"].b)
                S.pool(lambda e: e.tensor_tensor(out=o["XU"].t[:], in0=bc_m(nmT), in1=gcb, op=ALU.subtract), o["gc"].b + cst.b, o["XU"].b)
                yield
                S.dve(lambda e: e.scalar_tensor_tensor(out=o["dl"].t[:], in0=v3(PA[0:64, :]), scalar=-1.0, in1=o["XL"].t[:], op0=ALU.mult, op1=ALU.add),
                      [bA] + o["XL"].b, o["dl"].b)
                S.dve(lambda e: e.tensor_tensor(out=o["du"].t[:], in0=v3(PA[0:64, :]), in1=o["XU"].t[:], op=ALU.add), [bA] + o["XU"].b, o["du"].b)
                S.act(lambda e: e.activation(out=o["egB"].t[:], in_=v3(PA[0:64, :]), func=AF.Exp), [bA], o["egB"].b)
                S.act(lambda e: e.activation(out=o["decay"].t[:], in_=o["dl"].t[:], func=AF.Exp), o["dl"].b, o["decay"].b)
                S.act(lambda e: e.activation(out=o["decayT"].t[:], in_=o["du"].t[:], func=AF.Exp), o["du"].b, o["decayT"].b)
                S.act(lambda e: e.activation(out=o["eg"].t[:], in_=o["gc"].t[:], func=AF.Exp), o["gc"].b, o["eg"].b)
                yield
                S.dve(lambda e: e.tensor_tensor(out=o["nbU"].t[:], in0=v3(PB[0:64, :]), in1=bc_m(stT), op=ALU.mult), [bB] + cst.b, o["nbU"].b)
                S.pool(lambda e: e.tensor_tensor(out=o["nbL"].t[:], in0=bc_h(b), in1=bc_m(st), op=ALU.mult), bgt.b + cst.b, o["nbL"].b)
                yield
                if CUT < 4:
                    return
                for h in range(8):
                    mm(PA[0:64, h * 64:(h + 1) * 64], kc[:, h, :], kc[:, h, :], True, True, qk.b, [bA])
                for h in range(8):
                    mm(PB[0:64, h * 64:(h + 1) * 64], kc[:, h, :], qc[:, h, :], True, True, qk.b, [bB])
                S.dve(lambda e: e.tensor_tensor(out=o["M1"].t[:], in0=v3(PA[0:64, :]), in1=o["decay"].t[:], op=ALU.mult), [bA] + o["decay"].b, o["M1"].b)
                S.dve(lambda e: e.tensor_tensor(out=o["M1T"].t[:], in0=v3(PA[0:64, :]), in1=o["decayT"].t[:], op=ALU.mult), [bA] + o["decayT"].b, o["M1T"].b)
                at = o["attnT"][r]
                S.dve(lambda e: e.tensor_tensor(out=at.t[:], in0=v3(PB[0:64, :]), in1=o["decayT"].t[:], op=ALU.mult), [bB] + o["decayT"].b, at.b)
                yield
                S.pool(lambda e: e.tensor_tensor(out=o["M1"].t[:], in0=o["M1"].t[:], in1=o["nbL"].t[:], op=ALU.mult), o["M1"].b + o["nbL"].b, o["M1"].b)
                S.pool(lambda e: e.tensor_tensor(out=o["M1T"].t[:], in0=o["M1T"].t[:], in1=o["nbU"].t[:], op=ALU.mult), o["M1T"].b + o["nbU"].b, o["M1T"].b)
                yield
                if CUT < 5:
                    return
                P0f, P0Tf = o["M1"], o["M1T"]
                BD, MA1, MA1T, MA2, MA2T = (gm(d, k) for k in range(5, 10))
                PAc, PWc = o["PA"][0], o["PW"][0]
                S.dve(lambda e, PAc=PAc: e.tensor_tensor(out=PAc.t[:, :, 0:64], in0=P0f.t[:], in1=bc_m(BD), op=ALU.mult), P0f.b + cst.b, PAc.b)
                S.pool(lambda e, PWc=PWc: e.tensor_tensor(out=PWc.t[:, :, 0:64], in0=P0Tf.t[:], in1=bc_m(BD), op=ALU.mult), P0Tf.b + cst.b, PWc.b)
                S.dve(lambda e, PAc=PAc: e.tensor_tensor(out=PAc.t[:, :, 64:128], in0=PAc.t[:, :, 0:64], in1=bc_m(id64), op=ALU.add), PAc.b + cst.b, PAc.b)
                S.pool(lambda e, PWc=PWc: e.tensor_tensor(out=PWc.t[:, :, 64:128], in0=PWc.t[:, :, 0:64], in1=bc_m(id64), op=ALU.add), PWc.b + cst.b, PWc.b)
                S.pool(lambda e: e.tensor_tensor(out=o["Zm1"].t[:], in0=P0Tf.t[:], in1=bc_m(MA1T), op=ALU.mult), P0Tf.b + cst.b, o["Zm1"].b)
                S.pool(lambda e: e.tensor_tensor(out=o["ZmT1"].t[:], in0=P0f.t[:], in1=bc_m(MA1), op=ALU.mult), P0f.b + cst.b, o["ZmT1"].b)
                S.pool(lambda e: e.tensor_tensor(out=o["ZmT2"].t[:], in0=P0f.t[:], in1=bc_m(MA2), op=ALU.mult), P0f.b + cst.b, o["ZmT2"].b)
                yield
                PAB = pst[2 * d][0:64, :].rearrange("p (h f) -> p h f", f=128)
                for k in range(4):
                    PAc, PWc = o["PA"][k % 2], o["PW"][k % 2]
                    PAn, PWn = o["PA"][(k + 1) % 2], o["PW"][(k + 1) % 2]
                    if k == 0:
                        for h in range(8):
                            mm(PA[0:64, h * 64:(h + 1) * 64], PWc.t[:, h, 0:64], PAc.t[:, h, 0:64], True, True, PAc.b + PWc.b, [bA])
                        for h in range(8):
                            mm(PC[0:64, h * 64:(h + 1) * 64], PAc.t[:, h, 0:64], PWc.t[:, h, 0:64], True, True, PAc.b + PWc.b, [bC])
                        S.act(lambda e, PAn=PAn: e.copy(out=PAn.t[:, :, 0:64], in_=v3(PA[0:64, :])), [bA], PAn.b)
                        S.dve(lambda e, PWn=PWn: e.tensor_copy(out=PWn.t[:, :, 0:64], in_=v3(PC[0:64, :])), [bC], PWn.b)
                        S.pool(lambda e, PAn=PAn, PAc=PAc: e.tensor_copy(out=PAn.t[:, :, 64:128], in_=PAc.t[:, :, 64:128]), PAc.b, PAn.b)
                        S.pool(lambda e, PWn=PWn, PWc=PWc: e.tensor_copy(out=PWn.t[:, :, 64:128], in_=PWc.t[:, :, 64:128]), PWc.b, PWn.b)
                    elif k < 3:
                        for h in range(8):
                            mm(PAB[:, h, :], PWc.t[:, h, 0:64], PAc.t[:, h, :], True, True, PAc.b + PWc.b, [bA, bB])
                        for h in range(8):
                            mm(PCD[:, h, :], PAc.t[:, h, 0:64], PWc.t[:, h, :], True, True, PAc.b + PWc.b, [bC, bD])
                        for hh in (0, 4):
                            bk1 = [bA] if hh == 0 else [bB]
                            bk2 = [bC] if hh == 0 else [bD]
                            S.act(lambda e, PAn=PAn, hh=hh: e.copy(out=PAn.t[:, hh:hh + 4, 0:64], in_=PAB[:, hh:hh + 4, 0:64]), bk1, PAn.b)
                            S.dve(lambda e, PAn=PAn, PAc=PAc, hh=hh: e.tensor_tensor(out=PAn.t[:, hh:hh + 4, 64:128], in0=PAB[:, hh:hh + 4, 64:128],
                                                                               in1=PAc.t[:, hh:hh + 4, 64:128], op=ALU.add), bk1 + PAc.b, PAn.b)
                            S.act(lambda e, PWn=PWn, hh=hh: e.copy(out=PWn.t[:, hh:hh + 4, 0:64], in_=PCD[:, hh:hh + 4, 0:64]), bk2, PWn.b)
                            S.dve(lambda e, PWn=PWn, PWc=PWc, hh=hh: e.tensor_tensor(out=PWn.t[:, hh:hh + 4, 64:128], in0=PCD[:, hh:hh + 4, 64:128],
                                                                               in1=PWc.t[:, hh:hh + 4, 64:128], op=ALU.add), bk2 + PWc.b, PWn.b)
                    else:
                        for h in range(8):
                            mm(PA[0:64, h * 64:(h + 1) * 64], PWc.t[:, h, 0:64], PAc.t[:, h, 64:128], True, True, PAc.b + PWc.b, [bA])
                        for h in range(8):
                            mm(PC[0:64, h * 64:(h + 1) * 64], PAc.t[:, h, 0:64], PWc.t[:, h, 64:128], True, True, PAc.b + PWc.b, [bC])
                        S.dve(lambda e, PAn=PAn, PAc=PAc: e.tensor_tensor(out=PAn.t[:, :, 64:128], in0=v3(PA[0:64, :]), in1=PAc.t[:, :, 64:128], op=ALU.add),
                              [bA] + PAc.b, PAn.b)
                        S.dve(lambda e, PWn=PWn, PWc=PWc: e.tensor_tensor(out=PWn.t[:, :, 64:128], in0=v3(PC[0:64, :]), in1=PWc.t[:, :, 64:128], op=ALU.add),
                              [bC] + PWc.b, PWn.b)
                    yield
                Tt, Wt = o["PA"][0], o["PW"][0]
                Tv, Wv = Tt.t[:, :, 64:128], Wt.t[:, :, 64:128]
                for h in range(8):
                    mm(PA[0:64, h * 64:(h + 1) * 64], o["ZmT1"].t[:, h, :], Wt.t[:, h, 64:128], True, True, o["ZmT1"].b + Wt.b, [bA])
                for h in range(8):
                    mm(PB[0:64, h * 64:(h + 1) * 64], o["Zm1"].t[:, h, :], Tt.t[:, h, 64:128], True, True, o["Zm1"].b + Tt.b, [bB])
                S.act(lambda e: e.copy(out=o["X"].t[:], in_=v3(PA[0:64, :])), [bA], o["X"].b)
                S.dve(lambda e: e.tensor_copy(out=o["Xp"].t[:], in_=v3(PB[0:64, :])), [bB], o["Xp"].b)
                yield
                for h in range(8):
                    mm(PC[0:64, h * 64:(h + 1) * 64], Tt.t[:, h, 64:128], o["X"].t[:, h, :], True, True, Tt.b + o["X"].b, [bC])
                for h in range(8):
                    mm(PD[0:64, h * 64:(h + 1) * 64], Wt.t[:, h, 64:128], o["Xp"].t[:, h, :], True, True, Wt.b + o["Xp"].b, [bD])
                S.dve(lambda e: e.tensor_tensor(out=Wv, in0=v3(PC[0:64, :]), in1=Wv, op=ALU.add), [bC] + Wt.b, Wt.b)
                S.dve(lambda e: e.tensor_tensor(out=Tv, in0=v3(PD[0:64, :]), in1=Tv, op=ALU.add), [bD] + Tt.b, Tt.b)
                yield
                for h in range(8):
                    mm(PA[0:64, h * 64:(h + 1) * 64], o["ZmT2"].t[:, h, :], Wt.t[:, h, 64:128], True, True, o["ZmT2"].b + Wt.b, [bA])
                S.act(lambda e: e.copy(out=o["X"].t[:], in_=v3(PA[0:64, :])), [bA], o["X"].b)
                for h in range(8):
                    mm(PC[0:64, h * 64:(h + 1) * 64], Tt.t[:, h, 64:128], o["X"].t[:, h, :], True, True, Tt.b + o["X"].b, [bC])
                S.dve(lambda e: e.tensor_tensor(out=Wv, in0=v3(PC[0:64, :]), in1=Wv, op=ALU.add), [bC] + Wt.b, Wt.b)
                if CUT < 6:
                    return
                TT = Wt
                S.dve(lambda e: e.tensor_tensor(out=o["beg"].t[:], in0=b, in1=o["eg"].t[:], op=ALU.mult), bgt.b + o["eg"].b, o["beg"].b)
                S.pool(lambda e: e.tensor_tensor(out=o["bv"].t[:], in0=vt.t[:], in1=bc_h(b), op=ALU.mult), vt.b + bgt.b, o["bv"].b)
                S.pool(lambda e: e.tensor_tensor(out=o["bek"].t[:], in0=kt.t[:], in1=bc_h(o["beg"].t[:]), op=ALU.mult), kt.b + o["beg"].b, o["bek"].b)
                kg, qg, gl = o["kg"][r], o["qgT"][r], o["glB"][r]
                S.pool(lambda e: e.tensor_tensor(out=kg.t[:], in0=kt.t[:], in1=bc_h(o["decayT"].t[:, :, last]), op=ALU.mult), kt.b + o["decayT"].b, kg.b)
                S.dve(lambda e: e.tensor_tensor(out=qg.t[:], in0=qc, in1=o["egB"].t[:], op=ALU.mult), qk.b + o["egB"].b, qg.b)
                S.act(lambda e: e.copy(out=gl.t[:], in_=o["egB"].t[:, :, last]), o["egB"].b, gl.b)
                yield
                for h in range(8):
                    mm(PA[0:64, h * 64:(h + 1) * 64], TT.t[:, h, 64:128], o["bv"].t[:, h, :], True, True, TT.b + o["bv"].b, [bA])
                for h in range(8):
                    mm(PB[0:64, h * 64:(h + 1) * 64], o["bek"].t[:, h, :], TT.t[:, h, 64:128], True, True, TT.b + o["bek"].b, [bB])
                up, wT = o["up"][r], o["wT"][r]
                S.act(lambda e: e.copy(out=up.t[:], in_=v3(PA[0:64, :])), [bA], up.b)
                S.dve(lambda e: e.tensor_copy(out=wT.t[:], in_=v3(PB[0:64, :])), [bB], wT.b)

            def b3(d, c, n):
                o = D_[d]
                yield
                PA, PB, PC = bank(4 * d), bank(4 * d + 1), bank(4 * d + 2)
                bA, bB, bC = pbuf[4 * d], pbuf[4 * d + 1], pbuf[4 * d + 2]
                tok0 = s0 + c * 64
                bi, cb = tok0 // 512, (tok0 % 512) // 64
                r = n % 2
                up, wT, kg, qg, gl, at = o["up"][r], o["wT"][r], o["kg"][r], o["qgT"][r], o["glB"][r], o["attnT"][r]
                for h in range(8):
                    mm(PA[0:64, h * 64:(h + 1) * 64], wT.t[:, h, :], o["sbf"].t[:, h, :], True, True, wT.b + o["sbf"].b, [bA])
                if CUT < 8:
                    return
                S.dve(lambda e: e.tensor_tensor(out=o["u"].t[:], in0=up.t[:], in1=v3(PA[0:64, :]), op=ALU.subtract), up.b + [bA], o["u"].b)
                yield
                if CUT < 9:
                    return
                for h in range(8):
                    mm(PC[0:64, h * 64:(h + 1) * 64], o["sbf"].t[:, h, :], qg.t[:, h, :], True, False, o["sbf"].b + qg.b, [bC])
                    mm(PC[0:64, h * 64:(h + 1) * 64], o["u"].t[:, h, :], at.t[:, h, :], False, True, o["u"].b + at.b, [bC])
                if CUT < 10:
                    return
                for h in range(8):
                    mm(PB[0:64, h * 64:(h + 1) * 64], kg.t[:, h, :], o["u"].t[:, h, :], True, True, kg.b + o["u"].b, [bB])
                if CUT < 11:
                    return
                ost = o["ost"][0]
                c4 = (tok0 % 256) // 64
                S.act(lambda e: e.copy(out=ost.t[:, :, c4 * 64:(c4 + 1) * 64], in_=v3(PC[0:64, :])), [bC], ost.b)
                S.dve(lambda e: e.tensor_tensor(out=o["s1"].t[:], in0=o["s"].t[:], in1=bc_h(gl.t[:]), op=ALU.mult), o["s"].b + gl.b, o["s1"].b)
                S.dve(lambda e: e.tensor_tensor(out=o["s"].t[:], in0=o["s1"].t[:], in1=v3(PB[0:64, :]), op=ALU.add), o["s1"].b + [bB], o["s"].b)
                S.act(lambda e: e.copy(out=o["sbf"].t[:], in_=o["s"].t[:]), o["s"].b, o["sbf"].b)
                if CUT < 12:
                    return
                if c4 == (3 if d == 0 else 0):
                    lo = (tok0 // 256) * 256
                    for h in range(8):
                        S.dma("sp", oT_d[d, h * 64:(h + 1) * 64, lo:lo + 256], ost.t[:, h, :], reads=ost.b, writes=[db("oT", bi)])

            def run_gens(gs):
                while gs:
                    for g_ in list(gs):
                        try:
                            next(g_)
                        except StopIteration:
                            gs.remove(g_)

            for n in range(N):
                run_gens([b2(0, n, n), b2(1, N - 1 - n, n)])
                run_gens([b3(0, n, n), b3(1, N - 1 - n, n)])
            if not is_s:
                for d in range(2):
                    S.dma("sp", nst_d[si, l, d].rearrange("h k v -> k h v"), D_[d]["s"].t[:], reads=D_[d]["s"].b)
        S.barrier()

    def phase_B4(l):
        ar.reset()
        of = [TL([64, 8, 512], F32) for _ in range(2)]
        ob = [TL([64, 8, 512], F32) for _ in range(2)]
        zt = [TL([64, 8, 512], F32) for _ in range(2)]
        sqb = [TL([64, 512], BF16) for _ in range(2)]
        rs = [TL([64, 512], F32) for _ in range(2)]
        yo = [TL([64, 8, 512], BF16) for _ in range(2)]
        for bi in range(NB):
            r = bi % 2
            sl = slice(bi * 512, (bi + 1) * 512)
            for h in range(8):
                S.dma("sp", of[r].t[:, h, :], oT_d[0, h * 64:(h + 1) * 64, sl], reads=[db("oT", bi)], writes=of[r].b)
                S.dma("sp", ob[r].t[:, h, :], oT_d[1, h * 64:(h + 1) * 64, sl], reads=[db("oT", bi)], writes=ob[r].b)
                S.dma("sp", zt[r].t[:, h, :], zs_d[h * 64:(h + 1) * 64, sl], reads=[db("zs", bi)], writes=zt[r].b)
            S.dve(lambda e, r=r: e.tensor_tensor(out=of[r].t[:], in0=of[r].t[:], in1=ob[r].t[:], op=ALU.add), of[r].b + ob[r].b, of[r].b)
            for h in range(8):
                s_ = sqb[h % 2]
                r2 = rs[h % 2]
                S.act(lambda e, s_=s_, h=h, r=r: e.activation(out=s_.t[:], in_=of[r].t[:, h, :], func=AF.Square), of[r].b, s_.b)
                sb_ = h % 2
                mm(bank(sb_)[0:64, :], ones_bf.t[0:64, 0:64], s_.t[:], True, True, ones_bf.b + s_.b, [pbuf[sb_]])
                S.act(lambda e, r2=r2, sb_=sb_: e.activation(out=r2.t[:], in_=bank(sb_)[0:64, :], func=AF.Ln, bias=EPS, scale=1.0 / 64), [pbuf[sb_]], r2.b)
                S.act(lambda e, r2=r2: e.activation(out=r2.t[:], in_=r2.t[:], func=AF.Exp, scale=-0.5), r2.b, r2.b)
                S.dve(lambda e, r2=r2, h=h, r=r: e.scalar_tensor_tensor(out=r2.t[:], in0=of[r].t[:, h, :], scalar=anw.t[0:64, l:l + 1], in1=r2.t[:],
                                                                      op0=ALU.mult, op1=ALU.mult), of[r].b + r2.b + anw.b, r2.b)
                S.pool(lambda e, r2=r2, h=h, r=r: e.tensor_tensor(out=yo[r].t[:, h, :], in0=r2.t[:], in1=zt[r].t[:, h, :], op=ALU.mult), r2.b + zt[r].b, yo[r].b)
            for h in range(8):
                S.dma("sp", yT_d[0, h * 64:(h + 1) * 64, sl], yo[r].t[:, h, :], reads=yo[r].b, writes=[db("yT", bi)])
        S.barrier()

    def phase_C(l):
        ar.reset()
        NKT = Tk // 128
        KT = [TL([128, Tk], BF16) for _ in range(2)]
        V1 = [TL([128, NKT, 2, 65], BF16) for _ in range(2)]
        QT = [TL([128, 4, 512], BF16) for _ in range(2)]
        PTt = [TL([128, 512], BF16) for _ in range(4)]
        ckt = [TL([128, 128], F32) for _ in range(2)]
        kcs = TL([128, 512], BF16)
        rr = [TL([128, 512], F32) for _ in range(2)]
        bcs = [TL([64, 512], F32) for _ in range(2)]
        yst = [TL([64, 512], BF16) for _ in range(2)]
        for a in range(2):
            S.pool(lambda e, a=a: e.memset(V1[a].t[:, :, :, 64:65], 1.0), (), V1[a].b)
            S.dma("sp", KT[a].t[:, 0:512], kT_d[a, :, 0:512], reads=[db("kT", i) for i in range(NB)], writes=KT[a].b)
            S.dma("sp", KT[a].t[:, 1024:Tk], kT_d[a, :, 1024:Tk], reads=[db("kT", i) for i in range(NB)], writes=KT[a].b)
            for kv in range(2):
                S.dma("sp", V1[a].t[:, 0:4, kv, 0:64], vt_d[a, 0:512, kv * 64:(kv + 1) * 64].rearrange("(n p) d -> p n d", p=128),
                      reads=[db("vt", i) for i in range(NB)], writes=V1[a].b)
                S.dma("sp", V1[a].t[:, 8:NKT, kv, 0:64], vt_d[a, 1024:Tk, kv * 64:(kv + 1) * 64].rearrange("(n p) d -> p n d", p=128),
                      reads=[db("vt", i) for i in range(NB)], writes=V1[a].b)
                S.dma("pool", V1[a].t[:, 4:8, kv, 0:64], cv_d[a][l, :, kv * 64:(kv + 1) * 64].rearrange("(n p) d -> p n d", p=128), writes=V1[a].b)
            for kt_ in range(4):
                c_ = ckt[kt_ % 2]
                S.dma("sp", c_.t[:], ck_d[a][l, kt_ * 128:(kt_ + 1) * 128, :], writes=c_.b)
                tr(bank(7)[:, kt_ * 128:(kt_ + 1) * 128], c_.t[:], ident, c_.b + cst.b, [pbuf[7]])
            S.dve(lambda e, a=a: e.tensor_copy(out=KT[a].t[:, 512:1024], in_=bank(7)), [pbuf[7]], KT[a].b)
        cnt = {"s": 0, "p": 0, "acc": 0, "q": 0}
        for (s0, T, is_s) in seqs:
            QB = min(T, 512)
            for qb in range(T // QB):
                q0 = s0 + qb * QB
                bi = q0 // 512
                for a in range(2):
                    cnt["q"] += 1
                    Q = QT[cnt["q"] % 2]
                    S.dma("sp", Q.t[:, :, 0:QB], qT_d[a].rearrange("(m p) t -> p m t", p=128)[:, :, q0:q0 + QB], reads=[db("qT%d" % a, bi)], writes=Q.b)
                    chunks = []
                    if not is_s:
                        chunks = [(s0 // 128 + j, 0, QB, None) for j in range(T // 128)]
                    elif a == 0:
                        chunks = [(4 + j, 0, QB, None) for j in range(4)] + [(8 + j, 0, QB, None) for j in range(T // 128)]
                    else:
                        ctx = [(4 + j, 0, QB, None) for j in range(4)]
                        loc = []
                        for kc in range(T // 128):
                            lo = max((kc - 1) * 128, qb * QB)
                            hi = min((kc + 2) * 128, (qb + 1) * QB)
                            if lo >= hi:
                                continue
                            loc.append((8 + kc, lo - qb * QB, hi - qb * QB, (lo - (kc - 1) * 128)))
                        chunks = ctx[:1] + loc + ctx[1:]
                    nck = len(chunks)
                    items = [(h, ci) for h in range(8) for ci in range(nck)]
                    st_ = {}
                    accb = {}

                    def stage1(h, ci, Q=Q, a=a, chunks=chunks):
                        m, half = h % 4, h // 4
                        pb0 = 64 * half
                        if ci == 0:
                            cnt["acc"] += 1
                            accb[h] = 4 + cnt["acc"] % 2
                        kti, qlo, qhi, mcol = chunks[ci]
                        cnt["s"] += 1
                        sbk = cnt["s"] % 4
                        nq = qhi - qlo
                        mm(bank(sbk)[:, 0:nq], KT[a].t[pb0:pb0 + 64, kti * 128:(kti + 1) * 128], Q.t[pb0:pb0 + 64, m, qlo:qhi], True, True,
                           KT[a].b + Q.b, [pbuf[sbk]])
                        cnt["p"] += 1
                        Pt = PTt[cnt["p"] % 4]
                        S.act(lambda e, Pt=Pt, sbk=sbk, nq=nq: e.activation(out=Pt.t[:, 0:nq], in_=bank(sbk)[:, 0:nq], func=AF.Exp, scale=HD ** -0.5),
                              [pbuf[sbk]], Pt.b)
                        if mcol is not None and not (mcol == 128 and nq == 128):
                            S.dve(lambda e, Pt=Pt, nq=nq, mcol=mcol: e.tensor_tensor(out=Pt.t[:, 0:nq], in0=Pt.t[:, 0:nq], in1=mw_bf.t[:, mcol:mcol + nq], op=ALU.mult),
                                  Pt.b + mw_bf.b, Pt.b)
                        st_[(h, ci)] = Pt

                    def stage2(h, ci, a=a, chunks=chunks, nck=nck, QB=QB, q0=q0, bi=bi):
                        half = h // 4
                        kti, qlo, qhi, mcol = chunks[ci]
                        nq = qhi - qlo
                        Pt = st_.pop((h, ci))
                        ab = accb[h]
                        ACC = bank(ab)
                        mm(ACC[0:65, qlo:qhi], V1[a].t[:, kti, half, :], Pt.t[:, 0:nq], ci == 0, ci == nck - 1, V1[a].b + Pt.b, [pbuf[ab]])
                        if ci != nck - 1:
                            return
                        r_ = rr[h % 2]
                        if a == 1:
                            S.act(lambda e, r_=r_, ACC=ACC, h=h: e.activation(out=r_.t[64:65, 0:QB], in_=ACC[64:65, 0:QB], func=AF.Ln,
                                                                          bias=esink.t[64:65, l * 8 + h:l * 8 + h + 1]), [pbuf[ab]] + esink.b, r_.b)
                        else:
                            S.act(lambda e, r_=r_, ACC=ACC: e.activation(out=r_.t[64:65, 0:QB], in_=ACC[64:65, 0:QB], func=AF.Ln), [pbuf[ab]], r_.b)
                        S.act(lambda e, r_=r_: e.activation(out=r_.t[64:65, 0:QB], in_=r_.t[64:65, 0:QB], func=AF.Exp, scale=-1.0), r_.b, r_.b)
                        bb = 6 + h % 2
                        mm(bank(bb)[0:64, 0:QB], ones_f[64:65, 0:64], r_.t[64:65, 0:QB], True, True, cst.b + r_.b, [pbuf[bb]])
                        bc_ = bcs[h % 2]
                        S.act(lambda e, bc_=bc_, bb=bb: e.copy(out=bc_.t[:, 0:QB], in_=bank(bb)[0:64, 0:QB]), [pbuf[bb]], bc_.b)
                        y_ = yst[h % 2]
                        S.dve(lambda e, y_=y_, ACC=ACC, bc_=bc_: e.tensor_tensor(out=y_.t[:, 0:QB], in0=ACC[0:64, 0:QB], in1=bc_.t[:, 0:QB], op=ALU.mult),
                              [pbuf[ab]] + bc_.b, y_.b)
                        S.dma("sp", yT_d[1 + a, h * 64:(h + 1) * 64, q0:q0 + QB], y_.t[:, 0:QB], reads=y_.b, writes=[db("yT", bi)])

                    LA = 2
                    for i in range(len(items) + LA):
                        if i < len(items):
                            stage1(*items[i])
                        if i >= LA:
                            stage2(*items[i - LA])
        S.barrier()

    def phase_D1(l):
        ar.reset()
        wbr = [TL([128, 4, D], BF16) for _ in range(3)]
        wo = TL([128, 8, D], BF16)
        for j in range(3):
            load_w_bf16(wbr[j], lambda c, j=j: wbr_d[j][l, c * 128:(c + 1) * 128, :], 4)
        load_w_bf16(wo, lambda c: wo_d[l, c * 128:(c + 1) * 128, :], 8)
        yt = [TL([128, 12, 512], BF16) for _ in range(2)]
        sg = [TL([128, 24, 512], BF16) for _ in range(2)]
        xT = [TL([128, 8, 512], F32) for _ in range(2)]
        mg = TL([128, 8, 512], BF16)
        hT = TL([128, 8, 512], BF16)
        ta = [TL([128, 512], F32) for _ in range(2)]
        tb_ = [TL([128, 512], F32) for _ in range(2)]
        tc = [TL([128, 512], F32) for _ in range(2)]
        sq = [TL([128, 512], BF16) for _ in range(2)]
        tmp = [TL([128, 512], F32) for _ in range(2)]
        lnv = TL([128, 512], F32)
        rstd = TL([128, 512], F32)

        def loads(bi):
            r = bi % 2
            sl = slice(bi * 512, (bi + 1) * 512)
            S.dma("sp", yt[r].t[:], yT_d.rearrange("j (c p) t -> p (j c) t", p=128)[:, :, sl], reads=[db("yT", bi)], writes=yt[r].b)
            S.dma("sp", sg[r].t[:], sig_d.rearrange("(c p) t -> p c t", p=128)[:, :, sl], reads=[db("sig", bi)], writes=sg[r].b)
            S.dma("sp", xT[r].t[:], xT_d.rearrange("(c p) t -> p c t", p=128)[:, :, sl], reads=[db("xT", bi)], writes=xT[r].b)

        loads(0)
        for bi in range(NB):
            r = bi % 2
            mv = 0 if bi == 0 else 1
            if bi + 1 < NB:
                loads(bi + 1)
            x_ = xT[r]
            for oc in range(8):
                for j in range(3):
                    for c in range(4):
                        mm(bank(j), wbr[j].t[:, c, oc * 128:(oc + 1) * 128], yt[r].t[:, j * 4 + c, :], c == 0, c == 3, wbr[j].b + yt[r].b, [pbuf[j]])
                a_, b_, c_ = ta[oc % 2], tb_[oc % 2], tc[oc % 2]
                S.dve(lambda e, a_=a_, oc=oc, r=r: e.tensor_tensor(out=a_.t[:], in0=bank(0), in1=sg[r].t[:, oc, :], op=ALU.mult), [pbuf[0]] + sg[r].b, a_.b)
                S.dve(lambda e, b_=b_, oc=oc, r=r: e.tensor_tensor(out=b_.t[:], in0=bank(1), in1=sg[r].t[:, 8 + oc, :], op=ALU.mult), [pbuf[1]] + sg[r].b, b_.b)
                S.dve(lambda e, c_=c_, oc=oc, r=r: e.tensor_tensor(out=c_.t[:], in0=bank(2), in1=sg[r].t[:, 16 + oc, :], op=ALU.mult), [pbuf[2]] + sg[r].b, c_.b)
                S.pool(lambda e, a_=a_, b_=b_: e.tensor_tensor(out=a_.t[:], in0=a_.t[:], in1=b_.t[:], op=ALU.add), a_.b + b_.b, a_.b)
                S.pool(lambda e, a_=a_, c_=c_, oc=oc: e.tensor_tensor(out=mg.t[:, oc, :], in0=a_.t[:], in1=c_.t[:], op=ALU.add), a_.b + c_.b, mg.b)
            for oc in range(8):
                bk = 3 + oc % 2
                for c in range(8):
                    mm(bank(bk), wo.t[:, c, oc * 128:(oc + 1) * 128], mg.t[:, c, :], c == 0, c == 7, wo.b + mg.b, [pbuf[bk]])
                S.dve(lambda e, oc=oc, bk=bk, x_=x_, mv=mv: e.scalar_tensor_tensor(out=x_.t[:, oc, :], in0=bank(bk), scalar=modT.t[:, 16 + oc, mv:mv + 1],
                                                                                 in1=x_.t[:, oc, :], op0=ALU.mult, op1=ALU.add),
                      [pbuf[bk]] + x_.b + modT.b, x_.b)
            norm_block(x_, hT, mv, A2, 24, sq, tmp, 7, lnv, rstd)
            sl = slice(bi * 512, (bi + 1) * 512)
            S.dma("sp", xT_d.rearrange("(c p) t -> p c t", p=128)[:, :, sl], x_.t[:], reads=x_.b, writes=[db("xT", bi)])
            S.dma("sp", h2T_d.rearrange("(c p) t -> p c t", p=128)[:, :, sl], hT.t[:], reads=hT.b, writes=[db("h2T", bi)])
        S.barrier()

    def phase_D2(l, hf):
        ar.reset()
        HH = DFF // 2
        w1 = TL([128, 8, HH], BF16)
        w2 = TL([128, 16, D], BF16)
        load_w_bf16(w1, lambda c: wf1_d[l, c * 128:(c + 1) * 128, hf * HH:(hf + 1) * HH], 8)
        load_w_bf16(w2, lambda c: wf2_d[l, hf * HH + c * 128:hf * HH + (c + 1) * 128, :], 16)
        hT = [TL([128, 8, 512], BF16) for _ in range(2)]
        xT = [TL([128, 8, 512], F32) for _ in range(2)]
        rl = [TL([128, 512], BF16) for _ in range(3)]
        aT = TL([128, 16, 512], BF16)
        final = (l == depth - 1 and hf == 1)
        xo = [TL([128, D], F32) for _ in range(2)] if final else None

        def loads(bi):
            r = bi % 2
            sl = slice(bi * 512, (bi + 1) * 512)
            S.dma("sp", hT[r].t[:], h2T_d.rearrange("(c p) t -> p c t", p=128)[:, :, sl], reads=[db("h2T", bi)], writes=hT[r].b)
            S.dma("sp", xT[r].t[:], xT_d.rearrange("(c p) t -> p c t", p=128)[:, :, sl], reads=[db("xT", bi)], writes=xT[r].b)

        loads(0)
        for bi in range(NB):
            r = bi % 2
            mv = 0 if bi == 0 else 1
            if bi + 1 < NB:
                loads(bi + 1)
            x_ = xT[r]
            for oc in range(16):
                bk = oc % 3
                for c in range(8):
                    mm(bank(bk), w1.t[:, c, oc * 128:(oc + 1) * 128], hT[r].t[:, c, :], c == 0, c == 7, w1.b + hT[r].b, [pbuf[bk]])
                r_ = rl[oc % 3]
                S.act(lambda e, r_=r_, bk=bk: e.activation(out=r_.t[:], in_=bank(bk), func=AF.Relu), [pbuf[bk]], r_.b)
                S.pool(lambda e, r_=r_, oc=oc: e.tensor_tensor(out=aT.t[:, oc, :], in0=r_.t[:], in1=r_.t[:], op=ALU.mult), r_.b, aT.b)
            for oc in range(8):
                bk = 3 + oc % 2
                for c in range(16):
                    mm(bank(bk), w2.t[:, c, oc * 128:(oc + 1) * 128], aT.t[:, c, :], c == 0, c == 15, w2.b + aT.b, [pbuf[bk]])
                S.dve(lambda e, oc=oc, bk=bk, x_=x_, mv=mv: e.scalar_tensor_tensor(out=x_.t[:, oc, :], in0=bank(bk), scalar=modT.t[:, 40 + oc, mv:mv + 1],
                                                                                 in1=x_.t[:, oc, :], op0=ALU.mult, op1=ALU.add),
                      [pbuf[bk]] + x_.b + modT.b, x_.b)
            sl = slice(bi * 512, (bi + 1) * 512)
            if not final:
                S.dma("sp", xT_d.rearrange("(c p) t -> p c t", p=128)[:, :, sl], x_.t[:], reads=x_.b, writes=[db("xT", bi)])
            else:
                for tt in range(4):
                    xo_ = xo[tt % 2]
                    for c in range(8):
                        bk = 5 + (c // 4)
                        tr(bank(bk)[:, (c % 4) * 128:(c % 4 + 1) * 128], x_.t[:, c, tt * 128:(tt + 1) * 128], ident, x_.b + cst.b, [pbuf[bk]])
                    S.act(lambda e, xo_=xo_: e.copy(out=xo_.t[:, 0:512], in_=bank(5)), [pbuf[5]], xo_.b)
                    S.dve(lambda e, xo_=xo_: e.tensor_copy(out=xo_.t[:, 512:1024], in_=bank(6)), [pbuf[6]], xo_.b)
                    t0 = bi * 512 + tt * 128
                    dst = yp_d[t0:t0 + 128, :] if bi == 0 else ys_d[t0 - 512:t0 - 512 + 128, :]
                    S.dma("sp", dst, xo_.t[:], reads=xo_.b)
        S.barrier()

    plist = [("M", phase_M), ("A", phase_A), ("B1", phase_B1), ("B23", phase_B23), ("B4", phase_B4), ("C", phase_C), ("D1", phase_D1),
             ("D2a", lambda l: phase_D2(l, 0)), ("D2b", lambda l: phase_D2(l, 1))]
    done = False
    for l in range(depth):
        for nm_, fn_ in plist:
            if stop is not None and nm_ == stop:
                done = True
                break
            fn_(l)
        if done:
            break
    n_ops = len(S.ops)
    S.emit()
    return nc, n_ops


DEPTH = 4
DEC_SEQ = 4096
_cache = {}


def make_in_maps(inputs, depth, T_s, n_cores=8):
    f = lambda a: np.ascontiguousarray(np.asarray(a, dtype=np.float32))
    cst, rope = make_consts(T_s)
    maps = []
    for core in range(n_cores):
        b = core % 2
        vecs = np.zeros((16, D), np.float32)
        vecs[0] = inputs["c_ctx"]
        vecs[1] = inputs["c"][b]
        vecs[2:2 + depth] = inputs["ln1"]
        vecs[2 + depth:2 + 2 * depth] = inputs["ln2"]
        m = {
            "xp": f(inputs["x_prompt"][2 * core:2 * core + 2]).reshape(2 * TP, D),
            "xs": f(inputs["x_sample"][b]),
            "ckg": f(inputs["cache_k_glob"][b]).reshape(depth, PAST, 128),
            "ckw": f(inputs["cache_k_win"][b]).reshape(depth, PAST, 128),
            "cvg": f(inputs["cache_v_glob"][b]).reshape(depth, PAST, 128),
            "cvw": f(inputs["cache_v_win"][b]).reshape(depth, PAST, 128),
            "sd": f(inputs["state_delta"][b]),
            "vecs": vecs,
            "w_mod": f(inputs["w_mod"]), "b_mod": f(inputs["b_mod"]), "w_in": f(inputs["w_in"]),
            "conv": f(inputs["conv_qkv"]).reshape(depth * 5, 1536),
            "a_log": f(inputs["a_log"]).reshape(depth, 16), "dt_bias": f(inputs["dt_bias"]).reshape(depth, 16),
            "a_norm": f(inputs["a_norm"]), "qk_norm": f(inputs["qk_norm"]).reshape(depth * 4, 64), "sink": f(inputs["sink"]),
            "w_br_a": f(inputs["w_br_a"]), "w_br_b": f(inputs["w_br_b"]), "w_br_c": f(inputs["w_br_c"]),
            "w_o": f(inputs["w_o"]), "w_ff1": f(inputs["w_ff1"]), "w_ff2": f(inputs["w_ff2"]),
            "cst": cst, "rope": rope,
        }
        maps.append(m)
    return maps


def assemble(results, depth, T_s):
    yp = np.concatenate([r["yp"].reshape(2, TP, D) for r in results], axis=0)
    ys = np.stack([results[0]["ys"], results[1]["ys"]], axis=0)
    outs = [yp.astype(np.float32), ys.astype(np.float32)]
    for nm_ in ("nkg", "nvg", "nkw", "nvw"):
        outs.append(np.concatenate([r[nm_].reshape(2, depth, TP, 2, HD) for r in results], axis=0).astype(np.float32))
    outs.append(np.concatenate([r["nst"] for r in results], axis=0).astype(np.float32))
    return tuple(outs)


def kernel(**inputs):
    depth, T_s = DEPTH, DEC_SEQ
    key = (depth, T_s)
    if key not in _cache:
        _cache[key] = build(depth, T_s)[0]
    nc = _cache[key]
    maps = make_in_maps(inputs, depth, T_s)
    res = run_bass_kernel_spmd(nc, maps, core_ids=list(range(8)))
    return assemble(res.results, depth, T_s)
```

```python
import os
import numpy as np
import concourse.bass as bass
import concourse.mybir as mybir
from concourse.bass_utils import run_bass_kernel_spmd

F32 = mybir.dt.float32
BF16 = mybir.dt.bfloat16
F32R = mybir.dt.float32r
ALU = mybir.AluOpType
AF = mybir.ActivationFunctionType

ENGS = ("pe", "act", "dve", "pool", "sp")
DMA_RING = 12


class Buf:
    __slots__ = ("lw", "rd", "excl")

    def __init__(self, excl=False):
        self.lw = None
        self.rd = []
        self.excl = excl


class Op:
    __slots__ = ("eng", "fn", "dma", "deps", "signal", "semval", "waits", "dsem", "dval")

    def __init__(self, eng, fn, dma):
        self.eng = eng
        self.fn = fn
        self.dma = dma
        self.deps = []
        self.signal = False
        self.semval = None
        self.waits = []
        self.dsem = None
        self.dval = None


class Sched:
    def __init__(self, nc):
        self.nc = nc
        self.ops = []
        self.last = {}
        self.pend_dma = []

    def op(self, eng, fn, reads=(), writes=(), dma=False):
        o = Op(eng, fn, dma)
        deps = o.deps
        if any(b.excl for b in reads):
            writes = list(writes) + [b for b in reads if b.excl]
            reads = [b for b in reads if not b.excl]
        for b in reads:
            if b.lw is not None:
                deps.append(b.lw)
        for b in writes:
            if b.lw is not None:
                deps.append(b.lw)
            deps.extend(b.rd)
        for b in reads:
            if not dma:
                b.rd = [x for x in b.rd if x.dma or x.eng != eng]
            b.rd.append(o)
        for b in writes:
            b.lw = o
            b.rd = []
        self.ops.append(o)
        if dma:
            self.pend_dma.append(o)
        else:
            self.last[eng] = o
        return o

    def barrier(self):
        deps = list(self.last.values()) + self.pend_dma
        self.pend_dma = []
        for e in ENGS:
            o = Op(e, lambda en: en.nop(), False)
            o.deps = list(deps)
            self.ops.append(o)
            self.last[e] = o

    def pe(self, fn, reads=(), writes=()):
        return self.op("pe", fn, reads, writes)

    def act(self, fn, reads=(), writes=()):
        return self.op("act", fn, reads, writes)

    def dve(self, fn, reads=(), writes=()):
        return self.op("dve", fn, reads, writes)

    def pool(self, fn, reads=(), writes=()):
        return self.op("pool", fn, reads, writes)

    def dma(self, q, out, in_, reads=(), writes=()):
        return self.op(q, lambda e: e.dma_start(out=out, in_=in_), reads, writes, dma=True)

    def emit(self):
        nc = self.nc
        ops = self.ops
        cnt = {e: 0 for e in ENGS}
        for o in ops:
            for d in o.deps:
                if d.dma:
                    continue
                if d.eng == "pe" and o.eng == "pe" and not o.dma:
                    continue
                d.signal = True
        last = {}
        for o in ops:
            if not o.dma:
                last[o.eng] = o
        for o in last.values():
            o.signal = True
        dq = {e: 0 for e in ENGS}
        dma_hist = {e: [] for e in ENGS}
        for o in ops:
            if o.dma:
                j = dq[o.eng]
                dq[o.eng] += 1
                o.dsem = (o.eng, j % DMA_RING)
                o.dval = 16 * (j // DMA_RING + 1)
                dma_hist[o.eng].append(o)
            elif o.signal:
                cnt[o.eng] += 1
                o.semval = cnt[o.eng]
        known = {e: {} for e in ENGS}
        dcount = {e: 0 for e in ENGS}
        for o in ops:
            kn = known[o.eng]
            need = {}
            if o.dma:
                j = dcount[o.eng]
                dcount[o.eng] += 1
                if j >= DMA_RING:
                    prev = dma_hist[o.eng][j - DMA_RING]
                    need[("d",) + prev.dsem] = prev.dval
            for d in o.deps:
                if d.dma:
                    k = ("d",) + d.dsem
                    v = d.dval
                else:
                    if d.eng == "pe" and o.eng == "pe" and not o.dma:
                        continue
                    k = ("c", d.eng)
                    v = d.semval
                if need.get(k, 0) < v:
                    need[k] = v
            for k, v in need.items():
                if kn.get(k, 0) < v:
                    kn[k] = v
                    o.waits.append((k, v))
        final_waits = []
        for e, o in last.items():
            final_waits.append((("c", e), o.semval))
        for e in ENGS:
            for o in dma_hist[e][-DMA_RING:]:
                final_waits.append((("d",) + o.dsem, o.dval))
        from contextlib import ExitStack
        with ExitStack() as st:
            sems = {}
            for e in ENGS:
                sems[("c", e)] = st.enter_context(nc.semaphore(f"c_{e}"))
            for e in ENGS:
                for r in range(min(DMA_RING, dq[e])):
                    sems[("d", e, r)] = st.enter_context(nc.semaphore(f"d_{e}_{r}"))
            block = st.enter_context(nc.Block())
            per = {e: [o for o in ops if o.eng == e] for e in ENGS}

            def run(engobj, lst, final=None):
                for o in lst:
                    for k, v in o.waits:
                        engobj.wait_ge(sems[k], v)
                    ins = o.fn(engobj)
                    if o.dma:
                        ins.then_inc(sems[("d",) + o.dsem], 16)
                    elif o.signal:
                        ins.then_inc(sems[("c", o.eng)], 1)
                if final:
                    for k, v in final:
                        engobj.wait_ge(sems[k], v)

            @block.tensor
            def _(e):
                run(e, per["pe"])

            @block.scalar
            def _(e):
                run(e, per["act"])

            @block.vector
            def _(e):
                run(e, per["dve"])

            @block.gpsimd
            def _(e):
                run(e, per["pool"])

            @block.sync
            def _(e):
                run(e, per["sp"], final_waits)
        self.ops = []


D = 1024
NCH = 8
HD = 64
TP = 256
PAST = 512
DFF = 4096
IN_COLS = 6688
C_QKV, C_Z, C_BETA, C_ALPHA, C_BQ, C_BK, C_BV, C_CQ, C_CK, C_CV, C_GATE = (
    0, 1536, 2048, 2064, 2080, 2592, 2720, 2848, 3360, 3488, 3616)
EPS = 1e-6
NEG = -30000.0
NCST = 2176
SB_BASE = 16512
SB_TOP = 229344


def make_consts(T_s):
    cst = np.zeros((128, NCST), np.float32)
    cst[:, 0:128] = np.eye(128)
    cst[0:64, 128:192] = 1.0
    cst[64:128, 192:256] = 1.0
    for m in range(128):
        if m % 64 < 32:
            cst[m + 32, 256 + m] = -1.0
        else:
            cst[m - 32, 256 + m] = 1.0
    p = np.arange(64)[:, None]
    f = np.arange(64)[None, :]

    def ms(m):
        return ((p // (2 * m) == f // (2 * m)) & (p % (2 * m) >= m) & (f % (2 * m) < m)).astype(np.float32)

    bdm = (p // 16 == f // 16).astype(np.float32)
    for d in range(2):
        R = (p >= f) if d == 0 else (p <= f)
        RT = R.T
        base = 384 + d * 640
        ms1 = ms(16) if d == 0 else ms(16).T
        ms2 = ms(32) if d == 0 else ms(32).T
        tabs = [RT.astype(np.float32), np.where(R, 0.0, NEG), np.where(RT, 0.0, NEG),
                -(R & (p != f)).astype(np.float32), -(RT & (p != f)).astype(np.float32),
                bdm, ms1, ms1.T, ms2, ms2.T]
        for k, tb in enumerate(tabs):
            cst[0:64, base + k * 64:base + (k + 1) * 64] = tb
    cst[:, 1664:1792] = 1.0
    pk = np.arange(128)[:, None]
    fq = np.arange(128)[None, :]
    cst[:, 1792:1920] = (pk <= fq)
    cst[:, 1920:2048] = 1.0
    cst[:, 2048:2176] = (fq <= pk)
    t = np.arange(T_s)
    row_id = (t // 64).astype(np.float32)
    col_id = (t % 64).astype(np.float32)
    inv_freq = (10000.0 ** (-np.arange(16, dtype=np.float32) / 16)).astype(np.float32)
    ang = np.concatenate([row_id[:, None] * inv_freq, col_id[:, None] * inv_freq], axis=-1)
    fidx = np.arange(128) % 32
    rope = np.stack([np.cos(ang)[:, fidx].T, np.sin(ang)[:, fidx].T]).astype(np.float32)
    return cst, np.ascontiguousarray(rope)


def build(depth, T_s, debug=False, stop=None):
    nc = bass.Bass("TRN2", target_bir_lowering=False)
    Ttot = 2 * TP + T_s
    NB = Ttot // 512
    Tk = Ttot + PAST
    seqs = [(0, TP, 0), (TP, TP, 0), (2 * TP, T_s, 1)]
    NR5 = depth * 5

    def din(name, shape, dt=F32):
        return nc.dram_tensor(name, list(shape), dt, kind="ExternalInput").ap()

    def dout(name, shape, dt=F32):
        return nc.dram_tensor(name, list(shape), dt, kind="ExternalOutput").ap()

    def dscr(name, shape, dt=F32):
        if debug:
            return nc.dram_tensor(name, list(shape), dt, kind="ExternalOutput").ap()
        return nc.dram_tensor(name, list(shape), dt).ap()

    xp_d = din("xp", [2 * TP, D])
    xs_d = din("xs", [T_s, D])
    ck_d = [din("ckg", [depth, PAST, 128]), din("ckw", [depth, PAST, 128])]
    cv_d = [din("cvg", [depth, PAST, 128]), din("cvw", [depth, PAST, 128])]
    sd_d = din("sd", [depth, 2, 8, 64, 64])
    vecs_d = din("vecs", [16, D])
    wmod_d = din("w_mod", [depth, D, 6 * D])
    bmod_d = din("b_mod", [depth, 6 * D])
    win_d = din("w_in", [depth, D, IN_COLS])
    conv_d = din("conv", [NR5, 1536])
    alog_d = din("a_log", [depth, 16])
    dtb_d = din("dt_bias", [depth, 16])
    anorm_d = din("a_norm", [depth, 64])
    qkn_d = din("qk_norm", [depth * 4, 64])
    sink_d = din("sink", [depth, 8])
    wbr_d = [din("w_br_a", [depth, 512, D]), din("w_br_b", [depth, 512, D]), din("w_br_c", [depth, 512, D])]
    wo_d = din("w_o", [depth, D, D])
    wf1_d = din("w_ff1", [depth, D, DFF])
    wf2_d = din("w_ff2", [depth, DFF, D])
    cst_d = din("cst", [128, NCST])
    rope_d = din("rope", [2, 128, T_s])
    yp_d = dout("yp", [2 * TP, D])
    ys_d = dout("ys", [T_s, D])
    nk_d = [dout("nkg", [2, depth, TP, 128]), dout("nkw", [2, depth, TP, 128])]
    nv_d = [dout("nvg", [2, depth, TP, 128]), dout("nvw", [2, depth, TP, 128])]
    nst_d = dout("nst", [2, depth, 2, 8, 64, 64])
    xT_d = dscr("xT", [D, Ttot])
    raw_d = dscr("raw", [1536, Ttot])
    zs_d = dscr("zs", [512, Ttot])
    bg_d = dscr("bg", [Ttot, 32])
    qT_d = [dscr("qTB", [512, Ttot], BF16), dscr("qTC", [512, Ttot], BF16)]
    kT_d = dscr("kT", [2, 128, Tk], BF16)
    vt_d = dscr("vt", [2, Tk, 128], BF16)
    sig_d = dscr("sig", [3 * D, Ttot], BF16)
    gq_d = dscr("gq", [16, 64, Ttot])
    ktok_d = dscr("ktok", [Ttot, 512])
    vtok_d = dscr("vtok", [Ttot, 512])
    oT_d = dscr("oT", [2, 512, Ttot])
    yT_d = dscr("yT", [3, 512, Ttot], BF16)
    h2T_d = dscr("h2T", [D, Ttot], BF16)

    S = Sched(nc)
    uid = [0]

    class Arena:
        def __init__(self, base, top):
            self.base = base
            self.top = top
            self.off = base

        def reset(self):
            self.off = self.base

        def alloc(self, shape, dt):
            n = 1
            for s_ in shape[1:]:
                n *= s_
            nbytes = n * (4 if dt in (F32, F32R) else 2)
            nbytes = (nbytes + 63) // 64 * 64
            assert self.off + nbytes <= self.top, (self.off, nbytes, self.top)
            uid[0] += 1
            t = nc.alloc_sbuf_tensor_at(f"t{uid[0]}", list(shape), dt, offset=self.off)
            self.off += nbytes
            return t

    pers = Arena(SB_BASE, SB_BASE + 16384)
    ar = Arena(SB_BASE + 16384, SB_TOP)

    class TL:
        def __init__(self, shape, dt, nb=1, arena=None):
            self.t = (arena or ar).alloc(shape, dt)
            self.b = [Buf() for _ in range(nb)]

    pst = [nc.alloc_psum_tensor(f"ps{i}", [128, 1024], F32) for i in range(4)]
    pbuf = [Buf(excl=True) for _ in range(8)]

    def bank(i):
        return pst[i // 2][:, (i % 2) * 512:(i % 2) * 512 + 512]

    dbufs = {}

    def db(name, i=0):
        k = (name, i)
        if k not in dbufs:
            dbufs[k] = Buf()
        return dbufs[k]

    def mm(out, lhsT, rhs, start, stop, reads, writes, **kw):
        S.pe(lambda e: e.matmul(out, lhsT=lhsT, rhs=rhs, start=start, stop=stop, **kw), reads, writes)

    def tr(out, in_, ident, reads, writes):
        S.pe(lambda e: e.transpose(out=out, in_=in_, identity=ident), reads, writes)

    cst = TL([128, NCST], F32, arena=pers)
    ones_bf = TL([128, 128], BF16, arena=pers)
    bd_bf = TL([128, 128], BF16, arena=pers)
    mw_bf = TL([128, 384], BF16, arena=pers)
    vecsT = TL([128, 8, 16], F32, arena=pers)
    silT = TL([128, 8, 2], BF16, arena=pers)
    convq = TL([128, 16, NR5], F32, arena=pers)
    convv = TL([128, 4, NR5], F32, arena=pers)
    qkw = TL([128, depth * 4], F32, arena=pers)
    anw = TL([128, depth], F32, arena=pers)
    dtb = TL([128, depth * 16], F32, arena=pers)
    nexpA = TL([128, depth * 16], F32, arena=pers)
    esink = TL([128, depth * 8], F32, arena=pers)
    modT = TL([128, 48, 2], F32, arena=pers)
    A1 = TL([128, 8, 2], F32, arena=pers)
    A2 = TL([128, 8, 2], F32, arena=pers)
    ones2 = TL([128, 2], BF16, arena=pers)
    C = cst.t
    ident = C[:, 0:128]
    rotT = C[:, 256:384]
    ones_f = C[:, 1664:1792]

    def gm(d, k):
        b0 = 384 + d * 640 + k * 64
        return C[0:64, b0:b0 + 64]

    S.dma("sp", cst.t[:], cst_d, writes=cst.b)
    S.act(lambda e: e.copy(out=ones_bf.t[:], in_=C[:, 1664:1792]), cst.b, ones_bf.b)
    S.act(lambda e: e.copy(out=bd_bf.t[:], in_=C[:, 128:256]), cst.b, bd_bf.b)
    S.act(lambda e: e.copy(out=mw_bf.t[:], in_=C[:, 1792:2176]), cst.b, mw_bf.b)
    S.pool(lambda e: e.memset(ones2.t[:], 1.0), (), ones2.b)
    ar.reset()
    vr = TL([16, D], F32)
    cr = TL([NR5, 1536], F32)
    qr = TL([depth * 4, 128], F32)
    anr = TL([depth, 64], F32)
    S.dma("sp", vr.t[:], vecs_d, writes=vr.b)
    S.dma("sp", cr.t[:], conv_d, writes=cr.b)
    S.dma("sp", qr.t[:, 0:64], qkn_d, writes=qr.b)
    S.dma("sp", qr.t[:, 64:128], qkn_d, writes=qr.b)
    S.dma("sp", anr.t[:], anorm_d, writes=anr.b)
    S.dma("sp", dtb.t[:], dtb_d.rearrange("l x -> (l x)").partition_broadcast(128), writes=dtb.b)
    S.dma("sp", nexpA.t[:], alog_d.rearrange("l x -> (l x)").partition_broadcast(128), writes=nexpA.b)
    S.dma("sp", esink.t[:], sink_d.rearrange("l x -> (l x)").partition_broadcast(128), writes=esink.b)
    for c in range(8):
        tr(bank(0)[:, c * 16:(c + 1) * 16], vr.t[0:16, c * 128:(c + 1) * 128], ident[0:16, 0:16], vr.b + cst.b, [pbuf[0]])
    S.dve(lambda e: e.tensor_copy(out=vecsT.t[:], in_=bank(0)[:, 0:128].rearrange("p (c r) -> p c r", r=16)), [pbuf[0]], vecsT.b)
    for j in range(16):
        tr(bank(1)[0:64, j * NR5:(j + 1) * NR5], cr.t[0:NR5, j * 64:(j + 1) * 64], ident[0:NR5, 0:NR5], cr.b + cst.b, [pbuf[1]])
    S.dve(lambda e: e.tensor_copy(out=convq.t[0:64], in_=bank(1)[0:64, 0:16 * NR5].rearrange("p (c r) -> p c r", r=NR5)), [pbuf[1]], convq.b)
    for j in range(4):
        tr(bank(2)[:, j * NR5:(j + 1) * NR5], cr.t[0:NR5, 1024 + j * 128:1024 + (j + 1) * 128], ident[0:NR5, 0:NR5], cr.b + cst.b, [pbuf[2]])
    S.dve(lambda e: e.tensor_copy(out=convv.t[:], in_=bank(2)[:, 0:4 * NR5].rearrange("p (c r) -> p c r", r=NR5)), [pbuf[2]], convv.b)
    tr(bank(3)[:, 0:depth * 4], qr.t[0:depth * 4, :], ident[0:depth * 4, 0:depth * 4], qr.b + cst.b, [pbuf[3]])
    S.dve(lambda e: e.tensor_copy(out=qkw.t[:], in_=bank(3)[:, 0:depth * 4]), [pbuf[3]], qkw.b)
    tr(bank(3)[0:64, 64:64 + depth], anr.t[0:depth, :], ident[0:depth, 0:depth], anr.b + cst.b, [pbuf[3]])
    S.dve(lambda e: e.tensor_copy(out=anw.t[0:64], in_=bank(3)[0:64, 64:64 + depth]), [pbuf[3]], anw.b)
    sg = TL([128, 8, 2], F32)
    S.act(lambda e: e.activation(out=sg.t[:], in_=vecsT.t[:, :, 0:2], func=AF.Exp, scale=-1.0), vecsT.b, sg.b)
    S.act(lambda e: e.activation(out=sg.t[:], in_=sg.t[:], func=AF.Ln, bias=1.0), sg.b, sg.b)
    S.act(lambda e: e.activation(out=sg.t[:], in_=sg.t[:], func=AF.Exp, scale=-1.0), sg.b, sg.b)
    S.dve(lambda e: e.tensor_tensor(out=silT.t[:], in0=sg.t[:], in1=vecsT.t[:, :, 0:2], op=ALU.mult), sg.b + vecsT.b, silT.b)
    S.act(lambda e: e.activation(out=nexpA.t[:], in_=nexpA.t[:], func=AF.Exp), nexpA.b, nexpA.b)
    S.dve(lambda e: e.tensor_scalar(out=nexpA.t[:], in0=nexpA.t[:], scalar1=-1.0, scalar2=None, op0=ALU.mult), nexpA.b, nexpA.b)
    S.act(lambda e: e.activation(out=esink.t[:], in_=esink.t[:], func=AF.Exp), esink.b, esink.b)
    S.barrier()

    ar.reset()
    xin = [TL([128, D], F32) for _ in range(2)]
    xTb = [TL([128, 8, 512], F32) for _ in range(2)]
    for bi in range(NB):
        xo = xTb[bi % 2]
        for tt in range(4):
            t0 = bi * 512 + tt * 128
            xi = xin[tt % 2]
            src = xp_d[t0:t0 + 128, :] if bi == 0 else xs_d[t0 - 512:t0 - 512 + 128, :]
            S.dma("sp", xi.t[:], src, writes=xi.b)
            for c in range(8):
                bk = 2 * (c // 4) + (tt % 2) * 4
                tr(bank(bk)[:, (c % 4) * 128:(c % 4 + 1) * 128], xi.t[:, c * 128:(c + 1) * 128], ident, xi.b + cst.b, [pbuf[bk]])
            for half in range(2):
                bk = 2 * half + (tt % 2) * 4
                eng = S.act if half == 0 else S.dve
                src_ps = bank(bk).rearrange("p (c t) -> p c t", t=128)
                dst = xo.t[:, half * 4:half * 4 + 4, tt * 128:(tt + 1) * 128]
                if half == 0:
                    S.act(lambda e, dst=dst, src_ps=src_ps: e.copy(out=dst, in_=src_ps), [pbuf[bk]], xo.b)
                else:
                    S.dve(lambda e, dst=dst, src_ps=src_ps: e.tensor_copy(out=dst, in_=src_ps), [pbuf[bk]], xo.b)
        S.dma("sp", xT_d.rearrange("(c p) t -> p c t", p=128)[:, :, bi * 512:(bi + 1) * 512], xo.t[:], reads=xo.b, writes=[db("xT", bi)])
    S.barrier()

    def load_w_bf16(dst_tl, src_ap_fn, nchunk):
        for c in range(nchunk):
            S.dma("pool", dst_tl.t[:, c], src_ap_fn(c), writes=dst_tl.b)

    def phase_M(l):
        ar.reset()
        wm = [TL([128, 8, 1536], BF16) for _ in range(2)]
        bm = TL([1, 6 * D], BF16)
        S.dma("pool", bm.t[:], bmod_d[l:l + 1, :], writes=bm.b)
        for q in range(4):
            w = wm[q % 2]
            load_w_bf16(w, lambda c: wmod_d[l, c * 128:(c + 1) * 128, q * 1536:(q + 1) * 1536], 8)
            for gg in range(12):
                g = q * 12 + gg
                o_ = bank(0)[:, 2 * g:2 * g + 2]
                for c in range(8):
                    mm(o_, w.t[:, c, gg * 128:(gg + 1) * 128], silT.t[:, c, :], c == 0, False, w.b + silT.b, [pbuf[0]])
                mm(o_, bm.t[0:1, g * 128:(g + 1) * 128], ones2.t[0:1, :], False, True, bm.b + ones2.b, [pbuf[0]])
        S.dve(lambda e: e.tensor_copy(out=modT.t[:], in_=bank(0)[:, 0:96].rearrange("p (g v) -> p g v", v=2)), [pbuf[0]], modT.b)
        ln1 = vecsT.t[:, :, 2 + l:3 + l].to_broadcast([128, 8, 2])
        ln2 = vecsT.t[:, :, 2 + depth + l:3 + depth + l].to_broadcast([128, 8, 2])
        S.dve(lambda e: e.scalar_tensor_tensor(out=A1.t[:], in0=modT.t[:, 8:16, :], scalar=1.0, in1=ln1, op0=ALU.add, op1=ALU.mult), modT.b + vecsT.b, A1.b)
        S.dve(lambda e: e.scalar_tensor_tensor(out=A2.t[:], in0=modT.t[:, 32:40, :], scalar=1.0, in1=ln2, op0=ALU.add, op1=ALU.mult), modT.b + vecsT.b, A2.b)
        S.barrier()

    def norm_block(xT, hT, mv, Aap, shg, sq, tmp, statb, lnv, rstd):
        for c in range(8):
            s_ = sq[c % 2]
            S.act(lambda e, s_=s_, c=c: e.activation(out=s_.t[:], in_=xT.t[:, c, :], func=AF.Square), xT.b, s_.b)
            mm(bank(statb), ones_bf.t[:], s_.t[:], c == 0, c == 7, ones_bf.b + s_.b, [pbuf[statb]])
        S.act(lambda e: e.activation(out=lnv.t[:], in_=bank(statb), func=AF.Ln, bias=EPS, scale=1.0 / D), [pbuf[statb]], lnv.b)
        S.act(lambda e: e.activation(out=rstd.t[:], in_=lnv.t[:], func=AF.Exp, scale=-0.5), lnv.b, rstd.b)
        for c in range(8):
            t_ = tmp[c % 2]
            S.dve(lambda e, t_=t_, c=c: e.tensor_tensor(out=t_.t[:], in0=xT.t[:, c, :], in1=rstd.t[:], op=ALU.mult), xT.b + rstd.b, t_.b)
            S.act(lambda e, t_=t_, c=c: e.activation(out=hT.t[:, c, :], in_=t_.t[:], func=AF.Identity,
                                                    bias=modT.t[:, shg + c, mv:mv + 1], scale=Aap.t[:, c, mv:mv + 1]),
                  t_.b + modT.b + Aap.b, hT.b)

    def phase_A(l):
        ar.reset()
        win = TL([128, 8, IN_COLS], BF16)
        load_w_bf16(win, lambda c: win_d[l, c * 128:(c + 1) * 128, :], 8)
        xT = [TL([128, 8, 512], F32) for _ in range(2)]
        hT = TL([128, 8, 512], BF16)
        sq = [TL([128, 512], BF16) for _ in range(2)]
        tmp = [TL([128, 512], F32) for _ in range(2)]
        lnv = TL([128, 512], F32)
        rstd = TL([128, 512], F32)
        stf = [TL([128, 512], F32) for _ in range(4)]
        stb = [TL([128, 512], BF16) for _ in range(4)]
        qn = [TL([128, 512], F32) for _ in range(2)]
        t1 = [TL([128, 512], F32) for _ in range(2)]
        t2 = [TL([128, 512], F32) for _ in range(2)]
        cs = [TL([128, 2, 512], F32) for _ in range(2)]
        bgst = [TL([128, 32], F32) for _ in range(2)]
        e1 = [TL([128, 32], F32) for _ in range(2)]
        vst = [TL([128, 256], F32) for _ in range(2)]
        kst = t1
        cnt = {"f": 0, "b": 0, "q": 0, "g": 0, "t": 0}

        def nxt(k, n):
            cnt[k] += 1
            return cnt[k] % n

        def load_x(bi):
            S.dma("sp", xT[bi % 2].t[:], xT_d.rearrange("(c p) t -> p c t", p=128)[:, :, bi * 512:(bi + 1) * 512],
                  reads=[db("xT", bi)], writes=xT[bi % 2].b)

        load_x(0)
        for bi in range(NB):
            mv = 0 if bi == 0 else 1
            x_ = xT[bi % 2]
            if bi + 1 < NB:
                load_x(bi + 1)
            if mv:
                c_ = cs[bi % 2]
                S.dma("sp", c_.t[:], rope_d.rearrange("a p t -> p a t")[:, :, (bi - 1) * 512:bi * 512], writes=c_.b)
            norm_block(x_, hT, mv, A1, 0, sq, tmp, 7, lnv, rstd)
            t0 = bi * 512
            groups = []
            for j in range(16):
                groups.append(("raw", C_QKV + j * 64, 64, j * 64))
            for j in range(4):
                groups.append(("raw", C_QKV + 1024 + j * 128, 128, 1024 + j * 128))
            for j in range(8):
                groups.append(("z", C_Z + j * 64, 64, j * 64))
            for a, (cq, ckk) in enumerate(((C_BQ, C_BK), (C_CQ, C_CK))):
                for m in range(4):
                    groups.append(("q", (cq + m * 64, cq + (m + 4) * 64), 128, (a, m)))
                groups.append(("k", ckk, 128, (a, 0)))
            for j in range(24):
                groups.append(("gate", C_GATE + j * 128, 128, j * 128))
            pending = []
            for gi, (kind, col, M, info) in enumerate(groups):
                bk = gi % 4
                for c in range(8):
                    if kind == "q":
                        mm(bank(bk)[0:64, :], win.t[:, c, col[0]:col[0] + 64], hT.t[:, c, :], c == 0, c == 7, win.b + hT.b, [pbuf[bk]])
                        mm(bank(bk)[64:128, :], win.t[:, c, col[1]:col[1] + 64], hT.t[:, c, :], c == 0, c == 7, win.b + hT.b, [pbuf[bk]],
                           tile_position=(0, 64))
                    else:
                        mm(bank(bk)[0:M, :], win.t[:, c, col:col + M], hT.t[:, c, :], c == 0, c == 7, win.b + hT.b, [pbuf[bk]])
                P = bank(bk)
                while pending and pending[0][0] <= gi:
                    pending.pop(0)[1]()
                if kind == "raw":
                    s_ = stf[nxt("f", 4)]
                    S.act(lambda e, s_=s_, P=P, M=M: e.copy(out=s_.t[0:M, :], in_=P[0:M, :]), [pbuf[bk]], s_.b)
                    S.dma("sp", raw_d[info:info + M, t0:t0 + 512], s_.t[0:M, :], reads=s_.b, writes=[db("raw", bi)])
                elif kind == "z":
                    u_ = stf[nxt("f", 4)]
                    S.act(lambda e, u_=u_, P=P: e.activation(out=u_.t[0:64, :], in_=P[0:64, :], func=AF.Silu), [pbuf[bk]], u_.b)
                    S.dma("sp", zs_d[info:info + 64, t0:t0 + 512], u_.t[0:64, :], reads=u_.b, writes=[db("zs", bi)])
                elif kind == "gate":
                    o_ = stb[nxt("b", 4)]
                    S.act(lambda e, o_=o_, P=P: e.activation(out=o_.t[:], in_=P, func=AF.Sigmoid), [pbuf[bk]], o_.b)
                    S.dma("sp", sig_d[info:info + 128, t0:t0 + 512], o_.t[:], reads=o_.b, writes=[db("sig", bi)])
                else:
                    a, m = info
                    wi = a * 2 + (0 if kind == "q" else 1)
                    s_ = sq[nxt("q", 2)]
                    S.act(lambda e, s_=s_, P=P: e.activation(out=s_.t[:], in_=P, func=AF.Square), [pbuf[bk]], s_.b)
                    q_ = qn[cnt["q"] % 2]
                    a_ = t1[cnt["q"] % 2]
                    b_ = t2[cnt["q"] % 2]

                    def step1(s_=s_, P=P, bk=bk, wi=wi, q_=q_):
                        sb_ = 4 + nxt("g", 2)
                        mm(bank(sb_), bd_bf.t[:], s_.t[:], True, True, bd_bf.b + s_.b, [pbuf[sb_]])
                        r_ = stf[nxt("f", 4)]
                        S.act(lambda e: e.activation(out=r_.t[:], in_=bank(sb_), func=AF.Ln, bias=EPS, scale=1.0 / HD), [pbuf[sb_]], r_.b)
                        S.act(lambda e: e.activation(out=r_.t[:], in_=r_.t[:], func=AF.Exp, scale=-0.5), r_.b, r_.b)
                        S.dve(lambda e: e.scalar_tensor_tensor(out=q_.t[:], in0=P, scalar=qkw.t[:, l * 4 + wi:l * 4 + wi + 1],
                                                               in1=r_.t[:], op0=ALU.mult, op1=ALU.mult),
                              [pbuf[bk]] + r_.b + qkw.b, q_.b)

                    def step2(kind=kind, a=a, m=m, q_=q_, a_=a_, b_=b_, mv=mv, bi=bi, t0=t0):
                        o_ = stb[nxt("b", 4)]
                        if mv:
                            c_ = cs[bi % 2]
                            rb = 6
                            mm(bank(rb), rotT, q_.t[:], True, True, cst.b + q_.b, [pbuf[rb]])
                            S.dve(lambda e: e.tensor_tensor(out=a_.t[:], in0=q_.t[:], in1=c_.t[:, 0, :], op=ALU.mult), q_.b + c_.b, a_.b)
                            S.dve(lambda e: e.tensor_tensor(out=b_.t[:], in0=bank(6), in1=c_.t[:, 1, :], op=ALU.mult), [pbuf[rb]] + c_.b, b_.b)
                            S.pool(lambda e: e.tensor_tensor(out=o_.t[:], in0=a_.t[:], in1=b_.t[:], op=ALU.add), a_.b + b_.b, o_.b)
                        else:
                            S.act(lambda e: e.copy(out=o_.t[:], in_=q_.t[:]), q_.b, o_.b)
                        if kind == "q":
                            S.dma("sp", qT_d[a][m * 128:(m + 1) * 128, t0:t0 + 512], o_.t[:], reads=o_.b, writes=[db("qT%d" % a, bi)])
                        else:
                            kc0 = t0 if bi == 0 else t0 + PAST
                            S.dma("sp", kT_d[a, :, kc0:kc0 + 512], o_.t[:], reads=o_.b, writes=[db("kT", bi)])
                            if bi == 0:
                                for tt in range(4):
                                    tr(bank(6)[:, (tt % 4) * 128:(tt % 4 + 1) * 128], q_.t[:, tt * 128:(tt + 1) * 128], ident, q_.b + cst.b, [pbuf[6]])
                                k_ = kst[a]
                                S.dve(lambda e: e.tensor_copy(out=k_.t[:], in_=bank(6)), [pbuf[6]], k_.b)
                                for tt in range(4):
                                    S.dma("sp", nk_d[a][tt // 2, l, (tt % 2) * 128:(tt % 2 + 1) * 128, :], k_.t[:, tt * 128:(tt + 1) * 128], reads=k_.b)

                    pending.append((gi + 1, step1))
                    pending.append((gi + 2, step2))
            while pending:
                pending.pop(0)[1]()
            for tt in range(4):
                tb = 4 + (tt % 2)
                P = bank(tb)
                tok = slice(tt * 128, (tt + 1) * 128)
                for (c0, n_, o0) in ((C_BETA, 32, 0), (C_BV, 128, 32), (C_CV, 128, 160)):
                    for c in range(8):
                        mm(P[:, o0:o0 + n_], hT.t[:, c, tok], win.t[:, c, c0:c0 + n_], c == 0, c == 7, win.b + hT.b, [pbuf[tb]])
                g_ = bgst[tt % 2]
                e_ = e1[tt % 2]
                S.dve(lambda e, e_=e_, P=P: e.tensor_tensor(out=e_.t[:, 16:32], in0=P[:, 16:32], in1=dtb.t[:, l * 16:(l + 1) * 16], op=ALU.add), [pbuf[tb]] + dtb.b, e_.b)
                S.act(lambda e, e_=e_: e.activation(out=e_.t[:, 16:32], in_=e_.t[:, 16:32], func=AF.Exp), e_.b, e_.b)
                S.act(lambda e, e_=e_: e.activation(out=e_.t[:, 16:32], in_=e_.t[:, 16:32], func=AF.Ln, bias=1.0), e_.b, e_.b)
                S.dve(lambda e, e_=e_, g_=g_: e.tensor_tensor(out=g_.t[:, 16:32], in0=e_.t[:, 16:32], in1=nexpA.t[:, l * 16:(l + 1) * 16], op=ALU.mult), e_.b + nexpA.b, g_.b)
                S.act(lambda e, e_=e_, P=P: e.activation(out=e_.t[:, 0:16], in_=P[:, 0:16], func=AF.Exp, scale=-1.0), [pbuf[tb]], e_.b)
                S.act(lambda e, e_=e_: e.activation(out=e_.t[:, 0:16], in_=e_.t[:, 0:16], func=AF.Ln, bias=1.0), e_.b, e_.b)
                S.act(lambda e, e_=e_, g_=g_: e.activation(out=g_.t[:, 0:16], in_=e_.t[:, 0:16], func=AF.Exp, scale=-1.0), e_.b, g_.b)
                S.dma("sp", bg_d[t0 + tt * 128:t0 + (tt + 1) * 128, :], g_.t[:], reads=g_.b, writes=[db("bg", bi)])
                v_ = vst[tt % 2]
                S.act(lambda e, v_=v_, P=P: e.copy(out=v_.t[:], in_=P[:, 32:288]), [pbuf[tb]], v_.b)
                kc0 = (t0 if bi == 0 else t0 + PAST) + tt * 128
                for a in range(2):
                    S.dma("pool", vt_d[a, kc0:kc0 + 128, :], v_.t[:, a * 128:(a + 1) * 128], reads=v_.b, writes=[db("vt", bi)])
                    if bi == 0:
                        S.dma("sp", nv_d[a][tt // 2, l, (tt % 2) * 128:(tt % 2 + 1) * 128, :], v_.t[:, a * 128:(a + 1) * 128], reads=v_.b)
        S.barrier()

    def phase_B1(l):
        ar.reset()
        rawt = [TL([128, 520], F32) for _ in range(3)]
        acc = [TL([128, 512], F32) for _ in range(3)]
        fqk = TL([64, 16, 512], F32, nb=16)
        sqb = [TL([64, 512], BF16) for _ in range(4)]
        rs = [TL([64, 512], F32) for _ in range(4)]
        kn = TL([64, 8, 512], F32)
        qst = [TL([64, 512], F32) for _ in range(2)]
        fv = TL([128, 4, 512], F32)
        tst = [TL([128, 512], F32) for _ in range(2)]
        stat_banks = (0, 1, 6, 7)
        n_ = [0]
        for bi in range(NB):
            t0 = bi * 512
            segs = [(0, 256, 0, 256), (256, 512, 256, 512)] if bi == 0 else [(0, 512, seqs[2][0] - t0, seqs[2][0] + seqs[2][1] - t0)]
            for ti in range(20):
                M = 64 if ti < 16 else 128
                ch0 = ti * 64 if ti < 16 else 1024 + (ti - 16) * 128
                wtb = convq.b if ti < 16 else convv.b
                n_[0] += 1
                r_ = rawt[n_[0] % 3]
                a_ = acc[n_[0] % 3]
                for (a0, a1, s0, s1) in segs:
                    lo = max(a0 - 2, s0)
                    hi = min(a1 + 2, s1)
                    off = a0 if len(segs) == 1 else a0 + (4 if a0 else 0)
                    if lo > a0 - 2:
                        S.pool(lambda e, r_=r_, off=off, M=M: e.memset(r_.t[0:M, off:off + 2], 0.0), (), r_.b)
                    if hi < a1 + 2:
                        S.pool(lambda e, r_=r_, off=off, M=M, a0=a0, a1=a1: e.memset(r_.t[0:M, off + (a1 - a0) + 2:off + (a1 - a0) + 4], 0.0), (), r_.b)
                    S.dma("sp", r_.t[0:M, off + (lo - (a0 - 2)):off + (hi - (a0 - 2))], raw_d[ch0:ch0 + M, t0 + lo:t0 + hi],
                          reads=[db("raw", bi), db("raw", max(bi - 1, 0)), db("raw", min(bi + 1, NB - 1))], writes=r_.b)
                    n = a1 - a0
                    for k in range(5):
                        src = r_.t[0:M, off + k:off + k + n]
                        dst = a_.t[0:M, a0:a1]
                        sc = convv.t[:, ti - 16, l * 5 + k:l * 5 + k + 1] if ti >= 16 else convq.t[0:64, ti, l * 5 + k:l * 5 + k + 1]
                        if k == 0:
                            S.dve(lambda e, dst=dst, src=src, sc=sc: e.tensor_scalar(out=dst, in0=src, scalar1=sc, scalar2=None, op0=ALU.mult), r_.b + wtb, a_.b)
                        else:
                            S.dve(lambda e, dst=dst, src=src, sc=sc: e.scalar_tensor_tensor(out=dst, in0=src, scalar=sc, in1=dst, op0=ALU.mult, op1=ALU.add),
                                  r_.b + wtb + a_.b, a_.b)
                if ti >= 16:
                    S.act(lambda e, a_=a_, ti=ti: e.activation(out=fv.t[:, ti - 16, :], in_=a_.t[:], func=AF.Silu), a_.b, fv.b)
                else:
                    S.act(lambda e, a_=a_, ti=ti: e.activation(out=fqk.t[:, ti, :], in_=a_.t[0:64, :], func=AF.Silu), a_.b, [fqk.b[ti]])
            LA = 2
            inflight = {}

            def p2a(ti):
                n_[0] += 1
                s_ = sqb[ti % 4]
                sb_ = stat_banks[ti % 4]
                S.act(lambda e, s_=s_, ti=ti: e.activation(out=s_.t[:], in_=fqk.t[:, ti, :], func=AF.Square), [fqk.b[ti]], s_.b)
                mm(bank(sb_)[0:64, :], ones_bf.t[0:64, 0:64], s_.t[:], True, True, ones_bf.b + s_.b, [pbuf[sb_]])

            def p2b(ti):
                sb_ = stat_banks[ti % 4]
                r2 = rs[ti % 4]
                S.act(lambda e, r2=r2, sb_=sb_: e.activation(out=r2.t[:], in_=bank(sb_)[0:64, :], func=AF.Ln, bias=EPS), [pbuf[sb_]], r2.b)
                S.act(lambda e, r2=r2: e.activation(out=r2.t[:], in_=r2.t[:], func=AF.Exp, scale=-0.5), r2.b, r2.b)
                if ti < 8:
                    o_ = qst[ti % 2]
                    S.dve(lambda e, o_=o_, r2=r2, ti=ti: e.scalar_tensor_tensor(out=o_.t[:], in0=fqk.t[:, ti, :], scalar=HD ** -0.5, in1=r2.t[:],
                                                                            op0=ALU.mult, op1=ALU.mult), [fqk.b[ti]] + r2.b, o_.b)
                    S.dma("sp", gq_d[ti, :, t0:t0 + 512], o_.t[:], reads=o_.b, writes=[db("gq", bi)])
                else:
                    S.dve(lambda e, r2=r2, ti=ti: e.tensor_tensor(out=kn.t[:, ti - 8, :], in0=fqk.t[:, ti, :], in1=r2.t[:], op=ALU.mult), [fqk.b[ti]] + r2.b, kn.b)
                    S.dma("sp", gq_d[ti, :, t0:t0 + 512], kn.t[:, ti - 8, :], reads=kn.b, writes=[db("gq", bi)])

            for i in range(16 + LA):
                if i < 16:
                    p2a(i)
                if i >= LA:
                    p2b(i - LA)
            for tt in range(4):
                kb, vb = 2 + (tt % 2) * 2, 3 + (tt % 2) * 2
                for h in range(8):
                    tr(bank(kb)[:, h * 64:(h + 1) * 64], kn.t[:, h, tt * 128:(tt + 1) * 128], ident[0:64, 0:64], kn.b + cst.b, [pbuf[kb]])
                for j in range(4):
                    tr(bank(vb)[:, j * 128:(j + 1) * 128], fv.t[:, j, tt * 128:(tt + 1) * 128], ident, fv.b + cst.b, [pbuf[vb]])
                k_ = tst[0]
                v_ = tst[1]
                S.act(lambda e, k_=k_, kb=kb: e.copy(out=k_.t[:], in_=bank(kb)), [pbuf[kb]], k_.b)
                S.dve(lambda e, v_=v_, vb=vb: e.tensor_copy(out=v_.t[:], in_=bank(vb)), [pbuf[vb]], v_.b)
                S.dma("sp", ktok_d[t0 + tt * 128:t0 + (tt + 1) * 128, :], k_.t[:], reads=k_.b, writes=[db("ktok", bi)])
                S.dma("sp", vtok_d[t0 + tt * 128:t0 + (tt + 1) * 128, :], v_.t[:], reads=v_.b, writes=[db("vtok", bi)])
        S.barrier()

    def phase_B23(l):
        ar.reset()
        f3 = [64, 8, 64]

        def T3(dt=F32):
            return TL(f3, dt)

        D_ = []
        for d in range(2):
            o = {}
            o["qk"] = [TL([64, 16, 256], BF16) for _ in range(1)]
            o["ktok"] = [TL([64, 8, 64], F32) for _ in range(2)]
            o["vtok"] = [TL([64, 8, 64], F32) for _ in range(2)]
            o["bg"] = [TL([64, 32], F32) for _ in range(2)]
            tmp_ = [T3() for _ in range(7)]
            o["Gexp"], o["Bexp"], o["XL"], o["XU"], o["dl"], o["du"], o["M1T"] = tmp_
            o["decay"], o["decayT"], o["egB"], o["nbU"], o["nbL"], o["M1"] = tmp_[2], tmp_[3], tmp_[0], tmp_[1], tmp_[4], tmp_[5]
            o["gc"] = TL([64, 8], F32)
            o["eg"] = TL([64, 8], F32)
            o["beg"] = TL([64, 8], F32)
            o["PA"] = [TL([64, 8, 128], F32R) for _ in range(2)]
            o["PW"] = [TL([64, 8, 128], F32R) for _ in range(2)]
            o["Zm1"], o["ZmT1"], o["ZmT2"], o["X"], o["Xp"] = T3(F32R), T3(F32R), T3(F32R), T3(F32R), T3(F32R)
            o["bv"] = T3(F32R)
            o["bek"] = T3(F32R)
            o["up"] = [T3() for _ in range(2)]
            o["wT"] = [T3(BF16) for _ in range(2)]
            o["qgT"] = [T3(BF16) for _ in range(2)]
            o["attnT"] = [T3(BF16) for _ in range(2)]
            o["kg"] = [T3(BF16) for _ in range(2)]
            o["glB"] = [TL([64, 8], F32) for _ in range(2)]
            o["u"] = T3(BF16)
            o["s"] = T3()
            o["s1"] = T3()
            o["sbf"] = T3(BF16)
            o["ost"] = [TL([64, 8, 256], F32) for _ in range(1)]
            D_.append(o)
        ones64 = ones_f[0:64, 0:64]
        id64 = ident[0:64, 0:64]

        def bc_h(ap2):
            return ap2.unsqueeze(2).to_broadcast(f3)

        def bc_m(ap2):
            return ap2.unsqueeze(1).to_broadcast(f3)

        def v3(ap):
            return ap.rearrange("p (h f) -> p h f", f=64)

        for (s0, T, is_s) in seqs:
            si = seqs.index((s0, T, is_s))
            N = T // 64
            for d in range(2):
                o = D_[d]
                if is_s:
                    S.dma("sp", o["s"].t[:], sd_d[l, d].rearrange("h k v -> k h v"), writes=o["s"].b)
                else:
                    S.pool(lambda e, o=o: e.memset(o["s"].t[:], 0.0), (), o["s"].b)
                S.act(lambda e, o=o: e.copy(out=o["sbf"].t[:], in_=o["s"].t[:]), o["s"].b, o["sbf"].b)

            CUT = int(os.environ.get('B23CUT', '99'))

            def b2(d, c, n):
                o = D_[d]
                yield
                PA, PB, PC, PD = bank(4 * d), bank(4 * d + 1), bank(4 * d + 2), bank(4 * d + 3)
                bA, bB, bC, bD = pbuf[4 * d], pbuf[4 * d + 1], pbuf[4 * d + 2], pbuf[4 * d + 3]
                PCD = pst[2 * d + 1][0:64, :].rearrange("p (h f) -> p h f", f=128)
                tok0 = s0 + c * 64
                bi, cb = tok0 // 512, (tok0 % 512) // 64
                r = n % 2
                c4 = (tok0 % 256) // 64
                qk = o["qk"][0]
                if c4 == (0 if d == 0 else 3):
                    hb0 = (tok0 // 256) * 256
                    S.dma("pool", qk.t[:], gq_d.rearrange("j p t -> p j t")[:, :, hb0:hb0 + 256], reads=[db("gq", bi)], writes=qk.b)
                kt, vt, bgt = o["ktok"][r], o["vtok"][r], o["bg"][r]
                S.dma("sp", kt.t[:], ktok_d[tok0:tok0 + 64, :].rearrange("p (h f) -> p h f", f=64), reads=[db("ktok", bi)], writes=kt.b)
                S.dma("sp", vt.t[:], vtok_d[tok0:tok0 + 64, :].rearrange("p (h f) -> p h f", f=64), reads=[db("vtok", bi)], writes=vt.b)
                S.dma("sp", bgt.t[:], bg_d[tok0:tok0 + 64, :], reads=[db("bg", bi)], writes=bgt.b)
                yield
                g = bgt.t[:, 16 + d * 8:24 + d * 8]
                b = bgt.t[:, d * 8:d * 8 + 8]
                Ud, nm, nmT, st, stT = (gm(d, k) for k in range(5))
                last = 63 if d == 0 else 0
                qc = qk.t[:, 0:8, c4 * 64:(c4 + 1) * 64]
                kc = qk.t[:, 8:16, c4 * 64:(c4 + 1) * 64]
                if CUT < 2:
                    return
                mm(PC[0:64, 0:8], Ud, g, True, True, cst.b + bgt.b, [bC])
                S.dve(lambda e: e.tensor_tensor(out=o["Gexp"].t[:], in0=bc_h(g), in1=bc_m(Ud), op=ALU.mult), bgt.b + cst.b, o["Gexp"].b)
                S.pool(lambda e: e.tensor_tensor(out=o["Bexp"].t[:], in0=bc_h(b), in1=bc_m(id64), op=ALU.mult), bgt.b + cst.b, o["Bexp"].b)
                yield
                mm(PA[0:64, :], ones64, o["Gexp"].t[:].rearrange("p h f -> p (h f)"), True, True, cst.b + o["Gexp"].b, [bA])
                mm(PB[0:64, :], ones64, o["Bexp"].t[:].rearrange("p h f -> p (h f)"), True, True, cst.b + o["Bexp"].b, [bB])
                S.act(lambda e: e.copy(out=o["gc"].t[:], in_=PC[0:64, 0:8]), [bC], o["gc"].b)
                yield
                if CUT < 3:
                    return
                gcb = bc_h(o["gc"].t[:])
                S.dve(lambda e: e.tensor_tensor(out=o["XL"].t[:], in0=gcb, in1=bc_m(nm), op=ALU.add), o["gc"].b + cst.b, o["XL"].b)
                S.pool(lambda e: e.tensor_tensor(out=o["XU"].t[:], in0=bc_m(nmT), in1=gcb, op=ALU.subtract), o["gc"].b + cst.b, o["XU"].b)
                yield
                S.dve(lambda e: e.scalar_tensor_tensor(out=o["dl"].t[:], in0=v3(PA[0:64, :]), scalar=-1.0, in1=o["XL"].t[:], op0=ALU.mult, op1=ALU.add),
                      [bA] + o["XL"].b, o["dl"].b)
                S.dve(lambda e: e.tensor_tensor(out=o["du"].t[:], in0=v3(PA[0:64, :]), in1=o["XU"].t[:], op=ALU.add), [bA] + o["XU"].b, o["du"].b)
                S.act(lambda e: e.activation(out=o["egB"].t[:], in_=v3(PA[0:64, :]), func=AF.Exp), [bA], o["egB"].b)
                S.act(lambda e: e.activation(out=o["decay"].t[:], in_=o["dl"].t[:], func=AF.Exp), o["dl"].b, o["decay"].b)
                S.act(lambda e: e.activation(out=o["decayT"].t[:], in_=o["du"].t[:], func=AF.Exp), o["du"].b, o["decayT"].b)
                S.act(lambda e: e.activation(out=o["eg"].t[:], in_=o["gc"].t[:], func=AF.Exp), o["gc"].b, o["eg"].b)
                yield
                S.dve(lambda e: e.tensor_tensor(out=o["nbU"].t[:], in0=v3(PB[0:64, :]), in1=bc_m(stT), op=ALU.mult), [bB] + cst.b, o["nbU"].b)
                S.pool(lambda e: e.tensor_tensor(out=o["nbL"].t[:], in0=bc_h(b), in1=bc_m(st), op=ALU.mult), bgt.b + cst.b, o["nbL"].b)
                yield
                if CUT < 4:
                    return
                for h in range(8):
                    mm(PA[0:64, h * 64:(h + 1) * 64], kc[:, h, :], kc[:, h, :], True, True, qk.b, [bA])
                for h in range(8):
                    mm(PB[0:64, h * 64:(h + 1) * 64], kc[:, h, :], qc[:, h, :], True, True, qk.b, [bB])
                S.dve(lambda e: e.tensor_tensor(out=o["M1"].t[:], in0=v3(PA[0:64, :]), in1=o["decay"].t[:], op=ALU.mult), [bA] + o["decay"].b, o["M1"].b)
                S.dve(lambda e: e.tensor_tensor(out=o["M1T"].t[:], in0=v3(PA[0:64, :]), in1=o["decayT"].t[:], op=ALU.mult), [bA] + o["decayT"].b, o["M1T"].b)
                at = o["attnT"][r]
                S.dve(lambda e: e.tensor_tensor(out=at.t[:], in0=v3(PB[0:64, :]), in1=o["decayT"].t[:], op=ALU.mult), [bB] + o["decayT"].b, at.b)
                yield
                S.pool(lambda e: e.tensor_tensor(out=o["M1"].t[:], in0=o["M1"].t[:], in1=o["nbL"].t[:], op=ALU.mult), o["M1"].b + o["nbL"].b, o["M1"].b)
                S.pool(lambda e: e.tensor_tensor(out=o["M1T"].t[:], in0=o["M1T"].t[:], in1=o["nbU"].t[:], op=ALU.mult), o["M1T"].b + o["nbU"].b, o["M1T"].b)
                yield
                if CUT < 5:
                    return
                P0f, P0Tf = o["M1"], o["M1T"]
                BD, MA1, MA1T, MA2, MA2T = (gm(d, k) for k in range(5, 10))
                PAc, PWc = o["PA"][0], o["PW"][0]
                S.dve(lambda e, PAc=PAc: e.tensor_tensor(out=PAc.t[:, :, 0:64], in0=P0f.t[:], in1=bc_m(BD), op=ALU.mult), P0f.b + cst.b, PAc.b)
                S.pool(lambda e, PWc=PWc: e.tensor_tensor(out=PWc.t[:, :, 0:64], in0=P0Tf.t[:], in1=bc_m(BD), op=ALU.mult), P0Tf.b + cst.b, PWc.b)
                S.dve(lambda e, PAc=PAc: e.tensor_tensor(out=PAc.t[:, :, 64:128], in0=PAc.t[:, :, 0:64], in1=bc_m(id64), op=ALU.add), PAc.b + cst.b, PAc.b)
                S.pool(lambda e, PWc=PWc: e.tensor_tensor(out=PWc.t[:, :, 64:128], in0=PWc.t[:, :, 0:64], in1=bc_m(id64), op=ALU.add), PWc.b + cst.b, PWc.b)
                S.pool(lambda e: e.tensor_tensor(out=o["Zm1"].t[:], in0=P0Tf.t[:], in1=bc_m(MA1T), op=ALU.mult), P0Tf.b + cst.b, o["Zm1"].b)
                S.pool(lambda e: e.tensor_tensor(out=o["ZmT1"].t[:], in0=P0f.t[:], in1=bc_m(MA1), op=ALU.mult), P0f.b + cst.b, o["ZmT1"].b)
                S.pool(lambda e: e.tensor_tensor(out=o["ZmT2"].t[:], in0=P0f.t[:], in1=bc_m(MA2), op=ALU.mult), P0f.b + cst.b, o["ZmT2"].b)
                yield
                PAB = pst[2 * d][0:64, :].rearrange("p (h f) -> p h f", f=128)
                for k in range(4):
                    PAc, PWc = o["PA"][k % 2], o["PW"][k % 2]
                    PAn, PWn = o["PA"][(k + 1) % 2], o["PW"][(k + 1) % 2]
                    if k == 0:
                        for h in range(8):
                            mm(PA[0:64, h * 64:(h + 1) * 64], PWc.t[:, h, 0:64], PAc.t[:, h, 0:64], True, True, PAc.b + PWc.b, [bA])
                        for h in range(8):
                            mm(PC[0:64, h * 64:(h + 1) * 64], PAc.t[:, h, 0:64], PWc.t[:, h, 0:64], True, True, PAc.b + PWc.b, [bC])
                        S.act(lambda e, PAn=PAn: e.copy(out=PAn.t[:, :, 0:64], in_=v3(PA[0:64, :])), [bA], PAn.b)
                        S.dve(lambda e, PWn=PWn: e.tensor_copy(out=PWn.t[:, :, 0:64], in_=v3(PC[0:64, :])), [bC], PWn.b)
                        S.pool(lambda e, PAn=PAn, PAc=PAc: e.tensor_copy(out=PAn.t[:, :, 64:128], in_=PAc.t[:, :, 64:128]), PAc.b, PAn.b)
                        S.pool(lambda e, PWn=PWn, PWc=PWc: e.tensor_copy(out=PWn.t[:, :, 64:128], in_=PWc.t[:, :, 64:128]), PWc.b, PWn.b)
                    elif k < 3:
                        for h in range(8):
                            mm(PAB[:, h, :], PWc.t[:, h, 0:64], PAc.t[:, h, :], True, True, PAc.b + PWc.b, [bA, bB])
                        for h in range(8):
                            mm(PCD[:, h, :], PAc.t[:, h, 0:64], PWc.t[:, h, :], True, True, PAc.b + PWc.b, [bC, bD])
                        for hh in (0, 4):
                            bk1 = [bA] if hh == 0 else [bB]
                            bk2 = [bC] if hh == 0 else [bD]
                            S.act(lambda e, PAn=PAn, hh=hh: e.copy(out=PAn.t[:, hh:hh + 4, 0:64], in_=PAB[:, hh:hh + 4, 0:64]), bk1, PAn.b)
                            S.dve(lambda e, PAn=PAn, PAc=PAc, hh=hh: e.tensor_tensor(out=PAn.t[:, hh:hh + 4, 64:128], in0=PAB[:, hh:hh + 4, 64:128],
                                                                               in1=PAc.t[:, hh:hh + 4, 64:128], op=ALU.add), bk1 + PAc.b, PAn.b)
                            S.act(lambda e, PWn=PWn, hh=hh: e.copy(out=PWn.t[:, hh:hh + 4, 0:64], in_=PCD[:, hh:hh + 4, 0:64]), bk2, PWn.b)
                            S.dve(lambda e, PWn=PWn, PWc=PWc, hh=hh: e.tensor_tensor(out=PWn.t[:, hh:hh + 4, 64:128], in0=PCD[:, hh:hh + 4, 64:128],
                                                                               in1=PWc.t[:, hh:hh + 4, 64:128], op=ALU.add), bk2 + PWc.b, PWn.b)
                    else:
                        for h in range(8):
                            mm(PA[0:64, h * 64:(h + 1) * 64], PWc.t[:, h, 0:64], PAc.t[:, h, 64:128], True, True, PAc.b + PWc.b, [bA])
                        for h in range(8):
                            mm(PC[0:64, h * 64:(h + 1) * 64], PAc.t[:, h, 0:64], PWc.t[:, h, 64:128], True, True, PAc.b + PWc.b, [bC])
                        S.dve(lambda e, PAn=PAn, PAc=PAc: e.tensor_tensor(out=PAn.t[:, :, 64:128], in0=v3(PA[0:64, :]), in1=PAc.t[:, :, 64:128], op=ALU.add),
                              [bA] + PAc.b, PAn.b)
                        S.dve(lambda e, PWn=PWn, PWc=PWc: e.tensor_tensor(out=PWn.t[:, :, 64:128], in0=v3(PC[0:64, :]), in1=PWc.t[:, :, 64:128], op=ALU.add),
                              [bC] + PWc.b, PWn.b)
                    yield
                Tt, Wt = o["PA"][0], o["PW"][0]
                Tv, Wv = Tt.t[:, :, 64:128], Wt.t[:, :, 64:128]
                for h in range(8):
                    mm(PA[0:64, h * 64:(h + 1) * 64], o["ZmT1"].t[:, h, :], Wt.t[:, h, 64:128], True, True, o["ZmT1"].b + Wt.b, [bA])
                for h in range(8):
                    mm(PB[0:64, h * 64:(h + 1) * 64], o["Zm1"].t[:, h, :], Tt.t[:, h, 64:128], True, True, o["Zm1"].b + Tt.b, [bB])
                S.act(lambda e: e.copy(out=o["X"].t[:], in_=v3(PA[0:64, :])), [bA], o["X"].b)
                S.dve(lambda e: e.tensor_copy(out=o["Xp"].t[:], in_=v3(PB[0:64, :])), [bB], o["Xp"].b)
                yield
                for h in range(8):
                    mm(PC[0:64, h * 64:(h + 1) * 64], Tt.t[:, h, 64:128], o["X"].t[:, h, :], True, True, Tt.b + o["X"].b, [bC])
                for h in range(8):
                    mm(PD[0:64, h * 64:(h + 1) * 64], Wt.t[:, h, 64:128], o["Xp"].t[:, h, :], True, True, Wt.b + o["Xp"].b, [bD])
                S.dve(lambda e: e.tensor_tensor(out=Wv, in0=v3(PC[0:64, :]), in1=Wv, op=ALU.add), [bC] + Wt.b, Wt.b)
                S.dve(lambda e: e.tensor_tensor(out=Tv, in0=v3(PD[0:64, :]), in1=Tv, op=ALU.add), [bD] + Tt.b, Tt.b)
                yield
                for h in range(8):
                    mm(PA[0:64, h * 64:(h + 1) * 64], o["ZmT2"].t[:, h, :], Wt.t[:, h, 64:128], True, True, o["ZmT2"].b + Wt.b, [bA])
                S.act(lambda e: e.copy(out=o["X"].t[:], in_=v3(PA[0:64, :])), [bA], o["X"].b)
                for h in range(8):
                    mm(PC[0:64, h * 64:(h + 1) * 64], Tt.t[:, h, 64:128], o["X"].t[:, h, :], True, True, Tt.b + o["X"].b, [bC])
                S.dve(lambda e: e.tensor_tensor(out=Wv, in0=v3(PC[0:64, :]), in1=Wv, op=ALU.add), [bC] + Wt.b, Wt.b)
                if CUT < 6:
                    return
                TT = Wt
                S.dve(lambda e: e.tensor_tensor(out=o["beg"].t[:], in0=b, in1=o["eg"].t[:], op=ALU.mult), bgt.b + o["eg"].b, o["beg"].b)
                S.pool(lambda e: e.tensor_tensor(out=o["bv"].t[:], in0=vt.t[:], in1=bc_h(b), op=ALU.mult), vt.b + bgt.b, o["bv"].b)
                S.pool(lambda e: e.tensor_tensor(out=o["bek"].t[:], in0=kt.t[:], in1=bc_h(o["beg"].t[:]), op=ALU.mult), kt.b + o["beg"].b, o["bek"].b)
                kg, qg, gl = o["kg"][r], o["qgT"][r], o["glB"][r]
                S.pool(lambda e: e.tensor_tensor(out=kg.t[:], in0=kt.t[:], in1=bc_h(o["decayT"].t[:, :, last]), op=ALU.mult), kt.b + o["decayT"].b, kg.b)
                S.dve(lambda e: e.tensor_tensor(out=qg.t[:], in0=qc, in1=o["egB"].t[:], op=ALU.mult), qk.b + o["egB"].b, qg.b)
                S.act(lambda e: e.copy(out=gl.t[:], in_=o["egB"].t[:, :, last]), o["egB"].b, gl.b)
                yield
                for h in range(8):
                    mm(PA[0:64, h * 64:(h + 1) * 64], TT.t[:, h, 64:128], o["bv"].t[:, h, :], True, True, TT.b + o["bv"].b, [bA])
                for h in range(8):
                    mm(PB[0:64, h * 64:(h + 1) * 64], o["bek"].t[:, h, :], TT.t[:, h, 64:128], True, True, TT.b + o["bek"].b, [bB])
                up, wT = o["up"][r], o["wT"][r]
                S.act(lambda e: e.copy(out=up.t[:], in_=v3(PA[0:64, :])), [bA], up.b)
                S.dve(lambda e: e.tensor_copy(out=wT.t[:], in_=v3(PB[0:64, :])), [bB], wT.b)

            def b3(d, c, n):
                o = D_[d]
                yield
                PA, PB, PC = bank(4 * d), bank(4 * d + 1), bank(4 * d + 2)
                bA, bB, bC = pbuf[4 * d], pbuf[4 * d + 1], pbuf[4 * d + 2]
                tok0 = s0 + c * 64
                bi, cb = tok0 // 512, (tok0 % 512) // 64
                r = n % 2
                up, wT, kg, qg, gl, at = o["up"][r], o["wT"][r], o["kg"][r], o["qgT"][r], o["glB"][r], o["attnT"][r]
                for h in range(8):
                    mm(PA[0:64, h * 64:(h + 1) * 64], wT.t[:, h, :], o["sbf"].t[:, h, :], True, True, wT.b + o["sbf"].b, [bA])
                if CUT < 8:
                    return
                S.dve(lambda e: e.tensor_tensor(out=o["u"].t[:], in0=up.t[:], in1=v3(PA[0:64, :]), op=ALU.subtract), up.b + [bA], o["u"].b)
                yield
                if CUT < 9:
                    return
                for h in range(8):
                    mm(PC[0:64, h * 64:(h + 1) * 64], o["sbf"].t[:, h, :], qg.t[:, h, :], True, False, o["sbf"].b + qg.b, [bC])
                    mm(PC[0:64, h * 64:(h + 1) * 64], o["u"].t[:, h, :], at.t[:, h, :], False, True, o["u"].b + at.b, [bC])
                if CUT < 10:
                    return
                for h in range(8):
                    mm(PB[0:64, h * 64:(h + 1) * 64], kg.t[:, h, :], o["u"].t[:, h, :], True, True, kg.b + o["u"].b, [bB])
                if CUT < 11:
                    return
                ost = o["ost"][0]
                c4 = (tok0 % 256) // 64
                S.act(lambda e: e.copy(out=ost.t[:, :, c4 * 64:(c4 + 1) * 64], in_=v3(PC[0:64, :])), [bC], ost.b)
                S.dve(lambda e: e.tensor_tensor(out=o["s1"].t[:], in0=o["s"].t[:], in1=bc_h(gl.t[:]), op=ALU.mult), o["s"].b + gl.b, o["s1"].b)
                S.dve(lambda e: e.tensor_tensor(out=o["s"].t[:], in0=o["s1"].t[:], in1=v3(PB[0:64, :]), op=ALU.add), o["s1"].b + [bB], o["s"].b)
                S.act(lambda e: e.copy(out=o["sbf"].t[:], in_=o["s"].t[:]), o["s"].b, o["sbf"].b)
                if CUT < 12:
                    return
                if c4 == (3 if d == 0 else 0):
                    lo = (tok0 // 256) * 256
                    for h in range(8):
                        S.dma("sp", oT_d[d, h * 64:(h + 1) * 64, lo:lo + 256], ost.t[:, h, :], reads=ost.b, writes=[db("oT", bi)])

            def run_gens(gs):
                while gs:
                    for g_ in list(gs):
                        try:
                            next(g_)
                        except StopIteration:
                            gs.remove(g_)

            for n in range(N):
                run_gens([b2(0, n, n), b2(1, N - 1 - n, n)])
                run_gens([b3(0, n, n), b3(1, N - 1 - n, n)])
            if not is_s:
                for d in range(2):
                    S.dma("sp", nst_d[si, l, d].rearrange("h k v -> k h v"), D_[d]["s"].t[:], reads=D_[d]["s"].b)
        S.barrier()

    def phase_B4(l):
        ar.reset()
        of = [TL([64, 8, 512], F32) for _ in range(2)]
        ob = [TL([64, 8, 512], F32) for _ in range(2)]
        zt = [TL([64, 8, 512], F32) for _ in range(2)]
        sqb = [TL([64, 512], BF16) for _ in range(2)]
        rs = [TL([64, 512], F32) for _ in range(2)]
        yo = [TL([64, 8, 512], BF16) for _ in range(2)]
        for bi in range(NB):
            r = bi % 2
            sl = slice(bi * 512, (bi + 1) * 512)
            for h in range(8):
                S.dma("sp", of[r].t[:, h, :], oT_d[0, h * 64:(h + 1) * 64, sl], reads=[db("oT", bi)], writes=of[r].b)
                S.dma("sp", ob[r].t[:, h, :], oT_d[1, h * 64:(h + 1) * 64, sl], reads=[db("oT", bi)], writes=ob[r].b)
                S.dma("sp", zt[r].t[:, h, :], zs_d[h * 64:(h + 1) * 64, sl], reads=[db("zs", bi)], writes=zt[r].b)
            S.dve(lambda e, r=r: e.tensor_tensor(out=of[r].t[:], in0=of[r].t[:], in1=ob[r].t[:], op=ALU.add), of[r].b + ob[r].b, of[r].b)
            for h in range(8):
                s_ = sqb[h % 2]
                r2 = rs[h % 2]
                S.act(lambda e, s_=s_, h=h, r=r: e.activation(out=s_.t[:], in_=of[r].t[:, h, :], func=AF.Square), of[r].b, s_.b)
                sb_ = h % 2
                mm(bank(sb_)[0:64, :], ones_bf.t[0:64, 0:64], s_.t[:], True, True, ones_bf.b + s_.b, [pbuf[sb_]])
                S.act(lambda e, r2=r2, sb_=sb_: e.activation(out=r2.t[:], in_=bank(sb_)[0:64, :], func=AF.Ln, bias=EPS, scale=1.0 / 64), [pbuf[sb_]], r2.b)
                S.act(lambda e, r2=r2: e.activation(out=r2.t[:], in_=r2.t[:], func=AF.Exp, scale=-0.5), r2.b, r2.b)
                S.dve(lambda e, r2=r2, h=h, r=r: e.scalar_tensor_tensor(out=r2.t[:], in0=of[r].t[:, h, :], scalar=anw.t[0:64, l:l + 1], in1=r2.t[:],
                                                                      op0=ALU.mult, op1=ALU.mult), of[r].b + r2.b + anw.b, r2.b)
                S.pool(lambda e, r2=r2, h=h, r=r: e.tensor_tensor(out=yo[r].t[:, h, :], in0=r2.t[:], in1=zt[r].t[:, h, :], op=ALU.mult), r2.b + zt[r].b, yo[r].b)
            for h in range(8):
                S.dma("sp", yT_d[0, h * 64:(h + 1) * 64, sl], yo[r].t[:, h, :], reads=yo[r].b, writes=[db("yT", bi)])
        S.barrier()

    def phase_C(l):
        ar.reset()
        NKT = Tk // 128
        KT = [TL([128, Tk], BF16) for _ in range(2)]
        V1 = [TL([128, NKT, 2, 65], BF16) for _ in range(2)]
        QT = [TL([128, 4, 512], BF16) for _ in range(2)]
        PTt = [TL([128, 512], BF16) for _ in range(6)]
        ckt = [TL([128, 128], F32) for _ in range(2)]
        kcs = TL([128, 512], BF16)
        rr = [TL([128, 512], F32) for _ in range(2)]
        bcs = [TL([64, 512], F32) for _ in range(2)]
        yst = [TL([64, 512], BF16) for _ in range(2)]
        for a in range(2):
            S.pool(lambda e, a=a: e.memset(V1[a].t[:, :, :, 64:65], 1.0), (), V1[a].b)
            S.dma("sp", KT[a].t[:, 0:512], kT_d[a, :, 0:512], reads=[db("kT", i) for i in range(NB)], writes=KT[a].b)
            S.dma("sp", KT[a].t[:, 1024:Tk], kT_d[a, :, 1024:Tk], reads=[db("kT", i) for i in range(NB)], writes=KT[a].b)
            for kv in range(2):
                S.dma("sp", V1[a].t[:, 0:4, kv, 0:64], vt_d[a, 0:512, kv * 64:(kv + 1) * 64].rearrange("(n p) d -> p n d", p=128),
                      reads=[db("vt", i) for i in range(NB)], writes=V1[a].b)
                S.dma("sp", V1[a].t[:, 8:NKT, kv, 0:64], vt_d[a, 1024:Tk, kv * 64:(kv + 1) * 64].rearrange("(n p) d -> p n d", p=128),
                      reads=[db("vt", i) for i in range(NB)], writes=V1[a].b)
                S.dma("pool", V1[a].t[:, 4:8, kv, 0:64], cv_d[a][l, :, kv * 64:(kv + 1) * 64].rearrange("(n p) d -> p n d", p=128), writes=V1[a].b)
            for kt_ in range(4):
                c_ = ckt[kt_ % 2]
                S.dma("sp", c_.t[:], ck_d[a][l, kt_ * 128:(kt_ + 1) * 128, :], writes=c_.b)
                tr(bank(7)[:, kt_ * 128:(kt_ + 1) * 128], c_.t[:], ident, c_.b + cst.b, [pbuf[7]])
            S.dve(lambda e, a=a: e.tensor_copy(out=KT[a].t[:, 512:1024], in_=bank(7)), [pbuf[7]], KT[a].b)
        cnt = {"s": 0, "p": 0, "acc": 0, "q": 0}
        for (s0, T, is_s) in seqs:
            QB = min(T, 512)
            for qb in range(T // QB):
                q0 = s0 + qb * QB
                bi = q0 // 512
                for a in range(2):
                    cnt["q"] += 1
                    Q = QT[cnt["q"] % 2]
                    S.dma("sp", Q.t[:, :, 0:QB], qT_d[a].rearrange("(m p) t -> p m t", p=128)[:, :, q0:q0 + QB], reads=[db("qT%d" % a, bi)], writes=Q.b)
                    chunks = []
                    if not is_s:
                        chunks = [(s0 // 128 + j, 0, QB, None) for j in range(T // 128)]
                    elif a == 0:
                        chunks = [(4 + j, 0, QB, None) for j in range(4)] + [(8 + j, 0, QB, None) for j in range(T // 128)]
                    else:
                        ctx = [(4 + j, 0, QB, None) for j in range(4)]
                        loc = []
                        for kc in range(T // 128):
                            lo = max((kc - 1) * 128, qb * QB)
                            hi = min((kc + 2) * 128, (qb + 1) * QB)
                            if lo >= hi:
                                continue
                            loc.append((8 + kc, lo - qb * QB, hi - qb * QB, (lo - (kc - 1) * 128)))
                        chunks = ctx[:1] + loc + ctx[1:]
                    nck = len(chunks)
                    items = [(h, ci) for h in range(8) for ci in range(nck)]
                    st_ = {}
                    accb = {}

                    def stage1(h, ci, Q=Q, a=a, chunks=chunks):
                        m, half = h % 4, h // 4
                        pb0 = 64 * half
                        if ci == 0:
                            cnt["acc"] += 1
                            accb[h] = 4 + cnt["acc"] % 2
                        kti, qlo, qhi, mcol = chunks[ci]
                        cnt["s"] += 1
                        sbk = cnt["s"] % 4
                        nq = qhi - qlo
                        mm(bank(sbk)[:, 0:nq], KT[a].t[pb0:pb0 + 64, kti * 128:(kti + 1) * 128], Q.t[pb0:pb0 + 64, m, qlo:qhi], True, True,
                           KT[a].b + Q.b, [pbuf[sbk]])
                        cnt["p"] += 1
                        Pt = PTt[cnt["p"] % 6]
                        S.act(lambda e, Pt=Pt, sbk=sbk, nq=nq: e.activation(out=Pt.t[:, 0:nq], in_=bank(sbk)[:, 0:nq], func=AF.Exp, scale=HD ** -0.5),
                              [pbuf[sbk]], Pt.b)
                        if mcol is not None and not (mcol == 128 and nq == 128):
                            S.dve(lambda e, Pt=Pt, nq=nq, mcol=mcol: e.tensor_tensor(out=Pt.t[:, 0:nq], in0=Pt.t[:, 0:nq], in1=mw_bf.t[:, mcol:mcol + nq], op=ALU.mult),
                                  Pt.b + mw_bf.b, Pt.b)
                        st_[(h, ci)] = Pt

                    def stage2(h, ci, a=a, chunks=chunks, nck=nck, QB=QB, q0=q0, bi=bi):
                        half = h // 4
                        kti, qlo, qhi, mcol = chunks[ci]
                        nq = qhi - qlo
                        Pt = st_.pop((h, ci))
                        ab = accb[h]
                        ACC = bank(ab)
                        mm(ACC[0:65, qlo:qhi], V1[a].t[:, kti, half, :], Pt.t[:, 0:nq], ci == 0, ci == nck - 1, V1[a].b + Pt.b, [pbuf[ab]])
                        if ci != nck - 1:
                            return
                        r_ = rr[h % 2]
                        if a == 1:
                            S.act(lambda e, r_=r_, ACC=ACC, h=h: e.activation(out=r_.t[64:65, 0:QB], in_=ACC[64:65, 0:QB], func=AF.Ln,
                                                                          bias=esink.t[64:65, l * 8 + h:l * 8 + h + 1]), [pbuf[ab]] + esink.b, r_.b)
                        else:
                            S.act(lambda e, r_=r_, ACC=ACC: e.activation(out=r_.t[64:65, 0:QB], in_=ACC[64:65, 0:QB], func=AF.Ln), [pbuf[ab]], r_.b)
                        S.act(lambda e, r_=r_: e.activation(out=r_.t[64:65, 0:QB], in_=r_.t[64:65, 0:QB], func=AF.Exp, scale=-1.0), r_.b, r_.b)
                        bb = 6 + h % 2
                        mm(bank(bb)[0:64, 0:QB], ones_f[64:65, 0:64], r_.t[64:65, 0:QB], True, True, cst.b + r_.b, [pbuf[bb]])
                        bc_ = bcs[h % 2]
                        S.act(lambda e, bc_=bc_, bb=bb: e.copy(out=bc_.t[:, 0:QB], in_=bank(bb)[0:64, 0:QB]), [pbuf[bb]], bc_.b)
                        y_ = yst[h % 2]
                        S.dve(lambda e, y_=y_, ACC=ACC, bc_=bc_: e.tensor_tensor(out=y_.t[:, 0:QB], in0=ACC[0:64, 0:QB], in1=bc_.t[:, 0:QB], op=ALU.mult),
                              [pbuf[ab]] + bc_.b, y_.b)
                        S.dma("sp", yT_d[1 + a, h * 64:(h + 1) * 64, q0:q0 + QB], y_.t[:, 0:QB], reads=y_.b, writes=[db("yT", bi)])

                    LA = 3
                    for i in range(len(items) + LA):
                        if i < len(items):
                            stage1(*items[i])
                        if i >= LA:
                            stage2(*items[i - LA])
        S.barrier()

    def phase_D1(l):
        ar.reset()
        wbr = [TL([128, 4, D], BF16) for _ in range(3)]
        wo = TL([128, 8, D], BF16)
        for j in range(3):
            load_w_bf16(wbr[j], lambda c, j=j: wbr_d[j][l, c * 128:(c + 1) * 128, :], 4)
        load_w_bf16(wo, lambda c: wo_d[l, c * 128:(c + 1) * 128, :], 8)
        yt = [TL([128, 12, 512], BF16) for _ in range(2)]
        sg = [TL([128, 24, 512], BF16) for _ in range(2)]
        xT = [TL([128, 8, 512], F32) for _ in range(2)]
        mg = TL([128, 8, 512], BF16)
        hT = TL([128, 8, 512], BF16)
        ta = [TL([128, 512], F32) for _ in range(2)]
        tb_ = [TL([128, 512], F32) for _ in range(2)]
        tc = [TL([128, 512], F32) for _ in range(2)]
        sq = [TL([128, 512], BF16) for _ in range(2)]
        tmp = [TL([128, 512], F32) for _ in range(2)]
        lnv = TL([128, 512], F32)
        rstd = TL([128, 512], F32)

        def loads(bi):
            r = bi % 2
            sl = slice(bi * 512, (bi + 1) * 512)
            S.dma("sp", yt[r].t[:], yT_d.rearrange("j (c p) t -> p (j c) t", p=128)[:, :, sl], reads=[db("yT", bi)], writes=yt[r].b)
            S.dma("sp", sg[r].t[:], sig_d.rearrange("(c p) t -> p c t", p=128)[:, :, sl], reads=[db("sig", bi)], writes=sg[r].b)
            S.dma("sp", xT[r].t[:], xT_d.rearrange("(c p) t -> p c t", p=128)[:, :, sl], reads=[db("xT", bi)], writes=xT[r].b)

        loads(0)
        for bi in range(NB):
            r = bi % 2
            mv = 0 if bi == 0 else 1
            if bi + 1 < NB:
                loads(bi + 1)
            x_ = xT[r]
            for oc in range(8):
                for j in range(3):
                    for c in range(4):
                        mm(bank(j), wbr[j].t[:, c, oc * 128:(oc + 1) * 128], yt[r].t[:, j * 4 + c, :], c == 0, c == 3, wbr[j].b + yt[r].b, [pbuf[j]])
                a_, b_, c_ = ta[oc % 2], tb_[oc % 2], tc[oc % 2]
                S.dve(lambda e, a_=a_, oc=oc, r=r: e.tensor_tensor(out=a_.t[:], in0=bank(0), in1=sg[r].t[:, oc, :], op=ALU.mult), [pbuf[0]] + sg[r].b, a_.b)
                S.dve(lambda e, b_=b_, oc=oc, r=r: e.tensor_tensor(out=b_.t[:], in0=bank(1), in1=sg[r].t[:, 8 + oc, :], op=ALU.mult), [pbuf[1]] + sg[r].b, b_.b)
                S.dve(lambda e, c_=c_, oc=oc, r=r: e.tensor_tensor(out=c_.t[:], in0=bank(2), in1=sg[r].t[:, 16 + oc, :], op=ALU.mult), [pbuf[2]] + sg[r].b, c_.b)
                S.pool(lambda e, a_=a_, b_=b_: e.tensor_tensor(out=a_.t[:], in0=a_.t[:], in1=b_.t[:], op=ALU.add), a_.b + b_.b, a_.b)
                S.pool(lambda e, a_=a_, c_=c_, oc=oc: e.tensor_tensor(out=mg.t[:, oc, :], in0=a_.t[:], in1=c_.t[:], op=ALU.add), a_.b + c_.b, mg.b)
            for oc in range(8):
                bk = 3 + oc % 2
                for c in range(8):
                    mm(bank(bk), wo.t[:, c, oc * 128:(oc + 1) * 128], mg.t[:, c, :], c == 0, c == 7, wo.b + mg.b, [pbuf[bk]])
                S.dve(lambda e, oc=oc, bk=bk, x_=x_, mv=mv: e.scalar_tensor_tensor(out=x_.t[:, oc, :], in0=bank(bk), scalar=modT.t[:, 16 + oc, mv:mv + 1],
                                                                                 in1=x_.t[:, oc, :], op0=ALU.mult, op1=ALU.add),
                      [pbuf[bk]] + x_.b + modT.b, x_.b)
            norm_block(x_, hT, mv, A2, 24, sq, tmp, 7, lnv, rstd)
            sl = slice(bi * 512, (bi + 1) * 512)
            S.dma("sp", xT_d.rearrange("(c p) t -> p c t", p=128)[:, :, sl], x_.t[:], reads=x_.b, writes=[db("xT", bi)])
            S.dma("sp", h2T_d.rearrange("(c p) t -> p c t", p=128)[:, :, sl], hT.t[:], reads=hT.b, writes=[db("h2T", bi)])
        S.barrier()

    def phase_D2(l):
        ar.reset()
        HH = DFF // 2
        w1 = [TL([128, 8, HH], BF16) for _ in range(2)]
        w2s = TL([128, 16, D], BF16)
        w2 = [w2s, w2s]
        load_w_bf16(w1[0], lambda c: wf1_d[l, c * 128:(c + 1) * 128, 0:HH], 8)
        load_w_bf16(w2s, lambda c: wf2_d[l, c * 128:(c + 1) * 128, :], 16)
        load_w_bf16(w1[1], lambda c: wf1_d[l, c * 128:(c + 1) * 128, HH:2 * HH], 8)
        hT = [TL([128, 8, 512], BF16) for _ in range(2)]
        xT = [TL([128, 8, 512], F32) for _ in range(2)]
        rl = [TL([128, 512], BF16) for _ in range(3)]
        aT = TL([128, 16, 512], BF16)
        final = (l == depth - 1)
        xof = [TL([128, D], F32) for _ in range(2)] if final else None
        cnt_ = [0]

        def loads(bi):
            r = cnt_[0] % 2
            cnt_[0] += 1
            sl = slice(bi * 512, (bi + 1) * 512)
            S.dma("sp", hT[r].t[:], h2T_d.rearrange("(c p) t -> p c t", p=128)[:, :, sl], reads=[db("h2T", bi)], writes=hT[r].b)
            S.dma("sp", xT[r].t[:], xT_d.rearrange("(c p) t -> p c t", p=128)[:, :, sl], reads=[db("xT", bi)], writes=xT[r].b)
            return r

        seq_ = [(hf, bi) for hf in range(2) for bi in range(NB)]
        rnext = loads(seq_[0][1])
        for si_, (hf, bi) in enumerate(seq_):
            r = rnext
            mv = 0 if bi == 0 else 1
            if si_ + 1 < len(seq_):
                rnext = loads(seq_[si_ + 1][1])
            if si_ == NB:
                load_w_bf16(w2s, lambda c: wf2_d[l, HH + c * 128:HH + (c + 1) * 128, :], 16)
            x_ = xT[r]
            for oc in range(16):
                bk = oc % 3
                for c in range(8):
                    mm(bank(bk), w1[hf].t[:, c, oc * 128:(oc + 1) * 128], hT[r].t[:, c, :], c == 0, c == 7, w1[hf].b + hT[r].b, [pbuf[bk]])
                r_ = rl[oc % 3]
                S.act(lambda e, r_=r_, bk=bk: e.activation(out=r_.t[:], in_=bank(bk), func=AF.Relu), [pbuf[bk]], r_.b)
                S.pool(lambda e, r_=r_, oc=oc: e.tensor_tensor(out=aT.t[:, oc, :], in0=r_.t[:], in1=r_.t[:], op=ALU.mult), r_.b, aT.b)
            for oc in range(8):
                bk = 3 + oc % 2
                for c in range(16):
                    mm(bank(bk), w2[hf].t[:, c, oc * 128:(oc + 1) * 128], aT.t[:, c, :], c == 0, c == 15, w2[hf].b + aT.b, [pbuf[bk]])
                S.dve(lambda e, oc=oc, bk=bk, x_=x_, mv=mv: e.scalar_tensor_tensor(out=x_.t[:, oc, :], in0=bank(bk), scalar=modT.t[:, 40 + oc, mv:mv + 1],
                                                                                 in1=x_.t[:, oc, :], op0=ALU.mult, op1=ALU.add),
                      [pbuf[bk]] + x_.b + modT.b, x_.b)
            sl = slice(bi * 512, (bi + 1) * 512)
            if not (final and hf == 1):
                S.dma("sp", xT_d.rearrange("(c p) t -> p c t", p=128)[:, :, sl], x_.t[:], reads=x_.b, writes=[db("xT", bi)])
            else:
                for tt in range(4):
                    for c in range(8):
                        bk = 5 + (c // 4)
                        tr(bank(bk)[:, (c % 4) * 128:(c % 4 + 1) * 128], x_.t[:, c, tt * 128:(tt + 1) * 128], ident, x_.b + cst.b, [pbuf[bk]])
                    xo_ = xof[tt % 2]
                    S.act(lambda e, xo_=xo_: e.copy(out=xo_.t[:, 0:512], in_=bank(5)), [pbuf[5]], xo_.b)
                    S.dve(lambda e, xo_=xo_: e.tensor_copy(out=xo_.t[:, 512:1024], in_=bank(6)), [pbuf[6]], xo_.b)
                    t0 = bi * 512 + tt * 128
                    dst = yp_d[t0:t0 + 128, :] if bi == 0 else ys_d[t0 - 512:t0 - 512 + 128, :]
                    S.dma("sp", dst, xo_.t[:], reads=xo_.b)
        S.barrier()

    plist = [("M", phase_M), ("A", phase_A), ("B1", phase_B1), ("B23", phase_B23), ("B4", phase_B4), ("C", phase_C), ("D1", phase_D1),
             ("D2", phase_D2)]
    done = False
    for l in range(depth):
        for nm_, fn_ in plist:
            if stop is not None and nm_ == stop:
                done = True
                break
            fn_(l)
        if done:
            break
    n_ops = len(S.ops)
    S.emit()
    return nc, n_ops


DEPTH = 4
DEC_SEQ = 4096
_cache = {}


def make_in_maps(inputs, depth, T_s, n_cores=8):
    f = lambda a: np.ascontiguousarray(np.asarray(a, dtype=np.float32))
    cst, rope = make_consts(T_s)
    maps = []
    for core in range(n_cores):
        b = core % 2
        vecs = np.zeros((16, D), np.float32)
        vecs[0] = inputs["c_ctx"]
        vecs[1] = inputs["c"][b]
        vecs[2:2 + depth] = inputs["ln1"]
        vecs[2 + depth:2 + 2 * depth] = inputs["ln2"]
        m = {
            "xp": f(inputs["x_prompt"][2 * core:2 * core + 2]).reshape(2 * TP, D),
            "xs": f(inputs["x_sample"][b]),
            "ckg": f(inputs["cache_k_glob"][b]).reshape(depth, PAST, 128),
            "ckw": f(inputs["cache_k_win"][b]).reshape(depth, PAST, 128),
            "cvg": f(inputs["cache_v_glob"][b]).reshape(depth, PAST, 128),
            "cvw": f(inputs["cache_v_win"][b]).reshape(depth, PAST, 128),
            "sd": f(inputs["state_delta"][b]),
            "vecs": vecs,
            "w_mod": f(inputs["w_mod"]), "b_mod": f(inputs["b_mod"]), "w_in": f(inputs["w_in"]),
            "conv": f(inputs["conv_qkv"]).reshape(depth * 5, 1536),
            "a_log": f(inputs["a_log"]).reshape(depth, 16), "dt_bias": f(inputs["dt_bias"]).reshape(depth, 16),
            "a_norm": f(inputs["a_norm"]), "qk_norm": f(inputs["qk_norm"]).reshape(depth * 4, 64), "sink": f(inputs["sink"]),
            "w_br_a": f(inputs["w_br_a"]), "w_br_b": f(inputs["w_br_b"]), "w_br_c": f(inputs["w_br_c"]),
            "w_o": f(inputs["w_o"]), "w_ff1": f(inputs["w_ff1"]), "w_ff2": f(inputs["w_ff2"]),
            "cst": cst, "rope": rope,
        }
        maps.append(m)
    return maps


def assemble(results, depth, T_s):
    yp = np.concatenate([r["yp"].reshape(2, TP, D) for r in results], axis=0)
    ys = np.stack([results[0]["ys"], results[1]["ys"]], axis=0)
    outs = [yp.astype(np.float32), ys.astype(np.float32)]
    for nm_ in ("nkg", "nvg", "nkw", "nvw"):
        outs.append(np.concatenate([r[nm_].reshape(2, depth, TP, 2, HD) for r in results], axis=0).astype(np.float32))
    outs.append(np.concatenate([r["nst"] for r in results], axis=0).astype(np.float32))
    return tuple(outs)


def kernel(**inputs):
    depth, T_s = DEPTH, DEC_SEQ
    key = (depth, T_s)
    if key not in _cache:
        _cache[key] = build(depth, T_s)[0]
    nc = _cache[key]
    maps = make_in_maps(inputs, depth, T_s)
    res = run_bass_kernel_spmd(nc, maps, core_ids=list(range(8)))
    return assemble(res.results, depth, T_s)
```

```python
import os
import numpy as np
import concourse.bass as bass
import concourse.mybir as mybir
from concourse.bass_utils import run_bass_kernel_spmd

F32 = mybir.dt.float32
BF16 = mybir.dt.bfloat16
F32R = mybir.dt.float32r
ALU = mybir.AluOpType
AF = mybir.ActivationFunctionType

ENGS = ("pe", "act", "dve", "pool", "sp")
DMA_RING = 12


class Buf:
    __slots__ = ("lw", "rd", "excl")

    def __init__(self, excl=False):
        self.lw = None
        self.rd = []
        self.excl = excl


class Op:
    __slots__ = ("eng", "fn", "dma", "deps", "signal", "semval", "waits", "dsem", "dval")

    def __init__(self, eng, fn, dma):
        self.eng = eng
        self.fn = fn
        self.dma = dma
        self.deps = []
        self.signal = False
        self.semval = None
        self.waits = []
        self.dsem = None
        self.dval = None


class Sched:
    def __init__(self, nc):
        self.nc = nc
        self.ops = []
        self.last = {}
        self.pend_dma = []

    def op(self, eng, fn, reads=(), writes=(), dma=False):
        o = Op(eng, fn, dma)
        deps = o.deps
        if any(b.excl for b in reads):
            writes = list(writes) + [b for b in reads if b.excl]
            reads = [b for b in reads if not b.excl]
        for b in reads:
            if b.lw is not None:
                deps.append(b.lw)
        for b in writes:
            if b.lw is not None:
                deps.append(b.lw)
            deps.extend(b.rd)
        for b in reads:
            if not dma:
                b.rd = [x for x in b.rd if x.dma or x.eng != eng]
            b.rd.append(o)
        for b in writes:
            b.lw = o
            b.rd = []
        self.ops.append(o)
        if dma:
            self.pend_dma.append(o)
        else:
            self.last[eng] = o
        return o

    def barrier(self):
        deps = list(self.last.values()) + self.pend_dma
        self.pend_dma = []
        for e in ENGS:
            o = Op(e, lambda en: en.nop(), False)
            o.deps = list(deps)
            self.ops.append(o)
            self.last[e] = o

    def pe(self, fn, reads=(), writes=()):
        return self.op("pe", fn, reads, writes)

    def act(self, fn, reads=(), writes=()):
        return self.op("act", fn, reads, writes)

    def dve(self, fn, reads=(), writes=()):
        return self.op("dve", fn, reads, writes)

    def pool(self, fn, reads=(), writes=()):
        return self.op("pool", fn, reads, writes)

    def dma(self, q, out, in_, reads=(), writes=()):
        return self.op(q, lambda e: e.dma_start(out=out, in_=in_), reads, writes, dma=True)

    def emit(self):
        nc = self.nc
        ops = self.ops
        cnt = {e: 0 for e in ENGS}
        for o in ops:
            for d in o.deps:
                if d.dma:
                    continue
                if d.eng == "pe" and o.eng == "pe" and not o.dma:
                    continue
                d.signal = True
        last = {}
        for o in ops:
            if not o.dma:
                last[o.eng] = o
        for o in last.values():
            o.signal = True
        dq = {e: 0 for e in ENGS}
        dma_hist = {e: [] for e in ENGS}
        for o in ops:
            if o.dma:
                j = dq[o.eng]
                dq[o.eng] += 1
                o.dsem = (o.eng, j % DMA_RING)
                o.dval = 16 * (j // DMA_RING + 1)
                dma_hist[o.eng].append(o)
            elif o.signal:
                cnt[o.eng] += 1
                o.semval = cnt[o.eng]
        known = {e: {} for e in ENGS}
        dcount = {e: 0 for e in ENGS}
        for o in ops:
            kn = known[o.eng]
            need = {}
            if o.dma:
                j = dcount[o.eng]
                dcount[o.eng] += 1
                if j >= DMA_RING:
                    prev = dma_hist[o.eng][j - DMA_RING]
                    need[("d",) + prev.dsem] = prev.dval
            for d in o.deps:
                if d.dma:
                    k = ("d",) + d.dsem
                    v = d.dval
                else:
                    if d.eng == "pe" and o.eng == "pe" and not o.dma:
                        continue
                    k = ("c", d.eng)
                    v = d.semval
                if need.get(k, 0) < v:
                    need[k] = v
            for k, v in need.items():
                if kn.get(k, 0) < v:
                    kn[k] = v
                    o.waits.append((k, v))
        final_waits = []
        for e, o in last.items():
            final_waits.append((("c", e), o.semval))
        for e in ENGS:
            for o in dma_hist[e][-DMA_RING:]:
                final_waits.append((("d",) + o.dsem, o.dval))
        from contextlib import ExitStack
        with ExitStack() as st:
            sems = {}
            for e in ENGS:
                sems[("c", e)] = st.enter_context(nc.semaphore(f"c_{e}"))
            for e in ENGS:
                for r in range(min(DMA_RING, dq[e])):
                    sems[("d", e, r)] = st.enter_context(nc.semaphore(f"d_{e}_{r}"))
            block = st.enter_context(nc.Block())
            per = {e: [o for o in ops if o.eng == e] for e in ENGS}

            def run(engobj, lst, final=None):
                for o in lst:
                    for k, v in o.waits:
                        engobj.wait_ge(sems[k], v)
                    ins = o.fn(engobj)
                    if o.dma:
                        ins.then_inc(sems[("d",) + o.dsem], 16)
                    elif o.signal:
                        ins.then_inc(sems[("c", o.eng)], 1)
                if final:
                    for k, v in final:
                        engobj.wait_ge(sems[k], v)

            @block.tensor
            def _(e):
                run(e, per["pe"])

            @block.scalar
            def _(e):
                run(e, per["act"])

            @block.vector
            def _(e):
                run(e, per["dve"])

            @block.gpsimd
            def _(e):
                run(e, per["pool"])

            @block.sync
            def _(e):
                run(e, per["sp"], final_waits)
        self.ops = []


D = 1024
NCH = 8
HD = 64
TP = 256
PAST = 512
DFF = 4096
IN_COLS = 6688
C_QKV, C_Z, C_BETA, C_ALPHA, C_BQ, C_BK, C_BV, C_CQ, C_CK, C_CV, C_GATE = (
    0, 1536, 2048, 2064, 2080, 2592, 2720, 2848, 3360, 3488, 3616)
EPS = 1e-6
NEG = -30000.0
NCST = 2176
SB_BASE = 16512
SB_TOP = 229344


def make_consts(T_s):
    cst = np.zeros((128, NCST), np.float32)
    cst[:, 0:128] = np.eye(128)
    cst[0:64, 128:192] = 1.0
    cst[64:128, 192:256] = 1.0
    for m in range(128):
        if m % 64 < 32:
            cst[m + 32, 256 + m] = -1.0
        else:
            cst[m - 32, 256 + m] = 1.0
    p = np.arange(64)[:, None]
    f = np.arange(64)[None, :]

    def ms(m):
        return ((p // (2 * m) == f // (2 * m)) & (p % (2 * m) >= m) & (f % (2 * m) < m)).astype(np.float32)

    bdm = (p // 16 == f // 16).astype(np.float32)
    for d in range(2):
        R = (p >= f) if d == 0 else (p <= f)
        RT = R.T
        base = 384 + d * 640
        ms1 = ms(16) if d == 0 else ms(16).T
        ms2 = ms(32) if d == 0 else ms(32).T
        tabs = [RT.astype(np.float32), np.where(R, 0.0, NEG), np.where(RT, 0.0, NEG),
                -(R & (p != f)).astype(np.float32), -(RT & (p != f)).astype(np.float32),
                bdm, ms1, ms1.T, ms2, ms2.T]
        for k, tb in enumerate(tabs):
            cst[0:64, base + k * 64:base + (k + 1) * 64] = tb
    cst[:, 1664:1792] = 1.0
    pk = np.arange(128)[:, None]
    fq = np.arange(128)[None, :]
    cst[:, 1792:1920] = (pk <= fq)
    cst[:, 1920:2048] = 1.0
    cst[:, 2048:2176] = (fq <= pk)
    t = np.arange(T_s)
    row_id = (t // 64).astype(np.float32)
    col_id = (t % 64).astype(np.float32)
    inv_freq = (10000.0 ** (-np.arange(16, dtype=np.float32) / 16)).astype(np.float32)
    ang = np.concatenate([row_id[:, None] * inv_freq, col_id[:, None] * inv_freq], axis=-1)
    fidx = np.arange(128) % 32
    rope = np.stack([np.cos(ang)[:, fidx].T, np.sin(ang)[:, fidx].T]).astype(np.float32)
    return cst, np.ascontiguousarray(rope)


def build(depth, T_s, debug=False, stop=None):
    nc = bass.Bass("TRN2", target_bir_lowering=False)
    Ttot = 2 * TP + T_s
    NB = Ttot // 512
    Tk = Ttot + PAST
    seqs = [(0, TP, 0), (TP, TP, 0), (2 * TP, T_s, 1)]
    NR5 = depth * 5

    def din(name, shape, dt=F32):
        return nc.dram_tensor(name, list(shape), dt, kind="ExternalInput").ap()

    def dout(name, shape, dt=F32):
        return nc.dram_tensor(name, list(shape), dt, kind="ExternalOutput").ap()

    def dscr(name, shape, dt=F32):
        if debug:
            return nc.dram_tensor(name, list(shape), dt, kind="ExternalOutput").ap()
        return nc.dram_tensor(name, list(shape), dt).ap()

    xp_d = din("xp", [2 * TP, D])
    xs_d = din("xs", [T_s, D])
    ck_d = [din("ckg", [depth, PAST, 128]), din("ckw", [depth, PAST, 128])]
    cv_d = [din("cvg", [depth, PAST, 128]), din("cvw", [depth, PAST, 128])]
    sd_d = din("sd", [depth, 2, 8, 64, 64])
    vecs_d = din("vecs", [16, D])
    wmod_d = din("w_mod", [depth, D, 6 * D])
    bmod_d = din("b_mod", [depth, 6 * D])
    win_d = din("w_in", [depth, D, IN_COLS])
    conv_d = din("conv", [NR5, 1536])
    alog_d = din("a_log", [depth, 16])
    dtb_d = din("dt_bias", [depth, 16])
    anorm_d = din("a_norm", [depth, 64])
    qkn_d = din("qk_norm", [depth * 4, 64])
    sink_d = din("sink", [depth, 8])
    wbr_d = [din("w_br_a", [depth, 512, D]), din("w_br_b", [depth, 512, D]), din("w_br_c", [depth, 512, D])]
    wo_d = din("w_o", [depth, D, D])
    wf1_d = din("w_ff1", [depth, D, DFF])
    wf2_d = din("w_ff2", [depth, DFF, D])
    cst_d = din("cst", [128, NCST])
    rope_d = din("rope", [2, 128, T_s])
    yp_d = dout("yp", [2 * TP, D])
    ys_d = dout("ys", [T_s, D])
    nk_d = [dout("nkg", [2, depth, TP, 128]), dout("nkw", [2, depth, TP, 128])]
    nv_d = [dout("nvg", [2, depth, TP, 128]), dout("nvw", [2, depth, TP, 128])]
    nst_d = dout("nst", [2, depth, 2, 8, 64, 64])
    xT_d = dscr("xT", [D, Ttot])
    raw_d = dscr("raw", [1536, Ttot])
    zs_d = dscr("zs", [512, Ttot])
    bg_d = dscr("bg", [Ttot, 32])
    qT_d = [dscr("qTB", [512, Ttot], BF16), dscr("qTC", [512, Ttot], BF16)]
    kT_d = dscr("kT", [2, 128, Tk], BF16)
    vt_d = dscr("vt", [2, Tk, 128], BF16)
    sig_d = dscr("sig", [3 * D, Ttot], BF16)
    gq_d = dscr("gq", [16, 64, Ttot])
    ktok_d = dscr("ktok", [Ttot, 512])
    vtok_d = dscr("vtok", [Ttot, 512])
    oT_d = dscr("oT", [2, 512, Ttot])
    yT_d = dscr("yT", [3, 512, Ttot], BF16)
    h2T_d = dscr("h2T", [D, Ttot], BF16)

    S = Sched(nc)
    uid = [0]

    class Arena:
        def __init__(self, base, top):
            self.base = base
            self.top = top
            self.off = base

        def reset(self):
            self.off = self.base

        def alloc(self, shape, dt):
            n = 1
            for s_ in shape[1:]:
                n *= s_
            nbytes = n * (4 if dt in (F32, F32R) else 2)
            nbytes = (nbytes + 63) // 64 * 64
            assert self.off + nbytes <= self.top, (self.off, nbytes, self.top)
            uid[0] += 1
            t = nc.alloc_sbuf_tensor_at(f"t{uid[0]}", list(shape), dt, offset=self.off)
            self.off += nbytes
            return t

    pers = Arena(SB_BASE, SB_BASE + 16384)
    ar = Arena(SB_BASE + 16384, SB_TOP)

    class TL:
        def __init__(self, shape, dt, nb=1, arena=None):
            self.t = (arena or ar).alloc(shape, dt)
            self.b = [Buf() for _ in range(nb)]

    pst = [nc.alloc_psum_tensor(f"ps{i}", [128, 1024], F32) for i in range(4)]
    pbuf = [Buf(excl=True) for _ in range(8)]

    def bank(i):
        return pst[i // 2][:, (i % 2) * 512:(i % 2) * 512 + 512]

    dbufs = {}

    def db(name, i=0):
        k = (name, i)
        if k not in dbufs:
            dbufs[k] = Buf()
        return dbufs[k]

    def mm(out, lhsT, rhs, start, stop, reads, writes, **kw):
        S.pe(lambda e: e.matmul(out, lhsT=lhsT, rhs=rhs, start=start, stop=stop, **kw), reads, writes)

    def tr(out, in_, ident, reads, writes):
        S.pe(lambda e: e.transpose(out=out, in_=in_, identity=ident), reads, writes)

    cst = TL([128, NCST], F32, arena=pers)
    ones_bf = TL([128, 128], BF16, arena=pers)
    bd_bf = TL([128, 128], BF16, arena=pers)
    mw_bf = TL([128, 384], BF16, arena=pers)
    vecsT = TL([128, 8, 16], F32, arena=pers)
    silT = TL([128, 8, 2], BF16, arena=pers)
    convq = TL([128, 16, NR5], F32, arena=pers)
    convv = TL([128, 4, NR5], F32, arena=pers)
    qkw = TL([128, depth * 4], F32, arena=pers)
    anw = TL([128, depth], F32, arena=pers)
    dtb = TL([128, depth * 16], F32, arena=pers)
    nexpA = TL([128, depth * 16], F32, arena=pers)
    esink = TL([128, depth * 8], F32, arena=pers)
    modT = TL([128, 48, 2], F32, arena=pers)
    A1 = TL([128, 8, 2], F32, arena=pers)
    A2 = TL([128, 8, 2], F32, arena=pers)
    ones2 = TL([128, 2], BF16, arena=pers)
    C = cst.t
    ident = C[:, 0:128]
    rotT = C[:, 256:384]
    ones_f = C[:, 1664:1792]

    def gm(d, k):
        b0 = 384 + d * 640 + k * 64
        return C[0:64, b0:b0 + 64]

    S.dma("sp", cst.t[:], cst_d, writes=cst.b)
    S.act(lambda e: e.copy(out=ones_bf.t[:], in_=C[:, 1664:1792]), cst.b, ones_bf.b)
    S.act(lambda e: e.copy(out=bd_bf.t[:], in_=C[:, 128:256]), cst.b, bd_bf.b)
    S.act(lambda e: e.copy(out=mw_bf.t[:], in_=C[:, 1792:2176]), cst.b, mw_bf.b)
    S.pool(lambda e: e.memset(ones2.t[:], 1.0), (), ones2.b)
    ar.reset()
    vr = TL([16, D], F32)
    cr = TL([NR5, 1536], F32)
    qr = TL([depth * 4, 128], F32)
    anr = TL([depth, 64], F32)
    S.dma("sp", vr.t[:], vecs_d, writes=vr.b)
    S.dma("sp", cr.t[:], conv_d, writes=cr.b)
    S.dma("sp", qr.t[:, 0:64], qkn_d, writes=qr.b)
    S.dma("sp", qr.t[:, 64:128], qkn_d, writes=qr.b)
    S.dma("sp", anr.t[:], anorm_d, writes=anr.b)
    S.dma("sp", dtb.t[:], dtb_d.rearrange("l x -> (l x)").partition_broadcast(128), writes=dtb.b)
    S.dma("sp", nexpA.t[:], alog_d.rearrange("l x -> (l x)").partition_broadcast(128), writes=nexpA.b)
    S.dma("sp", esink.t[:], sink_d.rearrange("l x -> (l x)").partition_broadcast(128), writes=esink.b)
    for c in range(8):
        tr(bank(0)[:, c * 16:(c + 1) * 16], vr.t[0:16, c * 128:(c + 1) * 128], ident[0:16, 0:16], vr.b + cst.b, [pbuf[0]])
    S.dve(lambda e: e.tensor_copy(out=vecsT.t[:], in_=bank(0)[:, 0:128].rearrange("p (c r) -> p c r", r=16)), [pbuf[0]], vecsT.b)
    for j in range(16):
        tr(bank(1)[0:64, j * NR5:(j + 1) * NR5], cr.t[0:NR5, j * 64:(j + 1) * 64], ident[0:NR5, 0:NR5], cr.b + cst.b, [pbuf[1]])
    S.dve(lambda e: e.tensor_copy(out=convq.t[0:64], in_=bank(1)[0:64, 0:16 * NR5].rearrange("p (c r) -> p c r", r=NR5)), [pbuf[1]], convq.b)
    for j in range(4):
        tr(bank(2)[:, j * NR5:(j + 1) * NR5], cr.t[0:NR5, 1024 + j * 128:1024 + (j + 1) * 128], ident[0:NR5, 0:NR5], cr.b + cst.b, [pbuf[2]])
    S.dve(lambda e: e.tensor_copy(out=convv.t[:], in_=bank(2)[:, 0:4 * NR5].rearrange("p (c r) -> p c r", r=NR5)), [pbuf[2]], convv.b)
    tr(bank(3)[:, 0:depth * 4], qr.t[0:depth * 4, :], ident[0:depth * 4, 0:depth * 4], qr.b + cst.b, [pbuf[3]])
    S.dve(lambda e: e.tensor_copy(out=qkw.t[:], in_=bank(3)[:, 0:depth * 4]), [pbuf[3]], qkw.b)
    tr(bank(3)[0:64, 64:64 + depth], anr.t[0:depth, :], ident[0:depth, 0:depth], anr.b + cst.b, [pbuf[3]])
    S.dve(lambda e: e.tensor_copy(out=anw.t[0:64], in_=bank(3)[0:64, 64:64 + depth]), [pbuf[3]], anw.b)
    sg = TL([128, 8, 2], F32)
    S.act(lambda e: e.activation(out=sg.t[:], in_=vecsT.t[:, :, 0:2], func=AF.Exp, scale=-1.0), vecsT.b, sg.b)
    S.act(lambda e: e.activation(out=sg.t[:], in_=sg.t[:], func=AF.Ln, bias=1.0), sg.b, sg.b)
    S.act(lambda e: e.activation(out=sg.t[:], in_=sg.t[:], func=AF.Exp, scale=-1.0), sg.b, sg.b)
    S.dve(lambda e: e.tensor_tensor(out=silT.t[:], in0=sg.t[:], in1=vecsT.t[:, :, 0:2], op=ALU.mult), sg.b + vecsT.b, silT.b)
    S.act(lambda e: e.activation(out=nexpA.t[:], in_=nexpA.t[:], func=AF.Exp), nexpA.b, nexpA.b)
    S.dve(lambda e: e.tensor_scalar(out=nexpA.t[:], in0=nexpA.t[:], scalar1=-1.0, scalar2=None, op0=ALU.mult), nexpA.b, nexpA.b)
    S.act(lambda e: e.activation(out=esink.t[:], in_=esink.t[:], func=AF.Exp), esink.b, esink.b)
    S.barrier()

    ar.reset()
    xin = [TL([128, D], F32) for _ in range(2)]
    xTb = [TL([128, 8, 512], F32) for _ in range(2)]
    for bi in range(NB):
        xo = xTb[bi % 2]
        for tt in range(4):
            t0 = bi * 512 + tt * 128
            xi = xin[tt % 2]
            src = xp_d[t0:t0 + 128, :] if bi == 0 else xs_d[t0 - 512:t0 - 512 + 128, :]
            S.dma("sp", xi.t[:], src, writes=xi.b)
            for c in range(8):
                bk = 2 * (c // 4) + (tt % 2) * 4
                tr(bank(bk)[:, (c % 4) * 128:(c % 4 + 1) * 128], xi.t[:, c * 128:(c + 1) * 128], ident, xi.b + cst.b, [pbuf[bk]])
            for half in range(2):
                bk = 2 * half + (tt % 2) * 4
                eng = S.act if half == 0 else S.dve
                src_ps = bank(bk).rearrange("p (c t) -> p c t", t=128)
                dst = xo.t[:, half * 4:half * 4 + 4, tt * 128:(tt + 1) * 128]
                if half == 0:
                    S.act(lambda e, dst=dst, src_ps=src_ps: e.copy(out=dst, in_=src_ps), [pbuf[bk]], xo.b)
                else:
                    S.dve(lambda e, dst=dst, src_ps=src_ps: e.tensor_copy(out=dst, in_=src_ps), [pbuf[bk]], xo.b)
        S.dma("sp", xT_d.rearrange("(c p) t -> p c t", p=128)[:, :, bi * 512:(bi + 1) * 512], xo.t[:], reads=xo.b, writes=[db("xT", bi)])
    S.barrier()

    def load_w_bf16(dst_tl, src_ap_fn, nchunk):
        for c in range(nchunk):
            S.dma("pool", dst_tl.t[:, c], src_ap_fn(c), writes=dst_tl.b)

    def phase_M(l):
        ar.reset()
        wm = [TL([128, 8, 1536], BF16) for _ in range(2)]
        bm = TL([1, 6 * D], BF16)
        S.dma("pool", bm.t[:], bmod_d[l:l + 1, :], writes=bm.b)
        for q in range(4):
            w = wm[q % 2]
            load_w_bf16(w, lambda c: wmod_d[l, c * 128:(c + 1) * 128, q * 1536:(q + 1) * 1536], 8)
            for gg in range(12):
                g = q * 12 + gg
                o_ = bank(0)[:, 2 * g:2 * g + 2]
                for c in range(8):
                    mm(o_, w.t[:, c, gg * 128:(gg + 1) * 128], silT.t[:, c, :], c == 0, False, w.b + silT.b, [pbuf[0]])
                mm(o_, bm.t[0:1, g * 128:(g + 1) * 128], ones2.t[0:1, :], False, True, bm.b + ones2.b, [pbuf[0]])
        S.dve(lambda e: e.tensor_copy(out=modT.t[:], in_=bank(0)[:, 0:96].rearrange("p (g v) -> p g v", v=2)), [pbuf[0]], modT.b)
        ln1 = vecsT.t[:, :, 2 + l:3 + l].to_broadcast([128, 8, 2])
        ln2 = vecsT.t[:, :, 2 + depth + l:3 + depth + l].to_broadcast([128, 8, 2])
        S.dve(lambda e: e.scalar_tensor_tensor(out=A1.t[:], in0=modT.t[:, 8:16, :], scalar=1.0, in1=ln1, op0=ALU.add, op1=ALU.mult), modT.b + vecsT.b, A1.b)
        S.dve(lambda e: e.scalar_tensor_tensor(out=A2.t[:], in0=modT.t[:, 32:40, :], scalar=1.0, in1=ln2, op0=ALU.add, op1=ALU.mult), modT.b + vecsT.b, A2.b)
        S.barrier()

    def norm_block(xT, hT, mv, Aap, shg, sq, tmp, statb, lnv, rstd):
        for c in range(8):
            s_ = sq[c % 2]
            S.act(lambda e, s_=s_, c=c: e.activation(out=s_.t[:], in_=xT.t[:, c, :], func=AF.Square), xT.b, s_.b)
            mm(bank(statb), ones_bf.t[:], s_.t[:], c == 0, c == 7, ones_bf.b + s_.b, [pbuf[statb]])
        S.act(lambda e: e.activation(out=lnv.t[:], in_=bank(statb), func=AF.Ln, bias=EPS, scale=1.0 / D), [pbuf[statb]], lnv.b)
        S.act(lambda e: e.activation(out=rstd.t[:], in_=lnv.t[:], func=AF.Exp, scale=-0.5), lnv.b, rstd.b)
        for c in range(8):
            t_ = tmp[c % 2]
            S.dve(lambda e, t_=t_, c=c: e.tensor_tensor(out=t_.t[:], in0=xT.t[:, c, :], in1=rstd.t[:], op=ALU.mult), xT.b + rstd.b, t_.b)
            S.act(lambda e, t_=t_, c=c: e.activation(out=hT.t[:, c, :], in_=t_.t[:], func=AF.Identity,
                                                    bias=modT.t[:, shg + c, mv:mv + 1], scale=Aap.t[:, c, mv:mv + 1]),
                  t_.b + modT.b + Aap.b, hT.b)

    def phase_A(l):
        ar.reset()
        win = TL([128, 8, IN_COLS], BF16)
        load_w_bf16(win, lambda c: win_d[l, c * 128:(c + 1) * 128, :], 8)
        xT = [TL([128, 8, 512], F32) for _ in range(2)]
        hT = TL([128, 8, 512], BF16)
        sq = [TL([128, 512], BF16) for _ in range(2)]
        tmp = [TL([128, 512], F32) for _ in range(2)]
        lnv = TL([128, 512], F32)
        rstd = TL([128, 512], F32)
        stf = [TL([128, 512], F32) for _ in range(4)]
        stb = [TL([128, 512], BF16) for _ in range(4)]
        qn = [TL([128, 512], F32) for _ in range(2)]
        t1 = [TL([128, 512], F32) for _ in range(2)]
        t2 = [TL([128, 512], F32) for _ in range(2)]
        cs = [TL([128, 2, 512], F32) for _ in range(2)]
        bgst = [TL([128, 32], F32) for _ in range(2)]
        e1 = [TL([128, 32], F32) for _ in range(2)]
        vst = [TL([128, 256], F32) for _ in range(2)]
        kst = t1
        cnt = {"f": 0, "b": 0, "q": 0, "g": 0, "t": 0}

        def nxt(k, n):
            cnt[k] += 1
            return cnt[k] % n

        def load_x(bi):
            S.dma("sp", xT[bi % 2].t[:], xT_d.rearrange("(c p) t -> p c t", p=128)[:, :, bi * 512:(bi + 1) * 512],
                  reads=[db("xT", bi)], writes=xT[bi % 2].b)

        load_x(0)
        for bi in range(NB):
            mv = 0 if bi == 0 else 1
            x_ = xT[bi % 2]
            if bi + 1 < NB:
                load_x(bi + 1)
            if mv:
                c_ = cs[bi % 2]
                S.dma("sp", c_.t[:], rope_d.rearrange("a p t -> p a t")[:, :, (bi - 1) * 512:bi * 512], writes=c_.b)
            norm_block(x_, hT, mv, A1, 0, sq, tmp, 7, lnv, rstd)
            t0 = bi * 512
            groups = []
            for j in range(16):
                groups.append(("raw", C_QKV + j * 64, 64, j * 64))
            for j in range(4):
                groups.append(("raw", C_QKV + 1024 + j * 128, 128, 1024 + j * 128))
            for j in range(8):
                groups.append(("z", C_Z + j * 64, 64, j * 64))
            for a, (cq, ckk) in enumerate(((C_BQ, C_BK), (C_CQ, C_CK))):
                for m in range(4):
                    groups.append(("q", (cq + m * 64, cq + (m + 4) * 64), 128, (a, m)))
                groups.append(("k", ckk, 128, (a, 0)))
            for j in range(24):
                groups.append(("gate", C_GATE + j * 128, 128, j * 128))
            pending = []
            for gi, (kind, col, M, info) in enumerate(groups):
                bk = gi % 4
                for c in range(8):
                    if kind == "q":
                        mm(bank(bk)[0:64, :], win.t[:, c, col[0]:col[0] + 64], hT.t[:, c, :], c == 0, c == 7, win.b + hT.b, [pbuf[bk]])
                        mm(bank(bk)[64:128, :], win.t[:, c, col[1]:col[1] + 64], hT.t[:, c, :], c == 0, c == 7, win.b + hT.b, [pbuf[bk]],
                           tile_position=(0, 64))
                    else:
                        mm(bank(bk)[0:M, :], win.t[:, c, col:col + M], hT.t[:, c, :], c == 0, c == 7, win.b + hT.b, [pbuf[bk]])
                P = bank(bk)
                while pending and pending[0][0] <= gi:
                    pending.pop(0)[1]()
                if kind == "raw":
                    s_ = stf[nxt("f", 4)]
                    S.act(lambda e, s_=s_, P=P, M=M: e.copy(out=s_.t[0:M, :], in_=P[0:M, :]), [pbuf[bk]], s_.b)
                    S.dma("sp", raw_d[info:info + M, t0:t0 + 512], s_.t[0:M, :], reads=s_.b, writes=[db("raw", bi)])
                elif kind == "z":
                    u_ = stf[nxt("f", 4)]
                    S.act(lambda e, u_=u_, P=P: e.activation(out=u_.t[0:64, :], in_=P[0:64, :], func=AF.Silu), [pbuf[bk]], u_.b)
                    S.dma("sp", zs_d[info:info + 64, t0:t0 + 512], u_.t[0:64, :], reads=u_.b, writes=[db("zs", bi)])
                elif kind == "gate":
                    o_ = stb[nxt("b", 4)]
                    S.act(lambda e, o_=o_, P=P: e.activation(out=o_.t[:], in_=P, func=AF.Sigmoid), [pbuf[bk]], o_.b)
                    S.dma("sp", sig_d[info:info + 128, t0:t0 + 512], o_.t[:], reads=o_.b, writes=[db("sig", bi)])
                else:
                    a, m = info
                    wi = a * 2 + (0 if kind == "q" else 1)
                    s_ = sq[nxt("q", 2)]
                    S.act(lambda e, s_=s_, P=P: e.activation(out=s_.t[:], in_=P, func=AF.Square), [pbuf[bk]], s_.b)
                    q_ = qn[cnt["q"] % 2]
                    a_ = t1[cnt["q"] % 2]
                    b_ = t2[cnt["q"] % 2]

                    def step1(s_=s_, P=P, bk=bk, wi=wi, q_=q_):
                        sb_ = 4 + nxt("g", 2)
                        mm(bank(sb_), bd_bf.t[:], s_.t[:], True, True, bd_bf.b + s_.b, [pbuf[sb_]])
                        r_ = stf[nxt("f", 4)]
                        S.act(lambda e: e.activation(out=r_.t[:], in_=bank(sb_), func=AF.Ln, bias=EPS, scale=1.0 / HD), [pbuf[sb_]], r_.b)
                        S.act(lambda e: e.activation(out=r_.t[:], in_=r_.t[:], func=AF.Exp, scale=-0.5), r_.b, r_.b)
                        S.dve(lambda e: e.scalar_tensor_tensor(out=q_.t[:], in0=P, scalar=qkw.t[:, l * 4 + wi:l * 4 + wi + 1],
                                                               in1=r_.t[:], op0=ALU.mult, op1=ALU.mult),
                              [pbuf[bk]] + r_.b + qkw.b, q_.b)

                    def step2(kind=kind, a=a, m=m, q_=q_, a_=a_, b_=b_, mv=mv, bi=bi, t0=t0):
                        o_ = stb[nxt("b", 4)]
                        if mv:
                            c_ = cs[bi % 2]
                            rb = 6
                            mm(bank(rb), rotT, q_.t[:], True, True, cst.b + q_.b, [pbuf[rb]])
                            S.dve(lambda e: e.tensor_tensor(out=a_.t[:], in0=q_.t[:], in1=c_.t[:, 0, :], op=ALU.mult), q_.b + c_.b, a_.b)
                            S.dve(lambda e: e.tensor_tensor(out=b_.t[:], in0=bank(6), in1=c_.t[:, 1, :], op=ALU.mult), [pbuf[rb]] + c_.b, b_.b)
                            S.pool(lambda e: e.tensor_tensor(out=o_.t[:], in0=a_.t[:], in1=b_.t[:], op=ALU.add), a_.b + b_.b, o_.b)
                        else:
                            S.act(lambda e: e.copy(out=o_.t[:], in_=q_.t[:]), q_.b, o_.b)
                        if kind == "q":
                            S.dma("sp", qT_d[a][m * 128:(m + 1) * 128, t0:t0 + 512], o_.t[:], reads=o_.b, writes=[db("qT%d" % a, bi)])
                        else:
                            kc0 = t0 if bi == 0 else t0 + PAST
                            S.dma("sp", kT_d[a, :, kc0:kc0 + 512], o_.t[:], reads=o_.b, writes=[db("kT", bi)])
                            if bi == 0:
                                for tt in range(4):
                                    tr(bank(6)[:, (tt % 4) * 128:(tt % 4 + 1) * 128], q_.t[:, tt * 128:(tt + 1) * 128], ident, q_.b + cst.b, [pbuf[6]])
                                k_ = kst[a]
                                S.dve(lambda e: e.tensor_copy(out=k_.t[:], in_=bank(6)), [pbuf[6]], k_.b)
                                for tt in range(4):
                                    S.dma("sp", nk_d[a][tt // 2, l, (tt % 2) * 128:(tt % 2 + 1) * 128, :], k_.t[:, tt * 128:(tt + 1) * 128], reads=k_.b)

                    pending.append((gi + 1, step1))
                    pending.append((gi + 2, step2))
            while pending:
                pending.pop(0)[1]()
            for tt in range(4):
                tb = 4 + (tt % 2)
                P = bank(tb)
                tok = slice(tt * 128, (tt + 1) * 128)
                for (c0, n_, o0) in ((C_BETA, 32, 0), (C_BV, 128, 32), (C_CV, 128, 160)):
                    for c in range(8):
                        mm(P[:, o0:o0 + n_], hT.t[:, c, tok], win.t[:, c, c0:c0 + n_], c == 0, c == 7, win.b + hT.b, [pbuf[tb]])
                g_ = bgst[tt % 2]
                e_ = e1[tt % 2]
                S.dve(lambda e, e_=e_, P=P: e.tensor_tensor(out=e_.t[:, 16:32], in0=P[:, 16:32], in1=dtb.t[:, l * 16:(l + 1) * 16], op=ALU.add), [pbuf[tb]] + dtb.b, e_.b)
                S.act(lambda e, e_=e_: e.activation(out=e_.t[:, 16:32], in_=e_.t[:, 16:32], func=AF.Exp), e_.b, e_.b)
                S.act(lambda e, e_=e_: e.activation(out=e_.t[:, 16:32], in_=e_.t[:, 16:32], func=AF.Ln, bias=1.0), e_.b, e_.b)
                S.dve(lambda e, e_=e_, g_=g_: e.tensor_tensor(out=g_.t[:, 16:32], in0=e_.t[:, 16:32], in1=nexpA.t[:, l * 16:(l + 1) * 16], op=ALU.mult), e_.b + nexpA.b, g_.b)
                S.act(lambda e, e_=e_, P=P: e.activation(out=e_.t[:, 0:16], in_=P[:, 0:16], func=AF.Exp, scale=-1.0), [pbuf[tb]], e_.b)
                S.act(lambda e, e_=e_: e.activation(out=e_.t[:, 0:16], in_=e_.t[:, 0:16], func=AF.Ln, bias=1.0), e_.b, e_.b)
                S.act(lambda e, e_=e_, g_=g_: e.activation(out=g_.t[:, 0:16], in_=e_.t[:, 0:16], func=AF.Exp, scale=-1.0), e_.b, g_.b)
                S.dma("sp", bg_d[t0 + tt * 128:t0 + (tt + 1) * 128, :], g_.t[:], reads=g_.b, writes=[db("bg", bi)])
                v_ = vst[tt % 2]
                S.act(lambda e, v_=v_, P=P: e.copy(out=v_.t[:], in_=P[:, 32:288]), [pbuf[tb]], v_.b)
                kc0 = (t0 if bi == 0 else t0 + PAST) + tt * 128
                for a in range(2):
                    S.dma("pool", vt_d[a, kc0:kc0 + 128, :], v_.t[:, a * 128:(a + 1) * 128], reads=v_.b, writes=[db("vt", bi)])
                    if bi == 0:
                        S.dma("sp", nv_d[a][tt // 2, l, (tt % 2) * 128:(tt % 2 + 1) * 128, :], v_.t[:, a * 128:(a + 1) * 128], reads=v_.b)
        S.barrier()

    def phase_B1(l):
        ar.reset()
        rawt = [TL([128, 520], F32) for _ in range(3)]
        acc = [TL([128, 512], F32) for _ in range(3)]
        fqk = TL([64, 16, 512], F32, nb=16)
        sqb = [TL([64, 512], BF16) for _ in range(4)]
        rs = [TL([64, 512], F32) for _ in range(4)]
        kn = TL([64, 8, 512], F32)
        qst = [TL([64, 512], F32) for _ in range(2)]
        fv = TL([128, 4, 512], F32)
        tst = [TL([128, 512], F32) for _ in range(2)]
        stat_banks = (0, 1, 6, 7)
        n_ = [0]
        for bi in range(NB):
            t0 = bi * 512
            segs = [(0, 256, 0, 256), (256, 512, 256, 512)] if bi == 0 else [(0, 512, seqs[2][0] - t0, seqs[2][0] + seqs[2][1] - t0)]
            for ti in range(20):
                M = 64 if ti < 16 else 128
                ch0 = ti * 64 if ti < 16 else 1024 + (ti - 16) * 128
                wtb = convq.b if ti < 16 else convv.b
                n_[0] += 1
                r_ = rawt[n_[0] % 3]
                a_ = acc[n_[0] % 3]
                for (a0, a1, s0, s1) in segs:
                    lo = max(a0 - 2, s0)
                    hi = min(a1 + 2, s1)
                    off = a0 if len(segs) == 1 else a0 + (4 if a0 else 0)
                    if lo > a0 - 2:
                        S.pool(lambda e, r_=r_, off=off, M=M: e.memset(r_.t[0:M, off:off + 2], 0.0), (), r_.b)
                    if hi < a1 + 2:
                        S.pool(lambda e, r_=r_, off=off, M=M, a0=a0, a1=a1: e.memset(r_.t[0:M, off + (a1 - a0) + 2:off + (a1 - a0) + 4], 0.0), (), r_.b)
                    S.dma("sp", r_.t[0:M, off + (lo - (a0 - 2)):off + (hi - (a0 - 2))], raw_d[ch0:ch0 + M, t0 + lo:t0 + hi],
                          reads=[db("raw", bi), db("raw", max(bi - 1, 0)), db("raw", min(bi + 1, NB - 1))], writes=r_.b)
                    n = a1 - a0
                    for k in range(5):
                        src = r_.t[0:M, off + k:off + k + n]
                        dst = a_.t[0:M, a0:a1]
                        sc = convv.t[:, ti - 16, l * 5 + k:l * 5 + k + 1] if ti >= 16 else convq.t[0:64, ti, l * 5 + k:l * 5 + k + 1]
                        if k == 0:
                            S.dve(lambda e, dst=dst, src=src, sc=sc: e.tensor_scalar(out=dst, in0=src, scalar1=sc, scalar2=None, op0=ALU.mult), r_.b + wtb, a_.b)
                        else:
                            S.dve(lambda e, dst=dst, src=src, sc=sc: e.scalar_tensor_tensor(out=dst, in0=src, scalar=sc, in1=dst, op0=ALU.mult, op1=ALU.add),
                                  r_.b + wtb + a_.b, a_.b)
                if ti >= 16:
                    S.act(lambda e, a_=a_, ti=ti: e.activation(out=fv.t[:, ti - 16, :], in_=a_.t[:], func=AF.Silu), a_.b, fv.b)
                else:
                    S.act(lambda e, a_=a_, ti=ti: e.activation(out=fqk.t[:, ti, :], in_=a_.t[0:64, :], func=AF.Silu), a_.b, [fqk.b[ti]])
            LA = 2
            inflight = {}

            def p2a(ti):
                n_[0] += 1
                s_ = sqb[ti % 4]
                sb_ = stat_banks[ti % 4]
                S.act(lambda e, s_=s_, ti=ti: e.activation(out=s_.t[:], in_=fqk.t[:, ti, :], func=AF.Square), [fqk.b[ti]], s_.b)
                mm(bank(sb_)[0:64, :], ones_bf.t[0:64, 0:64], s_.t[:], True, True, ones_bf.b + s_.b, [pbuf[sb_]])

            def p2b(ti):
                sb_ = stat_banks[ti % 4]
                r2 = rs[ti % 4]
                S.act(lambda e, r2=r2, sb_=sb_: e.activation(out=r2.t[:], in_=bank(sb_)[0:64, :], func=AF.Ln, bias=EPS), [pbuf[sb_]], r2.b)
                S.act(lambda e, r2=r2: e.activation(out=r2.t[:], in_=r2.t[:], func=AF.Exp, scale=-0.5), r2.b, r2.b)
                if ti < 8:
                    o_ = qst[ti % 2]
                    S.dve(lambda e, o_=o_, r2=r2, ti=ti: e.scalar_tensor_tensor(out=o_.t[:], in0=fqk.t[:, ti, :], scalar=HD ** -0.5, in1=r2.t[:],
                                                                            op0=ALU.mult, op1=ALU.mult), [fqk.b[ti]] + r2.b, o_.b)
                    S.dma("sp", gq_d[ti, :, t0:t0 + 512], o_.t[:], reads=o_.b, writes=[db("gq", bi)])
                else:
                    S.dve(lambda e, r2=r2, ti=ti: e.tensor_tensor(out=kn.t[:, ti - 8, :], in0=fqk.t[:, ti, :], in1=r2.t[:], op=ALU.mult), [fqk.b[ti]] + r2.b, kn.b)
                    S.dma("sp", gq_d[ti, :, t0:t0 + 512], kn.t[:, ti - 8, :], reads=kn.b, writes=[db("gq", bi)])

            for i in range(16 + LA):
                if i < 16:
                    p2a(i)
                if i >= LA:
                    p2b(i - LA)
            for tt in range(4):
                kb, vb = 2 + (tt % 2) * 2, 3 + (tt % 2) * 2
                for h in range(8):
                    tr(bank(kb)[:, h * 64:(h + 1) * 64], kn.t[:, h, tt * 128:(tt + 1) * 128], ident[0:64, 0:64], kn.b + cst.b, [pbuf[kb]])
                for j in range(4):
                    tr(bank(vb)[:, j * 128:(j + 1) * 128], fv.t[:, j, tt * 128:(tt + 1) * 128], ident, fv.b + cst.b, [pbuf[vb]])
                k_ = tst[0]
                v_ = tst[1]
                S.act(lambda e, k_=k_, kb=kb: e.copy(out=k_.t[:], in_=bank(kb)), [pbuf[kb]], k_.b)
                S.dve(lambda e, v_=v_, vb=vb: e.tensor_copy(out=v_.t[:], in_=bank(vb)), [pbuf[vb]], v_.b)
                S.dma("sp", ktok_d[t0 + tt * 128:t0 + (tt + 1) * 128, :], k_.t[:], reads=k_.b, writes=[db("ktok", bi)])
                S.dma("sp", vtok_d[t0 + tt * 128:t0 + (tt + 1) * 128, :], v_.t[:], reads=v_.b, writes=[db("vtok", bi)])
        S.barrier()

    def phase_B23(l):
        ar.reset()
        f3 = [64, 8, 64]

        def T3(dt=F32):
            return TL(f3, dt)

        D_ = []
        for d in range(2):
            o = {}
            o["qk"] = [TL([64, 16, 256], BF16) for _ in range(1)]
            o["ktok"] = [TL([64, 8, 64], F32) for _ in range(2)]
            o["vtok"] = [TL([64, 8, 64], F32) for _ in range(2)]
            o["bg"] = [TL([64, 32], F32) for _ in range(2)]
            tmp_ = [T3() for _ in range(7)]
            o["Gexp"], o["Bexp"], o["XL"], o["XU"], o["dl"], o["du"], o["M1T"] = tmp_
            o["decay"], o["decayT"], o["egB"], o["nbU"], o["nbL"], o["M1"] = tmp_[2], tmp_[3], tmp_[0], tmp_[1], tmp_[4], tmp_[5]
            o["gc"] = TL([64, 8], F32)
            o["eg"] = TL([64, 8], F32)
            o["beg"] = TL([64, 8], F32)
            o["PA"] = [TL([64, 8, 128], F32R) for _ in range(2)]
            o["PW"] = [TL([64, 8, 128], F32R) for _ in range(2)]
            o["Zm1"], o["ZmT1"], o["ZmT2"], o["X"], o["Xp"] = T3(F32R), T3(F32R), T3(F32R), T3(F32R), T3(F32R)
            o["bv"] = T3(F32R)
            o["bek"] = T3(F32R)
            o["up"] = [T3() for _ in range(2)]
            o["wT"] = [T3(BF16) for _ in range(2)]
            o["qgT"] = [T3(BF16) for _ in range(2)]
            o["attnT"] = [T3(BF16) for _ in range(2)]
            o["kg"] = [T3(BF16) for _ in range(2)]
            o["glB"] = [TL([64, 8], F32) for _ in range(2)]
            o["u"] = T3(BF16)
            o["s"] = T3()
            o["s1"] = T3()
            o["sbf"] = T3(BF16)
            o["ost"] = [TL([64, 8, 256], F32) for _ in range(1)]
            D_.append(o)
        ones64 = ones_f[0:64, 0:64]
        id64 = ident[0:64, 0:64]

        def bc_h(ap2):
            return ap2.unsqueeze(2).to_broadcast(f3)

        def bc_m(ap2):
            return ap2.unsqueeze(1).to_broadcast(f3)

        def v3(ap):
            return ap.rearrange("p (h f) -> p h f", f=64)

        for (s0, T, is_s) in seqs:
            si = seqs.index((s0, T, is_s))
            N = T // 64
            for d in range(2):
                o = D_[d]
                if is_s:
                    S.dma("sp", o["s"].t[:], sd_d[l, d].rearrange("h k v -> k h v"), writes=o["s"].b)
                else:
                    S.pool(lambda e, o=o: e.memset(o["s"].t[:], 0.0), (), o["s"].b)
                S.act(lambda e, o=o: e.copy(out=o["sbf"].t[:], in_=o["s"].t[:]), o["s"].b, o["sbf"].b)

            CUT = int(os.environ.get('B23CUT', '99'))

            def b2(d, c, n):
                o = D_[d]
                yield
                PA, PB, PC, PD = bank(4 * d), bank(4 * d + 1), bank(4 * d + 2), bank(4 * d + 3)
                bA, bB, bC, bD = pbuf[4 * d], pbuf[4 * d + 1], pbuf[4 * d + 2], pbuf[4 * d + 3]
                PCD = pst[2 * d + 1][0:64, :].rearrange("p (h f) -> p h f", f=128)
                tok0 = s0 + c * 64
                bi, cb = tok0 // 512, (tok0 % 512) // 64
                r = n % 2
                c4 = (tok0 % 256) // 64
                qk = o["qk"][0]
                if c4 == (0 if d == 0 else 3):
                    hb0 = (tok0 // 256) * 256
                    S.dma("pool", qk.t[:], gq_d.rearrange("j p t -> p j t")[:, :, hb0:hb0 + 256], reads=[db("gq", bi)], writes=qk.b)
                kt, vt, bgt = o["ktok"][r], o["vtok"][r], o["bg"][r]
                S.dma("sp", kt.t[:], ktok_d[tok0:tok0 + 64, :].rearrange("p (h f) -> p h f", f=64), reads=[db("ktok", bi)], writes=kt.b)
                S.dma("sp", vt.t[:], vtok_d[tok0:tok0 + 64, :].rearrange("p (h f) -> p h f", f=64), reads=[db("vtok", bi)], writes=vt.b)
                S.dma("sp", bgt.t[:], bg_d[tok0:tok0 + 64, :], reads=[db("bg", bi)], writes=bgt.b)
                yield
                g = bgt.t[:, 16 + d * 8:24 + d * 8]
                b = bgt.t[:, d * 8:d * 8 + 8]
                Ud, nm, nmT, st, stT = (gm(d, k) for k in range(5))
                last = 63 if d == 0 else 0
                qc = qk.t[:, 0:8, c4 * 64:(c4 + 1) * 64]
                kc = qk.t[:, 8:16, c4 * 64:(c4 + 1) * 64]
                if CUT < 2:
                    return
                mm(PC[0:64, 0:8], Ud, g, True, True, cst.b + bgt.b, [bC])
                S.dve(lambda e: e.tensor_tensor(out=o["Gexp"].t[:], in0=bc_h(g), in1=bc_m(Ud), op=ALU.mult), bgt.b + cst.b, o["Gexp"].b)
                S.pool(lambda e: e.tensor_tensor(out=o["Bexp"].t[:], in0=bc_h(b), in1=bc_m(id64), op=ALU.mult), bgt.b + cst.b, o["Bexp"].b)
                yield
                mm(PA[0:64, :], ones64, o["Gexp"].t[:].rearrange("p h f -> p (h f)"), True, True, cst.b + o["Gexp"].b, [bA])
                mm(PB[0:64, :], ones64, o["Bexp"].t[:].rearrange("p h f -> p (h f)"), True, True, cst.b + o["Bexp"].b, [bB])
                S.act(lambda e: e.copy(out=o["gc"].t[:], in_=PC[0:64, 0:8]), [bC], o["gc"].b)
                yield
                if CUT < 3:
                    return
                gcb = bc_h(o["gc"].t[:])
                S.dve(lambda e: e.tensor_tensor(out=o["XL"].t[:], in0=gcb, in1=bc_m(nm), op=ALU.add), o["gc"].b + cst.b, o["XL"].b)
                S.pool(lambda e: e.tensor_tensor(out=o["XU"].t[:], in0=bc_m(nmT), in1=gcb, op=ALU.subtract), o["gc"].b + cst.b, o["XU"].b)
                yield
                S.dve(lambda e: e.scalar_tensor_tensor(out=o["dl"].t[:], in0=v3(PA[0:64, :]), scalar=-1.0, in1=o["XL"].t[:], op0=ALU.mult, op1=ALU.add),
                      [bA] + o["XL"].b, o["dl"].b)
                S.dve(lambda e: e.tensor_tensor(out=o["du"].t[:], in0=v3(PA[0:64, :]), in1=o["XU"].t[:], op=ALU.add), [bA] + o["XU"].b, o["du"].b)
                S.act(lambda e: e.activation(out=o["egB"].t[:], in_=v3(PA[0:64, :]), func=AF.Exp), [bA], o["egB"].b)
                S.act(lambda e: e.activation(out=o["decay"].t[:], in_=o["dl"].t[:], func=AF.Exp), o["dl"].b, o["decay"].b)
                S.act(lambda e: e.activation(out=o["decayT"].t[:], in_=o["du"].t[:], func=AF.Exp), o["du"].b, o["decayT"].b)
                S.act(lambda e: e.activation(out=o["eg"].t[:], in_=o["gc"].t[:], func=AF.Exp), o["gc"].b, o["eg"].b)
                yield
                S.dve(lambda e: e.tensor_tensor(out=o["nbU"].t[:], in0=v3(PB[0:64, :]), in1=bc_m(stT), op=ALU.mult), [bB] + cst.b, o["nbU"].b)
                S.pool(lambda e: e.tensor_tensor(out=o["nbL"].t[:], in0=bc_h(b), in1=bc_m(st), op=ALU.mult), bgt.b + cst.b, o["nbL"].b)
                yield
                if CUT < 4:
                    return
                for h in range(8):
                    mm(PA[0:64, h * 64:(h + 1) * 64], kc[:, h, :], kc[:, h, :], True, True, qk.b, [bA])
                for h in range(8):
                    mm(PB[0:64, h * 64:(h + 1) * 64], kc[:, h, :], qc[:, h, :], True, True, qk.b, [bB])
                S.dve(lambda e: e.tensor_tensor(out=o["M1"].t[:], in0=v3(PA[0:64, :]), in1=o["decay"].t[:], op=ALU.mult), [bA] + o["decay"].b, o["M1"].b)
                S.dve(lambda e: e.tensor_tensor(out=o["M1T"].t[:], in0=v3(PA[0:64, :]), in1=o["decayT"].t[:], op=ALU.mult), [bA] + o["decayT"].b, o["M1T"].b)
                at = o["attnT"][r]
                S.dve(lambda e: e.tensor_tensor(out=at.t[:], in0=v3(PB[0:64, :]), in1=o["decayT"].t[:], op=ALU.mult), [bB] + o["decayT"].b, at.b)
                yield
                S.pool(lambda e: e.tensor_tensor(out=o["M1"].t[:], in0=o["M1"].t[:], in1=o["nbL"].t[:], op=ALU.mult), o["M1"].b + o["nbL"].b, o["M1"].b)
                S.pool(lambda e: e.tensor_tensor(out=o["M1T"].t[:], in0=o["M1T"].t[:], in1=o["nbU"].t[:], op=ALU.mult), o["M1T"].b + o["nbU"].b, o["M1T"].b)
                yield
                if CUT < 5:
                    return
                P0f, P0Tf = o["M1"], o["M1T"]
                BD, MA1, MA1T, MA2, MA2T = (gm(d, k) for k in range(5, 10))
                PAc, PWc = o["PA"][0], o["PW"][0]
                S.dve(lambda e, PAc=PAc: e.tensor_tensor(out=PAc.t[:, :, 0:64], in0=P0f.t[:], in1=bc_m(BD), op=ALU.mult), P0f.b + cst.b, PAc.b)
                S.pool(lambda e, PWc=PWc: e.tensor_tensor(out=PWc.t[:, :, 0:64], in0=P0Tf.t[:], in1=bc_m(BD), op=ALU.mult), P0Tf.b + cst.b, PWc.b)
                S.dve(lambda e, PAc=PAc: e.tensor_tensor(out=PAc.t[:, :, 64:128], in0=PAc.t[:, :, 0:64], in1=bc_m(id64), op=ALU.add), PAc.b + cst.b, PAc.b)
                S.pool(lambda e, PWc=PWc: e.tensor_tensor(out=PWc.t[:, :, 64:128], in0=PWc.t[:, :, 0:64], in1=bc_m(id64), op=ALU.add), PWc.b + cst.b, PWc.b)
                S.pool(lambda e: e.tensor_tensor(out=o["Zm1"].t[:], in0=P0Tf.t[:], in1=bc_m(MA1T), op=ALU.mult), P0Tf.b + cst.b, o["Zm1"].b)
                S.pool(lambda e: e.tensor_tensor(out=o["ZmT1"].t[:], in0=P0f.t[:], in1=bc_m(MA1), op=ALU.mult), P0f.b + cst.b, o["ZmT1"].b)
                S.pool(lambda e: e.tensor_tensor(out=o["ZmT2"].t[:], in0=P0f.t[:], in1=bc_m(MA2), op=ALU.mult), P0f.b + cst.b, o["ZmT2"].b)
                yield
                PAB = pst[2 * d][0:64, :].rearrange("p (h f) -> p h f", f=128)
                for k in range(4):
                    PAc, PWc = o["PA"][k % 2], o["PW"][k % 2]
                    PAn, PWn = o["PA"][(k + 1) % 2], o["PW"][(k + 1) % 2]
                    if k == 0:
                        for h in range(8):
                            mm(PA[0:64, h * 64:(h + 1) * 64], PWc.t[:, h, 0:64], PAc.t[:, h, 0:64], True, True, PAc.b + PWc.b, [bA])
                        for h in range(8):
                            mm(PC[0:64, h * 64:(h + 1) * 64], PAc.t[:, h, 0:64], PWc.t[:, h, 0:64], True, True, PAc.b + PWc.b, [bC])
                        S.act(lambda e, PAn=PAn: e.copy(out=PAn.t[:, :, 0:64], in_=v3(PA[0:64, :])), [bA], PAn.b)
                        S.dve(lambda e, PWn=PWn: e.tensor_copy(out=PWn.t[:, :, 0:64], in_=v3(PC[0:64, :])), [bC], PWn.b)
                        S.pool(lambda e, PAn=PAn, PAc=PAc: e.tensor_copy(out=PAn.t[:, :, 64:128], in_=PAc.t[:, :, 64:128]), PAc.b, PAn.b)
                        S.pool(lambda e, PWn=PWn, PWc=PWc: e.tensor_copy(out=PWn.t[:, :, 64:128], in_=PWc.t[:, :, 64:128]), PWc.b, PWn.b)
                    elif k < 3:
                        for h in range(8):
                            mm(PAB[:, h, :], PWc.t[:, h, 0:64], PAc.t[:, h, :], True, True, PAc.b + PWc.b, [bA, bB])
                        for h in range(8):
                            mm(PCD[:, h, :], PAc.t[:, h, 0:64], PWc.t[:, h, :], True, True, PAc.b + PWc.b, [bC, bD])
                        for hh in (0, 4):
                            bk1 = [bA] if hh == 0 else [bB]
                            bk2 = [bC] if hh == 0 else [bD]
                            S.act(lambda e, PAn=PAn, hh=hh: e.copy(out=PAn.t[:, hh:hh + 4, 0:64], in_=PAB[:, hh:hh + 4, 0:64]), bk1, PAn.b)
                            S.dve(lambda e, PAn=PAn, PAc=PAc, hh=hh: e.tensor_tensor(out=PAn.t[:, hh:hh + 4, 64:128], in0=PAB[:, hh:hh + 4, 64:128],
                                                                               in1=PAc.t[:, hh:hh + 4, 64:128], op=ALU.add), bk1 + PAc.b, PAn.b)
                            S.act(lambda e, PWn=PWn, hh=hh: e.copy(out=PWn.t[:, hh:hh + 4, 0:64], in_=PCD[:, hh:hh + 4, 0:64]), bk2, PWn.b)
                            S.dve(lambda e, PWn=PWn, PWc=PWc, hh=hh: e.tensor_tensor(out=PWn.t[:, hh:hh + 4, 64:128], in0=PCD[:, hh:hh + 4, 64:128],
                                                                               in1=PWc.t[:, hh:hh + 4, 64:128], op=ALU.add), bk2 + PWc.b, PWn.b)
                    else:
                        for h in range(8):
                            mm(PA[0:64, h * 64:(h + 1) * 64], PWc.t[:, h, 0:64], PAc.t[:, h, 64:128], True, True, PAc.b + PWc.b, [bA])
                        for h in range(8):
                            mm(PC[0:64, h * 64:(h + 1) * 64], PAc.t[:, h, 0:64], PWc.t[:, h, 64:128], True, True, PAc.b + PWc.b, [bC])
                        S.dve(lambda e, PAn=PAn, PAc=PAc: e.tensor_tensor(out=PAn.t[:, :, 64:128], in0=v3(PA[0:64, :]), in1=PAc.t[:, :, 64:128], op=ALU.add),
                              [bA] + PAc.b, PAn.b)
                        S.dve(lambda e, PWn=PWn, PWc=PWc: e.tensor_tensor(out=PWn.t[:, :, 64:128], in0=v3(PC[0:64, :]), in1=PWc.t[:, :, 64:128], op=ALU.add),
                              [bC] + PWc.b, PWn.b)
                    yield
                Tt, Wt = o["PA"][0], o["PW"][0]
                Tv, Wv = Tt.t[:, :, 64:128], Wt.t[:, :, 64:128]
                for h in range(8):
                    mm(PA[0:64, h * 64:(h + 1) * 64], o["ZmT1"].t[:, h, :], Wt.t[:, h, 64:128], True, True, o["ZmT1"].b + Wt.b, [bA])
                for h in range(8):
                    mm(PB[0:64, h * 64:(h + 1) * 64], o["Zm1"].t[:, h, :], Tt.t[:, h, 64:128], True, True, o["Zm1"].b + Tt.b, [bB])
                S.act(lambda e: e.copy(out=o["X"].t[:], in_=v3(PA[0:64, :])), [bA], o["X"].b)
                S.dve(lambda e: e.tensor_copy(out=o["Xp"].t[:], in_=v3(PB[0:64, :])), [bB], o["Xp"].b)
                yield
                for h in range(8):
                    mm(PC[0:64, h * 64:(h + 1) * 64], Tt.t[:, h, 64:128], o["X"].t[:, h, :], True, True, Tt.b + o["X"].b, [bC])
                for h in range(8):
                    mm(PD[0:64, h * 64:(h + 1) * 64], Wt.t[:, h, 64:128], o["Xp"].t[:, h, :], True, True, Wt.b + o["Xp"].b, [bD])
                S.dve(lambda e: e.tensor_tensor(out=Wv, in0=v3(PC[0:64, :]), in1=Wv, op=ALU.add), [bC] + Wt.b, Wt.b)
                S.dve(lambda e: e.tensor_tensor(out=Tv, in0=v3(PD[0:64, :]), in1=Tv, op=ALU.add), [bD] + Tt.b, Tt.b)
                yield
                for h in range(8):
                    mm(PA[0:64, h * 64:(h + 1) * 64], o["ZmT2"].t[:, h, :], Wt.t[:, h, 64:128], True, True, o["ZmT2"].b + Wt.b, [bA])
                S.act(lambda e: e.copy(out=o["X"].t[:], in_=v3(PA[0:64, :])), [bA], o["X"].b)
                for h in range(8):
                    mm(PC[0:64, h * 64:(h + 1) * 64], Tt.t[:, h, 64:128], o["X"].t[:, h, :], True, True, Tt.b + o["X"].b, [bC])
                S.dve(lambda e: e.tensor_tensor(out=Wv, in0=v3(PC[0:64, :]), in1=Wv, op=ALU.add), [bC] + Wt.b, Wt.b)
                if CUT < 6:
                    return
                TT = Wt
                S.dve(lambda e: e.tensor_tensor(out=o["beg"].t[:], in0=b, in1=o["eg"].t[:], op=ALU.mult), bgt.b + o["eg"].b, o["beg"].b)
                S.pool(lambda e: e.tensor_tensor(out=o["bv"].t[:], in0=vt.t[:], in1=bc_h(b), op=ALU.mult), vt.b + bgt.b, o["bv"].b)
                S.pool(lambda e: e.tensor_tensor(out=o["bek"].t[:], in0=kt.t[:], in1=bc_h(o["beg"].t[:]), op=ALU.mult), kt.b + o["beg"].b, o["bek"].b)
                kg, qg, gl = o["kg"][r], o["qgT"][r], o["glB"][r]
                S.pool(lambda e: e.tensor_tensor(out=kg.t[:], in0=kt.t[:], in1=bc_h(o["decayT"].t[:, :, last]), op=ALU.mult), kt.b + o["decayT"].b, kg.b)
                S.dve(lambda e: e.tensor_tensor(out=qg.t[:], in0=qc, in1=o["egB"].t[:], op=ALU.mult), qk.b + o["egB"].b, qg.b)
                S.act(lambda e: e.copy(out=gl.t[:], in_=o["egB"].t[:, :, last]), o["egB"].b, gl.b)
                yield
                for h in range(8):
                    mm(PA[0:64, h * 64:(h + 1) * 64], TT.t[:, h, 64:128], o["bv"].t[:, h, :], True, True, TT.b + o["bv"].b, [bA])
                for h in range(8):
                    mm(PB[0:64, h * 64:(h + 1) * 64], o["bek"].t[:, h, :], TT.t[:, h, 64:128], True, True, TT.b + o["bek"].b, [bB])
                up, wT = o["up"][r], o["wT"][r]
                S.act(lambda e: e.copy(out=up.t[:], in_=v3(PA[0:64, :])), [bA], up.b)
                S.dve(lambda e: e.tensor_copy(out=wT.t[:], in_=v3(PB[0:64, :])), [bB], wT.b)

            def b3(d, c, n):
                o = D_[d]
                yield
                PA, PB, PC = bank(4 * d), bank(4 * d + 1), bank(4 * d + 2)
                bA, bB, bC = pbuf[4 * d], pbuf[4 * d + 1], pbuf[4 * d + 2]
                tok0 = s0 + c * 64
                bi, cb = tok0 // 512, (tok0 % 512) // 64
                r = n % 2
                up, wT, kg, qg, gl, at = o["up"][r], o["wT"][r], o["kg"][r], o["qgT"][r], o["glB"][r], o["attnT"][r]
                for h in range(8):
                    mm(PA[0:64, h * 64:(h + 1) * 64], wT.t[:, h, :], o["sbf"].t[:, h, :], True, True, wT.b + o["sbf"].b, [bA])
                if CUT < 8:
                    return
                S.dve(lambda e: e.tensor_tensor(out=o["u"].t[:], in0=up.t[:], in1=v3(PA[0:64, :]), op=ALU.subtract), up.b + [bA], o["u"].b)
                yield
                if CUT < 9:
                    return
                for h in range(8):
                    mm(PC[0:64, h * 64:(h + 1) * 64], o["sbf"].t[:, h, :], qg.t[:, h, :], True, False, o["sbf"].b + qg.b, [bC])
                    mm(PC[0:64, h * 64:(h + 1) * 64], o["u"].t[:, h, :], at.t[:, h, :], False, True, o["u"].b + at.b, [bC])
                if CUT < 10:
                    return
                for h in range(8):
                    mm(PB[0:64, h * 64:(h + 1) * 64], kg.t[:, h, :], o["u"].t[:, h, :], True, True, kg.b + o["u"].b, [bB])
                if CUT < 11:
                    return
                ost = o["ost"][0]
                c4 = (tok0 % 256) // 64
                S.act(lambda e: e.copy(out=ost.t[:, :, c4 * 64:(c4 + 1) * 64], in_=v3(PC[0:64, :])), [bC], ost.b)
                S.dve(lambda e: e.tensor_tensor(out=o["s1"].t[:], in0=o["s"].t[:], in1=bc_h(gl.t[:]), op=ALU.mult), o["s"].b + gl.b, o["s1"].b)
                S.dve(lambda e: e.tensor_tensor(out=o["s"].t[:], in0=o["s1"].t[:], in1=v3(PB[0:64, :]), op=ALU.add), o["s1"].b + [bB], o["s"].b)
                S.act(lambda e: e.copy(out=o["sbf"].t[:], in_=o["s"].t[:]), o["s"].b, o["sbf"].b)
                if CUT < 12:
                    return
                if c4 == (3 if d == 0 else 0):
                    lo = (tok0 // 256) * 256
                    for h in range(8):
                        S.dma("sp", oT_d[d, h * 64:(h + 1) * 64, lo:lo + 256], ost.t[:, h, :], reads=ost.b, writes=[db("oT", bi)])

            def run_gens(gs):
                while gs:
                    for g_ in list(gs):
                        try:
                            next(g_)
                        except StopIteration:
                            gs.remove(g_)

            for n in range(N):
                run_gens([b2(0, n, n), b2(1, N - 1 - n, n)])
                run_gens([b3(0, n, n), b3(1, N - 1 - n, n)])
            if not is_s:
                for d in range(2):
                    S.dma("sp", nst_d[si, l, d].rearrange("h k v -> k h v"), D_[d]["s"].t[:], reads=D_[d]["s"].b)
        S.barrier()

    def b4_make(l):
        of = [TL([64, 8, 512], F32) for _ in range(2)]
        ob = [TL([64, 8, 512], F32) for _ in range(2)]
        zt = [TL([64, 8, 512], F32) for _ in range(2)]
        sqb = [TL([64, 512], BF16) for _ in range(2)]
        rs = [TL([64, 512], F32) for _ in range(2)]
        yo = [TL([64, 8, 512], BF16) for _ in range(2)]
        SB4 = 3

        def block(bi):
            r = bi % 2
            sl = slice(bi * 512, (bi + 1) * 512)
            for h in range(8):
                S.dma("sp", of[r].t[:, h, :], oT_d[0, h * 64:(h + 1) * 64, sl], reads=[db("oT", bi)], writes=of[r].b)
                S.dma("sp", ob[r].t[:, h, :], oT_d[1, h * 64:(h + 1) * 64, sl], reads=[db("oT", bi)], writes=ob[r].b)
                S.dma("sp", zt[r].t[:, h, :], zs_d[h * 64:(h + 1) * 64, sl], reads=[db("zs", bi)], writes=zt[r].b)
            S.dve(lambda e, r=r: e.tensor_tensor(out=of[r].t[:], in0=of[r].t[:], in1=ob[r].t[:], op=ALU.add), of[r].b + ob[r].b, of[r].b)
            for h in range(8):
                s_ = sqb[h % 2]
                r2 = rs[h % 2]
                S.act(lambda e, s_=s_, h=h, r=r: e.activation(out=s_.t[:], in_=of[r].t[:, h, :], func=AF.Square), of[r].b, s_.b)
                mm(bank(SB4)[0:64, :], ones_bf.t[0:64, 0:64], s_.t[:], True, True, ones_bf.b + s_.b, [pbuf[SB4]])
                S.act(lambda e, r2=r2: e.activation(out=r2.t[:], in_=bank(SB4)[0:64, :], func=AF.Ln, bias=EPS, scale=1.0 / 64), [pbuf[SB4]], r2.b)
                S.act(lambda e, r2=r2: e.activation(out=r2.t[:], in_=r2.t[:], func=AF.Exp, scale=-0.5), r2.b, r2.b)
                S.dve(lambda e, r2=r2, h=h, r=r: e.scalar_tensor_tensor(out=r2.t[:], in0=of[r].t[:, h, :], scalar=anw.t[0:64, l:l + 1], in1=r2.t[:],
                                                                      op0=ALU.mult, op1=ALU.mult), of[r].b + r2.b + anw.b, r2.b)
                S.pool(lambda e, r2=r2, h=h, r=r: e.tensor_tensor(out=yo[r].t[:, h, :], in0=r2.t[:], in1=zt[r].t[:, h, :], op=ALU.mult), r2.b + zt[r].b, yo[r].b)
            for h in range(8):
                S.dma("sp", yT_d[0, h * 64:(h + 1) * 64, sl], yo[r].t[:, h, :], reads=yo[r].b, writes=[db("yT", bi)])

        return block

    def phase_C(l):
        ar.reset()
        b4_block = b4_make(l)
        b4_next = [0]
        NKT = Tk // 128
        KT = [TL([128, Tk], BF16) for _ in range(2)]
        V1 = [TL([128, NKT, 2, 65], BF16) for _ in range(2)]
        QT = [TL([128, 4, 512], BF16) for _ in range(2)]
        PTt = [TL([128, 512], BF16) for _ in range(6)]
        ckt = [TL([128, 128], F32) for _ in range(2)]
        kcs = TL([128, 512], BF16)
        rr = [TL([128, 512], F32) for _ in range(2)]
        bcs = [TL([64, 512], F32) for _ in range(2)]
        yst = [TL([64, 512], BF16) for _ in range(2)]
        for a in range(2):
            S.pool(lambda e, a=a: e.memset(V1[a].t[:, :, :, 64:65], 1.0), (), V1[a].b)
            S.dma("sp", KT[a].t[:, 0:512], kT_d[a, :, 0:512], reads=[db("kT", i) for i in range(NB)], writes=KT[a].b)
            S.dma("sp", KT[a].t[:, 1024:Tk], kT_d[a, :, 1024:Tk], reads=[db("kT", i) for i in range(NB)], writes=KT[a].b)
            for kv in range(2):
                S.dma("sp", V1[a].t[:, 0:4, kv, 0:64], vt_d[a, 0:512, kv * 64:(kv + 1) * 64].rearrange("(n p) d -> p n d", p=128),
                      reads=[db("vt", i) for i in range(NB)], writes=V1[a].b)
                S.dma("sp", V1[a].t[:, 8:NKT, kv, 0:64], vt_d[a, 1024:Tk, kv * 64:(kv + 1) * 64].rearrange("(n p) d -> p n d", p=128),
                      reads=[db("vt", i) for i in range(NB)], writes=V1[a].b)
                S.dma("pool", V1[a].t[:, 4:8, kv, 0:64], cv_d[a][l, :, kv * 64:(kv + 1) * 64].rearrange("(n p) d -> p n d", p=128), writes=V1[a].b)
            for kt_ in range(4):
                c_ = ckt[kt_ % 2]
                S.dma("sp", c_.t[:], ck_d[a][l, kt_ * 128:(kt_ + 1) * 128, :], writes=c_.b)
                tr(bank(7)[:, kt_ * 128:(kt_ + 1) * 128], c_.t[:], ident, c_.b + cst.b, [pbuf[7]])
            S.dve(lambda e, a=a: e.tensor_copy(out=KT[a].t[:, 512:1024], in_=bank(7)), [pbuf[7]], KT[a].b)
        cnt = {"s": 0, "p": 0, "acc": 0, "q": 0}
        for (s0, T, is_s) in seqs:
            QB = min(T, 512)
            for qb in range(T // QB):
                q0 = s0 + qb * QB
                bi = q0 // 512
                for a in range(2):
                    if a == 0 and b4_next[0] < NB:
                        b4_block(b4_next[0])
                        b4_next[0] += 1
                    cnt["q"] += 1
                    Q = QT[cnt["q"] % 2]
                    S.dma("sp", Q.t[:, :, 0:QB], qT_d[a].rearrange("(m p) t -> p m t", p=128)[:, :, q0:q0 + QB], reads=[db("qT%d" % a, bi)], writes=Q.b)
                    chunks = []
                    if not is_s:
                        chunks = [(s0 // 128 + j, 0, QB, None) for j in range(T // 128)]
                    elif a == 0:
                        chunks = [(4 + j, 0, QB, None) for j in range(4)] + [(8 + j, 0, QB, None) for j in range(T // 128)]
                    else:
                        ctx = [(4 + j, 0, QB, None) for j in range(4)]
                        loc = []
                        for kc in range(T // 128):
                            lo = max((kc - 1) * 128, qb * QB)
                            hi = min((kc + 2) * 128, (qb + 1) * QB)
                            if lo >= hi:
                                continue
                            loc.append((8 + kc, lo - qb * QB, hi - qb * QB, (lo - (kc - 1) * 128)))
                        chunks = ctx[:1] + loc + ctx[1:]
                    nck = len(chunks)
                    items = [(h, ci) for h in range(8) for ci in range(nck)]
                    st_ = {}
                    accb = {}

                    def stage1(h, ci, Q=Q, a=a, chunks=chunks):
                        m, half = h % 4, h // 4
                        pb0 = 64 * half
                        if ci == 0:
                            cnt["acc"] += 1
                            accb[h] = 4 + cnt["acc"] % 2
                        kti, qlo, qhi, mcol = chunks[ci]
                        cnt["s"] += 1
                        sbk = cnt["s"] % 3
                        nq = qhi - qlo
                        mm(bank(sbk)[:, 0:nq], KT[a].t[pb0:pb0 + 64, kti * 128:(kti + 1) * 128], Q.t[pb0:pb0 + 64, m, qlo:qhi], True, True,
                           KT[a].b + Q.b, [pbuf[sbk]])
                        cnt["p"] += 1
                        Pt = PTt[cnt["p"] % 6]
                        S.act(lambda e, Pt=Pt, sbk=sbk, nq=nq: e.activation(out=Pt.t[:, 0:nq], in_=bank(sbk)[:, 0:nq], func=AF.Exp, scale=HD ** -0.5),
                              [pbuf[sbk]], Pt.b)
                        if mcol is not None and not (mcol == 128 and nq == 128):
                            S.dve(lambda e, Pt=Pt, nq=nq, mcol=mcol: e.tensor_tensor(out=Pt.t[:, 0:nq], in0=Pt.t[:, 0:nq], in1=mw_bf.t[:, mcol:mcol + nq], op=ALU.mult),
                                  Pt.b + mw_bf.b, Pt.b)
                        st_[(h, ci)] = Pt

                    def stage2(h, ci, a=a, chunks=chunks, nck=nck, QB=QB, q0=q0, bi=bi):
                        half = h // 4
                        kti, qlo, qhi, mcol = chunks[ci]
                        nq = qhi - qlo
                        Pt = st_.pop((h, ci))
                        ab = accb[h]
                        ACC = bank(ab)
                        mm(ACC[0:65, qlo:qhi], V1[a].t[:, kti, half, :], Pt.t[:, 0:nq], ci == 0, ci == nck - 1, V1[a].b + Pt.b, [pbuf[ab]])
                        if ci != nck - 1:
                            return
                        r_ = rr[h % 2]
                        if a == 1:
                            S.act(lambda e, r_=r_, ACC=ACC, h=h: e.activation(out=r_.t[64:65, 0:QB], in_=ACC[64:65, 0:QB], func=AF.Ln,
                                                                          bias=esink.t[64:65, l * 8 + h:l * 8 + h + 1]), [pbuf[ab]] + esink.b, r_.b)
                        else:
                            S.act(lambda e, r_=r_, ACC=ACC: e.activation(out=r_.t[64:65, 0:QB], in_=ACC[64:65, 0:QB], func=AF.Ln), [pbuf[ab]], r_.b)
                        S.act(lambda e, r_=r_: e.activation(out=r_.t[64:65, 0:QB], in_=r_.t[64:65, 0:QB], func=AF.Exp, scale=-1.0), r_.b, r_.b)
                        bb = 6 + h % 2
                        mm(bank(bb)[0:64, 0:QB], ones_f[64:65, 0:64], r_.t[64:65, 0:QB], True, True, cst.b + r_.b, [pbuf[bb]])
                        bc_ = bcs[h % 2]
                        S.act(lambda e, bc_=bc_, bb=bb: e.copy(out=bc_.t[:, 0:QB], in_=bank(bb)[0:64, 0:QB]), [pbuf[bb]], bc_.b)
                        y_ = yst[h % 2]
                        S.dve(lambda e, y_=y_, ACC=ACC, bc_=bc_: e.tensor_tensor(out=y_.t[:, 0:QB], in0=ACC[0:64, 0:QB], in1=bc_.t[:, 0:QB], op=ALU.mult),
                              [pbuf[ab]] + bc_.b, y_.b)
                        S.dma("sp", yT_d[1 + a, h * 64:(h + 1) * 64, q0:q0 + QB], y_.t[:, 0:QB], reads=y_.b, writes=[db("yT", bi)])

                    LA = 2
                    for i in range(len(items) + LA):
                        if i < len(items):
                            stage1(*items[i])
                        if i >= LA:
                            stage2(*items[i - LA])
        while b4_next[0] < NB:
            b4_block(b4_next[0])
            b4_next[0] += 1
        S.barrier()

    def phase_D1(l):
        ar.reset()
        wbr = [TL([128, 4, D], BF16) for _ in range(3)]
        wo = TL([128, 8, D], BF16)
        for j in range(3):
            load_w_bf16(wbr[j], lambda c, j=j: wbr_d[j][l, c * 128:(c + 1) * 128, :], 4)
        load_w_bf16(wo, lambda c: wo_d[l, c * 128:(c + 1) * 128, :], 8)
        yt = [TL([128, 12, 512], BF16) for _ in range(2)]
        sg = [TL([128, 24, 512], BF16) for _ in range(2)]
        xT = [TL([128, 8, 512], F32) for _ in range(2)]
        mg = TL([128, 8, 512], BF16)
        hT = TL([128, 8, 512], BF16)
        ta = [TL([128, 512], F32) for _ in range(2)]
        tb_ = [TL([128, 512], F32) for _ in range(2)]
        tc = [TL([128, 512], F32) for _ in range(2)]
        sq = [TL([128, 512], BF16) for _ in range(2)]
        tmp = [TL([128, 512], F32) for _ in range(2)]
        lnv = TL([128, 512], F32)
        rstd = TL([128, 512], F32)

        def loads(bi):
            r = bi % 2
            sl = slice(bi * 512, (bi + 1) * 512)
            S.dma("sp", yt[r].t[:], yT_d.rearrange("j (c p) t -> p (j c) t", p=128)[:, :, sl], reads=[db("yT", bi)], writes=yt[r].b)
            S.dma("sp", sg[r].t[:], sig_d.rearrange("(c p) t -> p c t", p=128)[:, :, sl], reads=[db("sig", bi)], writes=sg[r].b)
            S.dma("sp", xT[r].t[:], xT_d.rearrange("(c p) t -> p c t", p=128)[:, :, sl], reads=[db("xT", bi)], writes=xT[r].b)

        loads(0)
        for bi in range(NB):
            r = bi % 2
            mv = 0 if bi == 0 else 1
            if bi + 1 < NB:
                loads(bi + 1)
            x_ = xT[r]
            for oc in range(8):
                for j in range(3):
                    for c in range(4):
                        mm(bank(j), wbr[j].t[:, c, oc * 128:(oc + 1) * 128], yt[r].t[:, j * 4 + c, :], c == 0, c == 3, wbr[j].b + yt[r].b, [pbuf[j]])
                a_, b_, c_ = ta[oc % 2], tb_[oc % 2], tc[oc % 2]
                S.dve(lambda e, a_=a_, oc=oc, r=r: e.tensor_tensor(out=a_.t[:], in0=bank(0), in1=sg[r].t[:, oc, :], op=ALU.mult), [pbuf[0]] + sg[r].b, a_.b)
                S.dve(lambda e, b_=b_, oc=oc, r=r: e.tensor_tensor(out=b_.t[:], in0=bank(1), in1=sg[r].t[:, 8 + oc, :], op=ALU.mult), [pbuf[1]] + sg[r].b, b_.b)
                S.dve(lambda e, c_=c_, oc=oc, r=r: e.tensor_tensor(out=c_.t[:], in0=bank(2), in1=sg[r].t[:, 16 + oc, :], op=ALU.mult), [pbuf[2]] + sg[r].b, c_.b)
                S.pool(lambda e, a_=a_, b_=b_: e.tensor_tensor(out=a_.t[:], in0=a_.t[:], in1=b_.t[:], op=ALU.add), a_.b + b_.b, a_.b)
                S.pool(lambda e, a_=a_, c_=c_, oc=oc: e.tensor_tensor(out=mg.t[:, oc, :], in0=a_.t[:], in1=c_.t[:], op=ALU.add), a_.b + c_.b, mg.b)
            for oc in range(8):
                bk = 3 + oc % 2
                for c in range(8):
                    mm(bank(bk), wo.t[:, c, oc * 128:(oc + 1) * 128], mg.t[:, c, :], c == 0, c == 7, wo.b + mg.b, [pbuf[bk]])
                S.dve(lambda e, oc=oc, bk=bk, x_=x_, mv=mv: e.scalar_tensor_tensor(out=x_.t[:, oc, :], in0=bank(bk), scalar=modT.t[:, 16 + oc, mv:mv + 1],
                                                                                 in1=x_.t[:, oc, :], op0=ALU.mult, op1=ALU.add),
                      [pbuf[bk]] + x_.b + modT.b, x_.b)
            norm_block(x_, hT, mv, A2, 24, sq, tmp, 7, lnv, rstd)
            sl = slice(bi * 512, (bi + 1) * 512)
            S.dma("sp", xT_d.rearrange("(c p) t -> p c t", p=128)[:, :, sl], x_.t[:], reads=x_.b, writes=[db("xT", bi)])
            S.dma("sp", h2T_d.rearrange("(c p) t -> p c t", p=128)[:, :, sl], hT.t[:], reads=hT.b, writes=[db("h2T", bi)])
        S.barrier()

    def phase_D2(l):
        ar.reset()
        HH = DFF // 2
        w1 = [TL([128, 8, HH], BF16) for _ in range(2)]
        w2s = TL([128, 16, D], BF16)
        w2 = [w2s, w2s]
        load_w_bf16(w1[0], lambda c: wf1_d[l, c * 128:(c + 1) * 128, 0:HH], 8)
        load_w_bf16(w2s, lambda c: wf2_d[l, c * 128:(c + 1) * 128, :], 16)
        load_w_bf16(w1[1], lambda c: wf1_d[l, c * 128:(c + 1) * 128, HH:2 * HH], 8)
        hT = [TL([128, 8, 512], BF16) for _ in range(2)]
        xT = [TL([128, 8, 512], F32) for _ in range(2)]
        rl = [TL([128, 512], BF16) for _ in range(3)]
        aT = TL([128, 16, 512], BF16)
        final = (l == depth - 1)
        xof = [TL([128, D], F32) for _ in range(2)] if final else None
        cnt_ = [0]

        def loads(bi):
            r = cnt_[0] % 2
            cnt_[0] += 1
            sl = slice(bi * 512, (bi + 1) * 512)
            S.dma("sp", hT[r].t[:], h2T_d.rearrange("(c p) t -> p c t", p=128)[:, :, sl], reads=[db("h2T", bi)], writes=hT[r].b)
            S.dma("sp", xT[r].t[:], xT_d.rearrange("(c p) t -> p c t", p=128)[:, :, sl], reads=[db("xT", bi)], writes=xT[r].b)
            return r

        seq_ = [(hf, bi) for hf in range(2) for bi in range(NB)]
        rnext = loads(seq_[0][1])
        for si_, (hf, bi) in enumerate(seq_):
            r = rnext
            mv = 0 if bi == 0 else 1
            if si_ + 1 < len(seq_):
                rnext = loads(seq_[si_ + 1][1])
            if si_ == NB:
                load_w_bf16(w2s, lambda c: wf2_d[l, HH + c * 128:HH + (c + 1) * 128, :], 16)
            x_ = xT[r]
            for oc in range(16):
                bk = oc % 3
                for c in range(8):
                    mm(bank(bk), w1[hf].t[:, c, oc * 128:(oc + 1) * 128], hT[r].t[:, c, :], c == 0, c == 7, w1[hf].b + hT[r].b, [pbuf[bk]])
                r_ = rl[oc % 3]
                S.act(lambda e, r_=r_, bk=bk: e.activation(out=r_.t[:], in_=bank(bk), func=AF.Relu), [pbuf[bk]], r_.b)
                S.pool(lambda e, r_=r_, oc=oc: e.tensor_tensor(out=aT.t[:, oc, :], in0=r_.t[:], in1=r_.t[:], op=ALU.mult), r_.b, aT.b)
            for oc in range(8):
                bk = 3 + oc % 2
                for c in range(16):
                    mm(bank(bk), w2[hf].t[:, c, oc * 128:(oc + 1) * 128], aT.t[:, c, :], c == 0, c == 15, w2[hf].b + aT.b, [pbuf[bk]])
                S.dve(lambda e, oc=oc, bk=bk, x_=x_, mv=mv: e.scalar_tensor_tensor(out=x_.t[:, oc, :], in0=bank(bk), scalar=modT.t[:, 40 + oc, mv:mv + 1],
                                                                                 in1=x_.t[:, oc, :], op0=ALU.mult, op1=ALU.add),
                      [pbuf[bk]] + x_.b + modT.b, x_.b)
            sl = slice(bi * 512, (bi + 1) * 512)
            if not (final and hf == 1):
                S.dma("sp", xT_d.rearrange("(c p) t -> p c t", p=128)[:, :, sl], x_.t[:], reads=x_.b, writes=[db("xT", bi)])
            else:
                for tt in range(4):
                    for c in range(8):
                        bk = 5 + (c // 4)
                        tr(bank(bk)[:, (c % 4) * 128:(c % 4 + 1) * 128], x_.t[:, c, tt * 128:(tt + 1) * 128], ident, x_.b + cst.b, [pbuf[bk]])
                    xo_ = xof[tt % 2]
                    S.act(lambda e, xo_=xo_: e.copy(out=xo_.t[:, 0:512], in_=bank(5)), [pbuf[5]], xo_.b)
                    S.dve(lambda e, xo_=xo_: e.tensor_copy(out=xo_.t[:, 512:1024], in_=bank(6)), [pbuf[6]], xo_.b)
                    t0 = bi * 512 + tt * 128
                    dst = yp_d[t0:t0 + 128, :] if bi == 0 else ys_d[t0 - 512:t0 - 512 + 128, :]
                    S.dma("sp", dst, xo_.t[:], reads=xo_.b)
        S.barrier()

    plist = [("M", phase_M), ("A", phase_A), ("B1", phase_B1), ("B23", phase_B23), ("C", phase_C), ("D1", phase_D1),
             ("D2", phase_D2)]
    done = False
    for l in range(depth):
        for nm_, fn_ in plist:
            if stop is not None and nm_ == stop:
                done = True
                break
            fn_(l)
        if done:
            break
    n_ops = len(S.ops)
    S.emit()
    return nc, n_ops


DEPTH = 4
DEC_SEQ = 4096
_cache = {}


def make_in_maps(inputs, depth, T_s, n_cores=8):
    f = lambda a: np.ascontiguousarray(np.asarray(a, dtype=np.float32))
    cst, rope = make_consts(T_s)
    maps = []
    for core in range(n_cores):
        b = core % 2
        vecs = np.zeros((16, D), np.float32)
        vecs[0] = inputs["c_ctx"]
        vecs[1] = inputs["c"][b]
        vecs[2:2 + depth] = inputs["ln1"]
        vecs[2 + depth:2 + 2 * depth] = inputs["ln2"]
        m = {
            "xp": f(inputs["x_prompt"][2 * core:2 * core + 2]).reshape(2 * TP, D),
            "xs": f(inputs["x_sample"][b]),
            "ckg": f(inputs["cache_k_glob"][b]).reshape(depth, PAST, 128),
            "ckw": f(inputs["cache_k_win"][b]).reshape(depth, PAST, 128),
            "cvg": f(inputs["cache_v_glob"][b]).reshape(depth, PAST, 128),
            "cvw": f(inputs["cache_v_win"][b]).reshape(depth, PAST, 128),
            "sd": f(inputs["state_delta"][b]),
            "vecs": vecs,
            "w_mod": f(inputs["w_mod"]), "b_mod": f(inputs["b_mod"]), "w_in": f(inputs["w_in"]),
            "conv": f(inputs["conv_qkv"]).reshape(depth * 5, 1536),
            "a_log": f(inputs["a_log"]).reshape(depth, 16), "dt_bias": f(inputs["dt_bias"]).reshape(depth, 16),
            "a_norm": f(inputs["a_norm"]), "qk_norm": f(inputs["qk_norm"]).reshape(depth * 4, 64), "sink": f(inputs["sink"]),
            "w_br_a": f(inputs["w_br_a"]), "w_br_b": f(inputs["w_br_b"]), "w_br_c": f(inputs["w_br_c"]),
            "w_o": f(inputs["w_o"]), "w_ff1": f(inputs["w_ff1"]), "w_ff2": f(inputs["w_ff2"]),
            "cst": cst, "rope": rope,
        }
        maps.append(m)
    return maps


def assemble(results, depth, T_s):
    yp = np.concatenate([r["yp"].reshape(2, TP, D) for r in results], axis=0)
    ys = np.stack([results[0]["ys"], results[1]["ys"]], axis=0)
    outs = [yp.astype(np.float32), ys.astype(np.float32)]
    for nm_ in ("nkg", "nvg", "nkw", "nvw"):
        outs.append(np.concatenate([r[nm_].reshape(2, depth, TP, 2, HD) for r in results], axis=0).astype(np.float32))
    outs.append(np.concatenate([r["nst"] for r in results], axis=0).astype(np.float32))
    return tuple(outs)


def kernel(**inputs):
    depth, T_s = DEPTH, DEC_SEQ
    key = (depth, T_s)
    if key not in _cache:
        _cache[key] = build(depth, T_s)[0]
    nc = _cache[key]
    maps = make_in_maps(inputs, depth, T_s)
    res = run_bass_kernel_spmd(nc, maps, core_ids=list(range(8)))
    return assemble(res.results, depth, T_s)
```

```python
import os
import numpy as np
import concourse.bass as bass
import concourse.mybir as mybir
from concourse.bass_utils import run_bass_kernel_spmd

F32 = mybir.dt.float32
BF16 = mybir.dt.bfloat16
F32R = mybir.dt.float32r
ALU = mybir.AluOpType
AF = mybir.ActivationFunctionType

ENGS = ("pe", "act", "dve", "pool", "sp")
DMA_RING = 12


class Buf:
    __slots__ = ("lw", "rd", "excl")

    def __init__(self, excl=False):
        self.lw = None
        self.rd = []
        self.excl = excl


class Op:
    __slots__ = ("eng", "fn", "dma", "deps", "signal", "semval", "waits", "dsem", "dval")

    def __init__(self, eng, fn, dma):
        self.eng = eng
        self.fn = fn
        self.dma = dma
        self.deps = []
        self.signal = False
        self.semval = None
        self.waits = []
        self.dsem = None
        self.dval = None


class Sched:
    def __init__(self, nc):
        self.nc = nc
        self.ops = []
        self.last = {}
        self.pend_dma = []

    def op(self, eng, fn, reads=(), writes=(), dma=False):
        o = Op(eng, fn, dma)
        deps = o.deps
        if any(b.excl for b in reads):
            writes = list(writes) + [b for b in reads if b.excl]
            reads = [b for b in reads if not b.excl]
        for b in reads:
            if b.lw is not None:
                deps.append(b.lw)
        for b in writes:
            if b.lw is not None:
                deps.append(b.lw)
            deps.extend(b.rd)
        for b in reads:
            if not dma:
                b.rd = [x for x in b.rd if x.dma or x.eng != eng]
            b.rd.append(o)
        for b in writes:
            b.lw = o
            b.rd = []
        self.ops.append(o)
        if dma:
            self.pend_dma.append(o)
        else:
            self.last[eng] = o
        return o

    def barrier(self):
        deps = list(self.last.values()) + self.pend_dma
        self.pend_dma = []
        for e in ENGS:
            o = Op(e, lambda en: en.nop(), False)
            o.deps = list(deps)
            self.ops.append(o)
            self.last[e] = o

    def pe(self, fn, reads=(), writes=()):
        return self.op("pe", fn, reads, writes)

    def act(self, fn, reads=(), writes=()):
        return self.op("act", fn, reads, writes)

    def dve(self, fn, reads=(), writes=()):
        return self.op("dve", fn, reads, writes)

    def pool(self, fn, reads=(), writes=()):
        return self.op("pool", fn, reads, writes)

    def dma(self, q, out, in_, reads=(), writes=()):
        return self.op(q, lambda e: e.dma_start(out=out, in_=in_), reads, writes, dma=True)

    def emit(self):
        nc = self.nc
        ops = self.ops
        cnt = {e: 0 for e in ENGS}
        for o in ops:
            for d in o.deps:
                if d.dma:
                    continue
                if d.eng == "pe" and o.eng == "pe" and not o.dma:
                    continue
                d.signal = True
        last = {}
        for o in ops:
            if not o.dma:
                last[o.eng] = o
        for o in last.values():
            o.signal = True
        dq = {e: 0 for e in ENGS}
        dma_hist = {e: [] for e in ENGS}
        for o in ops:
            if o.dma:
                j = dq[o.eng]
                dq[o.eng] += 1
                o.dsem = (o.eng, j % DMA_RING)
                o.dval = 16 * (j // DMA_RING + 1)
                dma_hist[o.eng].append(o)
            elif o.signal:
                cnt[o.eng] += 1
                o.semval = cnt[o.eng]
        known = {e: {} for e in ENGS}
        dcount = {e: 0 for e in ENGS}
        for o in ops:
            kn = known[o.eng]
            need = {}
            if o.dma:
                j = dcount[o.eng]
                dcount[o.eng] += 1
                if j >= DMA_RING:
                    prev = dma_hist[o.eng][j - DMA_RING]
                    need[("d",) + prev.dsem] = prev.dval
            for d in o.deps:
                if d.dma:
                    k = ("d",) + d.dsem
                    v = d.dval
                else:
                    if d.eng == "pe" and o.eng == "pe" and not o.dma:
                        continue
                    k = ("c", d.eng)
                    v = d.semval
                if need.get(k, 0) < v:
                    need[k] = v
            for k, v in need.items():
                if kn.get(k, 0) < v:
                    kn[k] = v
                    o.waits.append((k, v))
        final_waits = []
        for e, o in last.items():
            final_waits.append((("c", e), o.semval))
        for e in ENGS:
            for o in dma_hist[e][-DMA_RING:]:
                final_waits.append((("d",) + o.dsem, o.dval))
        from contextlib import ExitStack
        with ExitStack() as st:
            sems = {}
            for e in ENGS:
                sems[("c", e)] = st.enter_context(nc.semaphore(f"c_{e}"))
            for e in ENGS:
                for r in range(min(DMA_RING, dq[e])):
                    sems[("d", e, r)] = st.enter_context(nc.semaphore(f"d_{e}_{r}"))
            block = st.enter_context(nc.Block())
            per = {e: [o for o in ops if o.eng == e] for e in ENGS}

            def run(engobj, lst, final=None):
                for o in lst:
                    for k, v in o.waits:
                        engobj.wait_ge(sems[k], v)
                    ins = o.fn(engobj)
                    if o.dma:
                        ins.then_inc(sems[("d",) + o.dsem], 16)
                    elif o.signal:
                        ins.then_inc(sems[("c", o.eng)], 1)
                if final:
                    for k, v in final:
                        engobj.wait_ge(sems[k], v)

            @block.tensor
            def _(e):
                run(e, per["pe"])

            @block.scalar
            def _(e):
                run(e, per["act"])

            @block.vector
            def _(e):
                run(e, per["dve"])

            @block.gpsimd
            def _(e):
                run(e, per["pool"])

            @block.sync
            def _(e):
                run(e, per["sp"], final_waits)
        self.ops = []


D = 1024
NCH = 8
HD = 64
TP = 256
PAST = 512
DFF = 4096
IN_COLS = 6688
C_QKV, C_Z, C_BETA, C_ALPHA, C_BQ, C_BK, C_BV, C_CQ, C_CK, C_CV, C_GATE = (
    0, 1536, 2048, 2064, 2080, 2592, 2720, 2848, 3360, 3488, 3616)
EPS = 1e-6
NEG = -30000.0
NCST = 2176
SB_BASE = 16512
SB_TOP = 229344


def make_consts(T_s):
    cst = np.zeros((128, NCST), np.float32)
    cst[:, 0:128] = np.eye(128)
    cst[0:64, 128:192] = 1.0
    cst[64:128, 192:256] = 1.0
    for m in range(128):
        if m % 64 < 32:
            cst[m + 32, 256 + m] = -1.0
        else:
            cst[m - 32, 256 + m] = 1.0
    p = np.arange(64)[:, None]
    f = np.arange(64)[None, :]

    def ms(m):
        return ((p // (2 * m) == f // (2 * m)) & (p % (2 * m) >= m) & (f % (2 * m) < m)).astype(np.float32)

    bdm = (p // 16 == f // 16).astype(np.float32)
    for d in range(2):
        R = (p >= f) if d == 0 else (p <= f)
        RT = R.T
        base = 384 + d * 640
        ms1 = ms(16) if d == 0 else ms(16).T
        ms2 = ms(32) if d == 0 else ms(32).T
        tabs = [RT.astype(np.float32), np.where(R, 0.0, NEG), np.where(RT, 0.0, NEG),
                -(R & (p != f)).astype(np.float32), -(RT & (p != f)).astype(np.float32),
                bdm, ms1, ms1.T, ms2, ms2.T]
        for k, tb in enumerate(tabs):
            cst[0:64, base + k * 64:base + (k + 1) * 64] = tb
    cst[:, 1664:1792] = 1.0
    pk = np.arange(128)[:, None]
    fq = np.arange(128)[None, :]
    cst[:, 1792:1920] = (pk <= fq)
    cst[:, 1920:2048] = 1.0
    cst[:, 2048:2176] = (fq <= pk)
    t = np.arange(T_s)
    row_id = (t // 64).astype(np.float32)
    col_id = (t % 64).astype(np.float32)
    inv_freq = (10000.0 ** (-np.arange(16, dtype=np.float32) / 16)).astype(np.float32)
    ang = np.concatenate([row_id[:, None] * inv_freq, col_id[:, None] * inv_freq], axis=-1)
    fidx = np.arange(128) % 32
    rope = np.stack([np.cos(ang)[:, fidx].T, np.sin(ang)[:, fidx].T]).astype(np.float32)
    return cst, np.ascontiguousarray(rope)


def build(depth, T_s, debug=False, stop=None):
    nc = bass.Bass("TRN2", target_bir_lowering=False)
    Ttot = 2 * TP + T_s
    NB = Ttot // 512
    Tk = Ttot + PAST
    seqs = [(0, TP, 0), (TP, TP, 0), (2 * TP, T_s, 1)]
    NR5 = depth * 5

    def din(name, shape, dt=F32):
        return nc.dram_tensor(name, list(shape), dt, kind="ExternalInput").ap()

    def dout(name, shape, dt=F32):
        return nc.dram_tensor(name, list(shape), dt, kind="ExternalOutput").ap()

    def dscr(name, shape, dt=F32):
        if debug:
            return nc.dram_tensor(name, list(shape), dt, kind="ExternalOutput").ap()
        return nc.dram_tensor(name, list(shape), dt).ap()

    xp_d = din("xp", [2 * TP, D])
    xs_d = din("xs", [T_s, D])
    ck_d = [din("ckg", [depth, PAST, 128]), din("ckw", [depth, PAST, 128])]
    cv_d = [din("cvg", [depth, PAST, 128]), din("cvw", [depth, PAST, 128])]
    sd_d = din("sd", [depth, 2, 8, 64, 64])
    vecs_d = din("vecs", [16, D])
    wmod_d = din("w_mod", [depth, D, 6 * D])
    bmod_d = din("b_mod", [depth, 6 * D])
    win_d = din("w_in", [depth, D, IN_COLS])
    conv_d = din("conv", [NR5, 1536])
    alog_d = din("a_log", [depth, 16])
    dtb_d = din("dt_bias", [depth, 16])
    anorm_d = din("a_norm", [depth, 64])
    qkn_d = din("qk_norm", [depth * 4, 64])
    sink_d = din("sink", [depth, 8])
    wbr_d = [din("w_br_a", [depth, 512, D]), din("w_br_b", [depth, 512, D]), din("w_br_c", [depth, 512, D])]
    wo_d = din("w_o", [depth, D, D])
    wf1_d = din("w_ff1", [depth, D, DFF])
    wf2_d = din("w_ff2", [depth, DFF, D])
    cst_d = din("cst", [128, NCST])
    rope_d = din("rope", [2, 128, T_s])
    yp_d = dout("yp", [2 * TP, D])
    ys_d = dout("ys", [T_s, D])
    nk_d = [dout("nkg", [2, depth, TP, 128]), dout("nkw", [2, depth, TP, 128])]
    nv_d = [dout("nvg", [2, depth, TP, 128]), dout("nvw", [2, depth, TP, 128])]
    nst_d = dout("nst", [2, depth, 2, 8, 64, 64])
    xT_d = dscr("xT", [D, Ttot])
    raw_d = dscr("raw", [1536, Ttot])
    zs_d = dscr("zs", [512, Ttot])
    bg_d = dscr("bg", [Ttot, 32])
    qT_d = [dscr("qTB", [512, Ttot], BF16), dscr("qTC", [512, Ttot], BF16)]
    kT_d = dscr("kT", [2, 128, Tk], BF16)
    vt_d = dscr("vt", [2, Tk, 128], BF16)
    sig_d = dscr("sig", [3 * D, Ttot], BF16)
    gq_d = dscr("gq", [16, 64, Ttot])
    ktok_d = dscr("ktok", [Ttot, 512])
    vtok_d = dscr("vtok", [Ttot, 512])
    oT_d = dscr("oT", [2, 512, Ttot])
    yT_d = dscr("yT", [3, 512, Ttot], BF16)
    h2T_d = dscr("h2T", [D, Ttot], BF16)

    S = Sched(nc)
    uid = [0]

    class Arena:
        def __init__(self, base, top):
            self.base = base
            self.top = top
            self.off = base

        def reset(self):
            self.off = self.base

        def alloc(self, shape, dt):
            n = 1
            for s_ in shape[1:]:
                n *= s_
            nbytes = n * (4 if dt in (F32, F32R) else 2)
            nbytes = (nbytes + 63) // 64 * 64
            assert self.off + nbytes <= self.top, (self.off, nbytes, self.top)
            uid[0] += 1
            t = nc.alloc_sbuf_tensor_at(f"t{uid[0]}", list(shape), dt, offset=self.off)
            self.off += nbytes
            return t

    pers = Arena(SB_BASE, SB_BASE + 16384)
    ar = Arena(SB_BASE + 16384, SB_TOP)

    class TL:
        def __init__(self, shape, dt, nb=1, arena=None):
            self.t = (arena or ar).alloc(shape, dt)
            self.b = [Buf() for _ in range(nb)]

    pst = [nc.alloc_psum_tensor(f"ps{i}", [128, 1024], F32) for i in range(4)]
    pbuf = [Buf(excl=True) for _ in range(8)]

    def bank(i):
        return pst[i // 2][:, (i % 2) * 512:(i % 2) * 512 + 512]

    dbufs = {}

    def db(name, i=0):
        k = (name, i)
        if k not in dbufs:
            dbufs[k] = Buf()
        return dbufs[k]

    def mm(out, lhsT, rhs, start, stop, reads, writes, **kw):
        S.pe(lambda e: e.matmul(out, lhsT=lhsT, rhs=rhs, start=start, stop=stop, **kw), reads, writes)

    def tr(out, in_, ident, reads, writes):
        S.pe(lambda e: e.transpose(out=out, in_=in_, identity=ident), reads, writes)

    cst = TL([128, NCST], F32, arena=pers)
    ones_bf = TL([128, 128], BF16, arena=pers)
    bd_bf = TL([128, 128], BF16, arena=pers)
    mw_bf = TL([128, 384], BF16, arena=pers)
    vecsT = TL([128, 8, 16], F32, arena=pers)
    silT = TL([128, 8, 2], BF16, arena=pers)
    convq = TL([128, 16, NR5], F32, arena=pers)
    convv = TL([128, 4, NR5], F32, arena=pers)
    qkw = TL([128, depth * 4], F32, arena=pers)
    anw = TL([128, depth], F32, arena=pers)
    dtb = TL([128, depth * 16], F32, arena=pers)
    nexpA = TL([128, depth * 16], F32, arena=pers)
    esink = TL([128, depth * 8], F32, arena=pers)
    modT = TL([128, 48, 2], F32, arena=pers)
    A1 = TL([128, 8, 2], F32, arena=pers)
    A2 = TL([128, 8, 2], F32, arena=pers)
    ones2 = TL([128, 2], BF16, arena=pers)
    C = cst.t
    ident = C[:, 0:128]
    rotT = C[:, 256:384]
    ones_f = C[:, 1664:1792]

    def gm(d, k):
        b0 = 384 + d * 640 + k * 64
        return C[0:64, b0:b0 + 64]

    S.dma("sp", cst.t[:], cst_d, writes=cst.b)
    S.act(lambda e: e.copy(out=ones_bf.t[:], in_=C[:, 1664:1792]), cst.b, ones_bf.b)
    S.act(lambda e: e.copy(out=bd_bf.t[:], in_=C[:, 128:256]), cst.b, bd_bf.b)
    S.act(lambda e: e.copy(out=mw_bf.t[:], in_=C[:, 1792:2176]), cst.b, mw_bf.b)
    S.pool(lambda e: e.memset(ones2.t[:], 1.0), (), ones2.b)
    ar.reset()
    vr = TL([16, D], F32)
    cr = TL([NR5, 1536], F32)
    qr = TL([depth * 4, 128], F32)
    anr = TL([depth, 64], F32)
    S.dma("sp", vr.t[:], vecs_d, writes=vr.b)
    S.dma("sp", cr.t[:], conv_d, writes=cr.b)
    S.dma("sp", qr.t[:, 0:64], qkn_d, writes=qr.b)
    S.dma("sp", qr.t[:, 64:128], qkn_d, writes=qr.b)
    S.dma("sp", anr.t[:], anorm_d, writes=anr.b)
    S.dma("sp", dtb.t[:], dtb_d.rearrange("l x -> (l x)").partition_broadcast(128), writes=dtb.b)
    S.dma("sp", nexpA.t[:], alog_d.rearrange("l x -> (l x)").partition_broadcast(128), writes=nexpA.b)
    S.dma("sp", esink.t[:], sink_d.rearrange("l x -> (l x)").partition_broadcast(128), writes=esink.b)
    for c in range(8):
        tr(bank(0)[:, c * 16:(c + 1) * 16], vr.t[0:16, c * 128:(c + 1) * 128], ident[0:16, 0:16], vr.b + cst.b, [pbuf[0]])
    S.dve(lambda e: e.tensor_copy(out=vecsT.t[:], in_=bank(0)[:, 0:128].rearrange("p (c r) -> p c r", r=16)), [pbuf[0]], vecsT.b)
    for j in range(16):
        tr(bank(1)[0:64, j * NR5:(j + 1) * NR5], cr.t[0:NR5, j * 64:(j + 1) * 64], ident[0:NR5, 0:NR5], cr.b + cst.b, [pbuf[1]])
    S.dve(lambda e: e.tensor_copy(out=convq.t[0:64], in_=bank(1)[0:64, 0:16 * NR5].rearrange("p (c r) -> p c r", r=NR5)), [pbuf[1]], convq.b)
    for j in range(4):
        tr(bank(2)[:, j * NR5:(j + 1) * NR5], cr.t[0:NR5, 1024 + j * 128:1024 + (j + 1) * 128], ident[0:NR5, 0:NR5], cr.b + cst.b, [pbuf[2]])
    S.dve(lambda e: e.tensor_copy(out=convv.t[:], in_=bank(2)[:, 0:4 * NR5].rearrange("p (c r) -> p c r", r=NR5)), [pbuf[2]], convv.b)
    tr(bank(3)[:, 0:depth * 4], qr.t[0:depth * 4, :], ident[0:depth * 4, 0:depth * 4], qr.b + cst.b, [pbuf[3]])
    S.dve(lambda e: e.tensor_copy(out=qkw.t[:], in_=bank(3)[:, 0:depth * 4]), [pbuf[3]], qkw.b)
    tr(bank(3)[0:64, 64:64 + depth], anr.t[0:depth, :], ident[0:depth, 0:depth], anr.b + cst.b, [pbuf[3]])
    S.dve(lambda e: e.tensor_copy(out=anw.t[0:64], in_=bank(3)[0:64, 64:64 + depth]), [pbuf[3]], anw.b)
    sg = TL([128, 8, 2], F32)
    S.act(lambda e: e.activation(out=sg.t[:], in_=vecsT.t[:, :, 0:2], func=AF.Exp, scale=-1.0), vecsT.b, sg.b)
    S.act(lambda e: e.activation(out=sg.t[:], in_=sg.t[:], func=AF.Ln, bias=1.0), sg.b, sg.b)
    S.act(lambda e: e.activation(out=sg.t[:], in_=sg.t[:], func=AF.Exp, scale=-1.0), sg.b, sg.b)
    S.dve(lambda e: e.tensor_tensor(out=silT.t[:], in0=sg.t[:], in1=vecsT.t[:, :, 0:2], op=ALU.mult), sg.b + vecsT.b, silT.b)
    S.act(lambda e: e.activation(out=nexpA.t[:], in_=nexpA.t[:], func=AF.Exp), nexpA.b, nexpA.b)
    S.dve(lambda e: e.tensor_scalar(out=nexpA.t[:], in0=nexpA.t[:], scalar1=-1.0, scalar2=None, op0=ALU.mult), nexpA.b, nexpA.b)
    S.act(lambda e: e.activation(out=esink.t[:], in_=esink.t[:], func=AF.Exp), esink.b, esink.b)
    S.barrier()

    ar.reset()
    xin = [TL([128, D], F32) for _ in range(2)]
    xTb = [TL([128, 8, 512], F32) for _ in range(2)]
    for bi in range(NB):
        xo = xTb[bi % 2]
        for tt in range(4):
            t0 = bi * 512 + tt * 128
            xi = xin[tt % 2]
            src = xp_d[t0:t0 + 128, :] if bi == 0 else xs_d[t0 - 512:t0 - 512 + 128, :]
            S.dma("sp", xi.t[:], src, writes=xi.b)
            for c in range(8):
                bk = 2 * (c // 4) + (tt % 2) * 4
                tr(bank(bk)[:, (c % 4) * 128:(c % 4 + 1) * 128], xi.t[:, c * 128:(c + 1) * 128], ident, xi.b + cst.b, [pbuf[bk]])
            for half in range(2):
                bk = 2 * half + (tt % 2) * 4
                eng = S.act if half == 0 else S.dve
                src_ps = bank(bk).rearrange("p (c t) -> p c t", t=128)
                dst = xo.t[:, half * 4:half * 4 + 4, tt * 128:(tt + 1) * 128]
                if half == 0:
                    S.act(lambda e, dst=dst, src_ps=src_ps: e.copy(out=dst, in_=src_ps), [pbuf[bk]], xo.b)
                else:
                    S.dve(lambda e, dst=dst, src_ps=src_ps: e.tensor_copy(out=dst, in_=src_ps), [pbuf[bk]], xo.b)
        S.dma("sp", xT_d.rearrange("(c p) t -> p c t", p=128)[:, :, bi * 512:(bi + 1) * 512], xo.t[:], reads=xo.b, writes=[db("xT", bi)])
    S.barrier()

    def load_w_bf16(dst_tl, src_ap_fn, nchunk):
        for c in range(nchunk):
            S.dma("pool", dst_tl.t[:, c], src_ap_fn(c), writes=dst_tl.b)

    def phase_M(l):
        ar.reset()
        wm = [TL([128, 8, 1536], BF16) for _ in range(2)]
        bm = TL([1, 6 * D], BF16)
        S.dma("pool", bm.t[:], bmod_d[l:l + 1, :], writes=bm.b)
        for q in range(4):
            w = wm[q % 2]
            load_w_bf16(w, lambda c: wmod_d[l, c * 128:(c + 1) * 128, q * 1536:(q + 1) * 1536], 8)
            for gg in range(12):
                g = q * 12 + gg
                o_ = bank(0)[:, 2 * g:2 * g + 2]
                for c in range(8):
                    mm(o_, w.t[:, c, gg * 128:(gg + 1) * 128], silT.t[:, c, :], c == 0, False, w.b + silT.b, [pbuf[0]])
                mm(o_, bm.t[0:1, g * 128:(g + 1) * 128], ones2.t[0:1, :], False, True, bm.b + ones2.b, [pbuf[0]])
        S.dve(lambda e: e.tensor_copy(out=modT.t[:], in_=bank(0)[:, 0:96].rearrange("p (g v) -> p g v", v=2)), [pbuf[0]], modT.b)
        ln1 = vecsT.t[:, :, 2 + l:3 + l].to_broadcast([128, 8, 2])
        ln2 = vecsT.t[:, :, 2 + depth + l:3 + depth + l].to_broadcast([128, 8, 2])
        S.dve(lambda e: e.scalar_tensor_tensor(out=A1.t[:], in0=modT.t[:, 8:16, :], scalar=1.0, in1=ln1, op0=ALU.add, op1=ALU.mult), modT.b + vecsT.b, A1.b)
        S.dve(lambda e: e.scalar_tensor_tensor(out=A2.t[:], in0=modT.t[:, 32:40, :], scalar=1.0, in1=ln2, op0=ALU.add, op1=ALU.mult), modT.b + vecsT.b, A2.b)
        S.barrier()

    def norm_block(xT, hT, mv, Aap, shg, sq, tmp, statb, lnv, rstd):
        for c in range(8):
            s_ = sq[c % 2]
            S.act(lambda e, s_=s_, c=c: e.activation(out=s_.t[:], in_=xT.t[:, c, :], func=AF.Square), xT.b, s_.b)
            mm(bank(statb), ones_bf.t[:], s_.t[:], c == 0, c == 7, ones_bf.b + s_.b, [pbuf[statb]])
        S.act(lambda e: e.activation(out=lnv.t[:], in_=bank(statb), func=AF.Ln, bias=EPS, scale=1.0 / D), [pbuf[statb]], lnv.b)
        S.act(lambda e: e.activation(out=rstd.t[:], in_=lnv.t[:], func=AF.Exp, scale=-0.5), lnv.b, rstd.b)
        for c in range(8):
            t_ = tmp[c % 2]
            S.dve(lambda e, t_=t_, c=c: e.tensor_tensor(out=t_.t[:], in0=xT.t[:, c, :], in1=rstd.t[:], op=ALU.mult), xT.b + rstd.b, t_.b)
            S.act(lambda e, t_=t_, c=c: e.activation(out=hT.t[:, c, :], in_=t_.t[:], func=AF.Identity,
                                                    bias=modT.t[:, shg + c, mv:mv + 1], scale=Aap.t[:, c, mv:mv + 1]),
                  t_.b + modT.b + Aap.b, hT.b)

    def phase_A(l):
        ar.reset()
        win = TL([128, 8, IN_COLS], BF16)
        load_w_bf16(win, lambda c: win_d[l, c * 128:(c + 1) * 128, :], 8)
        xT = [TL([128, 8, 512], F32) for _ in range(2)]
        hT = TL([128, 8, 512], BF16)
        sq = [TL([128, 512], BF16) for _ in range(2)]
        tmp = [TL([128, 512], F32) for _ in range(2)]
        lnv = TL([128, 512], F32)
        rstd = TL([128, 512], F32)
        stf = [TL([128, 512], F32) for _ in range(4)]
        stb = [TL([128, 512], BF16) for _ in range(4)]
        qn = [TL([128, 512], F32) for _ in range(2)]
        t1 = [TL([128, 512], F32) for _ in range(2)]
        t2 = [TL([128, 512], F32) for _ in range(2)]
        cs = [TL([128, 2, 512], F32) for _ in range(2)]
        bgst = [TL([128, 32], F32) for _ in range(2)]
        e1 = [TL([128, 32], F32) for _ in range(2)]
        vst = [TL([128, 256], F32) for _ in range(2)]
        kst = t1
        cnt = {"f": 0, "b": 0, "q": 0, "g": 0, "t": 0}

        def nxt(k, n):
            cnt[k] += 1
            return cnt[k] % n

        def load_x(bi):
            S.dma("sp", xT[bi % 2].t[:], xT_d.rearrange("(c p) t -> p c t", p=128)[:, :, bi * 512:(bi + 1) * 512],
                  reads=[db("xT", bi)], writes=xT[bi % 2].b)

        load_x(0)
        for bi in range(NB):
            mv = 0 if bi == 0 else 1
            x_ = xT[bi % 2]
            if bi + 1 < NB:
                load_x(bi + 1)
            if mv:
                c_ = cs[bi % 2]
                S.dma("sp", c_.t[:], rope_d.rearrange("a p t -> p a t")[:, :, (bi - 1) * 512:bi * 512], writes=c_.b)
            norm_block(x_, hT, mv, A1, 0, sq, tmp, 7, lnv, rstd)
            t0 = bi * 512
            groups = []
            for j in range(16):
                groups.append(("raw", C_QKV + j * 64, 64, j * 64))
            for j in range(4):
                groups.append(("raw", C_QKV + 1024 + j * 128, 128, 1024 + j * 128))
            for j in range(8):
                groups.append(("z", C_Z + j * 64, 64, j * 64))
            for a, (cq, ckk) in enumerate(((C_BQ, C_BK), (C_CQ, C_CK))):
                for m in range(4):
                    groups.append(("q", (cq + m * 64, cq + (m + 4) * 64), 128, (a, m)))
                groups.append(("k", ckk, 128, (a, 0)))
            for j in range(24):
                groups.append(("gate", C_GATE + j * 128, 128, j * 128))
            pending = []
            for gi, (kind, col, M, info) in enumerate(groups):
                bk = gi % 4
                for c in range(8):
                    if kind == "q":
                        mm(bank(bk)[0:64, :], win.t[:, c, col[0]:col[0] + 64], hT.t[:, c, :], c == 0, c == 7, win.b + hT.b, [pbuf[bk]])
                        mm(bank(bk)[64:128, :], win.t[:, c, col[1]:col[1] + 64], hT.t[:, c, :], c == 0, c == 7, win.b + hT.b, [pbuf[bk]],
                           tile_position=(0, 64))
                    else:
                        mm(bank(bk)[0:M, :], win.t[:, c, col:col + M], hT.t[:, c, :], c == 0, c == 7, win.b + hT.b, [pbuf[bk]])
                P = bank(bk)
                while pending and pending[0][0] <= gi:
                    pending.pop(0)[1]()
                if kind == "raw":
                    s_ = stf[nxt("f", 4)]
                    S.act(lambda e, s_=s_, P=P, M=M: e.copy(out=s_.t[0:M, :], in_=P[0:M, :]), [pbuf[bk]], s_.b)
                    S.dma("sp", raw_d[info:info + M, t0:t0 + 512], s_.t[0:M, :], reads=s_.b, writes=[db("raw", bi)])
                elif kind == "z":
                    u_ = stf[nxt("f", 4)]
                    S.act(lambda e, u_=u_, P=P: e.activation(out=u_.t[0:64, :], in_=P[0:64, :], func=AF.Silu), [pbuf[bk]], u_.b)
                    S.dma("sp", zs_d[info:info + 64, t0:t0 + 512], u_.t[0:64, :], reads=u_.b, writes=[db("zs", bi)])
                elif kind == "gate":
                    o_ = stb[nxt("b", 4)]
                    S.act(lambda e, o_=o_, P=P: e.activation(out=o_.t[:], in_=P, func=AF.Sigmoid), [pbuf[bk]], o_.b)
                    S.dma("sp", sig_d[info:info + 128, t0:t0 + 512], o_.t[:], reads=o_.b, writes=[db("sig", bi)])
                else:
                    a, m = info
                    wi = a * 2 + (0 if kind == "q" else 1)
                    s_ = sq[nxt("q", 2)]
                    S.act(lambda e, s_=s_, P=P: e.activation(out=s_.t[:], in_=P, func=AF.Square), [pbuf[bk]], s_.b)
                    q_ = qn[cnt["q"] % 2]
                    a_ = t1[cnt["q"] % 2]
                    b_ = t2[cnt["q"] % 2]

                    def step1(s_=s_, P=P, bk=bk, wi=wi, q_=q_):
                        sb_ = 4 + nxt("g", 2)
                        mm(bank(sb_), bd_bf.t[:], s_.t[:], True, True, bd_bf.b + s_.b, [pbuf[sb_]])
                        r_ = stf[nxt("f", 4)]
                        S.act(lambda e: e.activation(out=r_.t[:], in_=bank(sb_), func=AF.Ln, bias=EPS, scale=1.0 / HD), [pbuf[sb_]], r_.b)
                        S.act(lambda e: e.activation(out=r_.t[:], in_=r_.t[:], func=AF.Exp, scale=-0.5), r_.b, r_.b)
                        S.dve(lambda e: e.scalar_tensor_tensor(out=q_.t[:], in0=P, scalar=qkw.t[:, l * 4 + wi:l * 4 + wi + 1],
                                                               in1=r_.t[:], op0=ALU.mult, op1=ALU.mult),
                              [pbuf[bk]] + r_.b + qkw.b, q_.b)

                    def step2(kind=kind, a=a, m=m, q_=q_, a_=a_, b_=b_, mv=mv, bi=bi, t0=t0):
                        o_ = stb[nxt("b", 4)]
                        if mv:
                            c_ = cs[bi % 2]
                            rb = 6
                            mm(bank(rb), rotT, q_.t[:], True, True, cst.b + q_.b, [pbuf[rb]])
                            S.dve(lambda e: e.tensor_tensor(out=a_.t[:], in0=q_.t[:], in1=c_.t[:, 0, :], op=ALU.mult), q_.b + c_.b, a_.b)
                            S.dve(lambda e: e.tensor_tensor(out=b_.t[:], in0=bank(6), in1=c_.t[:, 1, :], op=ALU.mult), [pbuf[rb]] + c_.b, b_.b)
                            S.pool(lambda e: e.tensor_tensor(out=o_.t[:], in0=a_.t[:], in1=b_.t[:], op=ALU.add), a_.b + b_.b, o_.b)
                        else:
                            S.act(lambda e: e.copy(out=o_.t[:], in_=q_.t[:]), q_.b, o_.b)
                        if kind == "q":
                            S.dma("sp", qT_d[a][m * 128:(m + 1) * 128, t0:t0 + 512], o_.t[:], reads=o_.b, writes=[db("qT%d" % a, bi)])
                        else:
                            kc0 = t0 if bi == 0 else t0 + PAST
                            S.dma("sp", kT_d[a, :, kc0:kc0 + 512], o_.t[:], reads=o_.b, writes=[db("kT", bi)])
                            if bi == 0:
                                for tt in range(4):
                                    tr(bank(6)[:, (tt % 4) * 128:(tt % 4 + 1) * 128], q_.t[:, tt * 128:(tt + 1) * 128], ident, q_.b + cst.b, [pbuf[6]])
                                k_ = kst[a]
                                S.dve(lambda e: e.tensor_copy(out=k_.t[:], in_=bank(6)), [pbuf[6]], k_.b)
                                for tt in range(4):
                                    S.dma("sp", nk_d[a][tt // 2, l, (tt % 2) * 128:(tt % 2 + 1) * 128, :], k_.t[:, tt * 128:(tt + 1) * 128], reads=k_.b)

                    pending.append((gi + 1, step1))
                    pending.append((gi + 2, step2))
            while pending:
                pending.pop(0)[1]()
            for tt in range(4):
                tb = 4 + (tt % 2)
                P = bank(tb)
                tok = slice(tt * 128, (tt + 1) * 128)
                for (c0, n_, o0) in ((C_BETA, 32, 0), (C_BV, 128, 32), (C_CV, 128, 160)):
                    for c in range(8):
                        mm(P[:, o0:o0 + n_], hT.t[:, c, tok], win.t[:, c, c0:c0 + n_], c == 0, c == 7, win.b + hT.b, [pbuf[tb]])
                g_ = bgst[tt % 2]
                e_ = e1[tt % 2]
                S.dve(lambda e, e_=e_, P=P: e.tensor_tensor(out=e_.t[:, 16:32], in0=P[:, 16:32], in1=dtb.t[:, l * 16:(l + 1) * 16], op=ALU.add), [pbuf[tb]] + dtb.b, e_.b)
                S.act(lambda e, e_=e_: e.activation(out=e_.t[:, 16:32], in_=e_.t[:, 16:32], func=AF.Exp), e_.b, e_.b)
                S.act(lambda e, e_=e_: e.activation(out=e_.t[:, 16:32], in_=e_.t[:, 16:32], func=AF.Ln, bias=1.0), e_.b, e_.b)
                S.dve(lambda e, e_=e_, g_=g_: e.tensor_tensor(out=g_.t[:, 16:32], in0=e_.t[:, 16:32], in1=nexpA.t[:, l * 16:(l + 1) * 16], op=ALU.mult), e_.b + nexpA.b, g_.b)
                S.act(lambda e, e_=e_, P=P: e.activation(out=e_.t[:, 0:16], in_=P[:, 0:16], func=AF.Exp, scale=-1.0), [pbuf[tb]], e_.b)
                S.act(lambda e, e_=e_: e.activation(out=e_.t[:, 0:16], in_=e_.t[:, 0:16], func=AF.Ln, bias=1.0), e_.b, e_.b)
                S.act(lambda e, e_=e_, g_=g_: e.activation(out=g_.t[:, 0:16], in_=e_.t[:, 0:16], func=AF.Exp, scale=-1.0), e_.b, g_.b)
                S.dma("sp", bg_d[t0 + tt * 128:t0 + (tt + 1) * 128, :], g_.t[:], reads=g_.b, writes=[db("bg", bi)])
                v_ = vst[tt % 2]
                S.act(lambda e, v_=v_, P=P: e.copy(out=v_.t[:], in_=P[:, 32:288]), [pbuf[tb]], v_.b)
                kc0 = (t0 if bi == 0 else t0 + PAST) + tt * 128
                for a in range(2):
                    S.dma("pool", vt_d[a, kc0:kc0 + 128, :], v_.t[:, a * 128:(a + 1) * 128], reads=v_.b, writes=[db("vt", bi)])
                    if bi == 0:
                        S.dma("sp", nv_d[a][tt // 2, l, (tt % 2) * 128:(tt % 2 + 1) * 128, :], v_.t[:, a * 128:(a + 1) * 128], reads=v_.b)
        S.barrier()

    def phase_B1(l):
        ar.reset()
        rawt = [TL([128, 520], F32) for _ in range(4)]
        acc = [TL([128, 512], F32) for _ in range(4)]
        fqk = TL([64, 16, 512], F32, nb=16)
        sqb = [TL([64, 512], BF16) for _ in range(4)]
        rs = [TL([64, 512], F32) for _ in range(4)]
        kn = TL([64, 8, 512], F32)
        qst = [TL([64, 512], F32) for _ in range(2)]
        fv = TL([128, 4, 512], F32)
        tst = [TL([128, 512], F32) for _ in range(2)]
        stat_banks = (0, 1, 6, 7)
        n_ = [0]
        for bi in range(NB):
            t0 = bi * 512
            segs = [(0, 256, 0, 256), (256, 512, 256, 512)] if bi == 0 else [(0, 512, seqs[2][0] - t0, seqs[2][0] + seqs[2][1] - t0)]
            for tp in range(0, 20, 2):
                work = []
                for ti in (tp, tp + 1):
                    M = 64 if ti < 16 else 128
                    ch0 = ti * 64 if ti < 16 else 1024 + (ti - 16) * 128
                    wtb = convq.b if ti < 16 else convv.b
                    n_[0] += 1
                    r_ = rawt[n_[0] % 4]
                    a_ = acc[n_[0] % 4]
                    for (a0, a1, s0, s1) in segs:
                        lo = max(a0 - 2, s0)
                        hi = min(a1 + 2, s1)
                        off = a0 if len(segs) == 1 else a0 + (4 if a0 else 0)
                        if lo > a0 - 2:
                            S.pool(lambda e, r_=r_, off=off, M=M: e.memset(r_.t[0:M, off:off + 2], 0.0), (), r_.b)
                        if hi < a1 + 2:
                            S.pool(lambda e, r_=r_, off=off, M=M, a0=a0, a1=a1: e.memset(r_.t[0:M, off + (a1 - a0) + 2:off + (a1 - a0) + 4], 0.0), (), r_.b)
                        S.dma("sp", r_.t[0:M, off + (lo - (a0 - 2)):off + (hi - (a0 - 2))], raw_d[ch0:ch0 + M, t0 + lo:t0 + hi],
                              reads=[db("raw", bi), db("raw", max(bi - 1, 0)), db("raw", min(bi + 1, NB - 1))], writes=r_.b)
                        work.append((ti, M, wtb, r_, a_, a0, a1, off))
                for k in range(5):
                    for (ti, M, wtb, r_, a_, a0, a1, off) in work:
                        n = a1 - a0
                        src = r_.t[0:M, off + k:off + k + n]
                        dst = a_.t[0:M, a0:a1]
                        sc = convv.t[:, ti - 16, l * 5 + k:l * 5 + k + 1] if ti >= 16 else convq.t[0:64, ti, l * 5 + k:l * 5 + k + 1]
                        if k == 0:
                            S.dve(lambda e, dst=dst, src=src, sc=sc: e.tensor_scalar(out=dst, in0=src, scalar1=sc, scalar2=None, op0=ALU.mult), r_.b + wtb, a_.b)
                        else:
                            S.dve(lambda e, dst=dst, src=src, sc=sc: e.scalar_tensor_tensor(out=dst, in0=src, scalar=sc, in1=dst, op0=ALU.mult, op1=ALU.add),
                                  r_.b + wtb + a_.b, a_.b)
                done = set()
                for (ti, M, wtb, r_, a_, a0, a1, off) in work:
                    if ti in done:
                        continue
                    done.add(ti)
                    if ti >= 16:
                        S.act(lambda e, a_=a_, ti=ti: e.activation(out=fv.t[:, ti - 16, :], in_=a_.t[:], func=AF.Silu), a_.b, fv.b)
                    else:
                        S.act(lambda e, a_=a_, ti=ti: e.activation(out=fqk.t[:, ti, :], in_=a_.t[0:64, :], func=AF.Silu), a_.b, [fqk.b[ti]])
            LA = 2
            inflight = {}

            def p2a(ti):
                n_[0] += 1
                s_ = sqb[ti % 4]
                sb_ = stat_banks[ti % 4]
                S.act(lambda e, s_=s_, ti=ti: e.activation(out=s_.t[:], in_=fqk.t[:, ti, :], func=AF.Square), [fqk.b[ti]], s_.b)
                mm(bank(sb_)[0:64, :], ones_bf.t[0:64, 0:64], s_.t[:], True, True, ones_bf.b + s_.b, [pbuf[sb_]])

            def p2b(ti):
                sb_ = stat_banks[ti % 4]
                r2 = rs[ti % 4]
                S.act(lambda e, r2=r2, sb_=sb_: e.activation(out=r2.t[:], in_=bank(sb_)[0:64, :], func=AF.Ln, bias=EPS), [pbuf[sb_]], r2.b)
                S.act(lambda e, r2=r2: e.activation(out=r2.t[:], in_=r2.t[:], func=AF.Exp, scale=-0.5), r2.b, r2.b)
                if ti < 8:
                    o_ = qst[ti % 2]
                    S.dve(lambda e, o_=o_, r2=r2, ti=ti: e.scalar_tensor_tensor(out=o_.t[:], in0=fqk.t[:, ti, :], scalar=HD ** -0.5, in1=r2.t[:],
                                                                            op0=ALU.mult, op1=ALU.mult), [fqk.b[ti]] + r2.b, o_.b)
                    S.dma("sp", gq_d[ti, :, t0:t0 + 512], o_.t[:], reads=o_.b, writes=[db("gq", bi)])
                else:
                    S.dve(lambda e, r2=r2, ti=ti: e.tensor_tensor(out=kn.t[:, ti - 8, :], in0=fqk.t[:, ti, :], in1=r2.t[:], op=ALU.mult), [fqk.b[ti]] + r2.b, kn.b)
                    S.dma("sp", gq_d[ti, :, t0:t0 + 512], kn.t[:, ti - 8, :], reads=kn.b, writes=[db("gq", bi)])

            for i in range(16 + LA):
                if i < 16:
                    p2a(i)
                if i >= LA:
                    p2b(i - LA)
            for tt in range(4):
                kb, vb = 2 + (tt % 2) * 2, 3 + (tt % 2) * 2
                for h in range(8):
                    tr(bank(kb)[:, h * 64:(h + 1) * 64], kn.t[:, h, tt * 128:(tt + 1) * 128], ident[0:64, 0:64], kn.b + cst.b, [pbuf[kb]])
                for j in range(4):
                    tr(bank(vb)[:, j * 128:(j + 1) * 128], fv.t[:, j, tt * 128:(tt + 1) * 128], ident, fv.b + cst.b, [pbuf[vb]])
                k_ = tst[0]
                v_ = tst[1]
                S.act(lambda e, k_=k_, kb=kb: e.copy(out=k_.t[:], in_=bank(kb)), [pbuf[kb]], k_.b)
                S.dve(lambda e, v_=v_, vb=vb: e.tensor_copy(out=v_.t[:], in_=bank(vb)), [pbuf[vb]], v_.b)
                S.dma("sp", ktok_d[t0 + tt * 128:t0 + (tt + 1) * 128, :], k_.t[:], reads=k_.b, writes=[db("ktok", bi)])
                S.dma("sp", vtok_d[t0 + tt * 128:t0 + (tt + 1) * 128, :], v_.t[:], reads=v_.b, writes=[db("vtok", bi)])
        S.barrier()

    def phase_B23(l):
        ar.reset()
        f3 = [64, 8, 64]

        def T3(dt=F32):
            return TL(f3, dt)

        D_ = []
        for d in range(2):
            o = {}
            o["qk"] = [TL([64, 16, 256], BF16) for _ in range(1)]
            o["ktok"] = [TL([64, 8, 64], F32) for _ in range(2)]
            o["vtok"] = [TL([64, 8, 64], F32) for _ in range(2)]
            o["bg"] = [TL([64, 32], F32) for _ in range(2)]
            tmp_ = [T3() for _ in range(7)]
            o["Gexp"], o["Bexp"], o["XL"], o["XU"], o["dl"], o["du"], o["M1T"] = tmp_
            o["decay"], o["decayT"], o["egB"], o["nbU"], o["nbL"], o["M1"] = tmp_[2], tmp_[3], tmp_[0], tmp_[1], tmp_[4], tmp_[5]
            o["gc"] = TL([64, 8], F32)
            o["eg"] = TL([64, 8], F32)
            o["beg"] = TL([64, 8], F32)
            o["PA"] = [TL([64, 8, 128], F32R) for _ in range(2)]
            o["PW"] = [TL([64, 8, 128], F32R) for _ in range(2)]
            o["Zm1"], o["ZmT1"], o["ZmT2"], o["X"], o["Xp"] = T3(F32R), T3(F32R), T3(F32R), T3(F32R), T3(F32R)
            o["bv"] = T3(F32R)
            o["bek"] = T3(F32R)
            o["up"] = [T3() for _ in range(2)]
            o["wT"] = [T3(BF16) for _ in range(2)]
            o["qgT"] = [T3(BF16) for _ in range(2)]
            o["attnT"] = [T3(BF16) for _ in range(2)]
            o["kg"] = [T3(BF16) for _ in range(2)]
            o["glB"] = [TL([64, 8], F32) for _ in range(2)]
            o["u"] = T3(BF16)
            o["s"] = T3()
            o["s1"] = T3()
            o["sbf"] = T3(BF16)
            o["ost"] = [TL([64, 8, 256], F32) for _ in range(1)]
            D_.append(o)
        ones64 = ones_f[0:64, 0:64]
        id64 = ident[0:64, 0:64]

        def bc_h(ap2):
            return ap2.unsqueeze(2).to_broadcast(f3)

        def bc_m(ap2):
            return ap2.unsqueeze(1).to_broadcast(f3)

        def v3(ap):
            return ap.rearrange("p (h f) -> p h f", f=64)

        for (s0, T, is_s) in seqs:
            si = seqs.index((s0, T, is_s))
            N = T // 64
            for d in range(2):
                o = D_[d]
                if is_s:
                    S.dma("sp", o["s"].t[:], sd_d[l, d].rearrange("h k v -> k h v"), writes=o["s"].b)
                else:
                    S.pool(lambda e, o=o: e.memset(o["s"].t[:], 0.0), (), o["s"].b)
                S.act(lambda e, o=o: e.copy(out=o["sbf"].t[:], in_=o["s"].t[:]), o["s"].b, o["sbf"].b)

            CUT = int(os.environ.get('B23CUT', '99'))

            def b2(d, c, n):
                o = D_[d]
                yield
                PA, PB, PC, PD = bank(4 * d), bank(4 * d + 1), bank(4 * d + 2), bank(4 * d + 3)
                bA, bB, bC, bD = pbuf[4 * d], pbuf[4 * d + 1], pbuf[4 * d + 2], pbuf[4 * d + 3]
                PCD = pst[2 * d + 1][0:64, :].rearrange("p (h f) -> p h f", f=128)
                tok0 = s0 + c * 64
                bi, cb = tok0 // 512, (tok0 % 512) // 64
                r = n % 2
                c4 = (tok0 % 256) // 64
                qk = o["qk"][0]
                if c4 == (0 if d == 0 else 3):
                    hb0 = (tok0 // 256) * 256
                    S.dma("pool", qk.t[:], gq_d.rearrange("j p t -> p j t")[:, :, hb0:hb0 + 256], reads=[db("gq", bi)], writes=qk.b)
                kt, vt, bgt = o["ktok"][r], o["vtok"][r], o["bg"][r]
                S.dma("sp", kt.t[:], ktok_d[tok0:tok0 + 64, :].rearrange("p (h f) -> p h f", f=64), reads=[db("ktok", bi)], writes=kt.b)
                S.dma("sp", vt.t[:], vtok_d[tok0:tok0 + 64, :].rearrange("p (h f) -> p h f", f=64), reads=[db("vtok", bi)], writes=vt.b)
                S.dma("sp", bgt.t[:], bg_d[tok0:tok0 + 64, :], reads=[db("bg", bi)], writes=bgt.b)
                yield
                g = bgt.t[:, 16 + d * 8:24 + d * 8]
                b = bgt.t[:, d * 8:d * 8 + 8]
                Ud, nm, nmT, st, stT = (gm(d, k) for k in range(5))
                last = 63 if d == 0 else 0
                qc = qk.t[:, 0:8, c4 * 64:(c4 + 1) * 64]
                kc = qk.t[:, 8:16, c4 * 64:(c4 + 1) * 64]
                if CUT < 2:
                    return
                mm(PC[0:64, 0:8], Ud, g, True, True, cst.b + bgt.b, [bC])
                S.dve(lambda e: e.tensor_tensor(out=o["Gexp"].t[:], in0=bc_h(g), in1=bc_m(Ud), op=ALU.mult), bgt.b + cst.b, o["Gexp"].b)
                S.pool(lambda e: e.tensor_tensor(out=o["Bexp"].t[:], in0=bc_h(b), in1=bc_m(id64), op=ALU.mult), bgt.b + cst.b, o["Bexp"].b)
                yield
                mm(PA[0:64, :], ones64, o["Gexp"].t[:].rearrange("p h f -> p (h f)"), True, True, cst.b + o["Gexp"].b, [bA])
                mm(PB[0:64, :], ones64, o["Bexp"].t[:].rearrange("p h f -> p (h f)"), True, True, cst.b + o["Bexp"].b, [bB])
                S.act(lambda e: e.copy(out=o["gc"].t[:], in_=PC[0:64, 0:8]), [bC], o["gc"].b)
                yield
                if CUT < 3:
                    return
                gcb = bc_h(o["gc"].t[:])
                S.dve(lambda e: e.tensor_tensor(out=o["XL"].t[:], in0=gcb, in1=bc_m(nm), op=ALU.add), o["gc"].b + cst.b, o["XL"].b)
                S.pool(lambda e: e.tensor_tensor(out=o["XU"].t[:], in0=bc_m(nmT), in1=gcb, op=ALU.subtract), o["gc"].b + cst.b, o["XU"].b)
                yield
                S.dve(lambda e: e.scalar_tensor_tensor(out=o["dl"].t[:], in0=v3(PA[0:64, :]), scalar=-1.0, in1=o["XL"].t[:], op0=ALU.mult, op1=ALU.add),
                      [bA] + o["XL"].b, o["dl"].b)
                S.dve(lambda e: e.tensor_tensor(out=o["du"].t[:], in0=v3(PA[0:64, :]), in1=o["XU"].t[:], op=ALU.add), [bA] + o["XU"].b, o["du"].b)
                S.act(lambda e: e.activation(out=o["egB"].t[:], in_=v3(PA[0:64, :]), func=AF.Exp), [bA], o["egB"].b)
                S.act(lambda e: e.activation(out=o["decay"].t[:], in_=o["dl"].t[:], func=AF.Exp), o["dl"].b, o["decay"].b)
                S.act(lambda e: e.activation(out=o["decayT"].t[:], in_=o["du"].t[:], func=AF.Exp), o["du"].b, o["decayT"].b)
                S.act(lambda e: e.activation(out=o["eg"].t[:], in_=o["gc"].t[:], func=AF.Exp), o["gc"].b, o["eg"].b)
                yield
                S.dve(lambda e: e.tensor_tensor(out=o["nbU"].t[:], in0=v3(PB[0:64, :]), in1=bc_m(stT), op=ALU.mult), [bB] + cst.b, o["nbU"].b)
                S.pool(lambda e: e.tensor_tensor(out=o["nbL"].t[:], in0=bc_h(b), in1=bc_m(st), op=ALU.mult), bgt.b + cst.b, o["nbL"].b)
                yield
                S.pool(lambda e: e.tensor_tensor(out=o["decay"].t[:], in0=o["decay"].t[:], in1=o["nbL"].t[:], op=ALU.mult), o["decay"].b + o["nbL"].b, o["decay"].b)
                S.pool(lambda e: e.tensor_tensor(out=o["M1T"].t[:], in0=o["decayT"].t[:], in1=o["nbU"].t[:], op=ALU.mult), o["decayT"].b + o["nbU"].b, o["M1T"].b)
                if CUT < 4:
                    return
                for h in range(8):
                    mm(PA[0:64, h * 64:(h + 1) * 64], kc[:, h, :], kc[:, h, :], True, True, qk.b, [bA])
                for h in range(8):
                    mm(PB[0:64, h * 64:(h + 1) * 64], kc[:, h, :], qc[:, h, :], True, True, qk.b, [bB])
                S.dve(lambda e: e.tensor_tensor(out=o["M1"].t[:], in0=v3(PA[0:64, :]), in1=o["decay"].t[:], op=ALU.mult), [bA] + o["decay"].b, o["M1"].b)
                S.dve(lambda e: e.tensor_tensor(out=o["M1T"].t[:], in0=v3(PA[0:64, :]), in1=o["M1T"].t[:], op=ALU.mult), [bA] + o["M1T"].b, o["M1T"].b)
                at = o["attnT"][r]
                S.dve(lambda e: e.tensor_tensor(out=at.t[:], in0=v3(PB[0:64, :]), in1=o["decayT"].t[:], op=ALU.mult), [bB] + o["decayT"].b, at.b)
                yield
                if CUT < 5:
                    return
                P0f, P0Tf = o["M1"], o["M1T"]
                BD, MA1, MA1T, MA2, MA2T = (gm(d, k) for k in range(5, 10))
                PAc, PWc = o["PA"][0], o["PW"][0]
                S.dve(lambda e, PAc=PAc: e.tensor_tensor(out=PAc.t[:, :, 0:64], in0=P0f.t[:], in1=bc_m(BD), op=ALU.mult), P0f.b + cst.b, PAc.b)
                S.pool(lambda e, PWc=PWc: e.tensor_tensor(out=PWc.t[:, :, 0:64], in0=P0Tf.t[:], in1=bc_m(BD), op=ALU.mult), P0Tf.b + cst.b, PWc.b)
                S.dve(lambda e, PAc=PAc: e.tensor_tensor(out=PAc.t[:, :, 64:128], in0=PAc.t[:, :, 0:64], in1=bc_m(id64), op=ALU.add), PAc.b + cst.b, PAc.b)
                S.pool(lambda e, PWc=PWc: e.tensor_tensor(out=PWc.t[:, :, 64:128], in0=PWc.t[:, :, 0:64], in1=bc_m(id64), op=ALU.add), PWc.b + cst.b, PWc.b)
                S.pool(lambda e: e.tensor_tensor(out=o["Zm1"].t[:], in0=P0Tf.t[:], in1=bc_m(MA1T), op=ALU.mult), P0Tf.b + cst.b, o["Zm1"].b)
                S.pool(lambda e: e.tensor_tensor(out=o["ZmT1"].t[:], in0=P0f.t[:], in1=bc_m(MA1), op=ALU.mult), P0f.b + cst.b, o["ZmT1"].b)
                S.pool(lambda e: e.tensor_tensor(out=o["ZmT2"].t[:], in0=P0f.t[:], in1=bc_m(MA2), op=ALU.mult), P0f.b + cst.b, o["ZmT2"].b)
                yield
                PAB = pst[2 * d][0:64, :].rearrange("p (h f) -> p h f", f=128)
                for k in range(4):
                    PAc, PWc = o["PA"][k % 2], o["PW"][k % 2]
                    PAn, PWn = o["PA"][(k + 1) % 2], o["PW"][(k + 1) % 2]
                    if k == 0:
                        for h in range(8):
                            mm(PA[0:64, h * 64:(h + 1) * 64], PWc.t[:, h, 0:64], PAc.t[:, h, 0:64], True, True, PAc.b + PWc.b, [bA])
                        for h in range(8):
                            mm(PC[0:64, h * 64:(h + 1) * 64], PAc.t[:, h, 0:64], PWc.t[:, h, 0:64], True, True, PAc.b + PWc.b, [bC])
                        S.act(lambda e, PAn=PAn: e.copy(out=PAn.t[:, :, 0:64], in_=v3(PA[0:64, :])), [bA], PAn.b)
                        S.dve(lambda e, PWn=PWn: e.tensor_copy(out=PWn.t[:, :, 0:64], in_=v3(PC[0:64, :])), [bC], PWn.b)
                        S.pool(lambda e, PAn=PAn, PAc=PAc: e.tensor_copy(out=PAn.t[:, :, 64:128], in_=PAc.t[:, :, 64:128]), PAc.b, PAn.b)
                        S.pool(lambda e, PWn=PWn, PWc=PWc: e.tensor_copy(out=PWn.t[:, :, 64:128], in_=PWc.t[:, :, 64:128]), PWc.b, PWn.b)
                    elif k < 3:
                        for h in range(8):
                            mm(PAB[:, h, :], PWc.t[:, h, 0:64], PAc.t[:, h, :], True, True, PAc.b + PWc.b, [bA, bB])
                        for h in range(8):
                            mm(PCD[:, h, :], PAc.t[:, h, 0:64], PWc.t[:, h, :], True, True, PAc.b + PWc.b, [bC, bD])
                        for hh in (0, 4):
                            bk1 = [bA] if hh == 0 else [bB]
                            bk2 = [bC] if hh == 0 else [bD]
                            S.act(lambda e, PAn=PAn, hh=hh: e.copy(out=PAn.t[:, hh:hh + 4, 0:64], in_=PAB[:, hh:hh + 4, 0:64]), bk1, PAn.b)
                            S.dve(lambda e, PAn=PAn, PAc=PAc, hh=hh: e.tensor_tensor(out=PAn.t[:, hh:hh + 4, 64:128], in0=PAB[:, hh:hh + 4, 64:128],
                                                                               in1=PAc.t[:, hh:hh + 4, 64:128], op=ALU.add), bk1 + PAc.b, PAn.b)
                            S.act(lambda e, PWn=PWn, hh=hh: e.copy(out=PWn.t[:, hh:hh + 4, 0:64], in_=PCD[:, hh:hh + 4, 0:64]), bk2, PWn.b)
                            S.dve(lambda e, PWn=PWn, PWc=PWc, hh=hh: e.tensor_tensor(out=PWn.t[:, hh:hh + 4, 64:128], in0=PCD[:, hh:hh + 4, 64:128],
                                                                               in1=PWc.t[:, hh:hh + 4, 64:128], op=ALU.add), bk2 + PWc.b, PWn.b)
                    else:
                        for h in range(8):
                            mm(PA[0:64, h * 64:(h + 1) * 64], PWc.t[:, h, 0:64], PAc.t[:, h, 64:128], True, True, PAc.b + PWc.b, [bA])
                        for h in range(8):
                            mm(PC[0:64, h * 64:(h + 1) * 64], PAc.t[:, h, 0:64], PWc.t[:, h, 64:128], True, True, PAc.b + PWc.b, [bC])
                        S.dve(lambda e, PAn=PAn, PAc=PAc: e.tensor_tensor(out=PAn.t[:, :, 64:128], in0=v3(PA[0:64, :]), in1=PAc.t[:, :, 64:128], op=ALU.add),
                              [bA] + PAc.b, PAn.b)
                        S.dve(lambda e, PWn=PWn, PWc=PWc: e.tensor_tensor(out=PWn.t[:, :, 64:128], in0=v3(PC[0:64, :]), in1=PWc.t[:, :, 64:128], op=ALU.add),
                              [bC] + PWc.b, PWn.b)
                    yield
                Tt, Wt = o["PA"][0], o["PW"][0]
                Tv, Wv = Tt.t[:, :, 64:128], Wt.t[:, :, 64:128]
                for h in range(8):
                    mm(PA[0:64, h * 64:(h + 1) * 64], o["ZmT1"].t[:, h, :], Wt.t[:, h, 64:128], True, True, o["ZmT1"].b + Wt.b, [bA])
                for h in range(8):
                    mm(PB[0:64, h * 64:(h + 1) * 64], o["Zm1"].t[:, h, :], Tt.t[:, h, 64:128], True, True, o["Zm1"].b + Tt.b, [bB])
                S.act(lambda e: e.copy(out=o["X"].t[:], in_=v3(PA[0:64, :])), [bA], o["X"].b)
                S.dve(lambda e: e.tensor_copy(out=o["Xp"].t[:], in_=v3(PB[0:64, :])), [bB], o["Xp"].b)
                yield
                for h in range(8):
                    mm(PC[0:64, h * 64:(h + 1) * 64], Tt.t[:, h, 64:128], o["X"].t[:, h, :], True, True, Tt.b + o["X"].b, [bC])
                for h in range(8):
                    mm(PD[0:64, h * 64:(h + 1) * 64], Wt.t[:, h, 64:128], o["Xp"].t[:, h, :], True, True, Wt.b + o["Xp"].b, [bD])
                S.dve(lambda e: e.tensor_tensor(out=Wv, in0=v3(PC[0:64, :]), in1=Wv, op=ALU.add), [bC] + Wt.b, Wt.b)
                S.dve(lambda e: e.tensor_tensor(out=Tv, in0=v3(PD[0:64, :]), in1=Tv, op=ALU.add), [bD] + Tt.b, Tt.b)
                yield
                for h in range(8):
                    mm(PA[0:64, h * 64:(h + 1) * 64], o["ZmT2"].t[:, h, :], Wt.t[:, h, 64:128], True, True, o["ZmT2"].b + Wt.b, [bA])
                S.act(lambda e: e.copy(out=o["X"].t[:], in_=v3(PA[0:64, :])), [bA], o["X"].b)
                for h in range(8):
                    mm(PC[0:64, h * 64:(h + 1) * 64], Tt.t[:, h, 64:128], o["X"].t[:, h, :], True, True, Tt.b + o["X"].b, [bC])
                S.dve(lambda e: e.tensor_tensor(out=Wv, in0=v3(PC[0:64, :]), in1=Wv, op=ALU.add), [bC] + Wt.b, Wt.b)
                if CUT < 6:
                    return
                TT = Wt
                S.dve(lambda e: e.tensor_tensor(out=o["beg"].t[:], in0=b, in1=o["eg"].t[:], op=ALU.mult), bgt.b + o["eg"].b, o["beg"].b)
                S.pool(lambda e: e.tensor_tensor(out=o["bv"].t[:], in0=vt.t[:], in1=bc_h(b), op=ALU.mult), vt.b + bgt.b, o["bv"].b)
                S.pool(lambda e: e.tensor_tensor(out=o["bek"].t[:], in0=kt.t[:], in1=bc_h(o["beg"].t[:]), op=ALU.mult), kt.b + o["beg"].b, o["bek"].b)
                kg, qg, gl = o["kg"][r], o["qgT"][r], o["glB"][r]
                S.pool(lambda e: e.tensor_tensor(out=kg.t[:], in0=kt.t[:], in1=bc_h(o["decayT"].t[:, :, last]), op=ALU.mult), kt.b + o["decayT"].b, kg.b)
                S.dve(lambda e: e.tensor_tensor(out=qg.t[:], in0=qc, in1=o["egB"].t[:], op=ALU.mult), qk.b + o["egB"].b, qg.b)
                S.act(lambda e: e.copy(out=gl.t[:], in_=o["egB"].t[:, :, last]), o["egB"].b, gl.b)
                yield
                for h in range(8):
                    mm(PA[0:64, h * 64:(h + 1) * 64], TT.t[:, h, 64:128], o["bv"].t[:, h, :], True, True, TT.b + o["bv"].b, [bA])
                for h in range(8):
                    mm(PB[0:64, h * 64:(h + 1) * 64], o["bek"].t[:, h, :], TT.t[:, h, 64:128], True, True, TT.b + o["bek"].b, [bB])
                up, wT = o["up"][r], o["wT"][r]
                S.act(lambda e: e.copy(out=up.t[:], in_=v3(PA[0:64, :])), [bA], up.b)
                S.dve(lambda e: e.tensor_copy(out=wT.t[:], in_=v3(PB[0:64, :])), [bB], wT.b)

            def b3(d, c, n):
                o = D_[d]
                yield
                PA, PB, PC = bank(4 * d), bank(4 * d + 1), bank(4 * d + 2)
                bA, bB, bC = pbuf[4 * d], pbuf[4 * d + 1], pbuf[4 * d + 2]
                tok0 = s0 + c * 64
                bi, cb = tok0 // 512, (tok0 % 512) // 64
                r = n % 2
                up, wT, kg, qg, gl, at = o["up"][r], o["wT"][r], o["kg"][r], o["qgT"][r], o["glB"][r], o["attnT"][r]
                for h in range(8):
                    mm(PA[0:64, h * 64:(h + 1) * 64], wT.t[:, h, :], o["sbf"].t[:, h, :], True, True, wT.b + o["sbf"].b, [bA])
                if CUT < 8:
                    return
                S.dve(lambda e: e.tensor_tensor(out=o["u"].t[:], in0=up.t[:], in1=v3(PA[0:64, :]), op=ALU.subtract), up.b + [bA], o["u"].b)
                yield
                if CUT < 9:
                    return
                for h in range(8):
                    mm(PC[0:64, h * 64:(h + 1) * 64], o["sbf"].t[:, h, :], qg.t[:, h, :], True, False, o["sbf"].b + qg.b, [bC])
                    mm(PC[0:64, h * 64:(h + 1) * 64], o["u"].t[:, h, :], at.t[:, h, :], False, True, o["u"].b + at.b, [bC])
                if CUT < 10:
                    return
                for h in range(8):
                    mm(PB[0:64, h * 64:(h + 1) * 64], kg.t[:, h, :], o["u"].t[:, h, :], True, True, kg.b + o["u"].b, [bB])
                if CUT < 11:
                    return
                ost = o["ost"][0]
                c4 = (tok0 % 256) // 64
                S.act(lambda e: e.copy(out=ost.t[:, :, c4 * 64:(c4 + 1) * 64], in_=v3(PC[0:64, :])), [bC], ost.b)
                S.dve(lambda e: e.tensor_tensor(out=o["s1"].t[:], in0=o["s"].t[:], in1=bc_h(gl.t[:]), op=ALU.mult), o["s"].b + gl.b, o["s1"].b)
                S.dve(lambda e: e.tensor_tensor(out=o["s"].t[:], in0=o["s1"].t[:], in1=v3(PB[0:64, :]), op=ALU.add), o["s1"].b + [bB], o["s"].b)
                S.act(lambda e: e.copy(out=o["sbf"].t[:], in_=o["s"].t[:]), o["s"].b, o["sbf"].b)
                if CUT < 12:
                    return
                if c4 == (3 if d == 0 else 0):
                    lo = (tok0 // 256) * 256
                    for h in range(8):
                        S.dma("sp", oT_d[d, h * 64:(h + 1) * 64, lo:lo + 256], ost.t[:, h, :], reads=ost.b, writes=[db("oT", bi)])

            def run_gens(gs):
                while gs:
                    for g_ in list(gs):
                        try:
                            next(g_)
                        except StopIteration:
                            gs.remove(g_)

            for n in range(N):
                run_gens([b2(0, n, n), b2(1, N - 1 - n, n)])
                run_gens([b3(0, n, n), b3(1, N - 1 - n, n)])
            if not is_s:
                for d in range(2):
                    S.dma("sp", nst_d[si, l, d].rearrange("h k v -> k h v"), D_[d]["s"].t[:], reads=D_[d]["s"].b)
        S.barrier()

    def b4_make(l):
        of = [TL([64, 8, 512], F32) for _ in range(2)]
        ob = [TL([64, 8, 512], F32) for _ in range(2)]
        zt = [TL([64, 8, 512], F32) for _ in range(2)]
        sqb = [TL([64, 512], BF16) for _ in range(2)]
        rs = [TL([64, 512], F32) for _ in range(2)]
        yo = [TL([64, 8, 512], BF16) for _ in range(2)]
        SB4 = 3

        def block(bi):
            r = bi % 2
            sl = slice(bi * 512, (bi + 1) * 512)
            for h in range(8):
                S.dma("sp", of[r].t[:, h, :], oT_d[0, h * 64:(h + 1) * 64, sl], reads=[db("oT", bi)], writes=of[r].b)
                S.dma("sp", ob[r].t[:, h, :], oT_d[1, h * 64:(h + 1) * 64, sl], reads=[db("oT", bi)], writes=ob[r].b)
                S.dma("sp", zt[r].t[:, h, :], zs_d[h * 64:(h + 1) * 64, sl], reads=[db("zs", bi)], writes=zt[r].b)
            S.dve(lambda e, r=r: e.tensor_tensor(out=of[r].t[:], in0=of[r].t[:], in1=ob[r].t[:], op=ALU.add), of[r].b + ob[r].b, of[r].b)
            for h in range(8):
                s_ = sqb[h % 2]
                r2 = rs[h % 2]
                S.act(lambda e, s_=s_, h=h, r=r: e.activation(out=s_.t[:], in_=of[r].t[:, h, :], func=AF.Square), of[r].b, s_.b)
                mm(bank(SB4)[0:64, :], ones_bf.t[0:64, 0:64], s_.t[:], True, True, ones_bf.b + s_.b, [pbuf[SB4]])
                S.act(lambda e, r2=r2: e.activation(out=r2.t[:], in_=bank(SB4)[0:64, :], func=AF.Ln, bias=EPS, scale=1.0 / 64), [pbuf[SB4]], r2.b)
                S.act(lambda e, r2=r2: e.activation(out=r2.t[:], in_=r2.t[:], func=AF.Exp, scale=-0.5), r2.b, r2.b)
                S.dve(lambda e, r2=r2, h=h, r=r: e.scalar_tensor_tensor(out=r2.t[:], in0=of[r].t[:, h, :], scalar=anw.t[0:64, l:l + 1], in1=r2.t[:],
                                                                      op0=ALU.mult, op1=ALU.mult), of[r].b + r2.b + anw.b, r2.b)
                S.pool(lambda e, r2=r2, h=h, r=r: e.tensor_tensor(out=yo[r].t[:, h, :], in0=r2.t[:], in1=zt[r].t[:, h, :], op=ALU.mult), r2.b + zt[r].b, yo[r].b)
            for h in range(8):
                S.dma("sp", yT_d[0, h * 64:(h + 1) * 64, sl], yo[r].t[:, h, :], reads=yo[r].b, writes=[db("yT", bi)])

        return block

    def phase_C(l):
        ar.reset()
        b4_block = b4_make(l)
        b4_next = [0]
        NKT = Tk // 128
        KT = [TL([128, Tk], BF16) for _ in range(2)]
        V1 = [TL([128, NKT, 2, 65], BF16) for _ in range(2)]
        QT = [TL([128, 4, 512], BF16) for _ in range(2)]
        PTt = [TL([128, 512], BF16) for _ in range(6)]
        ckt = [TL([128, 128], F32) for _ in range(2)]
        kcs = TL([128, 512], BF16)
        rr = [TL([128, 512], F32) for _ in range(2)]
        bcs = [TL([64, 512], F32) for _ in range(2)]
        yst = [TL([64, 512], BF16) for _ in range(2)]
        for a in range(2):
            S.pool(lambda e, a=a: e.memset(V1[a].t[:, :, :, 64:65], 1.0), (), V1[a].b)
            S.dma("sp", KT[a].t[:, 0:512], kT_d[a, :, 0:512], reads=[db("kT", i) for i in range(NB)], writes=KT[a].b)
            S.dma("sp", KT[a].t[:, 1024:Tk], kT_d[a, :, 1024:Tk], reads=[db("kT", i) for i in range(NB)], writes=KT[a].b)
            for kv in range(2):
                S.dma("sp", V1[a].t[:, 0:4, kv, 0:64], vt_d[a, 0:512, kv * 64:(kv + 1) * 64].rearrange("(n p) d -> p n d", p=128),
                      reads=[db("vt", i) for i in range(NB)], writes=V1[a].b)
                S.dma("sp", V1[a].t[:, 8:NKT, kv, 0:64], vt_d[a, 1024:Tk, kv * 64:(kv + 1) * 64].rearrange("(n p) d -> p n d", p=128),
                      reads=[db("vt", i) for i in range(NB)], writes=V1[a].b)
                S.dma("pool", V1[a].t[:, 4:8, kv, 0:64], cv_d[a][l, :, kv * 64:(kv + 1) * 64].rearrange("(n p) d -> p n d", p=128), writes=V1[a].b)
            for kt_ in range(4):
                c_ = ckt[kt_ % 2]
                S.dma("sp", c_.t[:], ck_d[a][l, kt_ * 128:(kt_ + 1) * 128, :], writes=c_.b)
                tr(bank(7)[:, kt_ * 128:(kt_ + 1) * 128], c_.t[:], ident, c_.b + cst.b, [pbuf[7]])
            S.dve(lambda e, a=a: e.tensor_copy(out=KT[a].t[:, 512:1024], in_=bank(7)), [pbuf[7]], KT[a].b)
        cnt = {"s": 0, "p": 0, "acc": 0, "q": 0}
        for (s0, T, is_s) in seqs:
            QB = min(T, 512)
            for qb in range(T // QB):
                q0 = s0 + qb * QB
                bi = q0 // 512
                for a in range(2):
                    if a == 0 and b4_next[0] < NB:
                        b4_block(b4_next[0])
                        b4_next[0] += 1
                    cnt["q"] += 1
                    Q = QT[cnt["q"] % 2]
                    S.dma("sp", Q.t[:, :, 0:QB], qT_d[a].rearrange("(m p) t -> p m t", p=128)[:, :, q0:q0 + QB], reads=[db("qT%d" % a, bi)], writes=Q.b)
                    chunks = []
                    if not is_s:
                        chunks = [(s0 // 128 + j, 0, QB, None) for j in range(T // 128)]
                    elif a == 0:
                        chunks = [(4 + j, 0, QB, None) for j in range(4)] + [(8 + j, 0, QB, None) for j in range(T // 128)]
                    else:
                        ctx = [(4 + j, 0, QB, None) for j in range(4)]
                        loc = []
                        for kc in range(T // 128):
                            lo = max((kc - 1) * 128, qb * QB)
                            hi = min((kc + 2) * 128, (qb + 1) * QB)
                            if lo >= hi:
                                continue
                            loc.append((8 + kc, lo - qb * QB, hi - qb * QB, (lo - (kc - 1) * 128)))
                        chunks = ctx[:1] + loc + ctx[1:]
                    nck = len(chunks)
                    items = [(h, ci) for h in range(8) for ci in range(nck)]
                    st_ = {}
                    accb = {}

                    def stage1(h, ci, Q=Q, a=a, chunks=chunks):
                        m, half = h % 4, h // 4
                        pb0 = 64 * half
                        if ci == 0:
                            cnt["acc"] += 1
                            accb[h] = 4 + cnt["acc"] % 2
                        kti, qlo, qhi, mcol = chunks[ci]
                        cnt["s"] += 1
                        sbk = cnt["s"] % 3
                        nq = qhi - qlo
                        mm(bank(sbk)[:, 0:nq], KT[a].t[pb0:pb0 + 64, kti * 128:(kti + 1) * 128], Q.t[pb0:pb0 + 64, m, qlo:qhi], True, True,
                           KT[a].b + Q.b, [pbuf[sbk]])
                        cnt["p"] += 1
                        Pt = PTt[cnt["p"] % 6]
                        S.act(lambda e, Pt=Pt, sbk=sbk, nq=nq: e.activation(out=Pt.t[:, 0:nq], in_=bank(sbk)[:, 0:nq], func=AF.Exp, scale=HD ** -0.5),
                              [pbuf[sbk]], Pt.b)
                        if mcol is not None and not (mcol == 128 and nq == 128):
                            S.dve(lambda e, Pt=Pt, nq=nq, mcol=mcol: e.tensor_tensor(out=Pt.t[:, 0:nq], in0=Pt.t[:, 0:nq], in1=mw_bf.t[:, mcol:mcol + nq], op=ALU.mult),
                                  Pt.b + mw_bf.b, Pt.b)
                        st_[(h, ci)] = Pt

                    def stage2(h, ci, a=a, chunks=chunks, nck=nck, QB=QB, q0=q0, bi=bi):
                        half = h // 4
                        kti, qlo, qhi, mcol = chunks[ci]
                        nq = qhi - qlo
                        Pt = st_.pop((h, ci))
                        ab = accb[h]
                        ACC = bank(ab)
                        mm(ACC[0:65, qlo:qhi], V1[a].t[:, kti, half, :], Pt.t[:, 0:nq], ci == 0, ci == nck - 1, V1[a].b + Pt.b, [pbuf[ab]])
                        if ci != nck - 1:
                            return
                        r_ = rr[h % 2]
                        if a == 1:
                            S.act(lambda e, r_=r_, ACC=ACC, h=h: e.activation(out=r_.t[64:65, 0:QB], in_=ACC[64:65, 0:QB], func=AF.Ln,
                                                                          bias=esink.t[64:65, l * 8 + h:l * 8 + h + 1]), [pbuf[ab]] + esink.b, r_.b)
                        else:
                            S.act(lambda e, r_=r_, ACC=ACC: e.activation(out=r_.t[64:65, 0:QB], in_=ACC[64:65, 0:QB], func=AF.Ln), [pbuf[ab]], r_.b)
                        S.act(lambda e, r_=r_: e.activation(out=r_.t[64:65, 0:QB], in_=r_.t[64:65, 0:QB], func=AF.Exp, scale=-1.0), r_.b, r_.b)
                        bb = 6 + h % 2
                        mm(bank(bb)[0:64, 0:QB], ones_f[64:65, 0:64], r_.t[64:65, 0:QB], True, True, cst.b + r_.b, [pbuf[bb]])
                        bc_ = bcs[h % 2]
                        S.dve(lambda e, bc_=bc_, bb=bb: e.tensor_copy(out=bc_.t[:, 0:QB], in_=bank(bb)[0:64, 0:QB]), [pbuf[bb]], bc_.b)
                        y_ = yst[h % 2]
                        S.dve(lambda e, y_=y_, ACC=ACC, bc_=bc_: e.tensor_tensor(out=y_.t[:, 0:QB], in0=ACC[0:64, 0:QB], in1=bc_.t[:, 0:QB], op=ALU.mult),
                              [pbuf[ab]] + bc_.b, y_.b)
                        S.dma("sp", yT_d[1 + a, h * 64:(h + 1) * 64, q0:q0 + QB], y_.t[:, 0:QB], reads=y_.b, writes=[db("yT", bi)])

                    LA = 2
                    for i in range(len(items) + LA):
                        if i < len(items):
                            stage1(*items[i])
                        if i >= LA:
                            stage2(*items[i - LA])
        while b4_next[0] < NB:
            b4_block(b4_next[0])
            b4_next[0] += 1
        S.barrier()

    def phase_D1(l):
        ar.reset()
        wbr = [TL([128, 4, D], BF16) for _ in range(3)]
        wo = TL([128, 8, D], BF16)
        for j in range(3):
            load_w_bf16(wbr[j], lambda c, j=j: wbr_d[j][l, c * 128:(c + 1) * 128, :], 4)
        load_w_bf16(wo, lambda c: wo_d[l, c * 128:(c + 1) * 128, :], 8)
        yt = [TL([128, 12, 512], BF16) for _ in range(2)]
        sg = [TL([128, 24, 512], BF16) for _ in range(2)]
        xT = [TL([128, 8, 512], F32) for _ in range(2)]
        mg = TL([128, 8, 512], BF16)
        hT = TL([128, 8, 512], BF16)
        ta = [TL([128, 512], F32) for _ in range(2)]
        tb_ = [TL([128, 512], F32) for _ in range(2)]
        tc = [TL([128, 512], F32) for _ in range(2)]
        sq = [TL([128, 512], BF16) for _ in range(2)]
        tmp = [TL([128, 512], F32) for _ in range(2)]
        lnv = TL([128, 512], F32)
        rstd = TL([128, 512], F32)

        def loads(bi):
            r = bi % 2
            sl = slice(bi * 512, (bi + 1) * 512)
            S.dma("sp", yt[r].t[:], yT_d.rearrange("j (c p) t -> p (j c) t", p=128)[:, :, sl], reads=[db("yT", bi)], writes=yt[r].b)
            S.dma("sp", sg[r].t[:], sig_d.rearrange("(c p) t -> p c t", p=128)[:, :, sl], reads=[db("sig", bi)], writes=sg[r].b)
            S.dma("sp", xT[r].t[:], xT_d.rearrange("(c p) t -> p c t", p=128)[:, :, sl], reads=[db("xT", bi)], writes=xT[r].b)

        loads(0)
        for bi in range(NB):
            r = bi % 2
            mv = 0 if bi == 0 else 1
            if bi + 1 < NB:
                loads(bi + 1)
            x_ = xT[r]
            for oc in range(8):
                for j in range(3):
                    for c in range(4):
                        mm(bank(j), wbr[j].t[:, c, oc * 128:(oc + 1) * 128], yt[r].t[:, j * 4 + c, :], c == 0, c == 3, wbr[j].b + yt[r].b, [pbuf[j]])
                a_, b_, c_ = ta[oc % 2], tb_[oc % 2], tc[oc % 2]
                S.dve(lambda e, a_=a_, oc=oc, r=r: e.tensor_tensor(out=a_.t[:], in0=bank(0), in1=sg[r].t[:, oc, :], op=ALU.mult), [pbuf[0]] + sg[r].b, a_.b)
                S.dve(lambda e, b_=b_, oc=oc, r=r: e.tensor_tensor(out=b_.t[:], in0=bank(1), in1=sg[r].t[:, 8 + oc, :], op=ALU.mult), [pbuf[1]] + sg[r].b, b_.b)
                S.dve(lambda e, c_=c_, oc=oc, r=r: e.tensor_tensor(out=c_.t[:], in0=bank(2), in1=sg[r].t[:, 16 + oc, :], op=ALU.mult), [pbuf[2]] + sg[r].b, c_.b)
                S.pool(lambda e, a_=a_, b_=b_: e.tensor_tensor(out=a_.t[:], in0=a_.t[:], in1=b_.t[:], op=ALU.add), a_.b + b_.b, a_.b)
                S.pool(lambda e, a_=a_, c_=c_, oc=oc: e.tensor_tensor(out=mg.t[:, oc, :], in0=a_.t[:], in1=c_.t[:], op=ALU.add), a_.b + c_.b, mg.b)
            for oc in range(8):
                bk = 3 + oc % 2
                for c in range(8):
                    mm(bank(bk), wo.t[:, c, oc * 128:(oc + 1) * 128], mg.t[:, c, :], c == 0, c == 7, wo.b + mg.b, [pbuf[bk]])
                S.dve(lambda e, oc=oc, bk=bk, x_=x_, mv=mv: e.scalar_tensor_tensor(out=x_.t[:, oc, :], in0=bank(bk), scalar=modT.t[:, 16 + oc, mv:mv + 1],
                                                                                 in1=x_.t[:, oc, :], op0=ALU.mult, op1=ALU.add),
                      [pbuf[bk]] + x_.b + modT.b, x_.b)
            norm_block(x_, hT, mv, A2, 24, sq, tmp, 7, lnv, rstd)
            sl = slice(bi * 512, (bi + 1) * 512)
            S.dma("sp", xT_d.rearrange("(c p) t -> p c t", p=128)[:, :, sl], x_.t[:], reads=x_.b, writes=[db("xT", bi)])
            S.dma("sp", h2T_d.rearrange("(c p) t -> p c t", p=128)[:, :, sl], hT.t[:], reads=hT.b, writes=[db("h2T", bi)])
        S.barrier()

    def phase_D2(l):
        ar.reset()
        HH = DFF // 2
        w1 = [TL([128, 8, HH], BF16) for _ in range(2)]
        w2s = TL([128, 16, D], BF16)
        w2 = [w2s, w2s]
        load_w_bf16(w1[0], lambda c: wf1_d[l, c * 128:(c + 1) * 128, 0:HH], 8)
        load_w_bf16(w2s, lambda c: wf2_d[l, c * 128:(c + 1) * 128, :], 16)
        load_w_bf16(w1[1], lambda c: wf1_d[l, c * 128:(c + 1) * 128, HH:2 * HH], 8)
        hT = [TL([128, 8, 512], BF16) for _ in range(2)]
        xT = [TL([128, 8, 512], F32) for _ in range(2)]
        rl = [TL([128, 512], BF16) for _ in range(3)]
        aT = TL([128, 16, 512], BF16)
        final = (l == depth - 1)
        xof = [TL([128, D], F32) for _ in range(2)] if final else None
        cnt_ = [0]

        def loads(bi):
            r = cnt_[0] % 2
            cnt_[0] += 1
            sl = slice(bi * 512, (bi + 1) * 512)
            S.dma("sp", hT[r].t[:], h2T_d.rearrange("(c p) t -> p c t", p=128)[:, :, sl], reads=[db("h2T", bi)], writes=hT[r].b)
            S.dma("sp", xT[r].t[:], xT_d.rearrange("(c p) t -> p c t", p=128)[:, :, sl], reads=[db("xT", bi)], writes=xT[r].b)
            return r

        seq_ = [(hf, bi) for hf in range(2) for bi in range(NB)]
        rnext = loads(seq_[0][1])
        for si_, (hf, bi) in enumerate(seq_):
            r = rnext
            mv = 0 if bi == 0 else 1
            if si_ + 1 < len(seq_):
                rnext = loads(seq_[si_ + 1][1])
            if si_ == NB:
                load_w_bf16(w2s, lambda c: wf2_d[l, HH + c * 128:HH + (c + 1) * 128, :], 16)
            x_ = xT[r]
            for oc in range(16):
                bk = oc % 3
                for c in range(8):
                    mm(bank(bk), w1[hf].t[:, c, oc * 128:(oc + 1) * 128], hT[r].t[:, c, :], c == 0, c == 7, w1[hf].b + hT[r].b, [pbuf[bk]])
                r_ = rl[oc % 3]
                S.act(lambda e, r_=r_, bk=bk: e.activation(out=r_.t[:], in_=bank(bk), func=AF.Relu), [pbuf[bk]], r_.b)
                S.pool(lambda e, r_=r_, oc=oc: e.tensor_tensor(out=aT.t[:, oc, :], in0=r_.t[:], in1=r_.t[:], op=ALU.mult), r_.b, aT.b)
            for oc in range(8):
                bk = 3 + oc % 2
                for c in range(16):
                    mm(bank(bk), w2[hf].t[:, c, oc * 128:(oc + 1) * 128], aT.t[:, c, :], c == 0, c == 15, w2[hf].b + aT.b, [pbuf[bk]])
                S.dve(lambda e, oc=oc, bk=bk, x_=x_, mv=mv: e.scalar_tensor_tensor(out=x_.t[:, oc, :], in0=bank(bk), scalar=modT.t[:, 40 + oc, mv:mv + 1],
                                                                                 in1=x_.t[:, oc, :], op0=ALU.mult, op1=ALU.add),
                      [pbuf[bk]] + x_.b + modT.b, x_.b)
            sl = slice(bi * 512, (bi + 1) * 512)
            if not (final and hf == 1):
                S.dma("sp", xT_d.rearrange("(c p) t -> p c t", p=128)[:, :, sl], x_.t[:], reads=x_.b, writes=[db("xT", bi)])
            else:
                for tt in range(4):
                    for c in range(8):
                        bk = 5 + (c // 4)
                        tr(bank(bk)[:, (c % 4) * 128:(c % 4 + 1) * 128], x_.t[:, c, tt * 128:(tt + 1) * 128], ident, x_.b + cst.b, [pbuf[bk]])
                    xo_ = xof[tt % 2]
                    S.act(lambda e, xo_=xo_: e.copy(out=xo_.t[:, 0:512], in_=bank(5)), [pbuf[5]], xo_.b)
                    S.dve(lambda e, xo_=xo_: e.tensor_copy(out=xo_.t[:, 512:1024], in_=bank(6)), [pbuf[6]], xo_.b)
                    t0 = bi * 512 + tt * 128
                    dst = yp_d[t0:t0 + 128, :] if bi == 0 else ys_d[t0 - 512:t0 - 512 + 128, :]
                    S.dma("sp", dst, xo_.t[:], reads=xo_.b)
        S.barrier()

    plist = [("M", phase_M), ("A", phase_A), ("B1", phase_B1), ("B23", phase_B23), ("C", phase_C), ("D1", phase_D1),
             ("D2", phase_D2)]
    done = False
    for l in range(depth):
        for nm_, fn_ in plist:
            if stop is not None and nm_ == stop:
                done = True
                break
            fn_(l)
        if done:
            break
    n_ops = len(S.ops)
    S.emit()
    return nc, n_ops


DEPTH = 4
DEC_SEQ = 4096
_cache = {}


def make_in_maps(inputs, depth, T_s, n_cores=8):
    f = lambda a: np.ascontiguousarray(np.asarray(a, dtype=np.float32))
    cst, rope = make_consts(T_s)
    maps = []
    for core in range(n_cores):
        b = core % 2
        vecs = np.zeros((16, D), np.float32)
        vecs[0] = inputs["c_ctx"]
        vecs[1] = inputs["c"][b]
        vecs[2:2 + depth] = inputs["ln1"]
        vecs[2 + depth:2 + 2 * depth] = inputs["ln2"]
        m = {
            "xp": f(inputs["x_prompt"][2 * core:2 * core + 2]).reshape(2 * TP, D),
            "xs": f(inputs["x_sample"][b]),
            "ckg": f(inputs["cache_k_glob"][b]).reshape(depth, PAST, 128),
            "ckw": f(inputs["cache_k_win"][b]).reshape(depth, PAST, 128),
            "cvg": f(inputs["cache_v_glob"][b]).reshape(depth, PAST, 128),
            "cvw": f(inputs["cache_v_win"][b]).reshape(depth, PAST, 128),
            "sd": f(inputs["state_delta"][b]),
            "vecs": vecs,
            "w_mod": f(inputs["w_mod"]), "b_mod": f(inputs["b_mod"]), "w_in": f(inputs["w_in"]),
            "conv": f(inputs["conv_qkv"]).reshape(depth * 5, 1536),
            "a_log": f(inputs["a_log"]).reshape(depth, 16), "dt_bias": f(inputs["dt_bias"]).reshape(depth, 16),
            "a_norm": f(inputs["a_norm"]), "qk_norm": f(inputs["qk_norm"]).reshape(depth * 4, 64), "sink": f(inputs["sink"]),
            "w_br_a": f(inputs["w_br_a"]), "w_br_b": f(inputs["w_br_b"]), "w_br_c": f(inputs["w_br_c"]),
            "w_o": f(inputs["w_o"]), "w_ff1": f(inputs["w_ff1"]), "w_ff2": f(inputs["w_ff2"]),
            "cst": cst, "rope": rope,
        }
        maps.append(m)
    return maps


def assemble(results, depth, T_s):
    yp = np.concatenate([r["yp"].reshape(2, TP, D) for r in results], axis=0)
    ys = np.stack([results[0]["ys"], results[1]["ys"]], axis=0)
    outs = [yp.astype(np.float32), ys.astype(np.float32)]
    for nm_ in ("nkg", "nvg", "nkw", "nvw"):
        outs.append(np.concatenate([r[nm_].reshape(2, depth, TP, 2, HD) for r in results], axis=0).astype(np.float32))
    outs.append(np.concatenate([r["nst"] for r in results], axis=0).astype(np.float32))
    return tuple(outs)


def kernel(**inputs):
    depth, T_s = DEPTH, DEC_SEQ
    key = (depth, T_s)
    if key not in _cache:
        _cache[key] = build(depth, T_s)[0]
    nc = _cache[key]
    maps = make_in_maps(inputs, depth, T_s)
    res = run_bass_kernel_spmd(nc, maps, core_ids=list(range(8)))
    return assemble(res.results, depth, T_s)
```
